# Optimizing a Trainium2 kernel written in Bass

```python
import math
import jax, jax.numpy as jnp
from jax import lax
import numpy as np

D_MODEL = 2048
BATCH = 2
SEQ = 4096
DEPTH = 2

GRID_W = 64
CTX_LEN = 256
HEAD_DIM = 128
N_HEAD_SLOTS = D_MODEL // HEAD_DIM
MLSTM_HEADS = N_HEAD_SLOTS // 4
MLSTM_DK = HEAD_DIM
MLSTM_DV = HEAD_DIM
GLA_HEADS = N_HEAD_SLOTS // 4
GLA_DK = HEAD_DIM // 2
GLA_DV = HEAD_DIM
GLA_RANK = 16
GLA_TAU = 16.0
DIFF_HEADS = N_HEAD_SLOTS // 2
DIFF_DQK = HEAD_DIM // 2
DIFF_DV = HEAD_DIM
CONV_W = 3
CHUNK = 64
Q_BLOCK = 128
ROPE_BASE = 10000.0
ROPE_AXIS_DIM = DIFF_DQK // 2
FFN_HIDDEN = ((8 * D_MODEL + 3 * 256 - 1) // (3 * 256)) * 256
NORM_EPS = 1e-6

M_QK = MLSTM_HEADS * MLSTM_DK
M_V = MLSTM_HEADS * MLSTM_DV
M_GATES = 2 * 2 * MLSTM_HEADS
G_QK = GLA_HEADS * GLA_DK
G_V = GLA_HEADS * GLA_DV
G_LR = 2 * GLA_RANK
D_QK = DIFF_HEADS * 2 * DIFF_DQK
D_V = DIFF_HEADS * DIFF_DV
MIX_SPLITS = (M_QK, M_QK, M_V, M_V, M_GATES, G_QK, G_QK, G_V, G_V, G_LR, D_QK, D_QK, D_V)
MIX_OFFSETS = [int(o) for o in np.cumsum(MIX_SPLITS)[:-1]]
IN_COLS = sum(MIX_SPLITS)
MIX_WIDTH = M_V + G_V + D_V

kernel_name = "hybrid_mlstm_gla_diffattn_dit_block"


def rmsnorm(x, w):
    xf = x.astype(jnp.float32)
    y = xf * lax.rsqrt(jnp.mean(xf * xf, axis=-1, keepdims=True) + NORM_EPS)
    return (y * w.astype(jnp.float32)).astype(x.dtype)


def split_heads(a, n_heads):
    b, n, _ = a.shape
    return a.reshape(b, n, n_heads, -1).transpose(0, 2, 1, 3)


def merge_heads(a):
    b, h, n, d = a.shape
    return a.transpose(0, 2, 1, 3).reshape(b, n, h * d)


def centred_conv(a, w, bias):
    half = CONV_W // 2
    n = a.shape[1]
    ap = jnp.pad(a, ((0, 0), (half, half), (0, 0)))
    return sum(ap[:, j:j + n] * w[j] for j in range(CONV_W)) + bias


def axial_rope_tables(n_tokens):
    rows = n_tokens // GRID_W
    row = jnp.repeat(jnp.arange(rows, dtype=jnp.float32), GRID_W)
    col = jnp.tile(jnp.arange(GRID_W, dtype=jnp.float32), rows)
    half = ROPE_AXIS_DIM // 2
    inv_freq = ROPE_BASE ** (-jnp.arange(half, dtype=jnp.float32) / half)
    ang_r = row[:, None] * inv_freq
    ang_c = col[:, None] * inv_freq
    ang = jnp.concatenate([ang_r, ang_r, ang_c, ang_c], axis=-1)
    return jnp.cos(ang), jnp.sin(ang)


def apply_rope(x, cos, sin):
    def rot_half(a):
        a1, a2 = jnp.split(a, 2, axis=-1)
        return jnp.concatenate([-a2, a1], axis=-1)
    x_row, x_col = jnp.split(x, 2, axis=-1)
    rotated = jnp.concatenate([rot_half(x_row), rot_half(x_col)], axis=-1)
    cs = cos[None, :, None, None, :]
    sn = sin[None, :, None, None, :]
    return (x * cs + rotated * sn).astype(x.dtype)


def to_chunks(a):
    b, h, t = a.shape[:3]
    return jnp.moveaxis(a.reshape(b, h, t // CHUNK, CHUNK, *a.shape[3:]), 2, 0)


def from_chunks(a):
    nc, b, h, l, d = a.shape
    return jnp.moveaxis(a, 0, 2).reshape(b, h, nc * l, d)


def mlstm_scan(q, k, v, log_i, log_f):
    f32 = jnp.float32
    q, k, v = q.astype(f32), k.astype(f32), v.astype(f32)
    b, h, _, dk = q.shape
    dv = v.shape[-1]
    tril = jnp.tril(jnp.ones((CHUNK, CHUNK), dtype=bool))

    def step(carry, inp):
        c_state, n_state, m_state = carry
        qc, kc, vc, ic, fc = inp
        cum_f = jnp.cumsum(fc, axis=-1)
        dmat = jnp.where(tril, cum_f[..., :, None] - cum_f[..., None, :] + ic[..., None, :], -jnp.inf)
        m_inter = cum_f + m_state[..., None]
        m_t = jnp.maximum(m_inter, jnp.max(dmat, axis=-1))
        w_inter = jnp.exp(m_inter - m_t)
        s = jnp.einsum('bhtd,bhsd->bhts', qc, kc) * jnp.exp(dmat - m_t[..., None])
        num = w_inter[..., None] * jnp.einsum('bhtd,bhde->bhte', qc, c_state) + jnp.einsum('bhts,bhse->bhte', s, vc)
        den = w_inter * jnp.einsum('bhtd,bhd->bht', qc, n_state) + jnp.sum(s, axis=-1)
        out = num / jnp.maximum(jnp.abs(den), jnp.exp(-m_t))[..., None]
        f_end = cum_f[..., -1]
        dec = f_end[..., None] - cum_f + ic
        m_new = jnp.maximum(f_end + m_state, jnp.max(dec, axis=-1))
        a_prev = jnp.exp(f_end + m_state - m_new)
        ws = jnp.exp(dec - m_new[..., None])
        c_new = a_prev[..., None, None] * c_state + jnp.einsum('bhs,bhsd,bhse->bhde', ws, kc, vc)
        n_new = a_prev[..., None] * n_state + jnp.einsum('bhs,bhsd->bhd', ws, kc)
        return (c_new, n_new, m_new), out

    init = (jnp.zeros((b, h, dk, dv), f32), jnp.zeros((b, h, dk), f32), jnp.zeros((b, h), f32))
    xs = (to_chunks(q), to_chunks(k), to_chunks(v), to_chunks(log_i.astype(f32)), to_chunks(log_f.astype(f32)))
    _, out = lax.scan(step, init, xs)
    return from_chunks(out)


def gla_scan(q, k, v, log_a):
    f32 = jnp.float32
    q, k, v, log_a = q.astype(f32), k.astype(f32), v.astype(f32), log_a.astype(f32)
    b, h, _, dk = q.shape
    dv = v.shape[-1]
    tril = jnp.tril(jnp.ones((CHUNK, CHUNK), dtype=bool))[:, :, None]

    def step(state, inp):
        qc, kc, vc, gc = inp
        g = jnp.cumsum(gc, axis=-2)
        inter = jnp.einsum('bhtd,bhde->bhte', qc * jnp.exp(g), state)
        gap = jnp.where(tril, g[..., :, None, :] - g[..., None, :, :], -jnp.inf)
        att = jnp.einsum('bhtd,bhsd,bhtsd->bhts', qc, kc, jnp.exp(gap))
        out = inter + jnp.einsum('bhts,bhse->bhte', att, vc)
        g_end = g[..., -1:, :]
        state_new = jnp.exp(g_end[..., 0, :])[..., None] * state + jnp.einsum('bhsd,bhse->bhde', kc * jnp.exp(g_end - g), vc)
        return state_new, out

    init = jnp.zeros((b, h, dk, dv), f32)
    _, out = lax.scan(step, init, (to_chunks(q), to_chunks(k), to_chunks(v), to_chunks(log_a)))
    return from_chunks(out)


def join_segments(a_ctx, a_lat, rev):
    if rev:
        a_ctx, a_lat = jnp.flip(a_ctx, 2), jnp.flip(a_lat, 2)
    return jnp.concatenate([a_ctx, a_lat], axis=2)


def bidirectional(scan_fn, ctx_dirs, lat_dirs):
    outs_c, outs_l = [], []
    for d in range(2):
        rev = d == 1
        seqs = [join_segments(a_c, a_l, rev) for a_c, a_l in zip(ctx_dirs[d], lat_dirs[d])]
        out = scan_fn(*seqs)
        n_ctx = ctx_dirs[d][0].shape[2]
        o_c, o_l = out[:, :, :n_ctx], out[:, :, n_ctx:]
        if rev:
            o_c, o_l = jnp.flip(o_c, 2), jnp.flip(o_l, 2)
        outs_c.append(o_c)
        outs_l.append(o_l)
    return outs_c[0] + outs_c[1], outs_l[0] + outs_l[1]


def diff_block(q, k, v, lam):
    s = jnp.einsum('bhqmd,bhkmd->bhmqk', q, k).astype(jnp.float32) * (DIFF_DQK ** -0.5)
    p = jax.nn.softmax(s, axis=-1)
    weights = p[:, :, 0] - lam * p[:, :, 1]
    return jnp.einsum('bhqk,bhkd->bhqd', weights.astype(v.dtype), v)


def diff_attention_blocked(q, k, v, lam):
    b, h, n = q.shape[:3]
    nb = n // Q_BLOCK
    qb = jnp.moveaxis(q.reshape(b, h, nb, Q_BLOCK, 2, DIFF_DQK), 2, 0)
    out = lax.map(lambda q_blk: diff_block(q_blk, k, v, lam), qb)
    return jnp.moveaxis(out, 0, 2).reshape(b, h, n, DIFF_DV)


def token_mixers(pc, pl, conv_w, conv_b, m_gate_b, m_norm, g_w2, g_b, g_norm, d_lam, d_subln,
                 lam_init, rope_cos, rope_sin, need_ctx):
    f32 = jnp.float32

    def mlstm_prep(p):
        mq, mk, mv, _, mg = p[0:5]
        b, n, _ = mq.shape
        qk = jax.nn.silu(centred_conv(jnp.concatenate([mq, mk], axis=-1), conv_w, conv_b))
        q, k = jnp.split(qk, 2, axis=-1)
        gates = (mg + m_gate_b).astype(f32).reshape(b, n, 2, 2, MLSTM_HEADS).transpose(2, 3, 0, 4, 1)
        return (split_heads(q, MLSTM_HEADS) * (MLSTM_DK ** -0.5), split_heads(k, MLSTM_HEADS),
                split_heads(mv, MLSTM_HEADS), gates)

    def mlstm_dir(m, d):
        return (m[0], m[1], m[2], m[3][d, 0], jax.nn.log_sigmoid(m[3][d, 1]))

    m_c, m_l = mlstm_prep(pc), mlstm_prep(pl)
    hm_c, hm_l = bidirectional(mlstm_scan, [mlstm_dir(m_c, d) for d in range(2)],
                               [mlstm_dir(m_l, d) for d in range(2)])
    m_w = m_norm.reshape(MLSTM_HEADS, 1, MLSTM_DV)

    def mlstm_out(hm, p):
        return merge_heads(rmsnorm(hm, m_w)) * jax.nn.sigmoid(p[3])

    def gla_prep(p):
        gq, gk, gv, _, glr = p[5:10]
        b, n, _ = gq.shape
        lr = glr.reshape(b, n, 2, GLA_RANK)
        log_a = [split_heads(jax.nn.log_sigmoid((jnp.einsum('bnr,rk->bnk', lr[:, :, d], g_w2[d]) + g_b[d]).astype(f32)) / GLA_TAU, GLA_HEADS)
                 for d in range(2)]
        return (split_heads(gq, GLA_HEADS) * (GLA_DK ** -0.5), split_heads(gk, GLA_HEADS),
                split_heads(gv, GLA_HEADS), log_a)

    g_c, g_l = gla_prep(pc), gla_prep(pl)
    hg_c, hg_l = bidirectional(gla_scan, [(g_c[0], g_c[1], g_c[2], g_c[3][d]) for d in range(2)],
                               [(g_l[0], g_l[1], g_l[2], g_l[3][d]) for d in range(2)])
    g_w = g_norm.reshape(GLA_HEADS, 1, GLA_DV)

    def gla_out(hg, p):
        return merge_heads(rmsnorm(hg, g_w)) * jax.nn.silu(p[8])

    def diff_prep(p, rotary):
        dq, dk, dv = p[10:13]
        b, n, _ = dq.shape
        q = dq.reshape(b, n, DIFF_HEADS, 2, DIFF_DQK)
        k = dk.reshape(b, n, DIFF_HEADS, 2, DIFF_DQK)
        if rotary:
            q, k = apply_rope(q, rope_cos, rope_sin), apply_rope(k, rope_cos, rope_sin)
        return q.transpose(0, 2, 1, 3, 4), k.transpose(0, 2, 1, 3, 4), split_heads(dv, DIFF_HEADS)

    lam = (jnp.exp(jnp.sum(d_lam[0] * d_lam[1])) - jnp.exp(jnp.sum(d_lam[2] * d_lam[3]))).astype(f32) + lam_init
    q_c, k_c, v_c = diff_prep(pc, False)
    q_l, k_l, v_l = diff_prep(pl, True)
    k_all = jnp.concatenate([k_c, k_l], axis=2)
    v_all = jnp.concatenate([v_c, v_l], axis=2)
    hd_l = diff_attention_blocked(q_l, k_all, v_all, lam)

    def diff_out(hd):
        return merge_heads(rmsnorm(hd, d_subln) * (1.0 - lam_init))

    y_lat = jnp.concatenate([mlstm_out(hm_l, pl), gla_out(hg_l, pl), diff_out(hd_l)], axis=-1)
    y_ctx = None
    if need_ctx:
        hd_c = diff_block(q_c, k_c, v_c, lam)
        y_ctx = jnp.concatenate([mlstm_out(hm_c, pc), gla_out(hg_c, pc), diff_out(hd_c)], axis=-1)
    return y_ctx, y_lat


def swiglu(h, w_gate, w_up, w_down):
    return (jax.nn.silu(h @ w_gate) * (h @ w_up)) @ w_down


def hybrid_layer(xc, xl, mod_c, mod_l, n_mix_pre, n_mix_post, n_ffn_pre, n_ffn_post, w_in,
                 conv_w, conv_b, m_gate_b, m_norm, g_w2, g_b, g_norm, d_lam, d_subln, w_out,
                 w_gate, w_up, w_down, lam_init, rope_cos, rope_sin, need_ctx):
    def pre(x, w, mod, i):
        return rmsnorm(x, w) * (1.0 + mod[:, i + 1]) + mod[:, i]

    hc = pre(xc, n_mix_pre, mod_c, 0)
    hl = pre(xl, n_mix_pre, mod_l, 0)
    pc = jnp.split(hc @ w_in, MIX_OFFSETS, axis=-1)
    pl = jnp.split(hl @ w_in, MIX_OFFSETS, axis=-1)
    y_ctx, y_lat = token_mixers(pc, pl, conv_w, conv_b, m_gate_b, m_norm, g_w2, g_b, g_norm,
                                d_lam, d_subln, lam_init, rope_cos, rope_sin, need_ctx)
    xl = xl + mod_l[:, 2] * rmsnorm(y_lat.astype(xl.dtype) @ w_out, n_mix_post)
    xl = xl + mod_l[:, 5] * rmsnorm(swiglu(pre(xl, n_ffn_pre, mod_l, 3), w_gate, w_up, w_down), n_ffn_post)
    if need_ctx:
        xc = xc + mod_c[:, 2] * rmsnorm(y_ctx.astype(xc.dtype) @ w_out, n_mix_post)
        xc = xc + mod_c[:, 5] * rmsnorm(swiglu(pre(xc, n_ffn_pre, mod_c, 3), w_gate, w_up, w_down), n_ffn_post)
    return xc, xl


def setup_inputs(seed: int = 0) -> dict:
    key = jax.random.key(seed)
    ks = iter(jax.random.split(key, 32))
    f32 = jnp.float32

    def nrm(shape, scale):
        return jax.random.normal(next(ks), shape, f32) * scale

    def gain(shape):
        return 1.0 + nrm(shape, 0.05)

    gate_offset = jnp.stack([jnp.zeros((MLSTM_HEADS,), f32), jnp.linspace(3.0, 6.0, MLSTM_HEADS, dtype=f32)])
    gate_offset = jnp.tile(gate_offset[None], (2, 1, 1)).reshape(-1)
    return {
        "x": nrm((BATCH, SEQ, D_MODEL), 1.0),
        "c": nrm((BATCH, D_MODEL), 1.0),
        "ctx": nrm((BATCH, CTX_LEN, D_MODEL), 1.0),
        "c_ctx": nrm((D_MODEL,), 1.0),
        "w_mod": nrm((DEPTH, D_MODEL, 6 * D_MODEL), 0.5 * D_MODEL ** -0.5),
        "b_mod": nrm((DEPTH, 6 * D_MODEL), 0.02),
        "norm_mix_pre": gain((DEPTH, D_MODEL)),
        "norm_mix_post": gain((DEPTH, D_MODEL)),
        "norm_ffn_pre": gain((DEPTH, D_MODEL)),
        "norm_ffn_post": gain((DEPTH, D_MODEL)),
        "w_in": nrm((DEPTH, D_MODEL, IN_COLS), D_MODEL ** -0.5),
        "mlstm_conv_w": nrm((DEPTH, CONV_W, 2 * M_QK), CONV_W ** -0.5),
        "mlstm_conv_b": nrm((DEPTH, 2 * M_QK), 0.02),
        "mlstm_gate_b": gate_offset[None] + nrm((DEPTH, M_GATES), 0.1),
        "mlstm_norm": gain((DEPTH, M_V)),
        "gla_gate_w2": nrm((DEPTH, 2, GLA_RANK, G_QK), GLA_RANK ** -0.5),
        "gla_gate_b": nrm((DEPTH, 2, G_QK), 0.1),
        "gla_norm": gain((DEPTH, G_V)),
        "diff_lambda": nrm((DEPTH, 4, DIFF_DQK), 0.1),
        "diff_subln": gain((DEPTH, DIFF_DV)),
        "w_out": nrm((DEPTH, MIX_WIDTH, D_MODEL), MIX_WIDTH ** -0.5),
        "w_ffn_gate": nrm((DEPTH, D_MODEL, FFN_HIDDEN), D_MODEL ** -0.5),
        "w_ffn_up": nrm((DEPTH, D_MODEL, FFN_HIDDEN), D_MODEL ** -0.5),
        "w_ffn_down": nrm((DEPTH, FFN_HIDDEN, D_MODEL), FFN_HIDDEN ** -0.5),
    }


def reference(x, c, ctx, c_ctx, w_mod, b_mod, norm_mix_pre, norm_mix_post, norm_ffn_pre,
              norm_ffn_post, w_in, mlstm_conv_w, mlstm_conv_b, mlstm_gate_b, mlstm_norm,
              gla_gate_w2, gla_gate_b, gla_norm, diff_lambda, diff_subln, w_out,
              w_ffn_gate, w_ffn_up, w_ffn_down):
    n_lat = x.shape[1]
    rope_cos, rope_sin = axial_rope_tables(n_lat)
    d = x.shape[-1]
    xc, xl = ctx, x
    for layer in range(DEPTH):
        mod_l = (jax.nn.silu(c) @ w_mod[layer] + b_mod[layer]).reshape(c.shape[0], 6, 1, d)
        mod_c = (jax.nn.silu(c_ctx) @ w_mod[layer] + b_mod[layer]).reshape(1, 6, 1, d)
        lam_init = 0.8 - 0.6 * math.exp(-0.3 * layer)
        xc, xl = hybrid_layer(
            xc, xl, mod_c, mod_l, norm_mix_pre[layer], norm_mix_post[layer], norm_ffn_pre[layer],
            norm_ffn_post[layer], w_in[layer], mlstm_conv_w[layer], mlstm_conv_b[layer],
            mlstm_gate_b[layer], mlstm_norm[layer], gla_gate_w2[layer], gla_gate_b[layer],
            gla_norm[layer], diff_lambda[layer], diff_subln[layer], w_out[layer],
            w_ffn_gate[layer], w_ffn_up[layer], w_ffn_down[layer], lam_init, rope_cos, rope_sin,
            layer < DEPTH - 1)
    return xl
```

```python
import contextlib
import math

import ml_dtypes
import numpy as np

import concourse.bass as bass
import concourse.mybir as mybir
from concourse.bass_utils import run_bass_kernel_spmd

F32 = mybir.dt.float32
BF16 = mybir.dt.bfloat16
ALU = mybir.AluOpType
AF = mybir.ActivationFunctionType
AX = mybir.AxisListType

D = 2048
NCH = 16
FFN = 5632
NJ = 44
DEPTH = 2
CTX = 256
SEQ = 4096
NTOK = CTX + SEQ
EPS = 1e-6
IN_COLS = 6704
NCORES = 8


class Prog:
    K_RING = 12

    def __init__(self):
        self.nc = bass.Bass("TRN2", target_bir_lowering=False)
        self.es = contextlib.ExitStack()
        nc = self.nc
        self.eng = {}
        for name, h in (("pe", nc.tensor), ("act", nc.scalar), ("dve", nc.vector),
                        ("pool", nc.gpsimd), ("sp", nc.sync)):
            sem = self.es.enter_context(nc.semaphore("s_" + name))
            self.eng[name] = dict(h=h, sem=sem, sn="s_" + name, cnt=0, waited={})
        self.ring = {}
        self.rpos = {}
        for q in ("sp", "pool", "act"):
            self.ring[q] = []
            for i in range(self.K_RING):
                sem = self.es.enter_context(nc.semaphore("d_%s%d" % (q, i)))
                self.ring[q].append(dict(sem=sem, sn="d_%s%d" % (q, i), val=0))
            self.rpos[q] = 0
        self.lastw = {}
        self.readers = {}
        self.n_ops = 0
        self.psum_banks = []

    def sbuf(self, name, shape, dtype):
        return self.es.enter_context(self.nc.sbuf_tensor("sb_" + name, list(shape), dtype))

    def psum(self, name, shape, dtype):
        return self.es.enter_context(self.nc.psum_tensor("pp_" + name, list(shape), dtype))

    def dram(self, name, shape, dtype, kind):
        return self.nc.dram_tensor(name, list(shape), dtype, kind=kind).ap()

    def _collect(self, engname, reads, writes):
        own = self.eng[engname]["sn"]
        need = {}

        def add(ev, is_war):
            sn, sh, v = ev
            if sn == own:
                if engname == "pe":
                    return
            if sn not in need or need[sn][1] < v:
                need[sn] = (sh, v)

        for k in reads:
            e = self.lastw.get(k)
            if e is not None:
                add(e, False)
            if k.startswith("ps"):
                for e in self.readers.get(k, ()):
                    add(e, True)
        for k in writes:
            e = self.lastw.get(k)
            if e is not None:
                add(e, False)
            for e in self.readers.get(k, ()):
                add(e, True)
        return need

    def _emit_waits(self, engname, need):
        E = self.eng[engname]
        for sn, (sh, v) in need.items():
            if E["waited"].get(sn, 0) >= v:
                continue
            E["h"].wait_ge(sh, v)
            E["waited"][sn] = v

    def _record(self, ev, reads, writes):
        for k in writes:
            self.lastw[k] = ev
            self.readers[k] = []
        for k in reads:
            self.readers.setdefault(k, []).append(ev)

    def op(self, engname, fn, reads=(), writes=(), inc=True):
        E = self.eng[engname]
        need = self._collect(engname, reads, writes)
        self._emit_waits(engname, need)
        ins = fn(E["h"])
        if inc:
            E["cnt"] += 1
            ins.then_inc(E["sem"], 1)
            ev = (E["sn"], E["sem"], E["cnt"])
        else:
            ev = (E["sn"], E["sem"], E["cnt"] + 1)
        self._record(ev, reads, writes)
        self.n_ops += 1
        return ins

    def dma(self, q, out, in_, reads=(), writes=(), **kw):
        E = self.eng[q]
        slot = self.ring[q][self.rpos[q]]
        self.rpos[q] = (self.rpos[q] + 1) % self.K_RING
        need = self._collect(q, reads, writes)
        if slot["val"] > 0:
            if slot["sn"] not in need or need[slot["sn"]][1] < slot["val"]:
                need[slot["sn"]] = (slot["sem"], slot["val"])
        self._emit_waits(q, need)
        slot["val"] += 16
        E["h"].dma_start(out=out, in_=in_, **kw).then_inc(slot["sem"], 16)
        ev = (slot["sn"], slot["sem"], slot["val"])
        self._record(ev, reads, writes)
        self.n_ops += 1

    def finish(self):
        E = self.eng["sp"]
        for q in ("sp", "pool", "act"):
            for slot in self.ring[q]:
                if slot["val"] > 0 and E["waited"].get(slot["sn"], 0) < slot["val"]:
                    E["h"].wait_ge(slot["sem"], slot["val"])
                    E["waited"][slot["sn"]] = slot["val"]
        for name in ("pe", "act", "dve", "pool"):
            X = self.eng[name]
            if X["cnt"] > 0 and E["waited"].get(X["sn"], 0) < X["cnt"]:
                E["h"].wait_ge(X["sem"], X["cnt"])
        self.es.close()
        return self.nc


class Ctx:
    def __init__(self, P, ident_dram):
        self.P = P
        self.ps = [P.psum("ps%d" % i, [128, 512], F32) for i in range(8)]
        self.ident = P.sbuf("ident", [128, 128], F32)
        P.dma("sp", self.ident[:, :], ident_dram[:, :], writes=["ident"])
        self.identb = P.sbuf("identb", [128, 128], BF16)
        P.op("dve", lambda e: e.tensor_copy(out=self.identb[:, :], in_=self.ident[:, :]),
             reads=["ident"], writes=["identb"])
        self.junk = P.sbuf("junk", [128, 2048], BF16)
        self.stat = P.sbuf("stat", [128, 64], F32)
        self.stat_i = 0
        self.rot = {}

    def bank(self, group, n):
        lo, cnt = group
        i = self.rot.get(group, 0)
        self.rot[group] = (i + 1) % cnt
        return lo + i

    def statcol(self, n=1):
        i = self.stat_i
        if i + n > 64:
            i = 0
        self.stat_i = i + n
        return i


def emit_rstd(P, C, ss_ap, ss_key, rows, out_ap, out_key, n_feat):
    P.op("dve", lambda e: e.tensor_scalar(out=out_ap, in0=ss_ap, scalar1=1.0 / n_feat, scalar2=EPS,
                                          op0=ALU.mult, op1=ALU.add),
         reads=[ss_key], writes=[out_key])
    P.op("act", lambda e: e.activation(out=out_ap, in_=out_ap, func=AF.Sqrt),
         reads=[out_key], writes=[out_key])
    P.op("dve", lambda e: e.reciprocal(out=out_ap, in_=out_ap),
         reads=[out_key], writes=[out_key])


def emit_linear_fm(P, C, w_src, K, nout, stage, rhs, blocks, epilogue, banks=(0, 4), wq="pool",
                   name="lin"):
    ns = len(stage)
    for j in range(nout):
        st, skey = stage[j % ns]
        P.dma(wq, st, w_src(j), writes=[skey])
        for bi, (c0, n) in enumerate(blocks):
            b = C.bank(banks, 1)
            pk = "ps%d" % b
            pap = C.ps[b][:, 0:n]
            for k in range(K):
                r_ap, r_keys = rhs(k, c0, n)
                P.op("pe", lambda e, pap=pap, k=k, r_ap=r_ap: e.matmul(
                    pap, lhsT=st[:, k * 128:(k + 1) * 128], rhs=r_ap, start=(k == 0), stop=(k == K - 1)),
                    reads=[skey] + list(r_keys), writes=[pk], inc=(k == K - 1))
            epilogue(j, bi, pap, pk, c0, n)


def emit_to_tokmajor(P, C, srcT, src_key_fn, t0, rows, banks=(4, 4)):
    outs = []
    for n in range(4):
        b = C.bank(banks, 1)
        pk = "ps%d" % b
        for i in range(4):
            f = n * 4 + i
            P.op("pe", lambda e, b=b, i=i, f=f: e.transpose(
                C.ps[b][0:rows, i * 128:(i + 1) * 128], srcT[:, f, t0:t0 + rows], C.ident[:, :]),
                reads=[src_key_fn(f), "ident"], writes=[pk], inc=(i == 3))
        outs.append((C.ps[b][0:rows, :], pk))
    return outs


def emit_postnorm_residual(P, C, outs, rows, x_ap, x_key, g_ap, g_key, tmp_ap, tmp_key):
    c = C.statcol(6)
    ss = C.stat[0:rows, c:c + 4]
    for n, (pap, pk) in enumerate(outs):
        P.op("act", lambda e, pap=pap, n=n: e.activation(
            out=C.junk[0:rows, 0:512], in_=pap, func=AF.Square, accum_out=C.stat[0:rows, c + n:c + n + 1]),
            reads=[pk], writes=["junk", "stat%d" % (c + n)])
    tot = C.stat[0:rows, c + 4:c + 5]
    P.op("dve", lambda e: e.reduce_sum(out=tot, in_=ss, axis=AX.X),
         reads=["stat%d" % (c + n) for n in range(4)], writes=["stat%d" % (c + 4)])
    rstd = C.stat[0:rows, c + 5:c + 6]
    emit_rstd(P, C, tot, "stat%d" % (c + 4), rows, rstd, "stat%d" % (c + 5), D)
    for n, (pap, pk) in enumerate(outs):
        sl = slice(n * 512, (n + 1) * 512)
        P.op("dve", lambda e, pap=pap, sl=sl: e.scalar_tensor_tensor(
            out=tmp_ap[0:rows, sl], in0=pap, scalar=rstd, in1=g_ap[0:rows, sl], op0=ALU.mult, op1=ALU.mult),
            reads=[pk, "stat%d" % (c + 5), g_key], writes=[tmp_key])
        P.op("pool", lambda e, sl=sl: e.tensor_tensor(
            out=x_ap[0:rows, sl], in0=x_ap[0:rows, sl], in1=tmp_ap[0:rows, sl], op=ALU.add),
            reads=[tmp_key, x_key], writes=[x_key])


def emit_prenorm_T(P, C, x_ap, x_key, rows, xs_ap, xs_key, a_col, sh_col, mod_key, dst_fn, banks=(4, 4)):
    c = C.statcol(2)
    ss = C.stat[0:rows, c:c + 1]
    P.op("act", lambda e: e.activation(out=C.junk[0:rows, :], in_=x_ap[0:rows, :], func=AF.Square, accum_out=ss),
         reads=[x_key], writes=["junk", "stat%d" % c])
    rstd = C.stat[0:rows, c + 1:c + 2]
    emit_rstd(P, C, ss, "stat%d" % c, rows, rstd, "stat%d" % (c + 1), D)
    P.op("act", lambda e: e.activation(out=xs_ap[0:rows, :], in_=x_ap[0:rows, :], func=AF.Copy, scale=rstd),
         reads=[x_key, "stat%d" % (c + 1)], writes=[xs_key])
    for n in range(4):
        b = C.bank(banks, 1)
        pk = "ps%d" % b
        for i in range(4):
            f = n * 4 + i
            P.op("pe", lambda e, b=b, i=i, f=f: e.transpose(
                C.ps[b][:, i * 128:i * 128 + rows], xs_ap[0:rows, f * 128:(f + 1) * 128], C.ident[0:rows, 0:rows]),
                reads=[xs_key, "ident"], writes=[pk], inc=(i == 3))
        for i in range(4):
            f = n * 4 + i
            d_ap, d_key = dst_fn(f)
            eng = "dve" if (n % 2 == 0) else "act"
            if eng == "dve":
                P.op("dve", lambda e, b=b, i=i, f=f, d_ap=d_ap: e.tensor_scalar(
                    out=d_ap, in0=C.ps[b][:, i * 128:i * 128 + rows], scalar1=a_col[:, f:f + 1],
                    scalar2=sh_col[:, f:f + 1], op0=ALU.mult, op1=ALU.add),
                    reads=[pk, mod_key], writes=[d_key])
            else:
                P.op("act", lambda e, b=b, i=i, f=f, d_ap=d_ap: e.activation(
                    out=d_ap, in_=C.ps[b][:, i * 128:i * 128 + rows], func=AF.Identity,
                    scale=a_col[:, f:f + 1], bias=sh_col[:, f:f + 1]),
                    reads=[pk, mod_key], writes=[d_key])


def tile_groups(n_ctx, n_lat):
    tiles = []
    if n_ctx:
        tiles.append((0, n_ctx, True))
    for i in range(n_lat // 128):
        tiles.append((n_ctx + i * 128, 128, False))
    groups = []
    cur = []
    cur_n = 0
    for t in tiles:
        if cur and cur_n + t[1] > 576:
            groups.append(cur)
            cur, cur_n = [], 0
        cur.append(t)
        cur_n += t[1]
    if cur:
        groups.append(cur)
    out = []
    for g in groups:
        g0 = g[0][0]
        gn = sum(t[1] for t in g)
        out.append((g0, gn, g))
    return out


def blocks_of(gn):
    bl = []
    c = 0
    while c < gn:
        n = min(512, gn - c)
        bl.append((c, n))
        c += n
    return bl


def build_dense(n_ctx, n_lat, do_c, do_a, a_ctx=True):
    P = Prog()
    nc = P.nc
    ntok = n_ctx + n_lat
    x_in = P.dram("x", [ntok, D], F32, "ExternalInput")
    ident_d = P.dram("ident", [128, 128], F32, "ExternalInput")
    modcols = P.dram("modcols", [128, 12, NCH], F32, "ExternalInput")
    normcols = P.dram("normcols", [128, 4, NCH], F32, "ExternalInput")
    if do_c:
        modrows = P.dram("modrows", [12, D], F32, "ExternalInput")
        normrows = P.dram("normrows", [4, D], F32, "ExternalInput")
        yT_in = P.dram("yT", [D, ntok], BF16, "ExternalInput")
        wo_r = P.dram("wo_r", [NCH, 128, NCH * 128], F32, "ExternalInput")
        wg_r = P.dram("wg_r", [NJ, 128, NCH * 128], F32, "ExternalInput")
        wu_r = P.dram("wu_r", [NJ, 128, NCH * 128], F32, "ExternalInput")
        wd_r = P.dram("wd_r", [NCH, 128, NJ * 128], F32, "ExternalInput")
        x_out = P.dram("x_out", [ntok, D], F32, "ExternalOutput")
    if do_a:
        hT_out = P.dram("hT_out", [D, ntok], BF16, "ExternalOutput")

    C = Ctx(P, ident_d)
    GN = 576
    actT = P.sbuf("actT", [128, NCH, GN], BF16)
    x1 = P.sbuf("x1", [128, 5, D], F32)
    xs = P.sbuf("xs", [128, D], F32)
    mc = P.sbuf("mc", [128, 12, NCH], F32)
    ncol = P.sbuf("ncol", [128, 4, NCH], F32)
    acol = P.sbuf("acol", [128, 8, NCH], F32)
    P.dma("sp", mc[:, :, :], modcols[:, :, :], writes=["mc"])
    P.dma("sp", ncol[:, :, :], normcols[:, :, :], writes=["ncol"])
    for idx, (nrm, mrow) in enumerate(((0, 1), (0, 7), (2, 4), (2, 10))):
        P.op("dve", lambda e, idx=idx, nrm=nrm, mrow=mrow: e.scalar_tensor_tensor(
            out=acol[:, idx, :], in0=mc[:, mrow, :], scalar=1.0, in1=ncol[:, nrm, :], op0=ALU.add, op1=ALU.mult),
            reads=["mc", "ncol"], writes=["acol"])
    if do_c:
        oT = P.sbuf("oT", [128, NCH, GN], F32)
        aT = P.sbuf("aT", [128, NJ, GN], BF16)
        gbuf = P.sbuf("gbuf", [128, 2, D], F32)
        NS = 6
        wst = P.sbuf("wst", [128, NS, NCH * 128], BF16)
        stage = [(wst[:, i, :], "wst%d" % i) for i in range(NS)]
        sg = P.sbuf("sg", [128, 2, 512], F32)

    def load_g(which):
        nrow = 1 if which == 2 else 3
        P.dma("sp", xs[:, :], normrows[nrow:nrow + 1, :].to_broadcast([128, D]), writes=["xs"])
        for v, mrow in enumerate((which, 6 + which)):
            P.dma("sp", gbuf[:, v, :], modrows[mrow:mrow + 1, :].to_broadcast([128, D]), writes=["gbuf%d" % v])
            P.op("pool", lambda e, v=v: e.tensor_tensor(out=gbuf[:, v, :], in0=gbuf[:, v, :], in1=xs[:, :], op=ALU.mult),
                 reads=["xs", "gbuf%d" % v], writes=["gbuf%d" % v])

    groups = tile_groups(n_ctx, n_lat)
    for (g0, gn, tiles) in groups:
        blocks = blocks_of(gn)
        for ti, (t0, rows, is_ctx) in enumerate(tiles):
            P.dma("sp", x1[0:rows, ti, :], x_in[t0:t0 + rows, :], writes=["x1_%d" % ti])
        if do_c:
            for f in range(NCH):
                P.dma("sp", actT[:, f, 0:gn], yT_in[f * 128:(f + 1) * 128, g0:g0 + gn], writes=["actT%d" % f])

            def ep_copy(j, bi, pap, pk, c0, n):
                P.op("act", lambda e: e.activation(out=oT[:, j, c0:c0 + n], in_=pap, func=AF.Copy),
                     reads=[pk], writes=["oT%d" % j])

            emit_linear_fm(P, C, lambda j: wo_r[j, :, :], NCH, NCH, stage,
                           lambda k, c0, n: (actT[:, k, c0:c0 + n], ["actT%d" % k]), blocks, ep_copy)
            load_g(2)
            for ti, (t0, rows, is_ctx) in enumerate(tiles):
                outs = emit_to_tokmajor(P, C, oT, lambda f: "oT%d" % f, t0 - g0, rows)
                v = 1 if is_ctx else 0
                emit_postnorm_residual(P, C, outs, rows, x1[:, ti, :], "x1_%d" % ti, gbuf[:, v, :], "gbuf%d" % v,
                                       xs, "xs")
            for ti, (t0, rows, is_ctx) in enumerate(tiles):
                a_i, s_row = (3, 9) if is_ctx else (2, 3)
                emit_prenorm_T(P, C, x1[:, ti, :], "x1_%d" % ti, rows, xs, "xs", acol[:, a_i, :], mc[:, s_row, :],
                               "acol", lambda f, t0=t0, rows=rows: (actT[:, f, t0 - g0:t0 - g0 + rows], "actT%d" % f))
            def w_gu(jj):
                return (wg_r if jj % 2 == 0 else wu_r)[jj // 2, :, :]

            def ep_gu(jj, bi, pap, pk, c0, n):
                j = jj // 2
                if jj % 2 == 0:
                    P.op("act", lambda e: e.activation(out=sg[:, bi, 0:n], in_=pap, func=AF.Silu),
                         reads=[pk], writes=["sg%d" % bi])
                else:
                    P.op("dve", lambda e: e.tensor_tensor(out=aT[:, j, c0:c0 + n], in0=sg[:, bi, 0:n], in1=pap, op=ALU.mult),
                         reads=[pk, "sg%d" % bi], writes=["aT%d" % j])

            emit_linear_fm(P, C, w_gu, NCH, 2 * NJ, stage,
                           lambda k, c0, n: (actT[:, k, c0:c0 + n], ["actT%d" % k]), blocks, ep_gu)
            subs = [(0, 16), (16, 16), (32, 12)]
            ns = len(stage)
            cnt = 0
            for f in range(NCH):
                sts = []
                for (k0, kk) in subs:
                    st, skey = stage[cnt % ns]
                    cnt += 1
                    P.dma("pool", st[:, 0:kk * 128], wd_r[f, :, k0 * 128:(k0 + kk) * 128], writes=[skey])
                    sts.append((st, skey, k0, kk))
                for bi, (c0, n) in enumerate(blocks):
                    b = C.bank((0, 4), 1)
                    pk = "ps%d" % b
                    pap = C.ps[b][:, 0:n]
                    for (st, skey, k0, kk) in sts:
                        for k in range(kk):
                            kg = k0 + k
                            P.op("pe", lambda e, pap=pap, st=st, k=k, kg=kg: e.matmul(
                                pap, lhsT=st[:, k * 128:(k + 1) * 128], rhs=aT[:, kg, c0:c0 + n],
                                start=(kg == 0), stop=(kg == NJ - 1)),
                                reads=[skey, "aT%d" % kg], writes=[pk], inc=(kg == NJ - 1))
                    P.op("act", lambda e, pap=pap, f=f, c0=c0, n=n: e.activation(out=oT[:, f, c0:c0 + n], in_=pap, func=AF.Copy),
                         reads=[pk], writes=["oT%d" % f])
            load_g(5)
            for ti, (t0, rows, is_ctx) in enumerate(tiles):
                outs = emit_to_tokmajor(P, C, oT, lambda f: "oT%d" % f, t0 - g0, rows)
                v = 1 if is_ctx else 0
                emit_postnorm_residual(P, C, outs, rows, x1[:, ti, :], "x1_%d" % ti, gbuf[:, v, :], "gbuf%d" % v,
                                       xs, "xs")
                P.dma("sp", x_out[t0:t0 + rows, :], x1[0:rows, ti, :], reads=["x1_%d" % ti])
        if do_a:
            for ti, (t0, rows, is_ctx) in enumerate(tiles):
                a_i, s_row = (1, 6) if is_ctx else (0, 0)
                emit_prenorm_T(P, C, x1[:, ti, :], "x1_%d" % ti, rows, xs, "xs", acol[:, a_i, :], mc[:, s_row, :],
                               "acol", lambda f, t0=t0, rows=rows: (actT[:, f, t0 - g0:t0 - g0 + rows], "actT%d" % f))
            for f in range(NCH):
                P.dma("sp", hT_out[f * 128:(f + 1) * 128, g0:g0 + gn], actT[:, f, 0:gn], reads=["actT%d" % f])
    return P.finish()


def token_blocks(N):
    bl = []
    c = 0
    while c < N:
        n = min(512, N - c)
        bl.append((c, n))
        c += n
    return bl


def chunk_order(nctx_c, ncn, d):
    if d == 0:
        return list(range(ncn))
    return list(range(nctx_c - 1, -1, -1)) + list(range(ncn - 1, nctx_c - 1, -1))


class HStream:
    def __init__(self, P, hT, N, name="hblk"):
        self.P, self.hT, self.N = P, hT, N
        self.buf = P.sbuf(name, [128, 2, NCH, 512], BF16)
        self.i = 0

    def load(self, c0, n):
        s = self.i % 2
        self.i += 1
        keys = ["hb%d_%d" % (s, k) for k in range(NCH)]
        for k in range(NCH):
            self.P.dma("sp", self.buf[:, s, k, 0:n], self.hT[k * 128:(k + 1) * 128, c0:c0 + n], writes=[keys[k]])
        return s, keys


def emit_proj_fm(P, C, H, s, keys, w_ap, wkey, M, n, bank):
    pk = "ps%d" % bank
    for k in range(NCH):
        P.op("pe", lambda e, k=k: e.matmul(C.ps[bank][0:M, 0:n], lhsT=w_ap[:, k, 0:M], rhs=H.buf[:, s, k, 0:n],
                                           start=(k == 0), stop=(k == NCH - 1)),
             reads=[wkey, keys[k]], writes=[pk], inc=(k == NCH - 1))
    return C.ps[bank][0:M, 0:n], pk


def emit_proj_tm(P, C, H, s, keys, w_ap, wkey, t0, rows, ncols, bank):
    pk = "ps%d" % bank
    for k in range(NCH):
        P.op("pe", lambda e, k=k: e.matmul(C.ps[bank][0:rows, 0:ncols], lhsT=H.buf[:, s, k, t0:t0 + rows],
                                           rhs=w_ap[:, k, 0:ncols], start=(k == 0), stop=(k == NCH - 1)),
             reads=[wkey, keys[k]], writes=[pk], inc=(k == NCH - 1))
    return C.ps[bank][0:rows, 0:ncols], pk


def build_mlstm(n_ctx, n_lat, debug=False, dirs=(0, 1)):
    P = Prog()
    N = n_ctx + n_lat
    NCN = N // 64
    NCC = n_ctx // 64
    hT = P.dram("hT", [D, N], BF16, "ExternalInput")
    ident_d = P.dram("ident", [128, 128], F32, "ExternalInput")
    w_fm = P.dram("w_fm", [3, 128, NCH * 128], F32, "ExternalInput")
    w_g = P.dram("w_g", [128, NCH * 4], F32, "ExternalInput")
    w_v = P.dram("w_v", [128, NCH * 128], F32, "ExternalInput")
    cw_d = P.dram("cw", [128, 8], F32, "ExternalInput")
    gb_d = P.dram("gb", [4, 1], F32, "ExternalInput")
    mn_d = P.dram("mn", [128, 1], F32, "ExternalInput")
    masks_d = P.dram("masks", [64, 2, 64], F32, "ExternalInput")
    tri_d = P.dram("tri", [NCN, 2, NCN], F32, "ExternalInput")
    gscr = P.dram("gscr", [4, N], F32, "Internal")
    yT = P.dram("yT", [128, N], BF16, "ExternalOutput")

    C = Ctx(P, ident_d)
    H = HStream(P, hT, N)
    wfm = P.sbuf("wfm", [128, 3, NCH, 128], BF16)
    wg = P.sbuf("wg", [128, NCH, 4], BF16)
    wv = P.sbuf("wv", [128, NCH, 128], BF16)
    for g in range(3):
        P.dma("pool", wfm[:, g, :, :], w_fm[g, :, :], writes=["wfm%d" % g])
    P.dma("pool", wg[:, :, :], w_g[:, :], writes=["wg"])
    P.dma("pool", wv[:, :, :], w_v[:, :], writes=["wv"])
    cw = P.sbuf("cw", [128, 8], F32)
    gb = P.sbuf("gb", [4, 1], F32)
    mn = P.sbuf("mn", [128, 1], F32)
    masks = P.sbuf("masks", [64, 2, 64], F32)
    tri = P.sbuf("tri", [NCN, 2, NCN], F32)
    for t, src, key in ((cw, cw_d, "cw"), (gb, gb_d, "gb"), (mn, mn_d, "mn")):
        P.dma("sp", t[:, :], src[:, :], writes=[key])
    P.dma("sp", masks[:, :, :], masks_d[:, :, :], writes=["masks"])
    P.dma("sp", tri[:, :, :], tri_d[:, :, :], writes=["tri"])

    big = P.sbuf("big", [128, 2 * N], F32)
    acc = P.sbuf("acc", [128, N], F32)
    mqT = P.sbuf("mqT", [128, N], BF16)
    mkT = P.sbuf("mkT", [128, N], BF16)
    moT = P.sbuf("moT", [128, N], BF16)
    v64 = P.sbuf("v64", [64, NCN, 128], BF16)
    ktok = P.sbuf("ktok", [64, NCN, 128], BF16)
    gsb = P.sbuf("gsb", [4, 512], F32)
    zeros = P.sbuf("zeros", [128, 512], F32)
    ones = P.sbuf("ones", [128, 128], F32)
    P.op("pool", lambda e: e.memset(zeros[:, :], 0.0), writes=["zeros"])
    P.op("pool", lambda e: e.memset(ones[:, :], 1.0), writes=["ones"])

    for (c0, n) in token_blocks(N):
        s, keys = H.load(c0, n)
        for g, (dst, dkey) in enumerate(((big[:, 0:N], "mqraw"), (big[:, N:2 * N], "mkraw"))):
            b = C.bank((0, 4), 1)
            pap, pk = emit_proj_fm(P, C, H, s, keys, wfm[:, g, :, :], "wfm%d" % g, 128, n, b)
            P.op("act", lambda e, pap=pap, dst=dst: e.activation(out=dst[:, c0:c0 + n], in_=pap, func=AF.Copy),
                 reads=[pk], writes=[dkey])
        b = C.bank((0, 4), 1)
        pap, pk = emit_proj_fm(P, C, H, s, keys, wfm[:, 2, :, :], "wfm2", 128, n, b)
        P.op("act", lambda e, pap=pap: e.activation(out=moT[:, c0:c0 + n], in_=pap, func=AF.Sigmoid),
             reads=[pk], writes=["moT"])
        b = C.bank((0, 4), 1)
        pap, pk = emit_proj_fm(P, C, H, s, keys, wg, "wg", 4, n, b)
        P.op("dve", lambda e, pap=pap: e.tensor_scalar(out=gsb[:, 0:n], in0=pap, scalar1=gb[:, 0:1], scalar2=None, op0=ALU.add),
             reads=[pk, "gb"], writes=["gsb"])
        P.dma("sp", gscr[:, c0:c0 + n], gsb[:, 0:n], reads=["gsb"], writes=["gscr%d" % c0])
        for t in range(n // 64):
            b = C.bank((4, 4), 1)
            pap, pk = emit_proj_tm(P, C, H, s, keys, wv, "wv", t * 64, 64, 128, b)
            ci = c0 // 64 + t
            P.op("dve", lambda e, pap=pap, ci=ci: e.tensor_copy(out=v64[:, ci, :], in_=pap), reads=[pk], writes=["v64_%d" % ci])

    segs = [(a, b_) for (a, b_) in ((0, n_ctx), (n_ctx, N)) if b_ > a]
    for qi, (raw, rkey, dstT, dkey) in enumerate(((big[:, 0:N], "mqraw", mqT, "mqT"), (big[:, N:2 * N], "mkraw", mkT, "mkT"))):
        w0, w1, w2, bcol = cw[:, 3 * qi:3 * qi + 1], cw[:, 3 * qi + 1:3 * qi + 2], cw[:, 3 * qi + 2:3 * qi + 3], cw[:, 6 + qi:7 + qi]
        P.op("dve", lambda e, raw=raw, w1=w1, bcol=bcol: e.tensor_scalar(out=acc[:, :], in0=raw, scalar1=w1, scalar2=bcol, op0=ALU.mult, op1=ALU.add),
             reads=[rkey, "cw"], writes=["acc"])
        for (a, b_) in segs:
            P.op("dve", lambda e, raw=raw, w0=w0, a=a, b_=b_: e.scalar_tensor_tensor(
                out=acc[:, a + 1:b_], in0=raw[:, a:b_ - 1], scalar=w0, in1=acc[:, a + 1:b_], op0=ALU.mult, op1=ALU.add),
                reads=[rkey, "cw", "acc"], writes=["acc"])
            P.op("dve", lambda e, raw=raw, w2=w2, a=a, b_=b_: e.scalar_tensor_tensor(
                out=acc[:, a:b_ - 1], in0=raw[:, a + 1:b_], scalar=w2, in1=acc[:, a:b_ - 1], op0=ALU.mult, op1=ALU.add),
                reads=[rkey, "cw", "acc"], writes=["acc"])
        P.op("act", lambda e: e.activation(out=acc[:, :], in_=acc[:, :], func=AF.Silu), reads=["acc"], writes=["acc"])
        if qi == 0:
            P.op("dve", lambda e: e.tensor_scalar(out=mqT[:, :], in0=acc[:, :], scalar1=128.0 ** -0.5, scalar2=None, op0=ALU.mult),
                 reads=["acc"], writes=["mqT"])
        else:
            P.op("dve", lambda e: e.tensor_copy(out=mkT[:, :], in_=acc[:, :]), reads=["acc"], writes=["mkT"])
            for c in range(NCN):
                b = C.bank((4, 4), 1)
                pk = "ps%d" % b
                P.op("pe", lambda e, b=b, c=c: e.transpose(C.ps[b][0:64, 0:128], acc[:, c * 64:(c + 1) * 64], C.ident[:, :]),
                     reads=["acc", "ident"], writes=[pk])
                P.op("act", lambda e, b=b, c=c: e.activation(out=ktok[:, c, :], in_=C.ps[b][0:64, 0:128], func=AF.Copy),
                     reads=[pk], writes=["ktok%d" % c])

    st = P.sbuf("st", [NCN, 40, 64], F32)
    sc = P.sbuf("sc", [NCN, 32], F32)
    rowt = P.sbuf("rowt", [1, 4, NCN], F32)
    colW = P.sbuf("colW", [64, 2, 3, NCN], F32)
    bca = P.sbuf("bca", [128, 2, NCN], F32)
    diag = P.sbuf("diag", [NCN, NCN], F32)
    skey = lambda i: "st%d" % i
    ckey = lambda i: "sc%d" % i

    def rv(ap, d):
        return ap[:, ::-1] if d == 1 else ap

    gkeys = ["gscr%d" % c0 for (c0, n) in token_blocks(N)]
    for d in range(2):
        base = d * 20
        I_, F_, L_, Pl, Pt, A_, Al, Gc, W_, R_, E_, T1 = [st[:, base + i, :] for i in range(12)]
        kI, kF, kL, kPl, kPt, kA, kAl, kGc, kW, kR, kE, kT1 = [skey(base + i) for i in range(12)]
        cb = d * 16
        cPc, cMx, cGk, cGkp, cNGkp, cAl = [sc[:, cb + i:cb + i + 1] for i in range(6)]
        kcPc, kcMx, kcGk, kcGkp, kcNGkp, kcAl = [ckey(cb + i) for i in range(6)]
        P.dma("sp", I_, gscr[2 * d:2 * d + 1, :].rearrange("o (c l) -> (o c) l", l=64), reads=gkeys, writes=[kI])
        P.dma("sp", F_, gscr[2 * d + 1:2 * d + 2, :].rearrange("o (c l) -> (o c) l", l=64), reads=gkeys, writes=[kF])
        P.op("act", lambda e: e.activation(out=T1, in_=F_, func=AF.Exp, scale=-1.0), reads=[kF], writes=[kT1])
        P.op("act", lambda e: e.activation(out=L_, in_=T1, func=AF.Ln, bias=1.0), reads=[kT1], writes=[kL])
        P.op("dve", lambda e: e.tensor_tensor_scan(out=rv(Pl, d), data0=rv(L_, d), data1=zeros[0:NCN, 0:64], initial=0.0,
                                                   op0=ALU.add, op1=ALU.add), reads=[kL, "zeros"], writes=[kPl])
        last = (lambda ap: ap[:, 0:1]) if d == 1 else (lambda ap: ap[:, 63:64])
        b = C.bank((0, 4), 1)
        pk = "ps%d" % b
        P.op("pe", lambda e, b=b: e.matmul(C.ps[b][0:NCN, 0:1], lhsT=tri[:, d, :], rhs=last(Pl), start=True, stop=True),
             reads=["tri", kPl], writes=[pk])
        P.op("act", lambda e, b=b: e.activation(out=cPc, in_=C.ps[b][0:NCN, 0:1], func=AF.Copy), reads=[pk], writes=[kcPc])
        P.op("dve", lambda e: e.tensor_scalar(out=Pt, in0=Pl, scalar1=cPc, scalar2=None, op0=ALU.add), reads=[kPl, kcPc], writes=[kPt])
        P.op("dve", lambda e: e.tensor_tensor(out=A_, in0=I_, in1=Pt, op=ALU.add), reads=[kI, kPt], writes=[kA])
        P.op("dve", lambda e: e.tensor_tensor_scan(out=rv(Al, d), data0=rv(A_, d), data1=rv(A_, d), initial=-1e30,
                                                   op0=ALU.max, op1=ALU.max), reads=[kA], writes=[kAl])
        b = C.bank((0, 4), 1)
        pk = "ps%d" % b
        P.op("pe", lambda e, b=b: e.transpose(C.ps[b][0:1, 0:NCN], last(Al), C.ident[0:NCN, 0:NCN]),
             reads=[kAl, "ident"], writes=[pk])
        mxr, gpr, gkr = rowt[:, 0, :], rowt[:, 1, :], rowt[:, 2, :]
        P.op("act", lambda e, b=b: e.activation(out=mxr, in_=C.ps[b][0:1, 0:NCN], func=AF.Copy), reads=[pk], writes=["rowt0"])
        if d == 0:
            P.op("dve", lambda e: e.tensor_tensor_scan(out=gpr, data0=mxr, data1=mxr, initial=0.0, op0=ALU.max, op1=ALU.max),
                 reads=["rowt0"], writes=["rowt1"])
            P.op("dve", lambda e: e.memset(gkr[:, 0:1], 0.0), writes=["rowt2"])
            P.op("dve", lambda e: e.tensor_copy(out=gkr[:, 1:NCN], in_=gpr[:, 0:NCN - 1]), reads=["rowt1"], writes=["rowt2"])
        else:
            if NCC > 0:
                P.op("dve", lambda e: e.tensor_tensor_scan(out=gpr[:, 0:NCC][:, ::-1], data0=mxr[:, 0:NCC][:, ::-1],
                                                           data1=mxr[:, 0:NCC][:, ::-1], initial=0.0, op0=ALU.max, op1=ALU.max),
                     reads=["rowt0"], writes=["rowt1"])
                P.op("dve", lambda e: e.tensor_tensor_scan(out=gpr[:, NCC:NCN][:, ::-1], data0=mxr[:, NCC:NCN][:, ::-1],
                                                           data1=mxr[:, NCC:NCN][:, ::-1], initial=gpr[:, 0:1], op0=ALU.max, op1=ALU.max),
                     reads=["rowt0", "rowt1"], writes=["rowt1"])
                P.op("dve", lambda e: e.memset(gkr[:, NCC - 1:NCC], 0.0), writes=["rowt2"])
                if NCC > 1:
                    P.op("dve", lambda e: e.tensor_copy(out=gkr[:, 0:NCC - 1], in_=gpr[:, 1:NCC]), reads=["rowt1"], writes=["rowt2"])
                P.op("dve", lambda e: e.tensor_copy(out=gkr[:, NCN - 1:NCN], in_=gpr[:, 0:1]), reads=["rowt1"], writes=["rowt2"])
            else:
                P.op("dve", lambda e: e.tensor_tensor_scan(out=gpr[:, ::-1], data0=mxr[:, ::-1], data1=mxr[:, ::-1], initial=0.0,
                                                           op0=ALU.max, op1=ALU.max), reads=["rowt0"], writes=["rowt1"])
                P.op("dve", lambda e: e.memset(gkr[:, NCN - 1:NCN], 0.0), writes=["rowt2"])
            P.op("dve", lambda e: e.tensor_copy(out=gkr[:, NCC:NCN - 1], in_=gpr[:, NCC + 1:NCN]), reads=["rowt1"], writes=["rowt2"])
        for (row, rk, col, ck) in ((gkr, "rowt2", cGk, kcGk), (gpr, "rowt1", cGkp, kcGkp)):
            b = C.bank((0, 4), 1)
            pk = "ps%d" % b
            P.op("pe", lambda e, b=b, row=row: e.transpose(C.ps[b][0:NCN, 0:1], row, C.ident[0:1, 0:1]),
                 reads=[rk, "ident"], writes=[pk])
            P.op("act", lambda e, b=b, col=col: e.activation(out=col, in_=C.ps[b][0:NCN, 0:1], func=AF.Copy), reads=[pk], writes=[ck])
        P.op("dve", lambda e: e.tensor_scalar(out=cNGkp, in0=cGkp, scalar1=-1.0, scalar2=None, op0=ALU.mult), reads=[kcGkp], writes=[kcNGkp])
        P.op("dve", lambda e: e.tensor_scalar(out=Gc, in0=Al, scalar1=cGk, scalar2=None, op0=ALU.max), reads=[kAl, kcGk], writes=[kGc])
        P.op("act", lambda e: e.activation(out=W_, in_=A_, func=AF.Exp, bias=cNGkp), reads=[kA, kcNGkp], writes=[kW])
        P.op("act", lambda e: e.activation(out=R_, in_=Gc, func=AF.Exp, scale=-1.0, bias=cGkp), reads=[kGc, kcGkp], writes=[kR])
        P.op("dve", lambda e: e.tensor_tensor(out=T1, in0=Pt, in1=Gc, op=ALU.subtract), reads=[kPt, kGc], writes=[kT1])
        P.op("act", lambda e: e.activation(out=E_, in_=T1, func=AF.Exp), reads=[kT1], writes=[kE])
        P.op("dve", lambda e: e.tensor_tensor(out=cAl, in0=cGk, in1=cGkp, op=ALU.subtract), reads=[kcGk, kcGkp], writes=[kcAl])
        P.op("act", lambda e: e.activation(out=cAl, in_=cAl, func=AF.Exp), reads=[kcAl], writes=[kcAl])
        for qi, (src, sk) in enumerate(((W_, kW), (R_, kR), (E_, kE))):
            b = C.bank((0, 4), 1)
            pk = "ps%d" % b
            P.op("pe", lambda e, b=b, src=src: e.transpose(C.ps[b][0:64, 0:NCN], src, C.ident[0:NCN, 0:NCN]),
                 reads=[sk, "ident"], writes=[pk])
            P.op("act", lambda e, b=b, qi=qi: e.activation(out=colW[:, d, qi, :], in_=C.ps[b][0:64, 0:NCN], func=AF.Copy),
                 reads=[pk], writes=["colW%d" % d])
        P.op("dve", lambda e: e.tensor_scalar(out=diag[:, :], in0=C.ident[0:NCN, 0:NCN], scalar1=cAl, scalar2=None, op0=ALU.mult),
             reads=["ident", kcAl], writes=["diag"])
        b = C.bank((0, 4), 1)
        pk = "ps%d" % b
        P.op("pe", lambda e, b=b: e.matmul(C.ps[b][:, 0:NCN], lhsT=ones[0:NCN, :], rhs=diag[:, :], start=True, stop=True),
             reads=["ones", "diag"], writes=[pk])
        P.op("act", lambda e, b=b: e.activation(out=bca[:, d, :], in_=C.ps[b][:, 0:NCN], func=AF.Copy), reads=[pk], writes=["bca%d" % d])

    hacc = big[0:64, :].rearrange("p (c e) -> p c e", e=128)
    Cst = P.sbuf("Cst", [128, 2, 132], F32)
    Cbf = P.sbuf("Cbf", [128, 2, 132], BF16)
    ctmp = P.sbuf("ctmp", [128, 2, 132], F32)
    vh = P.sbuf("vh", [64, 4, 132], BF16)
    PT = P.sbuf("PT", [64, 4, 64], BF16)
    maskb = P.sbuf("maskb", [64, 2, 64], F32)
    fcol = P.sbuf("fcol", [64, 8, 4], F32)
    hkey_guard = ["mqraw", "mkraw"]
    orders = [chunk_order(NCC, NCN, d) for d in range(2)]
    for d in range(2):
        P.op("pool", lambda e, d=d: e.memset(Cst[:, d, :], 0.0), writes=["Cst%d" % d])
        P.op("pool", lambda e, d=d: e.memset(Cbf[:, d, :], 0.0), writes=["Cbf%d" % d])
    it = 0
    hwritten = set()
    for step in range(NCN):
        for d in dirs:
            c = orders[d][step]
            cn = orders[d][step + 1] if step + 1 < NCN else None
            sl = slice(c * 64, (c + 1) * 64)
            j = it % 4
            it += 1
            wcol = colW[:, d, 0, c:c + 1]
            rcol = colW[:, d, 1, c:c + 1]
            ecol = colW[:, d, 2, c:c + 1]
            ck = "colW%d" % d
            P.op("act", lambda e, j=j, c=c, wcol=wcol: e.activation(out=vh[:, j, 0:128], in_=v64[:, c, :], func=AF.Copy, scale=wcol),
                 reads=["v64_%d" % c, ck], writes=["vh%d" % j])
            P.op("pool", lambda e, j=j, wcol=wcol: e.tensor_copy(out=vh[:, j, 128:129], in_=wcol), reads=[ck], writes=["vh%d" % j])
            b1 = C.bank((0, 3), 1)
            P.op("pe", lambda e, b1=b1, sl=sl: e.matmul(C.ps[b1][0:64, 0:64], lhsT=mkT[:, sl], rhs=mqT[:, sl], start=True, stop=True),
                 reads=["mkT", "mqT"], writes=["ps%d" % b1])
            P.op("dve", lambda e, b1=b1, j=j, d=d: e.tensor_tensor(out=PT[:, j, :], in0=C.ps[b1][0:64, 0:64], in1=masks[:, d, :], op=ALU.mult),
                 reads=["ps%d" % b1, "masks"], writes=["PT%d" % j])
            b2 = C.bank((3, 3), 1)
            pk2 = "ps%d" % b2
            P.op("pe", lambda e, b2=b2, sl=sl, d=d: e.matmul(C.ps[b2][0:64, 0:129], lhsT=mqT[:, sl], rhs=Cbf[:, d, 0:129], start=True, stop=False),
                 reads=["mqT", "Cbf%d" % d], writes=[pk2], inc=False)
            P.op("pe", lambda e, b2=b2, j=j: e.matmul(C.ps[b2][0:64, 0:129], lhsT=PT[:, j, :], rhs=vh[:, j, 0:129], start=False, stop=True),
                 reads=["PT%d" % j, "vh%d" % j], writes=[pk2])
            fj = it % 8
            f0, f1, f2 = fcol[:, fj, 0:1], fcol[:, fj, 1:2], fcol[:, fj, 2:3]
            fk = "fcol%d" % fj
            P.op("act", lambda e, b2=b2, f0=f0, rcol=rcol: e.activation(out=f0, in_=C.ps[b2][0:64, 128:129], func=AF.Abs, scale=rcol),
                 reads=[pk2, ck], writes=[fk])
            P.op("dve", lambda e, f0=f0, f1=f1, ecol=ecol: e.tensor_tensor(out=f1, in0=f0, in1=ecol, op=ALU.max), reads=[fk, ck], writes=[fk])
            P.op("dve", lambda e, f1=f1: e.reciprocal(out=f1, in_=f1), reads=[fk], writes=[fk])
            P.op("dve", lambda e, f1=f1, f2=f2, rcol=rcol: e.tensor_tensor(out=f2, in0=f1, in1=rcol, op=ALU.mult), reads=[fk, ck], writes=[fk])
            hk = "hacc%d" % c
            if c not in hwritten:
                hwritten.add(c)
                P.op("act", lambda e, b2=b2, c=c, f2=f2: e.activation(out=hacc[:, c, :], in_=C.ps[b2][0:64, 0:128], func=AF.Copy, scale=f2),
                     reads=[pk2, fk], writes=[hk] + hkey_guard)
            else:
                P.op("dve", lambda e, b2=b2, c=c, f2=f2: e.scalar_tensor_tensor(out=hacc[:, c, :], in0=C.ps[b2][0:64, 0:128], scalar=f2,
                                                                               in1=hacc[:, c, :], op0=ALU.mult, op1=ALU.add),
                     reads=[pk2, fk, hk], writes=[hk])
            if cn is not None:
                b3 = C.bank((6, 2), 1)
                pk3 = "ps%d" % b3
                P.op("pe", lambda e, b3=b3, c=c, j=j: e.matmul(C.ps[b3][:, 0:129], lhsT=ktok[:, c, :], rhs=vh[:, j, 0:129], start=True, stop=True),
                     reads=["ktok%d" % c, "vh%d" % j], writes=[pk3])
                acol = bca[:, d, cn:cn + 1]
                P.op("act", lambda e, b3=b3, d=d, acol=acol: e.activation(out=ctmp[:, d, 0:129], in_=C.ps[b3][:, 0:129], func=AF.Copy, scale=acol),
                     reads=[pk3, "bca%d" % d], writes=["ctmp%d" % d])
                P.op("dve", lambda e, d=d, acol=acol: e.scalar_tensor_tensor(out=Cst[:, d, 0:129], in0=Cst[:, d, 0:129], scalar=acol,
                                                                            in1=ctmp[:, d, 0:129], op0=ALU.mult, op1=ALU.add),
                     reads=["Cst%d" % d, "ctmp%d" % d, "bca%d" % d], writes=["Cst%d" % d])
                P.op("pool", lambda e, d=d: e.tensor_copy(out=Cbf[:, d, 0:129], in_=Cst[:, d, 0:129]), reads=["Cst%d" % d], writes=["Cbf%d" % d])

    if debug:
        d1 = P.dram("dbg_mq", [128, N], BF16, "ExternalOutput")
        d2 = P.dram("dbg_mk", [128, N], BF16, "ExternalOutput")
        d3 = P.dram("dbg_colW", [64, 2 * 3 * NCN], F32, "ExternalOutput")
        d4 = P.dram("dbg_bca", [128, 2 * NCN], F32, "ExternalOutput")
        d5 = P.dram("dbg_h", [64, NCN * 128], F32, "ExternalOutput")
        d6 = P.dram("dbg_v", [64, NCN * 128], BF16, "ExternalOutput")
        d7 = P.dram("dbg_kt", [64, NCN * 128], BF16, "ExternalOutput")
        P.dma("sp", d1[:, :], mqT[:, :], reads=["mqT"])
        P.dma("sp", d2[:, :], mkT[:, :], reads=["mkT"])
        P.dma("sp", d3[:, :], colW[:, :, :, :].rearrange("p a b c -> p (a b c)"), reads=["colW0", "colW1"])
        P.dma("sp", d4[:, :], bca[:, :, :].rearrange("p a c -> p (a c)"), reads=["bca0", "bca1"])
        P.dma("sp", d5[:, :], big[0:64, :], reads=["hacc%d" % c for c in range(NCN)])
        P.dma("sp", d6[:, :], v64[:, :, :].rearrange("p a c -> p (a c)"), reads=["v64_%d" % c for c in range(NCN)])
        P.dma("sp", d7[:, :], ktok[:, :, :].rearrange("p a c -> p (a c)"), reads=["ktok%d" % c for c in range(NCN)])
    ssq = P.sbuf("ssq", [64, NCN], F32)
    for c in range(NCN):
        P.op("act", lambda e, c=c: e.activation(out=C.junk[0:64, 0:128], in_=hacc[:, c, :], func=AF.Square, accum_out=ssq[:, c:c + 1]),
             reads=["hacc%d" % c], writes=["junk", "ssq"])
    emit_rstd(P, C, ssq[:, :], "ssq", 64, ssq[:, :], "ssq", 128)
    yst = acc
    for c in range(NCN):
        sl = slice(c * 64, (c + 1) * 64)
        P.op("act", lambda e, c=c: e.activation(out=hacc[:, c, :], in_=hacc[:, c, :], func=AF.Copy, scale=ssq[:, c:c + 1]),
             reads=["hacc%d" % c, "ssq"], writes=["hacc%d" % c])
        b = C.bank((0, 4), 1)
        pk = "ps%d" % b
        P.op("pe", lambda e, b=b, c=c: e.transpose(C.ps[b][:, 0:64], hacc[:, c, :], C.ident[0:64, 0:64]),
             reads=["hacc%d" % c, "ident"], writes=[pk])
        P.op("dve", lambda e, b=b, sl=sl: e.scalar_tensor_tensor(out=mkT[:, sl], in0=C.ps[b][:, 0:64], scalar=mn[:, 0:1], in1=moT[:, sl],
                                                               op0=ALU.mult, op1=ALU.mult),
             reads=[pk, "mn", "moT"], writes=["mkT"])
    P.dma("sp", yT[:, :], mkT[:, :], reads=["mkT"])
    return P.finish()


def mlstm_masks():
    s = np.arange(64)[:, None]
    t = np.arange(64)[None, :]
    m = np.zeros((64, 2, 64), np.float32)
    m[:, 0, :] = (t >= s)
    m[:, 1, :] = (t <= s)
    return m


def mlstm_tri(ncc, ncn):
    tri = np.zeros((ncn, 2, ncn), np.float32)
    for cp in range(ncn):
        for c in range(ncn):
            tri[cp, 0, c] = 1.0 if cp < c else 0.0
            cp_ctx, c_ctx = cp < ncc, c < ncc
            if cp_ctx == c_ctx:
                before = cp > c
            else:
                before = cp_ctx and not c_ctx
            tri[cp, 1, c] = 1.0 if before else 0.0
    return tri


def emit_head_finish(P, C, hacc, NCN, nw_col, nw_key, gateT, gate_key, outT, out_key, name, extra_w=()):
    ssq = P.sbuf("ssq_" + name, [64, NCN], F32)
    for c in range(NCN):
        P.op("act", lambda e, c=c: e.activation(out=C.junk[0:64, 0:128], in_=hacc[:, c, :], func=AF.Square, accum_out=ssq[:, c:c + 1]),
             reads=["hacc%d" % c], writes=["junk", "ssq"])
    emit_rstd(P, C, ssq[:, :], "ssq", 64, ssq[:, :], "ssq", 128)
    for c in range(NCN):
        sl = slice(c * 64, (c + 1) * 64)
        P.op("act", lambda e, c=c: e.activation(out=hacc[:, c, :], in_=hacc[:, c, :], func=AF.Copy, scale=ssq[:, c:c + 1]),
             reads=["hacc%d" % c, "ssq"], writes=["hacc%d" % c])
        b = C.bank((0, 4), 1)
        pk = "ps%d" % b
        P.op("pe", lambda e, b=b, c=c: e.transpose(C.ps[b][:, 0:64], hacc[:, c, :], C.ident[0:64, 0:64]),
             reads=["hacc%d" % c, "ident"], writes=[pk])
        P.op("dve", lambda e, b=b, sl=sl: e.scalar_tensor_tensor(out=outT[:, sl], in0=C.ps[b][:, 0:64], scalar=nw_col, in1=gateT[:, sl],
                                                               op0=ALU.mult, op1=ALU.mult),
             reads=[pk, nw_key, gate_key], writes=[out_key] + (list(extra_w) if c == 0 else []))


def gla_rmask():
    t = np.arange(512)
    m = np.zeros((64, 2, 512), np.float32)
    m[:, 0, :] = (t % 64 != 0)[None, :]
    m[:, 1, :] = (t % 64 != 63)[None, :]
    return m


def build_gla(n_ctx, n_lat):
    P = Prog()
    N = n_ctx + n_lat
    NCN = N // 64
    NCC = n_ctx // 64
    hT = P.dram("hT", [D, N], BF16, "ExternalInput")
    ident_d = P.dram("ident", [128, 128], F32, "ExternalInput")
    w_qk = P.dram("w_qk", [128, NCH * 128], F32, "ExternalInput")
    w_go = P.dram("w_go", [128, NCH * 128], F32, "ExternalInput")
    w_lr = P.dram("w_lr", [128, NCH * 32], F32, "ExternalInput")
    w_v = P.dram("w_v", [128, NCH * 128], F32, "ExternalInput")
    w2_d = P.dram("w2", [16, 2 * 64], F32, "ExternalInput")
    nb_d = P.dram("nb", [64, 2], F32, "ExternalInput")
    gn_d = P.dram("gn", [128, 1], F32, "ExternalInput")
    masks_d = P.dram("masks", [64, 2, 64], F32, "ExternalInput")
    rmask_d = P.dram("rmask", [64, 2, 512], F32, "ExternalInput")
    yT = P.dram("yT", [128, N], BF16, "ExternalOutput")

    C = Ctx(P, ident_d)
    H = HStream(P, hT, N)
    wqk = P.sbuf("wqk", [128, NCH, 128], BF16)
    wgo = P.sbuf("wgo", [128, NCH, 128], BF16)
    wlr = P.sbuf("wlr", [128, NCH, 32], BF16)
    wv = P.sbuf("wv", [128, NCH, 128], BF16)
    w2 = P.sbuf("w2", [16, 128], BF16)
    P.dma("pool", wqk[:, :, :], w_qk[:, :], writes=["wqk"])
    P.dma("pool", wgo[:, :, :], w_go[:, :], writes=["wgo"])
    P.dma("pool", wlr[:, :, :], w_lr[:, :], writes=["wlr"])
    P.dma("pool", wv[:, :, :], w_v[:, :], writes=["wv"])
    P.dma("pool", w2[:, :], w2_d[:, :], writes=["w2"])
    nb = P.sbuf("nb", [64, 2], F32)
    gn = P.sbuf("gn", [128, 1], F32)
    masks = P.sbuf("masks", [64, 2, 64], F32)
    rmask = P.sbuf("rmask", [64, 2, 512], F32)
    P.dma("sp", nb[:, :], nb_d[:, :], writes=["nb"])
    P.dma("sp", gn[:, :], gn_d[:, :], writes=["gn"])
    P.dma("sp", masks[:, :, :], masks_d[:, :, :], writes=["masks"])
    P.dma("sp", rmask[:, :, :], rmask_d[:, :, :], writes=["rmask"])

    qt = P.sbuf("qt", [64, 2, N], BF16)
    kt = P.sbuf("kt", [64, 2, N], BF16)
    ktok = P.sbuf("ktok", [64, 2, NCN, 64], BF16)
    dec = P.sbuf("dec", [64, 2, NCN], F32)
    goT = P.sbuf("goT", [128, N], BF16)
    v64 = P.sbuf("v64", [64, NCN, 128], BF16)
    hacc_t = P.sbuf("hacc", [64, NCN * 128], F32)
    hacc = hacc_t[:, :].rearrange("p (c e) -> p c e", e=128)
    youT = H.buf[:, :, :, :].rearrange("p a k n -> p (a k n)")[:, 0:N]
    hbkeys = ["hb%d_%d" % (s_, k_) for s_ in range(2) for k_ in range(NCH)]
    qf = P.sbuf("qf", [64, 512], F32)
    kf = P.sbuf("kf", [64, 512], F32)
    lrb = P.sbuf("lrb", [16, 2, 512], BF16)
    T1 = P.sbuf("T1", [64, 2, 512], F32)
    Lg = T1
    Gp = P.sbuf("Gp", [64, 2, 512], F32)
    E1 = P.sbuf("E1", [64, 2, 512], F32)
    E2 = P.sbuf("E2", [64, 2, 512], F32)
    ktmp = P.sbuf("ktmp", [64, 2, 512], F32)
    khf = Gp

    for (c0, n) in token_blocks(N):
        s, keys = H.load(c0, n)
        nch = n // 64
        cb0 = c0 // 64
        for g, (dst, dkey) in enumerate(((qf, "qf"), (kf, "kf"))):
            b = C.bank((0, 4), 1)
            pap, pk = emit_proj_fm(P, C, H, s, keys, wqk[:, :, g * 64:(g + 1) * 64], "wqk", 64, n, b)
            sc_ = 0.125 if g == 0 else 1.0
            P.op("act", lambda e, pap=pap, dst=dst, sc_=sc_: e.activation(out=dst[:, 0:n], in_=pap, func=AF.Copy, scale=sc_),
                 reads=[pk], writes=[dkey])
        b = C.bank((0, 4), 1)
        pap, pk = emit_proj_fm(P, C, H, s, keys, wgo, "wgo", 128, n, b)
        P.op("act", lambda e, pap=pap: e.activation(out=goT[:, c0:c0 + n], in_=pap, func=AF.Silu), reads=[pk], writes=["goT"])
        for d in range(2):
            b = C.bank((0, 4), 1)
            pap, pk = emit_proj_fm(P, C, H, s, keys, wlr[:, :, d * 16:(d + 1) * 16], "wlr", 16, n, b)
            P.op("dve", lambda e, pap=pap, d=d: e.tensor_copy(out=lrb[:, d, 0:n], in_=pap), reads=[pk], writes=["lrb%d" % d])
        for t in range(nch):
            b = C.bank((4, 4), 1)
            pap, pk = emit_proj_tm(P, C, H, s, keys, wv, "wv", t * 64, 64, 128, b)
            ci = cb0 + t
            P.op("dve", lambda e, pap=pap, ci=ci: e.tensor_copy(out=v64[:, ci, :], in_=pap), reads=[pk], writes=["v64_%d" % ci])
        for d in range(2):
            dk = str(d)
            b = C.bank((0, 4), 1)
            pk = "ps%d" % b
            P.op("pe", lambda e, b=b, d=d: e.matmul(C.ps[b][0:64, 0:n], lhsT=w2[:, d * 64:(d + 1) * 64], rhs=lrb[:, d, 0:n], start=True, stop=True),
                 reads=["w2", "lrb%d" % d], writes=[pk])
            P.op("act", lambda e, b=b, d=d: e.activation(out=T1[:, d, 0:n], in_=C.ps[b][0:64, 0:n], func=AF.Exp, scale=-1.0, bias=nb[:, d:d + 1]),
                 reads=[pk, "nb"], writes=["T1" + dk])
            P.op("act", lambda e, d=d: e.activation(out=Lg[:, d, 0:n], in_=T1[:, d, 0:n], func=AF.Ln, bias=1.0), reads=["T1" + dk], writes=["T1" + dk])
            rvv = (lambda ap: ap[:, ::-1]) if d == 1 else (lambda ap: ap)
            P.op("dve", lambda e, d=d, rvv=rvv: e.tensor_tensor_scan(out=rvv(Gp[:, d, 0:n]), data0=rvv(rmask[:, d, 0:n]), data1=rvv(Lg[:, d, 0:n]),
                                                                   initial=0.0, op0=ALU.mult, op1=ALU.add),
                 reads=["T1" + dk, "rmask"], writes=["Gp" + dk])
            P.op("act", lambda e, d=d: e.activation(out=E1[:, d, 0:n], in_=Gp[:, d, 0:n], func=AF.Exp, scale=-1.0 / 16.0), reads=["Gp" + dk], writes=["E1" + dk])
            P.op("act", lambda e, d=d: e.activation(out=E2[:, d, 0:n], in_=Gp[:, d, 0:n], func=AF.Exp, scale=1.0 / 16.0), reads=["Gp" + dk], writes=["E2" + dk])
            P.op("dve", lambda e, d=d: e.tensor_tensor(out=qt[:, d, c0:c0 + n], in0=qf[:, 0:n], in1=E1[:, d, 0:n], op=ALU.mult),
                 reads=["qf", "E1" + dk], writes=["qt" + dk])
            P.op("pool", lambda e, d=d: e.tensor_tensor(out=ktmp[:, d, 0:n], in0=kf[:, 0:n], in1=E2[:, d, 0:n], op=ALU.mult),
                 reads=["kf", "E2" + dk], writes=["ktmp" + dk])
            P.op("pool", lambda e, d=d: e.tensor_copy(out=kt[:, d, c0:c0 + n], in_=ktmp[:, d, 0:n]), reads=["ktmp" + dk], writes=["kt" + dk])
            endc = 0 if d == 1 else 63
            e3 = E1[:, d, 0:n].rearrange("p (c l) -> p c l", l=64)[:, :, endc:endc + 1]
            P.op("dve", lambda e, d=d, e3=e3: e.tensor_copy(out=dec[:, d, cb0:cb0 + nch].unsqueeze(2), in_=e3), reads=["E1" + dk], writes=["dec" + dk])
            P.op("dve", lambda e, d=d, e3=e3: e.tensor_tensor(out=khf[:, d, 0:n].rearrange("p (c l) -> p c l", l=64),
                                                             in0=ktmp[:, d, 0:n].rearrange("p (c l) -> p c l", l=64),
                                                             in1=e3.to_broadcast([64, nch, 64]), op=ALU.mult),
                 reads=["ktmp" + dk, "E1" + dk], writes=["Gp" + dk])
            for t in range(nch):
                b = C.bank((4, 4), 1)
                pk = "ps%d" % b
                ci = cb0 + t
                P.op("pe", lambda e, b=b, d=d, t=t: e.transpose(C.ps[b][0:64, 0:64], khf[:, d, t * 64:(t + 1) * 64], C.ident[0:64, 0:64]),
                     reads=["Gp" + dk, "ident"], writes=[pk])
                P.op("act", lambda e, b=b, d=d, ci=ci: e.activation(out=ktok[:, d, ci, :], in_=C.ps[b][0:64, 0:64], func=AF.Copy),
                     reads=[pk], writes=["ktok%d_%d" % (d, ci)])

    Sst = P.sbuf("Sst", [64, 2, 128], F32)
    Sbf = P.sbuf("Sbf", [64, 2, 128], BF16)
    PT = P.sbuf("PT", [64, 4, 64], BF16)
    orders = [chunk_order(NCC, NCN, d) for d in range(2)]
    for d in range(2):
        P.op("pool", lambda e, d=d: e.memset(Sst[:, d, :], 0.0), writes=["Sst%d" % d])
        P.op("pool", lambda e, d=d: e.memset(Sbf[:, d, :], 0.0), writes=["Sbf%d" % d])
    it = 0
    hwritten = set()
    for step in range(NCN):
        for d in range(2):
            dk = str(d)
            c = orders[d][step]
            last = step + 1 >= NCN
            sl = slice(c * 64, (c + 1) * 64)
            j = it % 4
            it += 1
            b1 = C.bank((0, 3), 1)
            P.op("pe", lambda e, b1=b1, sl=sl, d=d: e.matmul(C.ps[b1][0:64, 0:64], lhsT=kt[:, d, sl], rhs=qt[:, d, sl], start=True, stop=True),
                 reads=["kt" + dk, "qt" + dk], writes=["ps%d" % b1])
            P.op("dve", lambda e, b1=b1, j=j, d=d: e.tensor_tensor(out=PT[:, j, :], in0=C.ps[b1][0:64, 0:64], in1=masks[:, d, :], op=ALU.mult),
                 reads=["ps%d" % b1, "masks"], writes=["PT%d" % j])
            b2 = C.bank((3, 3), 1)
            pk2 = "ps%d" % b2
            P.op("pe", lambda e, b2=b2, sl=sl, d=d: e.matmul(C.ps[b2][0:64, 0:128], lhsT=qt[:, d, sl], rhs=Sbf[:, d, :], start=True, stop=False),
                 reads=["qt" + dk, "Sbf" + dk], writes=[pk2], inc=False)
            P.op("pe", lambda e, b2=b2, j=j, c=c: e.matmul(C.ps[b2][0:64, 0:128], lhsT=PT[:, j, :], rhs=v64[:, c, :], start=False, stop=True),
                 reads=["PT%d" % j, "v64_%d" % c], writes=[pk2])
            hk = "hacc%d" % c
            if c not in hwritten:
                hwritten.add(c)
                P.op("act", lambda e, b2=b2, c=c: e.activation(out=hacc[:, c, :], in_=C.ps[b2][0:64, 0:128], func=AF.Copy), reads=[pk2], writes=[hk])
            else:
                P.op("dve", lambda e, b2=b2, c=c: e.tensor_tensor(out=hacc[:, c, :], in0=hacc[:, c, :], in1=C.ps[b2][0:64, 0:128], op=ALU.add),
                     reads=[pk2, hk], writes=[hk])
            if not last:
                b3 = C.bank((6, 2), 1)
                pk3 = "ps%d" % b3
                P.op("pe", lambda e, b3=b3, c=c, d=d: e.matmul(C.ps[b3][0:64, 0:128], lhsT=ktok[:, d, c, :], rhs=v64[:, c, :], start=True, stop=True),
                     reads=["ktok%d_%d" % (d, c), "v64_%d" % c], writes=[pk3])
                P.op("dve", lambda e, b3=b3, d=d, c=c: e.scalar_tensor_tensor(out=Sst[:, d, :], in0=Sst[:, d, :], scalar=dec[:, d, c:c + 1],
                                                                             in1=C.ps[b3][0:64, 0:128], op0=ALU.mult, op1=ALU.add),
                     reads=["Sst" + dk, "dec" + dk, pk3], writes=["Sst" + dk])
                P.op("pool", lambda e, d=d: e.tensor_copy(out=Sbf[:, d, :], in_=Sst[:, d, :]), reads=["Sst" + dk], writes=["Sbf" + dk])

    emit_head_finish(P, C, hacc, NCN, gn[:, 0:1], "gn", goT, "goT", youT, "youT", "g", extra_w=hbkeys)
    P.dma("sp", yT[:, :], youT, reads=["youT"])
    return P.finish()


def rope_tables(n_lat):
    rows = n_lat // 64
    row = np.repeat(np.arange(rows, dtype=np.float32), 64)
    col = np.tile(np.arange(64, dtype=np.float32), rows)
    half = 8
    inv_freq = (10000.0 ** (-np.arange(half, dtype=np.float32) / half)).astype(np.float32)
    ang_r = row[:, None] * inv_freq
    ang_c = col[:, None] * inv_freq
    ang = np.concatenate([ang_r, ang_r, ang_c, ang_c], axis=-1)
    return ang


def rope_consts(n_lat):
    rows = n_lat // 64
    row = np.repeat(np.arange(rows, dtype=np.float32), 64)
    col = np.tile(np.arange(64, dtype=np.float32), rows)
    half = 16 // 1 // 2 * 1
    half = 16
    inv_freq = (np.float32(10000.0) ** (-np.arange(half, dtype=np.float32) / np.float32(half))).astype(np.float32)
    ang_r = (row[:, None] * inv_freq).astype(np.float32)
    ang_c = (col[:, None] * inv_freq).astype(np.float32)
    ang = np.concatenate([ang_r, ang_r, ang_c, ang_c], axis=-1)
    cos = np.cos(ang).astype(np.float32)
    sin = np.sin(ang).astype(np.float32)
    sgn = np.ones(64, np.float32)
    perm = np.zeros(64, np.int64)
    for d in range(64):
        blk, i = d // 32, d % 32
        if i < 16:
            perm[d] = blk * 32 + i + 16
            sgn[d] = -1.0
        else:
            perm[d] = blk * 32 + i - 16
    cosT = np.concatenate([cos.T, cos.T], 0)
    sinT = np.concatenate([(sin * sgn[None]).T, (sin * sgn[None]).T], 0)
    pm = np.zeros((128, 128), np.float32)
    for m in range(2):
        for d in range(64):
            pm[m * 64 + perm[d], m * 64 + d] = 1.0
    return np.ascontiguousarray(cosT), np.ascontiguousarray(sinT), pm


def build_attn(n_ctx, n_lat, lam_init, need_ctx):
    P = Prog()
    N = n_ctx + n_lat
    NT = N // 128
    hT = P.dram("hT", [D, N], BF16, "ExternalInput")
    ident_d = P.dram("ident", [128, 128], F32, "ExternalInput")
    w_qk = P.dram("w_qk", [4, 128, NCH * 128], F32, "ExternalInput")
    w_v = P.dram("w_v", [128, NCH * 256], F32, "ExternalInput")
    cos_d = P.dram("cosT", [128, n_lat], F32, "ExternalInput")
    sin_d = P.dram("sinT", [128, n_lat], F32, "ExternalInput")
    pm_d = P.dram("pm", [128, 128], F32, "ExternalInput")
    dl_d = P.dram("dlam", [1, 256], F32, "ExternalInput")
    sub_d = P.dram("subln", [128, 1], F32, "ExternalInput")
    yT = P.dram("yT", [256, N], BF16, "ExternalOutput")

    C = Ctx(P, ident_d)
    H = HStream(P, hT, N)
    wqk = P.sbuf("wqk", [128, 4, NCH, 128], BF16)
    wv = P.sbuf("wv", [128, NCH, 256], BF16)
    for g in range(4):
        P.dma("pool", wqk[:, g, :, :], w_qk[g, :, :], writes=["wqk%d" % g])
    P.dma("pool", wv[:, :, :], w_v[:, :], writes=["wv"])
    cosT = P.sbuf("cosT", [128, n_lat], F32)
    sinT = P.sbuf("sinT", [128, n_lat], F32)
    pm = P.sbuf("pm", [128, 128], F32)
    dl = P.sbuf("dl", [128, 256], F32)
    sub = P.sbuf("sub", [128, 1], F32)
    P.dma("sp", cosT[:, :], cos_d[:, :], writes=["cosT"])
    P.dma("sp", sinT[:, :], sin_d[:, :], writes=["sinT"])
    P.dma("sp", pm[:, :], pm_d[:, :], writes=["pm"])
    P.dma("sp", dl[:, :], dl_d[0:1, :].to_broadcast([128, 256]), writes=["dl"])
    P.dma("sp", sub[:, :], sub_d[:, :], writes=["sub"])
    lt = P.sbuf("lt", [128, 8], F32)
    ltmp = P.sbuf("ltmp", [128, 128], F32)
    for i in range(2):
        P.op("dve", lambda e, i=i: e.tensor_tensor(out=ltmp[:, i * 64:(i + 1) * 64], in0=dl[:, 128 * i:128 * i + 64], in1=dl[:, 128 * i + 64:128 * i + 128], op=ALU.mult),
             reads=["dl"], writes=["ltmp"])
        P.op("dve", lambda e, i=i: e.reduce_sum(out=lt[:, i:i + 1], in_=ltmp[:, i * 64:(i + 1) * 64], axis=AX.X), reads=["ltmp"], writes=["lt"])
    P.op("act", lambda e: e.activation(out=lt[:, 2:4], in_=lt[:, 0:2], func=AF.Exp), reads=["lt"], writes=["lt"])
    P.op("dve", lambda e: e.tensor_tensor(out=lt[:, 4:5], in0=lt[:, 3:4], in1=lt[:, 2:3], op=ALU.subtract), reads=["lt"], writes=["lt"])
    P.op("dve", lambda e: e.tensor_scalar(out=lt[:, 5:6], in0=lt[:, 4:5], scalar1=-float(lam_init), scalar2=None, op0=ALU.add), reads=["lt"], writes=["lt"])
    neglam = lt[:, 5:6]
    P.op("dve", lambda e: e.tensor_scalar(out=lt[:, 6:7], in0=sub[:, 0:1], scalar1=1.0 - float(lam_init), scalar2=None, op0=ALU.mult),
         reads=["sub"], writes=["lt"])
    subs = lt[:, 6:7]

    qkT = P.sbuf("qkT", [128, 4, N], BF16)
    vd = P.sbuf("vd", [128, NT, 256], BF16)
    yst = P.sbuf("yst", [128, 2, N], BF16)
    xf = P.sbuf("xf", [128, 512], F32)
    t1 = P.sbuf("t1", [128, 512], F32)
    t2 = P.sbuf("t2", [128, 512], F32)
    onesb = P.sbuf("onesb", [128, 1], BF16)
    P.op("pool", lambda e: e.memset(onesb[:, :], 1.0), writes=["onesb"])

    for (c0, n) in token_blocks(N):
        s, keys = H.load(c0, n)
        for g in range(4):
            b = C.bank((0, 4), 1)
            pap, pk = emit_proj_fm(P, C, H, s, keys, wqk[:, g, :, :], "wqk%d" % g, 128, n, b)
            sc_ = 0.125 if g < 2 else 1.0
            P.op("act", lambda e, pap=pap, sc_=sc_: e.activation(out=xf[:, 0:n], in_=pap, func=AF.Copy, scale=sc_), reads=[pk], writes=["xf"])
            nc_ = max(0, min(n, n_ctx - c0))
            if nc_ > 0:
                P.op("dve", lambda e, g=g, nc_=nc_: e.tensor_copy(out=qkT[:, g, c0:c0 + nc_], in_=xf[:, 0:nc_]), reads=["xf"], writes=["qkT%d" % g])
            if nc_ < n:
                l0 = c0 + nc_ - n_ctx
                nl = n - nc_
                b2 = C.bank((4, 4), 1)
                pk2 = "ps%d" % b2
                P.op("pe", lambda e, b2=b2, nc_=nc_, nl=nl: e.matmul(C.ps[b2][:, 0:nl], lhsT=pm[:, :], rhs=xf[:, nc_:nc_ + nl], start=True, stop=True),
                     reads=["pm", "xf"], writes=[pk2])
                P.op("dve", lambda e, nc_=nc_, nl=nl, l0=l0: e.tensor_tensor(out=t1[:, 0:nl], in0=xf[:, nc_:nc_ + nl], in1=cosT[:, l0:l0 + nl], op=ALU.mult),
                     reads=["xf", "cosT"], writes=["t1"])
                P.op("dve", lambda e, b2=b2, nl=nl, l0=l0: e.tensor_tensor(out=t2[:, 0:nl], in0=C.ps[b2][:, 0:nl], in1=sinT[:, l0:l0 + nl], op=ALU.mult),
                     reads=[pk2, "sinT"], writes=["t2"])
                P.op("pool", lambda e, g=g, nc_=nc_, nl=nl: e.tensor_tensor(out=qkT[:, g, c0 + nc_:c0 + n], in0=t1[:, 0:nl], in1=t2[:, 0:nl], op=ALU.add),
                     reads=["t1", "t2"], writes=["qkT%d" % g])
        for t in range(n // 128):
            b = C.bank((4, 4), 1)
            pap, pk = emit_proj_tm(P, C, H, s, keys, wv, "wv", t * 128, 128, 256, b)
            ti = c0 // 128 + t
            P.op("dve", lambda e, pap=pap, ti=ti: e.tensor_copy(out=vd[:, ti, :], in_=pap), reads=[pk], writes=["vd%d" % ti])

    Eb = P.sbuf("Eb", [128, 4, 512], BF16)
    nsb = P.sbuf("nsb", [128, 2, 512], F32)
    drow = P.sbuf("drow", [1, 2, 512], F32)
    rc = P.sbuf("rc", [128, 4, 4], F32)
    hd = P.sbuf("hd", [128, 2, 128], F32)
    qblocks = []
    if need_ctx and n_ctx > 0:
        qblocks.append((0, n_ctx, 0, n_ctx // 128))
    for (c0, n) in token_blocks(n_lat):
        qblocks.append((n_ctx + c0, n, 0, NT))
    ei = 0
    ri = 0
    for hh in range(2):
        for (q0, nq, kt0, kt1) in qblocks:
            for ki in range(kt0, kt1):
                for m in range(2):
                    bs = C.bank((0, 4), 1)
                    pks = "ps%d" % bs
                    P.op("pe", lambda e, bs=bs, m=m, ki=ki, q0=q0, nq=nq, hh=hh: e.matmul(
                        C.ps[bs][:, 0:nq], lhsT=qkT[64 * m:64 * m + 64, 2 + hh, ki * 128:(ki + 1) * 128],
                        rhs=qkT[64 * m:64 * m + 64, hh, q0:q0 + nq], start=True, stop=True),
                        reads=["qkT%d" % (2 + hh), "qkT%d" % hh], writes=[pks])
                    ej = ei % 4
                    ei += 1
                    P.op("act", lambda e, bs=bs, ej=ej, nq=nq: e.activation(out=Eb[:, ej, 0:nq], in_=C.ps[bs][:, 0:nq], func=AF.Exp),
                         reads=[pks], writes=["Eb%d" % ej])
                    P.op("pe", lambda e, m=m, ki=ki, ej=ej, nq=nq, hh=hh: e.matmul(
                        C.ps[4 + m][:, 0:nq], lhsT=vd[:, ki, hh * 128:(hh + 1) * 128], rhs=Eb[:, ej, 0:nq],
                        start=(ki == kt0), stop=(ki == kt1 - 1)),
                        reads=["vd%d" % ki, "Eb%d" % ej], writes=["ps%d" % (4 + m)], inc=False)
                    P.op("pe", lambda e, m=m, ki=ki, ej=ej, nq=nq: e.matmul(
                        C.ps[6 + m][0:1, 0:nq], lhsT=onesb[:, 0:1], rhs=Eb[:, ej, 0:nq],
                        start=(ki == kt0), stop=(ki == kt1 - 1)),
                        reads=["onesb", "Eb%d" % ej], writes=["ps%d" % (6 + m)])
            for m in range(2):
                P.op("act", lambda e, m=m, nq=nq: e.activation(out=nsb[:, m, 0:nq], in_=C.ps[4 + m][:, 0:nq], func=AF.Copy),
                     reads=["ps%d" % (4 + m)], writes=["nsb%d" % m])
                P.op("dve", lambda e, m=m, nq=nq: e.tensor_copy(out=drow[:, m, 0:nq], in_=C.ps[6 + m][0:1, 0:nq]),
                     reads=["ps%d" % (6 + m)], writes=["drow%d" % m])
            for qs in range(nq // 128):
                b = C.bank((0, 4), 1)
                pk = "ps%d" % b
                qsl = slice(qs * 128, (qs + 1) * 128)
                for m in range(2):
                    P.op("pe", lambda e, b=b, m=m, qsl=qsl: e.transpose(C.ps[b][:, m * 128:(m + 1) * 128], nsb[:, m, qsl], C.ident[:, :]),
                         reads=["nsb%d" % m, "ident"], writes=[pk], inc=False)
                for m in range(2):
                    P.op("pe", lambda e, b=b, m=m, qsl=qsl: e.transpose(C.ps[b][:, 256 + m:257 + m], drow[:, m, qsl], C.ident[0:1, 0:1]),
                         reads=["drow%d" % m, "ident"], writes=[pk], inc=(m == 1))
                rj = ri % 4
                ri += 1
                rk = "rc%d" % rj
                P.op("dve", lambda e, b=b, rj=rj: e.reciprocal(out=rc[:, rj, 0:2], in_=C.ps[b][:, 256:258]), reads=[pk], writes=[rk])
                P.op("dve", lambda e, rj=rj: e.tensor_scalar(out=rc[:, rj, 2:3], in0=rc[:, rj, 1:2], scalar1=neglam, scalar2=None, op0=ALU.mult),
                     reads=[rk, "lt"], writes=[rk])
                hj = rj % 2
                hk_ = "hd%d" % hj
                P.op("act", lambda e, b=b, rj=rj, hj=hj: e.activation(out=hd[:, hj, :], in_=C.ps[b][:, 0:128], func=AF.Copy, scale=rc[:, rj, 0:1]),
                     reads=[pk, rk], writes=[hk_])
                P.op("dve", lambda e, b=b, rj=rj, hj=hj: e.scalar_tensor_tensor(out=hd[:, hj, :], in0=C.ps[b][:, 128:256], scalar=rc[:, rj, 2:3],
                                                                              in1=hd[:, hj, :], op0=ALU.mult, op1=ALU.add),
                     reads=[pk, rk, hk_], writes=[hk_])
                c = C.statcol(2)
                P.op("act", lambda e, hj=hj, c=c: e.activation(out=C.junk[:, 0:128], in_=hd[:, hj, :], func=AF.Square, accum_out=C.stat[:, c:c + 1]),
                     reads=[hk_], writes=["junk", "stat%d" % c])
                emit_rstd(P, C, C.stat[:, c:c + 1], "stat%d" % c, 128, C.stat[:, c + 1:c + 2], "stat%d" % (c + 1), 128)
                P.op("act", lambda e, hj=hj, c=c: e.activation(out=hd[:, hj, :], in_=hd[:, hj, :], func=AF.Copy, scale=C.stat[:, c + 1:c + 2]),
                     reads=[hk_, "stat%d" % (c + 1)], writes=[hk_])
                b4 = C.bank((0, 4), 1)
                pk4 = "ps%d" % b4
                P.op("pe", lambda e, b4=b4, hj=hj: e.transpose(C.ps[b4][:, 0:128], hd[:, hj, :], C.ident[:, :]), reads=[hk_, "ident"], writes=[pk4])
                P.op("dve", lambda e, b4=b4, hh=hh, q0=q0, qs=qs: e.tensor_scalar(out=yst[:, hh, q0 + qs * 128:q0 + (qs + 1) * 128], in0=C.ps[b4][:, 0:128],
                                                                              scalar1=subs, scalar2=None, op0=ALU.mult),
                     reads=[pk4, "lt"], writes=["yst%d" % hh])
    for hh in range(2):
        if not (need_ctx and n_ctx > 0) and n_ctx > 0:
            P.op("pool", lambda e, hh=hh: e.memset(yst[:, hh, 0:n_ctx], 0.0), writes=["yst%d" % hh])
        P.dma("sp", yT[hh * 128:(hh + 1) * 128, :], yst[:, hh, :], reads=["yst%d" % hh])
    return P.finish()


MODC = 6 * D // NCORES


def build_mod():
    P = Prog()
    cT_d = P.dram("cT", [128, NCH * 3], F32, "ExternalInput")
    wm = P.dram("wm", [DEPTH, 128, NCH * MODC], F32, "ExternalInput")
    bm = P.dram("bm", [DEPTH, MODC], F32, "ExternalInput")
    out = P.dram("mod", [DEPTH, 3, MODC], F32, "ExternalOutput")
    cT = P.sbuf("cT", [128, NCH, 3], F32)
    P.dma("sp", cT[:, :, :], cT_d[:, :], writes=["cT"])
    P.op("act", lambda e: e.activation(out=cT[:, :, :], in_=cT[:, :, :], func=AF.Silu), reads=["cT"], writes=["cT"])
    ps = [P.psum("ps%d" % i, [128, 512], F32) for i in range(2)]
    wt = P.sbuf("wt", [128, 2, NCH, 512], F32)
    bt = P.sbuf("bt", [3, 2, 512], F32)
    ot = P.sbuf("ot", [3, 2, 512], F32)
    i = 0
    for l in range(DEPTH):
        wl = wm[l, :, :].rearrange("p (k m) -> p k m", m=MODC)
        for nb in range(MODC // 512):
            s = i % 2
            i += 1
            for k in range(NCH):
                P.dma("sp", wt[:, s, k, :], wl[:, k, nb * 512:(nb + 1) * 512], writes=["wt%d_%d" % (s, k)])
            P.dma("sp", bt[:, s, :], bm[l:l + 1, nb * 512:(nb + 1) * 512].to_broadcast([3, 512]), writes=["bt%d" % s])
            for k in range(NCH):
                P.op("pe", lambda e, s=s, k=k: e.matmul(ps[s][0:3, :], lhsT=cT[:, k, :], rhs=wt[:, s, k, :], start=(k == 0), stop=(k == NCH - 1)),
                     reads=["cT", "wt%d_%d" % (s, k)], writes=["ps%d" % s], inc=(k == NCH - 1))
            P.op("dve", lambda e, s=s: e.tensor_tensor(out=ot[:, s, :], in0=ps[s][0:3, :], in1=bt[:, s, :], op=ALU.add),
                 reads=["ps%d" % s, "bt%d" % s], writes=["ot%d" % s])
            P.dma("sp", out[l, :, nb * 512:(nb + 1) * 512], ot[:, s, :], reads=["ot%d" % s])
    return P.finish()


OFF = dict(m_q=0, m_k=512, m_v=1024, m_o=1536, m_g=2048, g_q=2064, g_k=2320, g_v=2576, g_out=3088, g_lr=3600,
           d_q=3632, d_k=4656, d_v=5680)


def _relay(w):
    K_, M = w.shape[0] // 128, w.shape[1]
    return np.ascontiguousarray(w.reshape(K_, 128, M).transpose(1, 0, 2).reshape(128, K_ * M))


def _relay_chunks(w):
    K_, J = w.shape[0] // 128, w.shape[1] // 128
    return np.ascontiguousarray(w.reshape(K_, 128, J, 128).transpose(2, 1, 0, 3).reshape(J, 128, K_ * 128))


def _cols16(v):
    v = np.asarray(v, np.float32).reshape(-1, NCH, 128)
    return np.ascontiguousarray(v.transpose(2, 0, 1))


_PROGS = {}
_DEBUG = None


def _prog(key, fn):
    if key not in _PROGS:
        _PROGS[key] = fn()
    return _PROGS[key]


def _run(nc, in_maps):
    res = run_bass_kernel_spmd(nc, in_maps, core_ids=list(range(NCORES)))
    return res.results


def kernel(x, c, ctx, c_ctx, w_mod, b_mod, norm_mix_pre, norm_mix_post, norm_ffn_pre, norm_ffn_post, w_in,
           mlstm_conv_w, mlstm_conv_b, mlstm_gate_b, mlstm_norm, gla_gate_w2, gla_gate_b, gla_norm,
           diff_lambda, diff_subln, w_out, w_ffn_gate, w_ffn_up, w_ffn_down):
    f32 = np.float32
    x = np.asarray(x, f32)
    ctx = np.asarray(ctx, f32)
    B = x.shape[0]
    ident = np.eye(128, dtype=f32)
    QT = SEQ // 4
    QC = CTX // 4

    cvec = np.concatenate([np.asarray(c, f32), np.asarray(c_ctx, f32)[None]], 0)
    cT = np.ascontiguousarray(cvec.reshape(3, NCH, 128).transpose(2, 1, 0).reshape(128, NCH * 3))
    w_mod = np.asarray(w_mod, f32)
    b_mod = np.asarray(b_mod, f32)
    maps = []
    for core in range(NCORES):
        cs = slice(core * MODC, (core + 1) * MODC)
        maps.append(dict(cT=cT, wm=np.stack([_relay(w_mod[l][:, cs]) for l in range(DEPTH)]),
                         bm=np.ascontiguousarray(b_mod[:, cs])))
    res = _run(_prog("mod", build_mod), maps)
    mod = np.concatenate([r["mod"] for r in res], axis=2)
    mod = mod.reshape(DEPTH, 3, 6, D)
    if _DEBUG is not None:
        _DEBUG["mod"] = mod

    def dense_maps(layer, xs_core, yT_core, do_c):
        la = min(layer + 1, DEPTH - 1) if do_c else layer
        norms = np.stack([np.asarray(a, f32)[layer] for a in (norm_mix_pre, norm_mix_post, norm_ffn_pre, norm_ffn_post)])
        ncols_src = norms.copy()
        ncols_src[0] = np.asarray(norm_mix_pre, f32)[la]
        ncols = _cols16(ncols_src)
        shared = {}
        if do_c:
            shared = dict(wo_r=_relay_chunks(np.asarray(w_out, f32)[layer]), wg_r=_relay_chunks(np.asarray(w_ffn_gate, f32)[layer]),
                          wu_r=_relay_chunks(np.asarray(w_ffn_up, f32)[layer]), wd_r=_relay_chunks(np.asarray(w_ffn_down, f32)[layer]),
                          normrows=norms)
        maps = []
        for core in range(NCORES):
            b = core // 4
            mrows = np.concatenate([mod[layer, b], mod[layer, 2]], 0)
            mcols_src = mrows.copy()
            mcols_src[0:2] = mod[la, b, 0:2]
            mcols_src[6:8] = mod[la, 2, 0:2]
            m = dict(x=xs_core[core], ident=ident, modcols=_cols16(mcols_src), normcols=ncols)
            if do_c:
                m.update(shared)
                m["modrows"] = np.ascontiguousarray(mrows)
                m["yT"] = yT_core[core]
            maps.append(m)
        return maps

    def gather_hT(res_list, n_ctx_core):
        out = []
        for b in range(B):
            hall = np.zeros((D, NTOK), dtype=ml_dtypes.bfloat16)
            for qq in range(4):
                h = res_list[b * 4 + qq]["hT_out"]
                hall[:, qq * QC:(qq + 1) * QC] = h[:, 0:QC]
                hall[:, CTX + qq * QT:CTX + (qq + 1) * QT] = h[:, QC:QC + QT]
            out.append(hall)
        return out

    xs_core = []
    for core in range(NCORES):
        b, qq = core // 4, core % 4
        xs_core.append(np.ascontiguousarray(np.concatenate([ctx[b, qq * QC:(qq + 1) * QC], x[b, qq * QT:(qq + 1) * QT]], 0)))
    res = _run(_prog("A", lambda: build_dense(QC, QT, False, True)), dense_maps(0, xs_core, None, False))
    hT_all = gather_hT(res, QC)
    if _DEBUG is not None:
        _DEBUG["hT0"] = hT_all

    w_in = np.asarray(w_in, f32)
    masks = mlstm_masks()
    tri = mlstm_tri(CTX // 64, NTOK // 64)
    rmask = gla_rmask()
    cosT, sinT, pm = rope_consts(SEQ)
    x_out = None
    for layer in range(DEPTH):
        last = layer == DEPTH - 1
        w = w_in[layer]
        lam_init = 0.8 - 0.6 * math.exp(-0.3 * layer)
        cw_all = np.asarray(mlstm_conv_w, f32)[layer]
        cb_all = np.asarray(mlstm_conv_b, f32)[layer]
        gb_all = np.asarray(mlstm_gate_b, f32)[layer]
        m_maps, g_maps, a_maps = [], [], []
        for core in range(NCORES):
            b, q = core // 4, core % 4
            cols = lambda name, a, n: w[:, OFF[name] + a:OFF[name] + a + n]
            cw = np.zeros((128, 8), f32)
            cw[:, 0:3] = cw_all[:, 128 * q:128 * q + 128].T
            cw[:, 3:6] = cw_all[:, 512 + 128 * q:512 + 128 * q + 128].T
            cw[:, 6] = cb_all[128 * q:128 * q + 128]
            cw[:, 7] = cb_all[512 + 128 * q:512 + 128 * q + 128]
            gidx = [OFF["m_g"] + i for i in (q, 4 + q, 8 + q, 12 + q)]
            m_maps.append(dict(
                hT=hT_all[b], ident=ident,
                w_fm=np.stack([_relay(cols("m_q", 128 * q, 128)), _relay(cols("m_k", 128 * q, 128)), _relay(cols("m_o", 128 * q, 128))]),
                w_g=_relay(w[:, gidx]), w_v=_relay(cols("m_v", 128 * q, 128)), cw=cw,
                gb=np.ascontiguousarray(gb_all[[q, 4 + q, 8 + q, 12 + q]].reshape(4, 1)),
                mn=np.ascontiguousarray(np.asarray(mlstm_norm, f32)[layer][128 * q:128 * q + 128].reshape(128, 1)),
                masks=masks, tri=tri))
            gw2 = np.asarray(gla_gate_w2, f32)[layer]
            gbb = np.asarray(gla_gate_b, f32)[layer]
            g_maps.append(dict(
                hT=hT_all[b], ident=ident,
                w_qk=_relay(np.concatenate([cols("g_q", 64 * q, 64), cols("g_k", 64 * q, 64)], 1)),
                w_go=_relay(cols("g_out", 128 * q, 128)), w_lr=_relay(cols("g_lr", 0, 32)), w_v=_relay(cols("g_v", 128 * q, 128)),
                w2=np.ascontiguousarray(np.concatenate([gw2[0][:, 64 * q:64 * q + 64], gw2[1][:, 64 * q:64 * q + 64]], 1)),
                nb=np.ascontiguousarray((gbb[:, 64 * q:64 * q + 64] * f32(-1.0)).T) if False else np.ascontiguousarray(np.negative(gbb[:, 64 * q:64 * q + 64]).T),
                gn=np.ascontiguousarray(np.asarray(gla_norm, f32)[layer][128 * q:128 * q + 128].reshape(128, 1)),
                masks=masks, rmask=rmask))
            a_maps.append(dict(
                hT=hT_all[b], ident=ident,
                w_qk=np.stack([_relay(cols("d_q", 256 * q, 128)), _relay(cols("d_q", 256 * q + 128, 128)),
                               _relay(cols("d_k", 256 * q, 128)), _relay(cols("d_k", 256 * q + 128, 128))]),
                w_v=_relay(cols("d_v", 256 * q, 256)), cosT=cosT, sinT=sinT, pm=pm,
                dlam=np.ascontiguousarray(np.asarray(diff_lambda, f32)[layer].reshape(1, 256)),
                subln=np.ascontiguousarray(np.asarray(diff_subln, f32)[layer].reshape(128, 1))))
        res_m = _run(_prog("mlstm", lambda: build_mlstm(CTX, SEQ)), m_maps)
        res_g = _run(_prog("gla", lambda: build_gla(CTX, SEQ)), g_maps)
        res_a = _run(_prog("attn%d" % layer, lambda: build_attn(CTX, SEQ, lam_init, not last)), a_maps)
        ymix = []
        for b in range(B):
            ym = np.zeros((D, NTOK), dtype=ml_dtypes.bfloat16)
            for q in range(4):
                core = b * 4 + q
                ym[128 * q:128 * q + 128] = res_m[core]["yT"]
                ym[512 + 128 * q:512 + 128 * q + 128] = res_g[core]["yT"]
                ym[1024 + 256 * q:1024 + 256 * q + 256] = res_a[core]["yT"]
            ymix.append(ym)
        if _DEBUG is not None:
            _DEBUG["ymix%d" % layer] = ymix
            if _DEBUG.get("stop_after_mix") == layer:
                return None
        n_ctx_core = 0 if last else QC
        yT_core = []
        for core in range(NCORES):
            b, qq = core // 4, core % 4
            parts = []
            if n_ctx_core:
                parts.append(ymix[b][:, qq * QC:(qq + 1) * QC])
            parts.append(ymix[b][:, CTX + qq * QT:CTX + (qq + 1) * QT])
            yT_core.append(np.ascontiguousarray(np.concatenate(parts, 1)))
        if last:
            xs_core = [np.ascontiguousarray(xc[xc.shape[0] - QT:]) for xc in xs_core]
        key = "C%d_%d" % (n_ctx_core, int(not last))
        res = _run(_prog(key, lambda: build_dense(n_ctx_core, QT, True, not last)), dense_maps(layer, xs_core, yT_core, True))
        xs_core = [r["x_out"] for r in res]
        if not last:
            hT_all = gather_hT(res, QC)
        if _DEBUG is not None:
            _DEBUG["xs%d" % layer] = xs_core
            _DEBUG["hT%d" % (layer + 1)] = hT_all
    out = np.zeros((B, SEQ, D), f32)
    for core in range(NCORES):
        b, qq = core // 4, core % 4
        out[b, qq * QT:(qq + 1) * QT] = xs_core[core][-QT:]
    return out
```

```python
import contextlib
import math

import ml_dtypes
import numpy as np

import concourse.bass as bass
import concourse.mybir as mybir
from concourse.bass_utils import run_bass_kernel_spmd

F32 = mybir.dt.float32
BF16 = mybir.dt.bfloat16
ALU = mybir.AluOpType
AF = mybir.ActivationFunctionType
AX = mybir.AxisListType

D = 2048
NCH = 16
FFN = 5632
NJ = 44
DEPTH = 2
CTX = 256
SEQ = 4096
NTOK = CTX + SEQ
EPS = 1e-6
IN_COLS = 6704
NCORES = 8


class Prog:
    K_RING = 12

    def __init__(self):
        self.nc = bass.Bass("TRN2", target_bir_lowering=False)
        self.es = contextlib.ExitStack()
        nc = self.nc
        self.eng = {}
        for name, h in (("pe", nc.tensor), ("act", nc.scalar), ("dve", nc.vector),
                        ("pool", nc.gpsimd), ("sp", nc.sync)):
            sem = self.es.enter_context(nc.semaphore("s_" + name))
            self.eng[name] = dict(h=h, sem=sem, sn="s_" + name, cnt=0, waited={})
        self.ring = {}
        self.rpos = {}
        for q in ("sp", "pool", "act"):
            self.ring[q] = []
            for i in range(self.K_RING):
                sem = self.es.enter_context(nc.semaphore("d_%s%d" % (q, i)))
                self.ring[q].append(dict(sem=sem, sn="d_%s%d" % (q, i), val=0))
            self.rpos[q] = 0
        self.lastw = {}
        self.readers = {}
        self.n_ops = 0
        self.psum_banks = []

    def sbuf(self, name, shape, dtype):
        return self.es.enter_context(self.nc.sbuf_tensor("sb_" + name, list(shape), dtype))

    def psum(self, name, shape, dtype):
        return self.es.enter_context(self.nc.psum_tensor("pp_" + name, list(shape), dtype))

    def dram(self, name, shape, dtype, kind):
        return self.nc.dram_tensor(name, list(shape), dtype, kind=kind).ap()

    def _collect(self, engname, reads, writes):
        own = self.eng[engname]["sn"]
        need = {}

        def add(ev, is_war):
            sn, sh, v = ev
            if sn == own:
                if engname == "pe":
                    return
            if sn not in need or need[sn][1] < v:
                need[sn] = (sh, v)

        for k in reads:
            e = self.lastw.get(k)
            if e is not None:
                add(e, False)
            if k.startswith("ps"):
                for e in self.readers.get(k, ()):
                    add(e, True)
        for k in writes:
            e = self.lastw.get(k)
            if e is not None:
                add(e, False)
            for e in self.readers.get(k, ()):
                add(e, True)
        return need

    def _emit_waits(self, engname, need):
        E = self.eng[engname]
        for sn, (sh, v) in need.items():
            if E["waited"].get(sn, 0) >= v:
                continue
            E["h"].wait_ge(sh, v)
            E["waited"][sn] = v

    def _record(self, ev, reads, writes):
        for k in writes:
            self.lastw[k] = ev
            self.readers[k] = []
        for k in reads:
            self.readers.setdefault(k, []).append(ev)

    def op(self, engname, fn, reads=(), writes=(), inc=True):
        E = self.eng[engname]
        need = self._collect(engname, reads, writes)
        self._emit_waits(engname, need)
        ins = fn(E["h"])
        if inc:
            E["cnt"] += 1
            ins.then_inc(E["sem"], 1)
            ev = (E["sn"], E["sem"], E["cnt"])
        else:
            ev = (E["sn"], E["sem"], E["cnt"] + 1)
        self._record(ev, reads, writes)
        self.n_ops += 1
        return ins

    def dma(self, q, out, in_, reads=(), writes=(), **kw):
        E = self.eng[q]
        slot = self.ring[q][self.rpos[q]]
        self.rpos[q] = (self.rpos[q] + 1) % self.K_RING
        need = self._collect(q, reads, writes)
        if slot["val"] > 0:
            if slot["sn"] not in need or need[slot["sn"]][1] < slot["val"]:
                need[slot["sn"]] = (slot["sem"], slot["val"])
        self._emit_waits(q, need)
        slot["val"] += 16
        E["h"].dma_start(out=out, in_=in_, **kw).then_inc(slot["sem"], 16)
        ev = (slot["sn"], slot["sem"], slot["val"])
        self._record(ev, reads, writes)
        self.n_ops += 1

    def coll(self, kind, in_ap, out_ap, reads=(), writes=(), groups=None):
        q = "pool"
        E = self.eng[q]
        slot = self.ring[q][self.rpos[q]]
        self.rpos[q] = (self.rpos[q] + 1) % self.K_RING
        need = self._collect(q, reads, writes)
        if slot["val"] > 0:
            if slot["sn"] not in need or need[slot["sn"]][1] < slot["val"]:
                need[slot["sn"]] = (slot["sem"], slot["val"])
        self._emit_waits(q, need)
        slot["val"] += 16
        groups = groups or [[0, 1, 2, 3], [4, 5, 6, 7]]
        E["h"].collective_compute(kind, ALU.bypass, groups, ins=[in_ap], outs=[out_ap]).then_inc(slot["sem"], 16)
        ev = (slot["sn"], slot["sem"], slot["val"])
        self._record(ev, reads, writes)
        self.n_ops += 1

    def barrier(self):
        evs = {}
        for name in ("pe", "act", "dve", "pool"):
            X = self.eng[name]
            if X["cnt"] > 0:
                evs[X["sn"]] = (X["sem"], X["cnt"])
        for q in ("sp", "pool", "act"):
            for slot in self.ring[q]:
                if slot["val"] > 0:
                    evs[slot["sn"]] = (slot["sem"], slot["val"])
        for name in ("pe", "act", "dve", "pool", "sp"):
            own = self.eng[name]["sn"]
            self._emit_waits(name, {k: v for k, v in evs.items() if k != own})
        self.lastw = {}
        self.readers = {}

    def finish(self):
        E = self.eng["sp"]
        for q in ("sp", "pool", "act"):
            for slot in self.ring[q]:
                if slot["val"] > 0 and E["waited"].get(slot["sn"], 0) < slot["val"]:
                    E["h"].wait_ge(slot["sem"], slot["val"])
                    E["waited"][slot["sn"]] = slot["val"]
        for name in ("pe", "act", "dve", "pool"):
            X = self.eng[name]
            if X["cnt"] > 0 and E["waited"].get(X["sn"], 0) < X["cnt"]:
                E["h"].wait_ge(X["sem"], X["cnt"])
        self.es.close()
        return self.nc


class Ctx:
    def __init__(self, P, ident_dram):
        self.P = P
        self.ps = [P.psum("ps%d" % i, [128, 512], F32) for i in range(8)]
        self.ident = P.sbuf("ident", [128, 128], F32)
        P.dma("sp", self.ident[:, :], ident_dram[:, :], writes=["ident"])
        self.identb = P.sbuf("identb", [128, 128], BF16)
        P.op("dve", lambda e: e.tensor_copy(out=self.identb[:, :], in_=self.ident[:, :]),
             reads=["ident"], writes=["identb"])
        self.junk = P.sbuf("junk", [128, 2048], BF16)
        self.stat = P.sbuf("stat", [128, 64], F32)
        self.stat_i = 0
        self.rot = {}

    def bank(self, group, n):
        lo, cnt = group
        i = self.rot.get(group, 0)
        self.rot[group] = (i + 1) % cnt
        return lo + i

    def statcol(self, n=1):
        i = self.stat_i
        if i + n > 64:
            i = 0
        self.stat_i = i + n
        return i


def emit_rstd(P, C, ss_ap, ss_key, rows, out_ap, out_key, n_feat):
    P.op("dve", lambda e: e.tensor_scalar(out=out_ap, in0=ss_ap, scalar1=1.0 / n_feat, scalar2=EPS,
                                          op0=ALU.mult, op1=ALU.add),
         reads=[ss_key], writes=[out_key])
    P.op("act", lambda e: e.activation(out=out_ap, in_=out_ap, func=AF.Sqrt),
         reads=[out_key], writes=[out_key])
    P.op("dve", lambda e: e.reciprocal(out=out_ap, in_=out_ap),
         reads=[out_key], writes=[out_key])


def emit_linear_fm(P, C, w_src, K, nout, stage, rhs, blocks, epilogue, banks=(0, 4), wq="pool",
                   name="lin"):
    ns = len(stage)
    for j in range(nout):
        st, skey = stage[j % ns]
        P.dma(wq, st, w_src(j), writes=[skey])
        for bi, (c0, n) in enumerate(blocks):
            b = C.bank(banks, 1)
            pk = "ps%d" % b
            pap = C.ps[b][:, 0:n]
            for k in range(K):
                r_ap, r_keys = rhs(k, c0, n)
                P.op("pe", lambda e, pap=pap, k=k, r_ap=r_ap: e.matmul(
                    pap, lhsT=st[:, k * 128:(k + 1) * 128], rhs=r_ap, start=(k == 0), stop=(k == K - 1)),
                    reads=[skey] + list(r_keys), writes=[pk], inc=(k == K - 1))
            epilogue(j, bi, pap, pk, c0, n)


def emit_to_tokmajor(P, C, srcT, src_key_fn, t0, rows, banks=(4, 4)):
    outs = []
    for n in range(4):
        b = C.bank(banks, 1)
        pk = "ps%d" % b
        for i in range(4):
            f = n * 4 + i
            P.op("pe", lambda e, b=b, i=i, f=f: e.transpose(
                C.ps[b][0:rows, i * 128:(i + 1) * 128], srcT[:, f, t0:t0 + rows], C.ident[:, :]),
                reads=[src_key_fn(f), "ident"], writes=[pk], inc=(i == 3))
        outs.append((C.ps[b][0:rows, :], pk))
    return outs


def emit_postnorm_residual(P, C, outs, rows, x_ap, x_key, g_ap, g_key, tmp_ap, tmp_key):
    c = C.statcol(6)
    ss = C.stat[0:rows, c:c + 4]
    for n, (pap, pk) in enumerate(outs):
        P.op("act", lambda e, pap=pap, n=n: e.activation(
            out=C.junk[0:rows, 0:512], in_=pap, func=AF.Square, accum_out=C.stat[0:rows, c + n:c + n + 1]),
            reads=[pk], writes=["junk", "stat%d" % (c + n)])
    tot = C.stat[0:rows, c + 4:c + 5]
    P.op("dve", lambda e: e.reduce_sum(out=tot, in_=ss, axis=AX.X),
         reads=["stat%d" % (c + n) for n in range(4)], writes=["stat%d" % (c + 4)])
    rstd = C.stat[0:rows, c + 5:c + 6]
    emit_rstd(P, C, tot, "stat%d" % (c + 4), rows, rstd, "stat%d" % (c + 5), D)
    for n, (pap, pk) in enumerate(outs):
        sl = slice(n * 512, (n + 1) * 512)
        P.op("dve", lambda e, pap=pap, sl=sl: e.scalar_tensor_tensor(
            out=tmp_ap[0:rows, sl], in0=pap, scalar=rstd, in1=g_ap[0:rows, sl], op0=ALU.mult, op1=ALU.mult),
            reads=[pk, "stat%d" % (c + 5), g_key], writes=[tmp_key])
        P.op("pool", lambda e, sl=sl: e.tensor_tensor(
            out=x_ap[0:rows, sl], in0=x_ap[0:rows, sl], in1=tmp_ap[0:rows, sl], op=ALU.add),
            reads=[tmp_key, x_key], writes=[x_key])


def emit_prenorm_T(P, C, x_ap, x_key, rows, xs_ap, xs_key, a_col, sh_col, mod_key, dst_fn, banks=(4, 4)):
    c = C.statcol(2)
    ss = C.stat[0:rows, c:c + 1]
    P.op("act", lambda e: e.activation(out=C.junk[0:rows, :], in_=x_ap[0:rows, :], func=AF.Square, accum_out=ss),
         reads=[x_key], writes=["junk", "stat%d" % c])
    rstd = C.stat[0:rows, c + 1:c + 2]
    emit_rstd(P, C, ss, "stat%d" % c, rows, rstd, "stat%d" % (c + 1), D)
    P.op("act", lambda e: e.activation(out=xs_ap[0:rows, :], in_=x_ap[0:rows, :], func=AF.Copy, scale=rstd),
         reads=[x_key, "stat%d" % (c + 1)], writes=[xs_key])
    for n in range(4):
        b = C.bank(banks, 1)
        pk = "ps%d" % b
        for i in range(4):
            f = n * 4 + i
            P.op("pe", lambda e, b=b, i=i, f=f: e.transpose(
                C.ps[b][:, i * 128:i * 128 + rows], xs_ap[0:rows, f * 128:(f + 1) * 128], C.ident[0:rows, 0:rows]),
                reads=[xs_key, "ident"], writes=[pk], inc=(i == 3))
        for i in range(4):
            f = n * 4 + i
            d_ap, d_key = dst_fn(f)
            eng = "dve" if (n % 2 == 0) else "act"
            if eng == "dve":
                P.op("dve", lambda e, b=b, i=i, f=f, d_ap=d_ap: e.tensor_scalar(
                    out=d_ap, in0=C.ps[b][:, i * 128:i * 128 + rows], scalar1=a_col[:, f:f + 1],
                    scalar2=sh_col[:, f:f + 1], op0=ALU.mult, op1=ALU.add),
                    reads=[pk, mod_key], writes=[d_key])
            else:
                P.op("act", lambda e, b=b, i=i, f=f, d_ap=d_ap: e.activation(
                    out=d_ap, in_=C.ps[b][:, i * 128:i * 128 + rows], func=AF.Identity,
                    scale=a_col[:, f:f + 1], bias=sh_col[:, f:f + 1]),
                    reads=[pk, mod_key], writes=[d_key])


def tile_groups(n_ctx, n_lat):
    tiles = []
    if n_ctx:
        tiles.append((0, n_ctx, True))
    for i in range(n_lat // 128):
        tiles.append((n_ctx + i * 128, 128, False))
    groups = []
    cur = []
    cur_n = 0
    for t in tiles:
        if cur and cur_n + t[1] > 576:
            groups.append(cur)
            cur, cur_n = [], 0
        cur.append(t)
        cur_n += t[1]
    if cur:
        groups.append(cur)
    out = []
    for g in groups:
        g0 = g[0][0]
        gn = sum(t[1] for t in g)
        out.append((g0, gn, g))
    return out


def blocks_of(gn):
    bl = []
    c = 0
    while c < gn:
        n = min(512, gn - c)
        bl.append((c, n))
        c += n
    return bl


def build_dense(n_ctx, n_lat, do_c, do_a, a_ctx=True):
    P = Prog()
    nc = P.nc
    ntok = n_ctx + n_lat
    x_in = P.dram("x", [ntok, D], F32, "ExternalInput")
    ident_d = P.dram("ident", [128, 128], F32, "ExternalInput")
    modcols = P.dram("modcols", [128, 12, NCH], F32, "ExternalInput")
    normcols = P.dram("normcols", [128, 4, NCH], F32, "ExternalInput")
    if do_c:
        modrows = P.dram("modrows", [12, D], F32, "ExternalInput")
        normrows = P.dram("normrows", [4, D], F32, "ExternalInput")
        yT_in = P.dram("yT", [D, ntok], BF16, "ExternalInput")
        wo_r = P.dram("wo_r", [NCH, 128, NCH * 128], F32, "ExternalInput")
        wg_r = P.dram("wg_r", [NJ, 128, NCH * 128], F32, "ExternalInput")
        wu_r = P.dram("wu_r", [NJ, 128, NCH * 128], F32, "ExternalInput")
        wd_r = P.dram("wd_r", [NCH, 128, NJ * 128], F32, "ExternalInput")
        x_out = P.dram("x_out", [ntok, D], F32, "ExternalOutput")
    if do_a:
        hT_out = P.dram("hT_out", [D, ntok], BF16, "ExternalOutput")

    C = Ctx(P, ident_d)
    GN = 576
    actT = P.sbuf("actT", [128, NCH, GN], BF16)
    x1 = P.sbuf("x1", [128, 5, D], F32)
    xs = P.sbuf("xs", [128, D], F32)
    mc = P.sbuf("mc", [128, 12, NCH], F32)
    ncol = P.sbuf("ncol", [128, 4, NCH], F32)
    acol = P.sbuf("acol", [128, 8, NCH], F32)
    P.dma("sp", mc[:, :, :], modcols[:, :, :], writes=["mc"])
    P.dma("sp", ncol[:, :, :], normcols[:, :, :], writes=["ncol"])
    for idx, (nrm, mrow) in enumerate(((0, 1), (0, 7), (2, 4), (2, 10))):
        P.op("dve", lambda e, idx=idx, nrm=nrm, mrow=mrow: e.scalar_tensor_tensor(
            out=acol[:, idx, :], in0=mc[:, mrow, :], scalar=1.0, in1=ncol[:, nrm, :], op0=ALU.add, op1=ALU.mult),
            reads=["mc", "ncol"], writes=["acol"])
    if do_c:
        oT = P.sbuf("oT", [128, NCH, GN], F32)
        aT = P.sbuf("aT", [128, NJ, GN], BF16)
        gbuf = P.sbuf("gbuf", [128, 2, D], F32)
        NS = 6
        wst = P.sbuf("wst", [128, NS, NCH * 128], BF16)
        stage = [(wst[:, i, :], "wst%d" % i) for i in range(NS)]
        sg = P.sbuf("sg", [128, 2, 512], F32)

    def load_g(which):
        nrow = 1 if which == 2 else 3
        P.dma("sp", xs[:, :], normrows[nrow:nrow + 1, :].to_broadcast([128, D]), writes=["xs"])
        for v, mrow in enumerate((which, 6 + which)):
            P.dma("sp", gbuf[:, v, :], modrows[mrow:mrow + 1, :].to_broadcast([128, D]), writes=["gbuf%d" % v])
            P.op("pool", lambda e, v=v: e.tensor_tensor(out=gbuf[:, v, :], in0=gbuf[:, v, :], in1=xs[:, :], op=ALU.mult),
                 reads=["xs", "gbuf%d" % v], writes=["gbuf%d" % v])

    groups = tile_groups(n_ctx, n_lat)
    for (g0, gn, tiles) in groups:
        blocks = blocks_of(gn)
        for ti, (t0, rows, is_ctx) in enumerate(tiles):
            P.dma("sp", x1[0:rows, ti, :], x_in[t0:t0 + rows, :], writes=["x1_%d" % ti])
        if do_c:
            for f in range(NCH):
                P.dma("sp", actT[:, f, 0:gn], yT_in[f * 128:(f + 1) * 128, g0:g0 + gn], writes=["actT%d" % f])

            def ep_copy(j, bi, pap, pk, c0, n):
                P.op("act", lambda e: e.activation(out=oT[:, j, c0:c0 + n], in_=pap, func=AF.Copy),
                     reads=[pk], writes=["oT%d" % j])

            emit_linear_fm(P, C, lambda j: wo_r[j, :, :], NCH, NCH, stage,
                           lambda k, c0, n: (actT[:, k, c0:c0 + n], ["actT%d" % k]), blocks, ep_copy)
            load_g(2)
            for ti, (t0, rows, is_ctx) in enumerate(tiles):
                outs = emit_to_tokmajor(P, C, oT, lambda f: "oT%d" % f, t0 - g0, rows)
                v = 1 if is_ctx else 0
                emit_postnorm_residual(P, C, outs, rows, x1[:, ti, :], "x1_%d" % ti, gbuf[:, v, :], "gbuf%d" % v,
                                       xs, "xs")
            for ti, (t0, rows, is_ctx) in enumerate(tiles):
                a_i, s_row = (3, 9) if is_ctx else (2, 3)
                emit_prenorm_T(P, C, x1[:, ti, :], "x1_%d" % ti, rows, xs, "xs", acol[:, a_i, :], mc[:, s_row, :],
                               "acol", lambda f, t0=t0, rows=rows: (actT[:, f, t0 - g0:t0 - g0 + rows], "actT%d" % f))
            def w_gu(jj):
                return (wg_r if jj % 2 == 0 else wu_r)[jj // 2, :, :]

            def ep_gu(jj, bi, pap, pk, c0, n):
                j = jj // 2
                if jj % 2 == 0:
                    P.op("act", lambda e: e.activation(out=sg[:, bi, 0:n], in_=pap, func=AF.Silu),
                         reads=[pk], writes=["sg%d" % bi])
                else:
                    P.op("dve", lambda e: e.tensor_tensor(out=aT[:, j, c0:c0 + n], in0=sg[:, bi, 0:n], in1=pap, op=ALU.mult),
                         reads=[pk, "sg%d" % bi], writes=["aT%d" % j])

            emit_linear_fm(P, C, w_gu, NCH, 2 * NJ, stage,
                           lambda k, c0, n: (actT[:, k, c0:c0 + n], ["actT%d" % k]), blocks, ep_gu)
            subs = [(0, 16), (16, 16), (32, 12)]
            ns = len(stage)
            cnt = 0
            for f in range(NCH):
                sts = []
                for (k0, kk) in subs:
                    st, skey = stage[cnt % ns]
                    cnt += 1
                    P.dma("pool", st[:, 0:kk * 128], wd_r[f, :, k0 * 128:(k0 + kk) * 128], writes=[skey])
                    sts.append((st, skey, k0, kk))
                for bi, (c0, n) in enumerate(blocks):
                    b = C.bank((0, 4), 1)
                    pk = "ps%d" % b
                    pap = C.ps[b][:, 0:n]
                    for (st, skey, k0, kk) in sts:
                        for k in range(kk):
                            kg = k0 + k
                            P.op("pe", lambda e, pap=pap, st=st, k=k, kg=kg: e.matmul(
                                pap, lhsT=st[:, k * 128:(k + 1) * 128], rhs=aT[:, kg, c0:c0 + n],
                                start=(kg == 0), stop=(kg == NJ - 1)),
                                reads=[skey, "aT%d" % kg], writes=[pk], inc=(kg == NJ - 1))
                    P.op("act", lambda e, pap=pap, f=f, c0=c0, n=n: e.activation(out=oT[:, f, c0:c0 + n], in_=pap, func=AF.Copy),
                         reads=[pk], writes=["oT%d" % f])
            load_g(5)
            for ti, (t0, rows, is_ctx) in enumerate(tiles):
                outs = emit_to_tokmajor(P, C, oT, lambda f: "oT%d" % f, t0 - g0, rows)
                v = 1 if is_ctx else 0
                emit_postnorm_residual(P, C, outs, rows, x1[:, ti, :], "x1_%d" % ti, gbuf[:, v, :], "gbuf%d" % v,
                                       xs, "xs")
                P.dma("sp", x_out[t0:t0 + rows, :], x1[0:rows, ti, :], reads=["x1_%d" % ti])
        if do_a:
            for ti, (t0, rows, is_ctx) in enumerate(tiles):
                a_i, s_row = (1, 6) if is_ctx else (0, 0)
                emit_prenorm_T(P, C, x1[:, ti, :], "x1_%d" % ti, rows, xs, "xs", acol[:, a_i, :], mc[:, s_row, :],
                               "acol", lambda f, t0=t0, rows=rows: (actT[:, f, t0 - g0:t0 - g0 + rows], "actT%d" % f))
            for f in range(NCH):
                P.dma("sp", hT_out[f * 128:(f + 1) * 128, g0:g0 + gn], actT[:, f, 0:gn], reads=["actT%d" % f])
    return P.finish()


def token_blocks(N):
    bl = []
    c = 0
    while c < N:
        n = min(512, N - c)
        bl.append((c, n))
        c += n
    return bl


def chunk_order(nctx_c, ncn, d):
    if d == 0:
        return list(range(ncn))
    return list(range(nctx_c - 1, -1, -1)) + list(range(ncn - 1, nctx_c - 1, -1))


class HStream:
    def __init__(self, P, hT, N, name="hblk"):
        self.P, self.hT, self.N = P, hT, N
        self.buf = P.sbuf(name, [128, 2, NCH, 512], BF16)
        self.i = 0

    def load(self, c0, n):
        s = self.i % 2
        self.i += 1
        keys = ["hb%d_%d" % (s, k) for k in range(NCH)]
        for k in range(NCH):
            self.P.dma("sp", self.buf[:, s, k, 0:n], self.hT[k * 128:(k + 1) * 128, c0:c0 + n], writes=[keys[k]])
        return s, keys


def emit_proj_fm(P, C, H, s, keys, w_ap, wkey, M, n, bank):
    pk = "ps%d" % bank
    for k in range(NCH):
        P.op("pe", lambda e, k=k: e.matmul(C.ps[bank][0:M, 0:n], lhsT=w_ap[:, k, 0:M], rhs=H.buf[:, s, k, 0:n],
                                           start=(k == 0), stop=(k == NCH - 1)),
             reads=[wkey, keys[k]], writes=[pk], inc=(k == NCH - 1))
    return C.ps[bank][0:M, 0:n], pk


def emit_proj_tm(P, C, H, s, keys, w_ap, wkey, t0, rows, ncols, bank):
    pk = "ps%d" % bank
    for k in range(NCH):
        P.op("pe", lambda e, k=k: e.matmul(C.ps[bank][0:rows, 0:ncols], lhsT=H.buf[:, s, k, t0:t0 + rows],
                                           rhs=w_ap[:, k, 0:ncols], start=(k == 0), stop=(k == NCH - 1)),
             reads=[wkey, keys[k]], writes=[pk], inc=(k == NCH - 1))
    return C.ps[bank][0:rows, 0:ncols], pk


def build_mlstm(n_ctx, n_lat, debug=False, dirs=(0, 1)):
    P = Prog()
    N = n_ctx + n_lat
    NCN = N // 64
    NCC = n_ctx // 64
    hT = P.dram("hT", [D, N], BF16, "ExternalInput")
    ident_d = P.dram("ident", [128, 128], F32, "ExternalInput")
    w_fm = P.dram("w_fm", [3, 128, NCH * 128], F32, "ExternalInput")
    w_g = P.dram("w_g", [128, NCH * 4], F32, "ExternalInput")
    w_v = P.dram("w_v", [128, NCH * 128], F32, "ExternalInput")
    cw_d = P.dram("cw", [128, 8], F32, "ExternalInput")
    gb_d = P.dram("gb", [4, 1], F32, "ExternalInput")
    mn_d = P.dram("mn", [128, 1], F32, "ExternalInput")
    masks_d = P.dram("masks", [64, 2, 64], F32, "ExternalInput")
    tri_d = P.dram("tri", [NCN, 2, NCN], F32, "ExternalInput")
    gscr = P.dram("gscr", [4, N], F32, "Internal")
    yT = P.dram("yT", [128, N], BF16, "ExternalOutput")

    C = Ctx(P, ident_d)
    H = HStream(P, hT, N)
    wfm = P.sbuf("wfm", [128, 3, NCH, 128], BF16)
    wg = P.sbuf("wg", [128, NCH, 4], BF16)
    wv = P.sbuf("wv", [128, NCH, 128], BF16)
    for g in range(3):
        P.dma("pool", wfm[:, g, :, :], w_fm[g, :, :], writes=["wfm%d" % g])
    P.dma("pool", wg[:, :, :], w_g[:, :], writes=["wg"])
    P.dma("pool", wv[:, :, :], w_v[:, :], writes=["wv"])
    cw = P.sbuf("cw", [128, 8], F32)
    gb = P.sbuf("gb", [4, 1], F32)
    mn = P.sbuf("mn", [128, 1], F32)
    masks = P.sbuf("masks", [64, 2, 64], F32)
    tri = P.sbuf("tri", [NCN, 2, NCN], F32)
    for t, src, key in ((cw, cw_d, "cw"), (gb, gb_d, "gb"), (mn, mn_d, "mn")):
        P.dma("sp", t[:, :], src[:, :], writes=[key])
    P.dma("sp", masks[:, :, :], masks_d[:, :, :], writes=["masks"])
    P.dma("sp", tri[:, :, :], tri_d[:, :, :], writes=["tri"])

    big = P.sbuf("big", [128, 2 * N], F32)
    acc = P.sbuf("acc", [128, N], F32)
    mqT = P.sbuf("mqT", [128, N], BF16)
    mkT = P.sbuf("mkT", [128, N], BF16)
    moT = P.sbuf("moT", [128, N], BF16)
    v64 = P.sbuf("v64", [64, NCN, 128], BF16)
    ktok = P.sbuf("ktok", [64, NCN, 128], BF16)
    gsb = P.sbuf("gsb", [4, 512], F32)
    zeros = P.sbuf("zeros", [128, 512], F32)
    ones = P.sbuf("ones", [128, 128], F32)
    P.op("pool", lambda e: e.memset(zeros[:, :], 0.0), writes=["zeros"])
    P.op("pool", lambda e: e.memset(ones[:, :], 1.0), writes=["ones"])

    for (c0, n) in token_blocks(N):
        s, keys = H.load(c0, n)
        for g, (dst, dkey) in enumerate(((big[:, 0:N], "mqraw"), (big[:, N:2 * N], "mkraw"))):
            b = C.bank((0, 4), 1)
            pap, pk = emit_proj_fm(P, C, H, s, keys, wfm[:, g, :, :], "wfm%d" % g, 128, n, b)
            P.op("act", lambda e, pap=pap, dst=dst: e.activation(out=dst[:, c0:c0 + n], in_=pap, func=AF.Copy),
                 reads=[pk], writes=[dkey])
        b = C.bank((0, 4), 1)
        pap, pk = emit_proj_fm(P, C, H, s, keys, wfm[:, 2, :, :], "wfm2", 128, n, b)
        P.op("act", lambda e, pap=pap: e.activation(out=moT[:, c0:c0 + n], in_=pap, func=AF.Sigmoid),
             reads=[pk], writes=["moT"])
        b = C.bank((0, 4), 1)
        pap, pk = emit_proj_fm(P, C, H, s, keys, wg, "wg", 4, n, b)
        P.op("dve", lambda e, pap=pap: e.tensor_scalar(out=gsb[:, 0:n], in0=pap, scalar1=gb[:, 0:1], scalar2=None, op0=ALU.add),
             reads=[pk, "gb"], writes=["gsb"])
        P.dma("sp", gscr[:, c0:c0 + n], gsb[:, 0:n], reads=["gsb"], writes=["gscr%d" % c0])
        for t in range(n // 64):
            b = C.bank((4, 4), 1)
            pap, pk = emit_proj_tm(P, C, H, s, keys, wv, "wv", t * 64, 64, 128, b)
            ci = c0 // 64 + t
            P.op("dve", lambda e, pap=pap, ci=ci: e.tensor_copy(out=v64[:, ci, :], in_=pap), reads=[pk], writes=["v64_%d" % ci])

    segs = [(a, b_) for (a, b_) in ((0, n_ctx), (n_ctx, N)) if b_ > a]
    for qi, (raw, rkey, dstT, dkey) in enumerate(((big[:, 0:N], "mqraw", mqT, "mqT"), (big[:, N:2 * N], "mkraw", mkT, "mkT"))):
        w0, w1, w2, bcol = cw[:, 3 * qi:3 * qi + 1], cw[:, 3 * qi + 1:3 * qi + 2], cw[:, 3 * qi + 2:3 * qi + 3], cw[:, 6 + qi:7 + qi]
        P.op("dve", lambda e, raw=raw, w1=w1, bcol=bcol: e.tensor_scalar(out=acc[:, :], in0=raw, scalar1=w1, scalar2=bcol, op0=ALU.mult, op1=ALU.add),
             reads=[rkey, "cw"], writes=["acc"])
        for (a, b_) in segs:
            P.op("dve", lambda e, raw=raw, w0=w0, a=a, b_=b_: e.scalar_tensor_tensor(
                out=acc[:, a + 1:b_], in0=raw[:, a:b_ - 1], scalar=w0, in1=acc[:, a + 1:b_], op0=ALU.mult, op1=ALU.add),
                reads=[rkey, "cw", "acc"], writes=["acc"])
            P.op("dve", lambda e, raw=raw, w2=w2, a=a, b_=b_: e.scalar_tensor_tensor(
                out=acc[:, a:b_ - 1], in0=raw[:, a + 1:b_], scalar=w2, in1=acc[:, a:b_ - 1], op0=ALU.mult, op1=ALU.add),
                reads=[rkey, "cw", "acc"], writes=["acc"])
        P.op("act", lambda e: e.activation(out=acc[:, :], in_=acc[:, :], func=AF.Silu), reads=["acc"], writes=["acc"])
        if qi == 0:
            P.op("dve", lambda e: e.tensor_scalar(out=mqT[:, :], in0=acc[:, :], scalar1=128.0 ** -0.5, scalar2=None, op0=ALU.mult),
                 reads=["acc"], writes=["mqT"])
        else:
            P.op("dve", lambda e: e.tensor_copy(out=mkT[:, :], in_=acc[:, :]), reads=["acc"], writes=["mkT"])
            for c in range(NCN):
                b = C.bank((4, 4), 1)
                pk = "ps%d" % b
                P.op("pe", lambda e, b=b, c=c: e.transpose(C.ps[b][0:64, 0:128], acc[:, c * 64:(c + 1) * 64], C.ident[:, :]),
                     reads=["acc", "ident"], writes=[pk])
                P.op("act", lambda e, b=b, c=c: e.activation(out=ktok[:, c, :], in_=C.ps[b][0:64, 0:128], func=AF.Copy),
                     reads=[pk], writes=["ktok%d" % c])

    st = P.sbuf("st", [NCN, 40, 64], F32)
    sc = P.sbuf("sc", [NCN, 32], F32)
    rowt = P.sbuf("rowt", [1, 4, NCN], F32)
    colW = P.sbuf("colW", [64, 2, 3, NCN], F32)
    bca = P.sbuf("bca", [128, 2, NCN], F32)
    diag = P.sbuf("diag", [NCN, NCN], F32)
    skey = lambda i: "st%d" % i
    ckey = lambda i: "sc%d" % i

    def rv(ap, d):
        return ap[:, ::-1] if d == 1 else ap

    gkeys = ["gscr%d" % c0 for (c0, n) in token_blocks(N)]
    for d in range(2):
        base = d * 20
        I_, F_, L_, Pl, Pt, A_, Al, Gc, W_, R_, E_, T1 = [st[:, base + i, :] for i in range(12)]
        kI, kF, kL, kPl, kPt, kA, kAl, kGc, kW, kR, kE, kT1 = [skey(base + i) for i in range(12)]
        cb = d * 16
        cPc, cMx, cGk, cGkp, cNGkp, cAl = [sc[:, cb + i:cb + i + 1] for i in range(6)]
        kcPc, kcMx, kcGk, kcGkp, kcNGkp, kcAl = [ckey(cb + i) for i in range(6)]
        P.dma("sp", I_, gscr[2 * d:2 * d + 1, :].rearrange("o (c l) -> (o c) l", l=64), reads=gkeys, writes=[kI])
        P.dma("sp", F_, gscr[2 * d + 1:2 * d + 2, :].rearrange("o (c l) -> (o c) l", l=64), reads=gkeys, writes=[kF])
        P.op("act", lambda e: e.activation(out=T1, in_=F_, func=AF.Exp, scale=-1.0), reads=[kF], writes=[kT1])
        P.op("act", lambda e: e.activation(out=L_, in_=T1, func=AF.Ln, bias=1.0), reads=[kT1], writes=[kL])
        P.op("dve", lambda e: e.tensor_tensor_scan(out=rv(Pl, d), data0=rv(L_, d), data1=zeros[0:NCN, 0:64], initial=0.0,
                                                   op0=ALU.add, op1=ALU.add), reads=[kL, "zeros"], writes=[kPl])
        last = (lambda ap: ap[:, 0:1]) if d == 1 else (lambda ap: ap[:, 63:64])
        b = C.bank((0, 4), 1)
        pk = "ps%d" % b
        P.op("pe", lambda e, b=b: e.matmul(C.ps[b][0:NCN, 0:1], lhsT=tri[:, d, :], rhs=last(Pl), start=True, stop=True),
             reads=["tri", kPl], writes=[pk])
        P.op("act", lambda e, b=b: e.activation(out=cPc, in_=C.ps[b][0:NCN, 0:1], func=AF.Copy), reads=[pk], writes=[kcPc])
        P.op("dve", lambda e: e.tensor_scalar(out=Pt, in0=Pl, scalar1=cPc, scalar2=None, op0=ALU.add), reads=[kPl, kcPc], writes=[kPt])
        P.op("dve", lambda e: e.tensor_tensor(out=A_, in0=I_, in1=Pt, op=ALU.add), reads=[kI, kPt], writes=[kA])
        P.op("dve", lambda e: e.tensor_tensor_scan(out=rv(Al, d), data0=rv(A_, d), data1=rv(A_, d), initial=-1e30,
                                                   op0=ALU.max, op1=ALU.max), reads=[kA], writes=[kAl])
        b = C.bank((0, 4), 1)
        pk = "ps%d" % b
        P.op("pe", lambda e, b=b: e.transpose(C.ps[b][0:1, 0:NCN], last(Al), C.ident[0:NCN, 0:NCN]),
             reads=[kAl, "ident"], writes=[pk])
        mxr, gpr, gkr = rowt[:, 0, :], rowt[:, 1, :], rowt[:, 2, :]
        P.op("act", lambda e, b=b: e.activation(out=mxr, in_=C.ps[b][0:1, 0:NCN], func=AF.Copy), reads=[pk], writes=["rowt0"])
        if d == 0:
            P.op("dve", lambda e: e.tensor_tensor_scan(out=gpr, data0=mxr, data1=mxr, initial=0.0, op0=ALU.max, op1=ALU.max),
                 reads=["rowt0"], writes=["rowt1"])
            P.op("dve", lambda e: e.memset(gkr[:, 0:1], 0.0), writes=["rowt2"])
            P.op("dve", lambda e: e.tensor_copy(out=gkr[:, 1:NCN], in_=gpr[:, 0:NCN - 1]), reads=["rowt1"], writes=["rowt2"])
        else:
            if NCC > 0:
                P.op("dve", lambda e: e.tensor_tensor_scan(out=gpr[:, 0:NCC][:, ::-1], data0=mxr[:, 0:NCC][:, ::-1],
                                                           data1=mxr[:, 0:NCC][:, ::-1], initial=0.0, op0=ALU.max, op1=ALU.max),
                     reads=["rowt0"], writes=["rowt1"])
                P.op("dve", lambda e: e.tensor_tensor_scan(out=gpr[:, NCC:NCN][:, ::-1], data0=mxr[:, NCC:NCN][:, ::-1],
                                                           data1=mxr[:, NCC:NCN][:, ::-1], initial=gpr[:, 0:1], op0=ALU.max, op1=ALU.max),
                     reads=["rowt0", "rowt1"], writes=["rowt1"])
                P.op("dve", lambda e: e.memset(gkr[:, NCC - 1:NCC], 0.0), writes=["rowt2"])
                if NCC > 1:
                    P.op("dve", lambda e: e.tensor_copy(out=gkr[:, 0:NCC - 1], in_=gpr[:, 1:NCC]), reads=["rowt1"], writes=["rowt2"])
                P.op("dve", lambda e: e.tensor_copy(out=gkr[:, NCN - 1:NCN], in_=gpr[:, 0:1]), reads=["rowt1"], writes=["rowt2"])
            else:
                P.op("dve", lambda e: e.tensor_tensor_scan(out=gpr[:, ::-1], data0=mxr[:, ::-1], data1=mxr[:, ::-1], initial=0.0,
                                                           op0=ALU.max, op1=ALU.max), reads=["rowt0"], writes=["rowt1"])
                P.op("dve", lambda e: e.memset(gkr[:, NCN - 1:NCN], 0.0), writes=["rowt2"])
            P.op("dve", lambda e: e.tensor_copy(out=gkr[:, NCC:NCN - 1], in_=gpr[:, NCC + 1:NCN]), reads=["rowt1"], writes=["rowt2"])
        for (row, rk, col, ck) in ((gkr, "rowt2", cGk, kcGk), (gpr, "rowt1", cGkp, kcGkp)):
            b = C.bank((0, 4), 1)
            pk = "ps%d" % b
            P.op("pe", lambda e, b=b, row=row: e.transpose(C.ps[b][0:NCN, 0:1], row, C.ident[0:1, 0:1]),
                 reads=[rk, "ident"], writes=[pk])
            P.op("act", lambda e, b=b, col=col: e.activation(out=col, in_=C.ps[b][0:NCN, 0:1], func=AF.Copy), reads=[pk], writes=[ck])
        P.op("dve", lambda e: e.tensor_scalar(out=cNGkp, in0=cGkp, scalar1=-1.0, scalar2=None, op0=ALU.mult), reads=[kcGkp], writes=[kcNGkp])
        P.op("dve", lambda e: e.tensor_scalar(out=Gc, in0=Al, scalar1=cGk, scalar2=None, op0=ALU.max), reads=[kAl, kcGk], writes=[kGc])
        P.op("act", lambda e: e.activation(out=W_, in_=A_, func=AF.Exp, bias=cNGkp), reads=[kA, kcNGkp], writes=[kW])
        P.op("act", lambda e: e.activation(out=R_, in_=Gc, func=AF.Exp, scale=-1.0, bias=cGkp), reads=[kGc, kcGkp], writes=[kR])
        P.op("dve", lambda e: e.tensor_tensor(out=T1, in0=Pt, in1=Gc, op=ALU.subtract), reads=[kPt, kGc], writes=[kT1])
        P.op("act", lambda e: e.activation(out=E_, in_=T1, func=AF.Exp), reads=[kT1], writes=[kE])
        P.op("dve", lambda e: e.tensor_tensor(out=cAl, in0=cGk, in1=cGkp, op=ALU.subtract), reads=[kcGk, kcGkp], writes=[kcAl])
        P.op("act", lambda e: e.activation(out=cAl, in_=cAl, func=AF.Exp), reads=[kcAl], writes=[kcAl])
        for qi, (src, sk) in enumerate(((W_, kW), (R_, kR), (E_, kE))):
            b = C.bank((0, 4), 1)
            pk = "ps%d" % b
            P.op("pe", lambda e, b=b, src=src: e.transpose(C.ps[b][0:64, 0:NCN], src, C.ident[0:NCN, 0:NCN]),
                 reads=[sk, "ident"], writes=[pk])
            P.op("act", lambda e, b=b, qi=qi: e.activation(out=colW[:, d, qi, :], in_=C.ps[b][0:64, 0:NCN], func=AF.Copy),
                 reads=[pk], writes=["colW%d" % d])
        P.op("dve", lambda e: e.tensor_scalar(out=diag[:, :], in0=C.ident[0:NCN, 0:NCN], scalar1=cAl, scalar2=None, op0=ALU.mult),
             reads=["ident", kcAl], writes=["diag"])
        b = C.bank((0, 4), 1)
        pk = "ps%d" % b
        P.op("pe", lambda e, b=b: e.matmul(C.ps[b][:, 0:NCN], lhsT=ones[0:NCN, :], rhs=diag[:, :], start=True, stop=True),
             reads=["ones", "diag"], writes=[pk])
        P.op("act", lambda e, b=b: e.activation(out=bca[:, d, :], in_=C.ps[b][:, 0:NCN], func=AF.Copy), reads=[pk], writes=["bca%d" % d])

    hacc = big[0:64, :].rearrange("p (c e) -> p c e", e=128)
    Cst = P.sbuf("Cst", [128, 2, 132], F32)
    Cbf = P.sbuf("Cbf", [128, 2, 132], BF16)
    ctmp = P.sbuf("ctmp", [128, 2, 132], F32)
    vh = P.sbuf("vh", [64, 4, 132], BF16)
    PT = P.sbuf("PT", [64, 4, 64], BF16)
    maskb = P.sbuf("maskb", [64, 2, 64], F32)
    fcol = P.sbuf("fcol", [64, 8, 4], F32)
    hkey_guard = ["mqraw", "mkraw"]
    orders = [chunk_order(NCC, NCN, d) for d in range(2)]
    for d in range(2):
        P.op("pool", lambda e, d=d: e.memset(Cst[:, d, :], 0.0), writes=["Cst%d" % d])
        P.op("pool", lambda e, d=d: e.memset(Cbf[:, d, :], 0.0), writes=["Cbf%d" % d])
    it = 0
    hwritten = set()
    for step in range(NCN):
        for d in dirs:
            c = orders[d][step]
            cn = orders[d][step + 1] if step + 1 < NCN else None
            sl = slice(c * 64, (c + 1) * 64)
            j = it % 4
            it += 1
            wcol = colW[:, d, 0, c:c + 1]
            rcol = colW[:, d, 1, c:c + 1]
            ecol = colW[:, d, 2, c:c + 1]
            ck = "colW%d" % d
            P.op("act", lambda e, j=j, c=c, wcol=wcol: e.activation(out=vh[:, j, 0:128], in_=v64[:, c, :], func=AF.Copy, scale=wcol),
                 reads=["v64_%d" % c, ck], writes=["vh%d" % j])
            P.op("pool", lambda e, j=j, wcol=wcol: e.tensor_copy(out=vh[:, j, 128:129], in_=wcol), reads=[ck], writes=["vh%d" % j])
            b1 = C.bank((0, 3), 1)
            P.op("pe", lambda e, b1=b1, sl=sl: e.matmul(C.ps[b1][0:64, 0:64], lhsT=mkT[:, sl], rhs=mqT[:, sl], start=True, stop=True),
                 reads=["mkT", "mqT"], writes=["ps%d" % b1])
            P.op("dve", lambda e, b1=b1, j=j, d=d: e.tensor_tensor(out=PT[:, j, :], in0=C.ps[b1][0:64, 0:64], in1=masks[:, d, :], op=ALU.mult),
                 reads=["ps%d" % b1, "masks"], writes=["PT%d" % j])
            b2 = C.bank((3, 3), 1)
            pk2 = "ps%d" % b2
            P.op("pe", lambda e, b2=b2, sl=sl, d=d: e.matmul(C.ps[b2][0:64, 0:129], lhsT=mqT[:, sl], rhs=Cbf[:, d, 0:129], start=True, stop=False),
                 reads=["mqT", "Cbf%d" % d], writes=[pk2], inc=False)
            P.op("pe", lambda e, b2=b2, j=j: e.matmul(C.ps[b2][0:64, 0:129], lhsT=PT[:, j, :], rhs=vh[:, j, 0:129], start=False, stop=True),
                 reads=["PT%d" % j, "vh%d" % j], writes=[pk2])
            fj = it % 8
            f0, f1, f2 = fcol[:, fj, 0:1], fcol[:, fj, 1:2], fcol[:, fj, 2:3]
            fk = "fcol%d" % fj
            P.op("act", lambda e, b2=b2, f0=f0, rcol=rcol: e.activation(out=f0, in_=C.ps[b2][0:64, 128:129], func=AF.Abs, scale=rcol),
                 reads=[pk2, ck], writes=[fk])
            P.op("dve", lambda e, f0=f0, f1=f1, ecol=ecol: e.tensor_tensor(out=f1, in0=f0, in1=ecol, op=ALU.max), reads=[fk, ck], writes=[fk])
            P.op("dve", lambda e, f1=f1: e.reciprocal(out=f1, in_=f1), reads=[fk], writes=[fk])
            P.op("dve", lambda e, f1=f1, f2=f2, rcol=rcol: e.tensor_tensor(out=f2, in0=f1, in1=rcol, op=ALU.mult), reads=[fk, ck], writes=[fk])
            hk = "hacc%d" % c
            if c not in hwritten:
                hwritten.add(c)
                P.op("act", lambda e, b2=b2, c=c, f2=f2: e.activation(out=hacc[:, c, :], in_=C.ps[b2][0:64, 0:128], func=AF.Copy, scale=f2),
                     reads=[pk2, fk], writes=[hk] + hkey_guard)
            else:
                P.op("dve", lambda e, b2=b2, c=c, f2=f2: e.scalar_tensor_tensor(out=hacc[:, c, :], in0=C.ps[b2][0:64, 0:128], scalar=f2,
                                                                               in1=hacc[:, c, :], op0=ALU.mult, op1=ALU.add),
                     reads=[pk2, fk, hk], writes=[hk])
            if cn is not None:
                b3 = C.bank((6, 2), 1)
                pk3 = "ps%d" % b3
                P.op("pe", lambda e, b3=b3, c=c, j=j: e.matmul(C.ps[b3][:, 0:129], lhsT=ktok[:, c, :], rhs=vh[:, j, 0:129], start=True, stop=True),
                     reads=["ktok%d" % c, "vh%d" % j], writes=[pk3])
                acol = bca[:, d, cn:cn + 1]
                P.op("act", lambda e, b3=b3, d=d, acol=acol: e.activation(out=ctmp[:, d, 0:129], in_=C.ps[b3][:, 0:129], func=AF.Copy, scale=acol),
                     reads=[pk3, "bca%d" % d], writes=["ctmp%d" % d])
                P.op("dve", lambda e, d=d, acol=acol: e.scalar_tensor_tensor(out=Cst[:, d, 0:129], in0=Cst[:, d, 0:129], scalar=acol,
                                                                            in1=ctmp[:, d, 0:129], op0=ALU.mult, op1=ALU.add),
                     reads=["Cst%d" % d, "ctmp%d" % d, "bca%d" % d], writes=["Cst%d" % d])
                P.op("pool", lambda e, d=d: e.tensor_copy(out=Cbf[:, d, 0:129], in_=Cst[:, d, 0:129]), reads=["Cst%d" % d], writes=["Cbf%d" % d])

    if debug:
        d1 = P.dram("dbg_mq", [128, N], BF16, "ExternalOutput")
        d2 = P.dram("dbg_mk", [128, N], BF16, "ExternalOutput")
        d3 = P.dram("dbg_colW", [64, 2 * 3 * NCN], F32, "ExternalOutput")
        d4 = P.dram("dbg_bca", [128, 2 * NCN], F32, "ExternalOutput")
        d5 = P.dram("dbg_h", [64, NCN * 128], F32, "ExternalOutput")
        d6 = P.dram("dbg_v", [64, NCN * 128], BF16, "ExternalOutput")
        d7 = P.dram("dbg_kt", [64, NCN * 128], BF16, "ExternalOutput")
        P.dma("sp", d1[:, :], mqT[:, :], reads=["mqT"])
        P.dma("sp", d2[:, :], mkT[:, :], reads=["mkT"])
        P.dma("sp", d3[:, :], colW[:, :, :, :].rearrange("p a b c -> p (a b c)"), reads=["colW0", "colW1"])
        P.dma("sp", d4[:, :], bca[:, :, :].rearrange("p a c -> p (a c)"), reads=["bca0", "bca1"])
        P.dma("sp", d5[:, :], big[0:64, :], reads=["hacc%d" % c for c in range(NCN)])
        P.dma("sp", d6[:, :], v64[:, :, :].rearrange("p a c -> p (a c)"), reads=["v64_%d" % c for c in range(NCN)])
        P.dma("sp", d7[:, :], ktok[:, :, :].rearrange("p a c -> p (a c)"), reads=["ktok%d" % c for c in range(NCN)])
    ssq = P.sbuf("ssq", [64, NCN], F32)
    for c in range(NCN):
        P.op("act", lambda e, c=c: e.activation(out=C.junk[0:64, 0:128], in_=hacc[:, c, :], func=AF.Square, accum_out=ssq[:, c:c + 1]),
             reads=["hacc%d" % c], writes=["junk", "ssq"])
    emit_rstd(P, C, ssq[:, :], "ssq", 64, ssq[:, :], "ssq", 128)
    yst = acc
    for c in range(NCN):
        sl = slice(c * 64, (c + 1) * 64)
        P.op("act", lambda e, c=c: e.activation(out=hacc[:, c, :], in_=hacc[:, c, :], func=AF.Copy, scale=ssq[:, c:c + 1]),
             reads=["hacc%d" % c, "ssq"], writes=["hacc%d" % c])
        b = C.bank((0, 4), 1)
        pk = "ps%d" % b
        P.op("pe", lambda e, b=b, c=c: e.transpose(C.ps[b][:, 0:64], hacc[:, c, :], C.ident[0:64, 0:64]),
             reads=["hacc%d" % c, "ident"], writes=[pk])
        P.op("dve", lambda e, b=b, sl=sl: e.scalar_tensor_tensor(out=mkT[:, sl], in0=C.ps[b][:, 0:64], scalar=mn[:, 0:1], in1=moT[:, sl],
                                                               op0=ALU.mult, op1=ALU.mult),
             reads=[pk, "mn", "moT"], writes=["mkT"])
    P.dma("sp", yT[:, :], mkT[:, :], reads=["mkT"])
    return P.finish()


def mlstm_masks():
    s = np.arange(64)[:, None]
    t = np.arange(64)[None, :]
    m = np.zeros((64, 2, 64), np.float32)
    m[:, 0, :] = (t >= s)
    m[:, 1, :] = (t <= s)
    return m


def mlstm_tri(ncc, ncn):
    tri = np.zeros((ncn, 2, ncn), np.float32)
    for cp in range(ncn):
        for c in range(ncn):
            tri[cp, 0, c] = 1.0 if cp < c else 0.0
            cp_ctx, c_ctx = cp < ncc, c < ncc
            if cp_ctx == c_ctx:
                before = cp > c
            else:
                before = cp_ctx and not c_ctx
            tri[cp, 1, c] = 1.0 if before else 0.0
    return tri


def emit_head_finish(P, C, hacc, NCN, nw_col, nw_key, gateT, gate_key, outT, out_key, name, extra_w=()):
    ssq = P.sbuf("ssq_" + name, [64, NCN], F32)
    for c in range(NCN):
        P.op("act", lambda e, c=c: e.activation(out=C.junk[0:64, 0:128], in_=hacc[:, c, :], func=AF.Square, accum_out=ssq[:, c:c + 1]),
             reads=["hacc%d" % c], writes=["junk", "ssq"])
    emit_rstd(P, C, ssq[:, :], "ssq", 64, ssq[:, :], "ssq", 128)
    for c in range(NCN):
        sl = slice(c * 64, (c + 1) * 64)
        P.op("act", lambda e, c=c: e.activation(out=hacc[:, c, :], in_=hacc[:, c, :], func=AF.Copy, scale=ssq[:, c:c + 1]),
             reads=["hacc%d" % c, "ssq"], writes=["hacc%d" % c])
        b = C.bank((0, 4), 1)
        pk = "ps%d" % b
        P.op("pe", lambda e, b=b, c=c: e.transpose(C.ps[b][:, 0:64], hacc[:, c, :], C.ident[0:64, 0:64]),
             reads=["hacc%d" % c, "ident"], writes=[pk])
        P.op("dve", lambda e, b=b, sl=sl: e.scalar_tensor_tensor(out=outT[:, sl], in0=C.ps[b][:, 0:64], scalar=nw_col, in1=gateT[:, sl],
                                                               op0=ALU.mult, op1=ALU.mult),
             reads=[pk, nw_key, gate_key], writes=[out_key] + (list(extra_w) if c == 0 else []))


def gla_rmask():
    t = np.arange(512)
    m = np.zeros((64, 2, 512), np.float32)
    m[:, 0, :] = (t % 64 != 0)[None, :]
    m[:, 1, :] = (t % 64 != 63)[None, :]
    return m


def build_gla(n_ctx, n_lat):
    P = Prog()
    N = n_ctx + n_lat
    NCN = N // 64
    NCC = n_ctx // 64
    hT = P.dram("hT", [D, N], BF16, "ExternalInput")
    ident_d = P.dram("ident", [128, 128], F32, "ExternalInput")
    w_qk = P.dram("w_qk", [128, NCH * 128], F32, "ExternalInput")
    w_go = P.dram("w_go", [128, NCH * 128], F32, "ExternalInput")
    w_lr = P.dram("w_lr", [128, NCH * 32], F32, "ExternalInput")
    w_v = P.dram("w_v", [128, NCH * 128], F32, "ExternalInput")
    w2_d = P.dram("w2", [16, 2 * 64], F32, "ExternalInput")
    nb_d = P.dram("nb", [64, 2], F32, "ExternalInput")
    gn_d = P.dram("gn", [128, 1], F32, "ExternalInput")
    masks_d = P.dram("masks", [64, 2, 64], F32, "ExternalInput")
    rmask_d = P.dram("rmask", [64, 2, 512], F32, "ExternalInput")
    yT = P.dram("yT", [128, N], BF16, "ExternalOutput")

    C = Ctx(P, ident_d)
    H = HStream(P, hT, N)
    wqk = P.sbuf("wqk", [128, NCH, 128], BF16)
    wgo = P.sbuf("wgo", [128, NCH, 128], BF16)
    wlr = P.sbuf("wlr", [128, NCH, 32], BF16)
    wv = P.sbuf("wv", [128, NCH, 128], BF16)
    w2 = P.sbuf("w2", [16, 128], BF16)
    P.dma("pool", wqk[:, :, :], w_qk[:, :], writes=["wqk"])
    P.dma("pool", wgo[:, :, :], w_go[:, :], writes=["wgo"])
    P.dma("pool", wlr[:, :, :], w_lr[:, :], writes=["wlr"])
    P.dma("pool", wv[:, :, :], w_v[:, :], writes=["wv"])
    P.dma("pool", w2[:, :], w2_d[:, :], writes=["w2"])
    nb = P.sbuf("nb", [64, 2], F32)
    gn = P.sbuf("gn", [128, 1], F32)
    masks = P.sbuf("masks", [64, 2, 64], F32)
    rmask = P.sbuf("rmask", [64, 2, 512], F32)
    P.dma("sp", nb[:, :], nb_d[:, :], writes=["nb"])
    P.dma("sp", gn[:, :], gn_d[:, :], writes=["gn"])
    P.dma("sp", masks[:, :, :], masks_d[:, :, :], writes=["masks"])
    P.dma("sp", rmask[:, :, :], rmask_d[:, :, :], writes=["rmask"])

    qt = P.sbuf("qt", [64, 2, N], BF16)
    kt = P.sbuf("kt", [64, 2, N], BF16)
    ktok = P.sbuf("ktok", [64, 2, NCN, 64], BF16)
    dec = P.sbuf("dec", [64, 2, NCN], F32)
    goT = P.sbuf("goT", [128, N], BF16)
    v64 = P.sbuf("v64", [64, NCN, 128], BF16)
    hacc_t = P.sbuf("hacc", [64, NCN * 128], F32)
    hacc = hacc_t[:, :].rearrange("p (c e) -> p c e", e=128)
    youT = H.buf[:, :, :, :].rearrange("p a k n -> p (a k n)")[:, 0:N]
    hbkeys = ["hb%d_%d" % (s_, k_) for s_ in range(2) for k_ in range(NCH)]
    qf = P.sbuf("qf", [64, 512], F32)
    kf = P.sbuf("kf", [64, 512], F32)
    lrb = P.sbuf("lrb", [16, 2, 512], BF16)
    T1 = P.sbuf("T1", [64, 2, 512], F32)
    Lg = T1
    Gp = P.sbuf("Gp", [64, 2, 512], F32)
    E1 = P.sbuf("E1", [64, 2, 512], F32)
    E2 = P.sbuf("E2", [64, 2, 512], F32)
    ktmp = P.sbuf("ktmp", [64, 2, 512], F32)
    khf = Gp

    for (c0, n) in token_blocks(N):
        s, keys = H.load(c0, n)
        nch = n // 64
        cb0 = c0 // 64
        for g, (dst, dkey) in enumerate(((qf, "qf"), (kf, "kf"))):
            b = C.bank((0, 4), 1)
            pap, pk = emit_proj_fm(P, C, H, s, keys, wqk[:, :, g * 64:(g + 1) * 64], "wqk", 64, n, b)
            sc_ = 0.125 if g == 0 else 1.0
            P.op("act", lambda e, pap=pap, dst=dst, sc_=sc_: e.activation(out=dst[:, 0:n], in_=pap, func=AF.Copy, scale=sc_),
                 reads=[pk], writes=[dkey])
        b = C.bank((0, 4), 1)
        pap, pk = emit_proj_fm(P, C, H, s, keys, wgo, "wgo", 128, n, b)
        P.op("act", lambda e, pap=pap: e.activation(out=goT[:, c0:c0 + n], in_=pap, func=AF.Silu), reads=[pk], writes=["goT"])
        for d in range(2):
            b = C.bank((0, 4), 1)
            pap, pk = emit_proj_fm(P, C, H, s, keys, wlr[:, :, d * 16:(d + 1) * 16], "wlr", 16, n, b)
            P.op("dve", lambda e, pap=pap, d=d: e.tensor_copy(out=lrb[:, d, 0:n], in_=pap), reads=[pk], writes=["lrb%d" % d])
        for t in range(nch):
            b = C.bank((4, 4), 1)
            pap, pk = emit_proj_tm(P, C, H, s, keys, wv, "wv", t * 64, 64, 128, b)
            ci = cb0 + t
            P.op("dve", lambda e, pap=pap, ci=ci: e.tensor_copy(out=v64[:, ci, :], in_=pap), reads=[pk], writes=["v64_%d" % ci])
        for d in range(2):
            dk = str(d)
            b = C.bank((0, 4), 1)
            pk = "ps%d" % b
            P.op("pe", lambda e, b=b, d=d: e.matmul(C.ps[b][0:64, 0:n], lhsT=w2[:, d * 64:(d + 1) * 64], rhs=lrb[:, d, 0:n], start=True, stop=True),
                 reads=["w2", "lrb%d" % d], writes=[pk])
            P.op("act", lambda e, b=b, d=d: e.activation(out=T1[:, d, 0:n], in_=C.ps[b][0:64, 0:n], func=AF.Exp, scale=-1.0, bias=nb[:, d:d + 1]),
                 reads=[pk, "nb"], writes=["T1" + dk])
            P.op("act", lambda e, d=d: e.activation(out=Lg[:, d, 0:n], in_=T1[:, d, 0:n], func=AF.Ln, bias=1.0), reads=["T1" + dk], writes=["T1" + dk])
            rvv = (lambda ap: ap[:, ::-1]) if d == 1 else (lambda ap: ap)
            P.op("dve", lambda e, d=d, rvv=rvv: e.tensor_tensor_scan(out=rvv(Gp[:, d, 0:n]), data0=rvv(rmask[:, d, 0:n]), data1=rvv(Lg[:, d, 0:n]),
                                                                   initial=0.0, op0=ALU.mult, op1=ALU.add),
                 reads=["T1" + dk, "rmask"], writes=["Gp" + dk])
            P.op("act", lambda e, d=d: e.activation(out=E1[:, d, 0:n], in_=Gp[:, d, 0:n], func=AF.Exp, scale=-1.0 / 16.0), reads=["Gp" + dk], writes=["E1" + dk])
            P.op("act", lambda e, d=d: e.activation(out=E2[:, d, 0:n], in_=Gp[:, d, 0:n], func=AF.Exp, scale=1.0 / 16.0), reads=["Gp" + dk], writes=["E2" + dk])
            P.op("dve", lambda e, d=d: e.tensor_tensor(out=qt[:, d, c0:c0 + n], in0=qf[:, 0:n], in1=E1[:, d, 0:n], op=ALU.mult),
                 reads=["qf", "E1" + dk], writes=["qt" + dk])
            P.op("pool", lambda e, d=d: e.tensor_tensor(out=ktmp[:, d, 0:n], in0=kf[:, 0:n], in1=E2[:, d, 0:n], op=ALU.mult),
                 reads=["kf", "E2" + dk], writes=["ktmp" + dk])
            P.op("pool", lambda e, d=d: e.tensor_copy(out=kt[:, d, c0:c0 + n], in_=ktmp[:, d, 0:n]), reads=["ktmp" + dk], writes=["kt" + dk])
            endc = 0 if d == 1 else 63
            e3 = E1[:, d, 0:n].rearrange("p (c l) -> p c l", l=64)[:, :, endc:endc + 1]
            P.op("dve", lambda e, d=d, e3=e3: e.tensor_copy(out=dec[:, d, cb0:cb0 + nch].unsqueeze(2), in_=e3), reads=["E1" + dk], writes=["dec" + dk])
            P.op("dve", lambda e, d=d, e3=e3: e.tensor_tensor(out=khf[:, d, 0:n].rearrange("p (c l) -> p c l", l=64),
                                                             in0=ktmp[:, d, 0:n].rearrange("p (c l) -> p c l", l=64),
                                                             in1=e3.to_broadcast([64, nch, 64]), op=ALU.mult),
                 reads=["ktmp" + dk, "E1" + dk], writes=["Gp" + dk])
            for t in range(nch):
                b = C.bank((4, 4), 1)
                pk = "ps%d" % b
                ci = cb0 + t
                P.op("pe", lambda e, b=b, d=d, t=t: e.transpose(C.ps[b][0:64, 0:64], khf[:, d, t * 64:(t + 1) * 64], C.ident[0:64, 0:64]),
                     reads=["Gp" + dk, "ident"], writes=[pk])
                P.op("act", lambda e, b=b, d=d, ci=ci: e.activation(out=ktok[:, d, ci, :], in_=C.ps[b][0:64, 0:64], func=AF.Copy),
                     reads=[pk], writes=["ktok%d_%d" % (d, ci)])

    Sst = P.sbuf("Sst", [64, 2, 128], F32)
    Sbf = P.sbuf("Sbf", [64, 2, 128], BF16)
    PT = P.sbuf("PT", [64, 4, 64], BF16)
    orders = [chunk_order(NCC, NCN, d) for d in range(2)]
    for d in range(2):
        P.op("pool", lambda e, d=d: e.memset(Sst[:, d, :], 0.0), writes=["Sst%d" % d])
        P.op("pool", lambda e, d=d: e.memset(Sbf[:, d, :], 0.0), writes=["Sbf%d" % d])
    it = 0
    hwritten = set()
    for step in range(NCN):
        for d in range(2):
            dk = str(d)
            c = orders[d][step]
            last = step + 1 >= NCN
            sl = slice(c * 64, (c + 1) * 64)
            j = it % 4
            it += 1
            b1 = C.bank((0, 3), 1)
            P.op("pe", lambda e, b1=b1, sl=sl, d=d: e.matmul(C.ps[b1][0:64, 0:64], lhsT=kt[:, d, sl], rhs=qt[:, d, sl], start=True, stop=True),
                 reads=["kt" + dk, "qt" + dk], writes=["ps%d" % b1])
            P.op("dve", lambda e, b1=b1, j=j, d=d: e.tensor_tensor(out=PT[:, j, :], in0=C.ps[b1][0:64, 0:64], in1=masks[:, d, :], op=ALU.mult),
                 reads=["ps%d" % b1, "masks"], writes=["PT%d" % j])
            b2 = C.bank((3, 3), 1)
            pk2 = "ps%d" % b2
            P.op("pe", lambda e, b2=b2, sl=sl, d=d: e.matmul(C.ps[b2][0:64, 0:128], lhsT=qt[:, d, sl], rhs=Sbf[:, d, :], start=True, stop=False),
                 reads=["qt" + dk, "Sbf" + dk], writes=[pk2], inc=False)
            P.op("pe", lambda e, b2=b2, j=j, c=c: e.matmul(C.ps[b2][0:64, 0:128], lhsT=PT[:, j, :], rhs=v64[:, c, :], start=False, stop=True),
                 reads=["PT%d" % j, "v64_%d" % c], writes=[pk2])
            hk = "hacc%d" % c
            if c not in hwritten:
                hwritten.add(c)
                P.op("act", lambda e, b2=b2, c=c: e.activation(out=hacc[:, c, :], in_=C.ps[b2][0:64, 0:128], func=AF.Copy), reads=[pk2], writes=[hk])
            else:
                P.op("dve", lambda e, b2=b2, c=c: e.tensor_tensor(out=hacc[:, c, :], in0=hacc[:, c, :], in1=C.ps[b2][0:64, 0:128], op=ALU.add),
                     reads=[pk2, hk], writes=[hk])
            if not last:
                b3 = C.bank((6, 2), 1)
                pk3 = "ps%d" % b3
                P.op("pe", lambda e, b3=b3, c=c, d=d: e.matmul(C.ps[b3][0:64, 0:128], lhsT=ktok[:, d, c, :], rhs=v64[:, c, :], start=True, stop=True),
                     reads=["ktok%d_%d" % (d, c), "v64_%d" % c], writes=[pk3])
                P.op("dve", lambda e, b3=b3, d=d, c=c: e.scalar_tensor_tensor(out=Sst[:, d, :], in0=Sst[:, d, :], scalar=dec[:, d, c:c + 1],
                                                                             in1=C.ps[b3][0:64, 0:128], op0=ALU.mult, op1=ALU.add),
                     reads=["Sst" + dk, "dec" + dk, pk3], writes=["Sst" + dk])
                P.op("pool", lambda e, d=d: e.tensor_copy(out=Sbf[:, d, :], in_=Sst[:, d, :]), reads=["Sst" + dk], writes=["Sbf" + dk])

    emit_head_finish(P, C, hacc, NCN, gn[:, 0:1], "gn", goT, "goT", youT, "youT", "g", extra_w=hbkeys)
    P.dma("sp", yT[:, :], youT, reads=["youT"])
    return P.finish()


def rope_tables(n_lat):
    rows = n_lat // 64
    row = np.repeat(np.arange(rows, dtype=np.float32), 64)
    col = np.tile(np.arange(64, dtype=np.float32), rows)
    half = 8
    inv_freq = (10000.0 ** (-np.arange(half, dtype=np.float32) / half)).astype(np.float32)
    ang_r = row[:, None] * inv_freq
    ang_c = col[:, None] * inv_freq
    ang = np.concatenate([ang_r, ang_r, ang_c, ang_c], axis=-1)
    return ang


def rope_consts(n_lat):
    rows = n_lat // 64
    row = np.repeat(np.arange(rows, dtype=np.float32), 64)
    col = np.tile(np.arange(64, dtype=np.float32), rows)
    half = 16 // 1 // 2 * 1
    half = 16
    inv_freq = (np.float32(10000.0) ** (-np.arange(half, dtype=np.float32) / np.float32(half))).astype(np.float32)
    ang_r = (row[:, None] * inv_freq).astype(np.float32)
    ang_c = (col[:, None] * inv_freq).astype(np.float32)
    ang = np.concatenate([ang_r, ang_r, ang_c, ang_c], axis=-1)
    cos = np.cos(ang).astype(np.float32)
    sin = np.sin(ang).astype(np.float32)
    sgn = np.ones(64, np.float32)
    perm = np.zeros(64, np.int64)
    for d in range(64):
        blk, i = d // 32, d % 32
        if i < 16:
            perm[d] = blk * 32 + i + 16
            sgn[d] = -1.0
        else:
            perm[d] = blk * 32 + i - 16
    cosT = np.concatenate([cos.T, cos.T], 0)
    sinT = np.concatenate([(sin * sgn[None]).T, (sin * sgn[None]).T], 0)
    pm = np.zeros((128, 128), np.float32)
    for m in range(2):
        for d in range(64):
            pm[m * 64 + perm[d], m * 64 + d] = 1.0
    return np.ascontiguousarray(cosT), np.ascontiguousarray(sinT), pm


def build_attn(n_ctx, n_lat, lam_init, need_ctx):
    P = Prog()
    N = n_ctx + n_lat
    NT = N // 128
    hT = P.dram("hT", [D, N], BF16, "ExternalInput")
    ident_d = P.dram("ident", [128, 128], F32, "ExternalInput")
    w_qk = P.dram("w_qk", [4, 128, NCH * 128], F32, "ExternalInput")
    w_v = P.dram("w_v", [128, NCH * 256], F32, "ExternalInput")
    cos_d = P.dram("cosT", [128, n_lat], F32, "ExternalInput")
    sin_d = P.dram("sinT", [128, n_lat], F32, "ExternalInput")
    pm_d = P.dram("pm", [128, 128], F32, "ExternalInput")
    dl_d = P.dram("dlam", [1, 256], F32, "ExternalInput")
    sub_d = P.dram("subln", [128, 1], F32, "ExternalInput")
    yT = P.dram("yT", [256, N], BF16, "ExternalOutput")

    C = Ctx(P, ident_d)
    H = HStream(P, hT, N)
    wqk = P.sbuf("wqk", [128, 4, NCH, 128], BF16)
    wv = P.sbuf("wv", [128, NCH, 256], BF16)
    for g in range(4):
        P.dma("pool", wqk[:, g, :, :], w_qk[g, :, :], writes=["wqk%d" % g])
    P.dma("pool", wv[:, :, :], w_v[:, :], writes=["wv"])
    cosT = P.sbuf("cosT", [128, n_lat], F32)
    sinT = P.sbuf("sinT", [128, n_lat], F32)
    pm = P.sbuf("pm", [128, 128], F32)
    dl = P.sbuf("dl", [128, 256], F32)
    sub = P.sbuf("sub", [128, 1], F32)
    P.dma("sp", cosT[:, :], cos_d[:, :], writes=["cosT"])
    P.dma("sp", sinT[:, :], sin_d[:, :], writes=["sinT"])
    P.dma("sp", pm[:, :], pm_d[:, :], writes=["pm"])
    P.dma("sp", dl[:, :], dl_d[0:1, :].to_broadcast([128, 256]), writes=["dl"])
    P.dma("sp", sub[:, :], sub_d[:, :], writes=["sub"])
    lt = P.sbuf("lt", [128, 8], F32)
    ltmp = P.sbuf("ltmp", [128, 128], F32)
    for i in range(2):
        P.op("dve", lambda e, i=i: e.tensor_tensor(out=ltmp[:, i * 64:(i + 1) * 64], in0=dl[:, 128 * i:128 * i + 64], in1=dl[:, 128 * i + 64:128 * i + 128], op=ALU.mult),
             reads=["dl"], writes=["ltmp"])
        P.op("dve", lambda e, i=i: e.reduce_sum(out=lt[:, i:i + 1], in_=ltmp[:, i * 64:(i + 1) * 64], axis=AX.X), reads=["ltmp"], writes=["lt"])
    P.op("act", lambda e: e.activation(out=lt[:, 2:4], in_=lt[:, 0:2], func=AF.Exp), reads=["lt"], writes=["lt"])
    P.op("dve", lambda e: e.tensor_tensor(out=lt[:, 4:5], in0=lt[:, 3:4], in1=lt[:, 2:3], op=ALU.subtract), reads=["lt"], writes=["lt"])
    P.op("dve", lambda e: e.tensor_scalar(out=lt[:, 5:6], in0=lt[:, 4:5], scalar1=-float(lam_init), scalar2=None, op0=ALU.add), reads=["lt"], writes=["lt"])
    neglam = lt[:, 5:6]
    P.op("dve", lambda e: e.tensor_scalar(out=lt[:, 6:7], in0=sub[:, 0:1], scalar1=1.0 - float(lam_init), scalar2=None, op0=ALU.mult),
         reads=["sub"], writes=["lt"])
    subs = lt[:, 6:7]

    qkT = P.sbuf("qkT", [128, 4, N], BF16)
    vd = P.sbuf("vd", [128, NT, 256], BF16)
    yst = P.sbuf("yst", [128, 2, N], BF16)
    xf = P.sbuf("xf", [128, 512], F32)
    t1 = P.sbuf("t1", [128, 512], F32)
    t2 = P.sbuf("t2", [128, 512], F32)
    onesb = P.sbuf("onesb", [128, 1], BF16)
    P.op("pool", lambda e: e.memset(onesb[:, :], 1.0), writes=["onesb"])

    for (c0, n) in token_blocks(N):
        s, keys = H.load(c0, n)
        for g in range(4):
            b = C.bank((0, 4), 1)
            pap, pk = emit_proj_fm(P, C, H, s, keys, wqk[:, g, :, :], "wqk%d" % g, 128, n, b)
            sc_ = 0.125 if g < 2 else 1.0
            P.op("act", lambda e, pap=pap, sc_=sc_: e.activation(out=xf[:, 0:n], in_=pap, func=AF.Copy, scale=sc_), reads=[pk], writes=["xf"])
            nc_ = max(0, min(n, n_ctx - c0))
            if nc_ > 0:
                P.op("dve", lambda e, g=g, nc_=nc_: e.tensor_copy(out=qkT[:, g, c0:c0 + nc_], in_=xf[:, 0:nc_]), reads=["xf"], writes=["qkT%d" % g])
            if nc_ < n:
                l0 = c0 + nc_ - n_ctx
                nl = n - nc_
                b2 = C.bank((4, 4), 1)
                pk2 = "ps%d" % b2
                P.op("pe", lambda e, b2=b2, nc_=nc_, nl=nl: e.matmul(C.ps[b2][:, 0:nl], lhsT=pm[:, :], rhs=xf[:, nc_:nc_ + nl], start=True, stop=True),
                     reads=["pm", "xf"], writes=[pk2])
                P.op("dve", lambda e, nc_=nc_, nl=nl, l0=l0: e.tensor_tensor(out=t1[:, 0:nl], in0=xf[:, nc_:nc_ + nl], in1=cosT[:, l0:l0 + nl], op=ALU.mult),
                     reads=["xf", "cosT"], writes=["t1"])
                P.op("dve", lambda e, b2=b2, nl=nl, l0=l0: e.tensor_tensor(out=t2[:, 0:nl], in0=C.ps[b2][:, 0:nl], in1=sinT[:, l0:l0 + nl], op=ALU.mult),
                     reads=[pk2, "sinT"], writes=["t2"])
                P.op("pool", lambda e, g=g, nc_=nc_, nl=nl: e.tensor_tensor(out=qkT[:, g, c0 + nc_:c0 + n], in0=t1[:, 0:nl], in1=t2[:, 0:nl], op=ALU.add),
                     reads=["t1", "t2"], writes=["qkT%d" % g])
        for t in range(n // 128):
            b = C.bank((4, 4), 1)
            pap, pk = emit_proj_tm(P, C, H, s, keys, wv, "wv", t * 128, 128, 256, b)
            ti = c0 // 128 + t
            P.op("dve", lambda e, pap=pap, ti=ti: e.tensor_copy(out=vd[:, ti, :], in_=pap), reads=[pk], writes=["vd%d" % ti])

    Eb = P.sbuf("Eb", [128, 8, 512], BF16)
    Eacc = P.sbuf("Eacc", [128, 4, 512], F32)
    ones32 = P.sbuf("ones32", [128, 1], F32)
    P.op("pool", lambda e: e.memset(ones32[:, :], 1.0), writes=["ones32"])
    nsb = P.sbuf("nsb", [128, 2, 512], F32)
    drow = P.sbuf("drow", [1, 2, 512], F32)
    rc = P.sbuf("rc", [128, 4, 4], F32)
    hd = P.sbuf("hd", [128, 2, 128], F32)
    qblocks = []
    if need_ctx and n_ctx > 0:
        qblocks.append((0, n_ctx, 0, n_ctx // 128))
    for (c0, n) in token_blocks(n_lat):
        qblocks.append((n_ctx + c0, n, 0, NT))
    LOOK = 2
    SB = (0, 5)
    ACC = 6
    DEN0 = 5
    state = dict(ei=0, ri=0)
    pending = []

    def flush(keep):
        while len(pending) > keep:
            pending.pop(0)()

    def emit_scores(hh, q0, nq, ki, m):
        bs = C.bank(SB, 1)
        pks = "ps%d" % bs
        P.op("pe", lambda e: e.matmul(
            C.ps[bs][:, 0:nq], lhsT=qkT[64 * m:64 * m + 64, 2 + hh, ki * 128:(ki + 1) * 128],
            rhs=qkT[64 * m:64 * m + 64, hh, q0:q0 + nq], start=True, stop=True),
            reads=["qkT%d" % (2 + hh), "qkT%d" % hh], writes=[pks])
        ej = state["ei"] % 8
        state["ei"] += 1
        return bs, pks, ej

    def emit_exp(bs, pks, ej, nq):
        P.op("act", lambda e: e.activation(out=Eb[:, ej, 0:nq], in_=C.ps[bs][:, 0:nq], func=AF.Exp),
             reads=[pks], writes=["Eb%d" % ej])

    def emit_pv(hh, nq, ki, m, ej, first, lastk, kt0):
        P.op("pe", lambda e: e.matmul(
            C.ps[ACC + m][:, 0:nq], lhsT=vd[:, ki, hh * 128:(hh + 1) * 128], rhs=Eb[:, ej, 0:nq], start=first, stop=lastk),
            reads=["vd%d" % ki, "Eb%d" % ej], writes=["ps%d" % (ACC + m)])
        if m == 0:
            P.op("pe", lambda e: e.matmul(C.ps[DEN0][0:1, 0:nq], lhsT=onesb[:, 0:1], rhs=Eb[:, ej, 0:nq], start=first, stop=lastk),
                 reads=["onesb", "Eb%d" % ej], writes=["ps%d" % DEN0])
        else:
            a = ki % 2
            eng = "dve" if a == 0 else "pool"
            if ki - kt0 < 2:
                P.op(eng, lambda e: e.tensor_copy(out=Eacc[:, a, 0:nq], in_=Eb[:, ej, 0:nq]), reads=["Eb%d" % ej], writes=["Eacc%d" % a])
            else:
                P.op(eng, lambda e: e.tensor_tensor(out=Eacc[:, a, 0:nq], in0=Eacc[:, a, 0:nq], in1=Eb[:, ej, 0:nq], op=ALU.add),
                     reads=["Eb%d" % ej, "Eacc%d" % a], writes=["Eacc%d" % a])

    def emit_finish(hh, q0, nq, na):
        for m in range(2):
            P.op("act", lambda e, m=m: e.activation(out=nsb[:, m, 0:nq], in_=C.ps[ACC + m][:, 0:nq], func=AF.Copy),
                 reads=["ps%d" % (ACC + m)], writes=["nsb%d" % m])
        P.op("dve", lambda e: e.tensor_copy(out=drow[:, 0, 0:nq], in_=C.ps[DEN0][0:1, 0:nq]), reads=["ps%d" % DEN0], writes=["drow0"])
        bd = C.bank(SB, 1)
        for a in range(na):
            P.op("pe", lambda e, a=a, bd=bd: e.matmul(C.ps[bd][0:1, 0:nq], lhsT=ones32[:, 0:1], rhs=Eacc[:, a, 0:nq], start=(a == 0), stop=(a == na - 1)),
                 reads=["ones32", "Eacc%d" % a], writes=["ps%d" % bd], inc=(a == na - 1))
        P.op("dve", lambda e, bd=bd: e.tensor_copy(out=drow[:, 1, 0:nq], in_=C.ps[bd][0:1, 0:nq]), reads=["ps%d" % bd], writes=["drow1"])
        b = C.bank(SB, 1)
        pk = "ps%d" % b
        for qs in range(nq // 128):
            qsl = slice(qs * 128, (qs + 1) * 128)
            for m in range(2):
                P.op("pe", lambda e, m=m, qsl=qsl: e.transpose(C.ps[b][:, m * 128:(m + 1) * 128], nsb[:, m, qsl], C.ident[:, :]),
                     reads=["nsb%d" % m, "ident"], writes=[pk], inc=False)
            for m in range(2):
                P.op("pe", lambda e, m=m, qsl=qsl: e.transpose(C.ps[b][:, 256 + m:257 + m], drow[:, m, qsl], C.ident[0:1, 0:1]),
                     reads=["drow%d" % m, "ident"], writes=[pk], inc=(m == 1))
            rj = state["ri"] % 4
            state["ri"] += 1
            rk = "rc%d" % rj
            P.op("dve", lambda e, rj=rj: e.reciprocal(out=rc[:, rj, 0:2], in_=C.ps[b][:, 256:258]), reads=[pk], writes=[rk])
            P.op("dve", lambda e, rj=rj: e.tensor_scalar(out=rc[:, rj, 2:3], in0=rc[:, rj, 1:2], scalar1=neglam, scalar2=None, op0=ALU.mult),
                 reads=[rk, "lt"], writes=[rk])
            hj = rj % 2
            hk_ = "hd%d" % hj
            P.op("act", lambda e, rj=rj, hj=hj: e.activation(out=hd[:, hj, :], in_=C.ps[b][:, 0:128], func=AF.Copy, scale=rc[:, rj, 0:1]),
                 reads=[pk, rk], writes=[hk_])
            P.op("dve", lambda e, rj=rj, hj=hj: e.scalar_tensor_tensor(out=hd[:, hj, :], in0=C.ps[b][:, 128:256], scalar=rc[:, rj, 2:3],
                                                                       in1=hd[:, hj, :], op0=ALU.mult, op1=ALU.add),
                 reads=[pk, rk, hk_], writes=[hk_])
            c = C.statcol(2)
            P.op("act", lambda e, hj=hj, c=c: e.activation(out=C.junk[:, 0:128], in_=hd[:, hj, :], func=AF.Square, accum_out=C.stat[:, c:c + 1]),
                 reads=[hk_], writes=["junk", "stat%d" % c])
            emit_rstd(P, C, C.stat[:, c:c + 1], "stat%d" % c, 128, C.stat[:, c + 1:c + 2], "stat%d" % (c + 1), 128)
            P.op("act", lambda e, hj=hj, c=c: e.activation(out=hd[:, hj, :], in_=hd[:, hj, :], func=AF.Copy, scale=C.stat[:, c + 1:c + 2]),
                 reads=[hk_, "stat%d" % (c + 1)], writes=[hk_])
            P.op("pe", lambda e, hj=hj: e.transpose(C.ps[b][:, 0:128], hd[:, hj, :], C.ident[:, :]), reads=[hk_, "ident"], writes=[pk])
            P.op("dve", lambda e, qs=qs: e.tensor_scalar(out=yst[:, hh, q0 + qs * 128:q0 + (qs + 1) * 128], in0=C.ps[b][:, 0:128],
                                                         scalar1=subs, scalar2=None, op0=ALU.mult),
                 reads=[pk, "lt"], writes=["yst%d" % hh])

    for hh in range(2):
        for (q0, nq, kt0, kt1) in qblocks:
            for ki in range(kt0, kt1):
                sc = [emit_scores(hh, q0, nq, ki, m) for m in range(2)]
                for m in range(2):
                    emit_exp(sc[m][0], sc[m][1], sc[m][2], nq)

                def unit(hh=hh, nq=nq, ki=ki, sc=sc, kt0=kt0, kt1=kt1):
                    for m in range(2):
                        emit_pv(hh, nq, ki, m, sc[m][2], ki == kt0, ki == kt1 - 1, kt0)
                pending.append(unit)
                flush(LOOK)
            pending.append(lambda hh=hh, q0=q0, nq=nq, na=min(2, kt1 - kt0): emit_finish(hh, q0, nq, na))
    flush(0)
    for hh in range(2):
        if not (need_ctx and n_ctx > 0) and n_ctx > 0:
            P.op("pool", lambda e, hh=hh: e.memset(yst[:, hh, 0:n_ctx], 0.0), writes=["yst%d" % hh])
        P.dma("sp", yT[hh * 128:(hh + 1) * 128, :], yst[:, hh, :], reads=["yst%d" % hh])
    return P.finish()


MODC = 6 * D // NCORES


def build_mod():
    P = Prog()
    cT_d = P.dram("cT", [128, NCH * 3], F32, "ExternalInput")
    wm = P.dram("wm", [DEPTH, 128, NCH * MODC], F32, "ExternalInput")
    bm = P.dram("bm", [DEPTH, MODC], F32, "ExternalInput")
    out = P.dram("mod", [DEPTH, 3, MODC], F32, "ExternalOutput")
    cT = P.sbuf("cT", [128, NCH, 3], F32)
    P.dma("sp", cT[:, :, :], cT_d[:, :], writes=["cT"])
    P.op("act", lambda e: e.activation(out=cT[:, :, :], in_=cT[:, :, :], func=AF.Silu), reads=["cT"], writes=["cT"])
    ps = [P.psum("ps%d" % i, [128, 512], F32) for i in range(2)]
    wt = P.sbuf("wt", [128, 2, NCH, 512], F32)
    bt = P.sbuf("bt", [3, 2, 512], F32)
    ot = P.sbuf("ot", [3, 2, 512], F32)
    i = 0
    for l in range(DEPTH):
        wl = wm[l, :, :].rearrange("p (k m) -> p k m", m=MODC)
        for nb in range(MODC // 512):
            s = i % 2
            i += 1
            for k in range(NCH):
                P.dma("sp", wt[:, s, k, :], wl[:, k, nb * 512:(nb + 1) * 512], writes=["wt%d_%d" % (s, k)])
            P.dma("sp", bt[:, s, :], bm[l:l + 1, nb * 512:(nb + 1) * 512].to_broadcast([3, 512]), writes=["bt%d" % s])
            for k in range(NCH):
                P.op("pe", lambda e, s=s, k=k: e.matmul(ps[s][0:3, :], lhsT=cT[:, k, :], rhs=wt[:, s, k, :], start=(k == 0), stop=(k == NCH - 1)),
                     reads=["cT", "wt%d_%d" % (s, k)], writes=["ps%d" % s], inc=(k == NCH - 1))
            P.op("dve", lambda e, s=s: e.tensor_tensor(out=ot[:, s, :], in0=ps[s][0:3, :], in1=bt[:, s, :], op=ALU.add),
                 reads=["ps%d" % s, "bt%d" % s], writes=["ot%d" % s])
            P.dma("sp", out[l, :, nb * 512:(nb + 1) * 512], ot[:, s, :], reads=["ot%d" % s])
    return P.finish()


OFF = dict(m_q=0, m_k=512, m_v=1024, m_o=1536, m_g=2048, g_q=2064, g_k=2320, g_v=2576, g_out=3088, g_lr=3600,
           d_q=3632, d_k=4656, d_v=5680)


def _relay(w):
    K_, M = w.shape[0] // 128, w.shape[1]
    return np.ascontiguousarray(w.reshape(K_, 128, M).transpose(1, 0, 2).reshape(128, K_ * M))


def _relay_chunks(w):
    K_, J = w.shape[0] // 128, w.shape[1] // 128
    return np.ascontiguousarray(w.reshape(K_, 128, J, 128).transpose(2, 1, 0, 3).reshape(J, 128, K_ * 128))


def _cols16(v):
    v = np.asarray(v, np.float32).reshape(-1, NCH, 128)
    return np.ascontiguousarray(v.transpose(2, 0, 1))


_PROGS = {}
_DEBUG = None


def _prog(key, fn):
    if key not in _PROGS:
        _PROGS[key] = fn()
    return _PROGS[key]


def _run(nc, in_maps):
    res = run_bass_kernel_spmd(nc, in_maps, core_ids=list(range(NCORES)))
    return res.results


def kernel(x, c, ctx, c_ctx, w_mod, b_mod, norm_mix_pre, norm_mix_post, norm_ffn_pre, norm_ffn_post, w_in,
           mlstm_conv_w, mlstm_conv_b, mlstm_gate_b, mlstm_norm, gla_gate_w2, gla_gate_b, gla_norm,
           diff_lambda, diff_subln, w_out, w_ffn_gate, w_ffn_up, w_ffn_down):
    f32 = np.float32
    x = np.asarray(x, f32)
    ctx = np.asarray(ctx, f32)
    B = x.shape[0]
    ident = np.eye(128, dtype=f32)
    QT = SEQ // 4
    QC = CTX // 4

    cvec = np.concatenate([np.asarray(c, f32), np.asarray(c_ctx, f32)[None]], 0)
    cT = np.ascontiguousarray(cvec.reshape(3, NCH, 128).transpose(2, 1, 0).reshape(128, NCH * 3))
    w_mod = np.asarray(w_mod, f32)
    b_mod = np.asarray(b_mod, f32)
    maps = []
    for core in range(NCORES):
        cs = slice(core * MODC, (core + 1) * MODC)
        maps.append(dict(cT=cT, wm=np.stack([_relay(w_mod[l][:, cs]) for l in range(DEPTH)]),
                         bm=np.ascontiguousarray(b_mod[:, cs])))
    res = _run(_prog("mod", build_mod), maps)
    mod = np.concatenate([r["mod"] for r in res], axis=2)
    mod = mod.reshape(DEPTH, 3, 6, D)
    if _DEBUG is not None:
        _DEBUG["mod"] = mod

    def dense_maps(layer, xs_core, yT_core, do_c):
        la = min(layer + 1, DEPTH - 1) if do_c else layer
        norms = np.stack([np.asarray(a, f32)[layer] for a in (norm_mix_pre, norm_mix_post, norm_ffn_pre, norm_ffn_post)])
        ncols_src = norms.copy()
        ncols_src[0] = np.asarray(norm_mix_pre, f32)[la]
        ncols = _cols16(ncols_src)
        shared = {}
        if do_c:
            shared = dict(wo_r=_relay_chunks(np.asarray(w_out, f32)[layer]), wg_r=_relay_chunks(np.asarray(w_ffn_gate, f32)[layer]),
                          wu_r=_relay_chunks(np.asarray(w_ffn_up, f32)[layer]), wd_r=_relay_chunks(np.asarray(w_ffn_down, f32)[layer]),
                          normrows=norms)
        maps = []
        for core in range(NCORES):
            b = core // 4
            mrows = np.concatenate([mod[layer, b], mod[layer, 2]], 0)
            mcols_src = mrows.copy()
            mcols_src[0:2] = mod[la, b, 0:2]
            mcols_src[6:8] = mod[la, 2, 0:2]
            m = dict(x=xs_core[core], ident=ident, modcols=_cols16(mcols_src), normcols=ncols)
            if do_c:
                m.update(shared)
                m["modrows"] = np.ascontiguousarray(mrows)
                m["yT"] = yT_core[core]
            maps.append(m)
        return maps

    def gather_hT(res_list, n_ctx_core):
        out = []
        for b in range(B):
            hall = np.zeros((D, NTOK), dtype=ml_dtypes.bfloat16)
            for qq in range(4):
                h = res_list[b * 4 + qq]["hT_out"]
                hall[:, qq * QC:(qq + 1) * QC] = h[:, 0:QC]
                hall[:, CTX + qq * QT:CTX + (qq + 1) * QT] = h[:, QC:QC + QT]
            out.append(hall)
        return out

    xs_core = []
    for core in range(NCORES):
        b, qq = core // 4, core % 4
        xs_core.append(np.ascontiguousarray(np.concatenate([ctx[b, qq * QC:(qq + 1) * QC], x[b, qq * QT:(qq + 1) * QT]], 0)))
    res = _run(_prog("A", lambda: build_dense(QC, QT, False, True)), dense_maps(0, xs_core, None, False))
    hT_all = gather_hT(res, QC)
    if _DEBUG is not None:
        _DEBUG["hT0"] = hT_all

    w_in = np.asarray(w_in, f32)
    masks = mlstm_masks()
    tri = mlstm_tri(CTX // 64, NTOK // 64)
    rmask = gla_rmask()
    cosT, sinT, pm = rope_consts(SEQ)
    x_out = None
    for layer in range(DEPTH):
        last = layer == DEPTH - 1
        w = w_in[layer]
        lam_init = 0.8 - 0.6 * math.exp(-0.3 * layer)
        cw_all = np.asarray(mlstm_conv_w, f32)[layer]
        cb_all = np.asarray(mlstm_conv_b, f32)[layer]
        gb_all = np.asarray(mlstm_gate_b, f32)[layer]
        m_maps, g_maps, a_maps = [], [], []
        for core in range(NCORES):
            b, q = core // 4, core % 4
            cols = lambda name, a, n: w[:, OFF[name] + a:OFF[name] + a + n]
            cw = np.zeros((128, 8), f32)
            cw[:, 0:3] = cw_all[:, 128 * q:128 * q + 128].T
            cw[:, 3:6] = cw_all[:, 512 + 128 * q:512 + 128 * q + 128].T
            cw[:, 6] = cb_all[128 * q:128 * q + 128]
            cw[:, 7] = cb_all[512 + 128 * q:512 + 128 * q + 128]
            gidx = [OFF["m_g"] + i for i in (q, 4 + q, 8 + q, 12 + q)]
            m_maps.append(dict(
                hT=hT_all[b], ident=ident,
                w_fm=np.stack([_relay(cols("m_q", 128 * q, 128)), _relay(cols("m_k", 128 * q, 128)), _relay(cols("m_o", 128 * q, 128))]),
                w_g=_relay(w[:, gidx]), w_v=_relay(cols("m_v", 128 * q, 128)), cw=cw,
                gb=np.ascontiguousarray(gb_all[[q, 4 + q, 8 + q, 12 + q]].reshape(4, 1)),
                mn=np.ascontiguousarray(np.asarray(mlstm_norm, f32)[layer][128 * q:128 * q + 128].reshape(128, 1)),
                masks=masks, tri=tri))
            gw2 = np.asarray(gla_gate_w2, f32)[layer]
            gbb = np.asarray(gla_gate_b, f32)[layer]
            g_maps.append(dict(
                hT=hT_all[b], ident=ident,
                w_qk=_relay(np.concatenate([cols("g_q", 64 * q, 64), cols("g_k", 64 * q, 64)], 1)),
                w_go=_relay(cols("g_out", 128 * q, 128)), w_lr=_relay(cols("g_lr", 0, 32)), w_v=_relay(cols("g_v", 128 * q, 128)),
                w2=np.ascontiguousarray(np.concatenate([gw2[0][:, 64 * q:64 * q + 64], gw2[1][:, 64 * q:64 * q + 64]], 1)),
                nb=np.ascontiguousarray((gbb[:, 64 * q:64 * q + 64] * f32(-1.0)).T) if False else np.ascontiguousarray(np.negative(gbb[:, 64 * q:64 * q + 64]).T),
                gn=np.ascontiguousarray(np.asarray(gla_norm, f32)[layer][128 * q:128 * q + 128].reshape(128, 1)),
                masks=masks, rmask=rmask))
            a_maps.append(dict(
                hT=hT_all[b], ident=ident,
                w_qk=np.stack([_relay(cols("d_q", 256 * q, 128)), _relay(cols("d_q", 256 * q + 128, 128)),
                               _relay(cols("d_k", 256 * q, 128)), _relay(cols("d_k", 256 * q + 128, 128))]),
                w_v=_relay(cols("d_v", 256 * q, 256)), cosT=cosT, sinT=sinT, pm=pm,
                dlam=np.ascontiguousarray(np.asarray(diff_lambda, f32)[layer].reshape(1, 256)),
                subln=np.ascontiguousarray(np.asarray(diff_subln, f32)[layer].reshape(128, 1))))
        res_m = _run(_prog("mlstm", lambda: build_mlstm(CTX, SEQ)), m_maps)
        res_g = _run(_prog("gla", lambda: build_gla(CTX, SEQ)), g_maps)
        res_a = _run(_prog("attn%d" % layer, lambda: build_attn(CTX, SEQ, lam_init, not last)), a_maps)
        ymix = []
        for b in range(B):
            ym = np.zeros((D, NTOK), dtype=ml_dtypes.bfloat16)
            for q in range(4):
                core = b * 4 + q
                ym[128 * q:128 * q + 128] = res_m[core]["yT"]
                ym[512 + 128 * q:512 + 128 * q + 128] = res_g[core]["yT"]
                ym[1024 + 256 * q:1024 + 256 * q + 256] = res_a[core]["yT"]
            ymix.append(ym)
        if _DEBUG is not None:
            _DEBUG["ymix%d" % layer] = ymix
            if _DEBUG.get("stop_after_mix") == layer:
                return None
        n_ctx_core = 0 if last else QC
        yT_core = []
        for core in range(NCORES):
            b, qq = core // 4, core % 4
            parts = []
            if n_ctx_core:
                parts.append(ymix[b][:, qq * QC:(qq + 1) * QC])
            parts.append(ymix[b][:, CTX + qq * QT:CTX + (qq + 1) * QT])
            yT_core.append(np.ascontiguousarray(np.concatenate(parts, 1)))
        if last:
            xs_core = [np.ascontiguousarray(xc[xc.shape[0] - QT:]) for xc in xs_core]
        key = "C%d_%d" % (n_ctx_core, int(not last))
        res = _run(_prog(key, lambda: build_dense(n_ctx_core, QT, True, not last)), dense_maps(layer, xs_core, yT_core, True))
        xs_core = [r["x_out"] for r in res]
        if not last:
            hT_all = gather_hT(res, QC)
        if _DEBUG is not None:
            _DEBUG["xs%d" % layer] = xs_core
            _DEBUG["hT%d" % (layer + 1)] = hT_all
    out = np.zeros((B, SEQ, D), f32)
    for core in range(NCORES):
        b, qq = core // 4, core % 4
        out[b, qq * QT:(qq + 1) * QT] = xs_core[core][-QT:]
    return out
```

```python
import contextlib
import math

import ml_dtypes
import numpy as np

import concourse.bass as bass
import concourse.mybir as mybir
from concourse.bass_utils import run_bass_kernel_spmd

F32 = mybir.dt.float32
BF16 = mybir.dt.bfloat16
ALU = mybir.AluOpType
AF = mybir.ActivationFunctionType
AX = mybir.AxisListType

D = 2048
NCH = 16
FFN = 5632
NJ = 44
DEPTH = 2
CTX = 256
SEQ = 4096
NTOK = CTX + SEQ
EPS = 1e-6
IN_COLS = 6704
NCORES = 8


class Prog:
    K_RING = 12

    def __init__(self):
        self.nc = bass.Bass("TRN2", target_bir_lowering=False)
        self.es = contextlib.ExitStack()
        nc = self.nc
        self.eng = {}
        for name, h in (("pe", nc.tensor), ("act", nc.scalar), ("dve", nc.vector),
                        ("pool", nc.gpsimd), ("sp", nc.sync)):
            sem = self.es.enter_context(nc.semaphore("s_" + name))
            self.eng[name] = dict(h=h, sem=sem, sn="s_" + name, cnt=0, waited={})
        self.ring = {}
        self.rpos = {}
        for q in ("sp", "pool", "act"):
            self.ring[q] = []
            for i in range(self.K_RING):
                sem = self.es.enter_context(nc.semaphore("d_%s%d" % (q, i)))
                self.ring[q].append(dict(sem=sem, sn="d_%s%d" % (q, i), val=0))
            self.rpos[q] = 0
        self.lastw = {}
        self.readers = {}
        self.n_ops = 0
        self.psum_banks = []

    def sbuf(self, name, shape, dtype):
        return self.es.enter_context(self.nc.sbuf_tensor("sb_" + name, list(shape), dtype))

    def psum(self, name, shape, dtype):
        return self.es.enter_context(self.nc.psum_tensor("pp_" + name, list(shape), dtype))

    def dram(self, name, shape, dtype, kind):
        return self.nc.dram_tensor(name, list(shape), dtype, kind=kind).ap()

    def _collect(self, engname, reads, writes):
        own = self.eng[engname]["sn"]
        need = {}

        def add(ev, is_war):
            sn, sh, v = ev
            if sn == own:
                if engname == "pe":
                    return
            if sn not in need or need[sn][1] < v:
                need[sn] = (sh, v)

        for k in reads:
            e = self.lastw.get(k)
            if e is not None:
                add(e, False)
            if k.startswith("ps"):
                for e in self.readers.get(k, ()):
                    add(e, True)
        for k in writes:
            e = self.lastw.get(k)
            if e is not None:
                add(e, False)
            for e in self.readers.get(k, ()):
                add(e, True)
        return need

    def _emit_waits(self, engname, need):
        E = self.eng[engname]
        for sn, (sh, v) in need.items():
            if E["waited"].get(sn, 0) >= v:
                continue
            E["h"].wait_ge(sh, v)
            E["waited"][sn] = v

    def _record(self, ev, reads, writes):
        for k in writes:
            self.lastw[k] = ev
            self.readers[k] = []
        for k in reads:
            self.readers.setdefault(k, []).append(ev)

    def op(self, engname, fn, reads=(), writes=(), inc=True):
        E = self.eng[engname]
        need = self._collect(engname, reads, writes)
        self._emit_waits(engname, need)
        ins = fn(E["h"])
        if inc:
            E["cnt"] += 1
            ins.then_inc(E["sem"], 1)
            ev = (E["sn"], E["sem"], E["cnt"])
        else:
            ev = (E["sn"], E["sem"], E["cnt"] + 1)
        self._record(ev, reads, writes)
        self.n_ops += 1
        return ins

    def dma(self, q, out, in_, reads=(), writes=(), **kw):
        E = self.eng[q]
        slot = self.ring[q][self.rpos[q]]
        self.rpos[q] = (self.rpos[q] + 1) % self.K_RING
        need = self._collect(q, reads, writes)
        if slot["val"] > 0:
            if slot["sn"] not in need or need[slot["sn"]][1] < slot["val"]:
                need[slot["sn"]] = (slot["sem"], slot["val"])
        self._emit_waits(q, need)
        slot["val"] += 16
        E["h"].dma_start(out=out, in_=in_, **kw).then_inc(slot["sem"], 16)
        ev = (slot["sn"], slot["sem"], slot["val"])
        self._record(ev, reads, writes)
        self.n_ops += 1

    def coll(self, kind, in_ap, out_ap, reads=(), writes=(), groups=None):
        q = "pool"
        E = self.eng[q]
        slot = self.ring[q][self.rpos[q]]
        self.rpos[q] = (self.rpos[q] + 1) % self.K_RING
        need = self._collect(q, reads, writes)
        if slot["val"] > 0:
            if slot["sn"] not in need or need[slot["sn"]][1] < slot["val"]:
                need[slot["sn"]] = (slot["sem"], slot["val"])
        self._emit_waits(q, need)
        slot["val"] += 16
        groups = groups or [[0, 1, 2, 3], [4, 5, 6, 7]]
        E["h"].collective_compute(kind, ALU.bypass, groups, ins=[in_ap], outs=[out_ap]).then_inc(slot["sem"], 16)
        ev = (slot["sn"], slot["sem"], slot["val"])
        self._record(ev, reads, writes)
        self.n_ops += 1

    def barrier(self):
        evs = {}
        for name in ("pe", "act", "dve", "pool"):
            X = self.eng[name]
            if X["cnt"] > 0:
                evs[X["sn"]] = (X["sem"], X["cnt"])
        for q in ("sp", "pool", "act"):
            for slot in self.ring[q]:
                if slot["val"] > 0:
                    evs[slot["sn"]] = (slot["sem"], slot["val"])
        for name in ("pe", "act", "dve", "pool", "sp"):
            own = self.eng[name]["sn"]
            self._emit_waits(name, {k: v for k, v in evs.items() if k != own})
        self.lastw = {}
        self.readers = {}

    def finish(self):
        E = self.eng["sp"]
        for q in ("sp", "pool", "act"):
            for slot in self.ring[q]:
                if slot["val"] > 0 and E["waited"].get(slot["sn"], 0) < slot["val"]:
                    E["h"].wait_ge(slot["sem"], slot["val"])
                    E["waited"][slot["sn"]] = slot["val"]
        for name in ("pe", "act", "dve", "pool"):
            X = self.eng[name]
            if X["cnt"] > 0 and E["waited"].get(X["sn"], 0) < X["cnt"]:
                E["h"].wait_ge(X["sem"], X["cnt"])
        self.es.close()
        return self.nc


class Ctx:
    def __init__(self, P, ident_dram):
        self.P = P
        self.ps = [P.psum("ps%d" % i, [128, 512], F32) for i in range(8)]
        self.ident = P.sbuf("ident", [128, 128], F32)
        P.dma("sp", self.ident[:, :], ident_dram[:, :], writes=["ident"])
        self.identb = P.sbuf("identb", [128, 128], BF16)
        P.op("dve", lambda e: e.tensor_copy(out=self.identb[:, :], in_=self.ident[:, :]),
             reads=["ident"], writes=["identb"])
        self.junk = P.sbuf("junk", [128, 2048], BF16)
        self.stat = P.sbuf("stat", [128, 64], F32)
        self.stat_i = 0
        self.rot = {}

    def bank(self, group, n):
        lo, cnt = group
        i = self.rot.get(group, 0)
        self.rot[group] = (i + 1) % cnt
        return lo + i

    def statcol(self, n=1):
        i = self.stat_i
        if i + n > 64:
            i = 0
        self.stat_i = i + n
        return i


def emit_rstd(P, C, ss_ap, ss_key, rows, out_ap, out_key, n_feat):
    P.op("dve", lambda e: e.tensor_scalar(out=out_ap, in0=ss_ap, scalar1=1.0 / n_feat, scalar2=EPS,
                                          op0=ALU.mult, op1=ALU.add),
         reads=[ss_key], writes=[out_key])
    P.op("act", lambda e: e.activation(out=out_ap, in_=out_ap, func=AF.Sqrt),
         reads=[out_key], writes=[out_key])
    P.op("dve", lambda e: e.reciprocal(out=out_ap, in_=out_ap),
         reads=[out_key], writes=[out_key])


def emit_linear_fm(P, C, w_src, K, nout, stage, rhs, blocks, epilogue, banks=(0, 4), wq="pool",
                   name="lin"):
    ns = len(stage)
    for j in range(nout):
        st, skey = stage[j % ns]
        P.dma(wq, st, w_src(j), writes=[skey])
        for bi, (c0, n) in enumerate(blocks):
            b = C.bank(banks, 1)
            pk = "ps%d" % b
            pap = C.ps[b][:, 0:n]
            for k in range(K):
                r_ap, r_keys = rhs(k, c0, n)
                P.op("pe", lambda e, pap=pap, k=k, r_ap=r_ap: e.matmul(
                    pap, lhsT=st[:, k * 128:(k + 1) * 128], rhs=r_ap, start=(k == 0), stop=(k == K - 1)),
                    reads=[skey] + list(r_keys), writes=[pk], inc=(k == K - 1))
            epilogue(j, bi, pap, pk, c0, n)


def emit_to_tokmajor(P, C, srcT, src_key_fn, t0, rows, banks=(4, 4)):
    outs = []
    for n in range(4):
        b = C.bank(banks, 1)
        pk = "ps%d" % b
        for i in range(4):
            f = n * 4 + i
            P.op("pe", lambda e, b=b, i=i, f=f: e.transpose(
                C.ps[b][0:rows, i * 128:(i + 1) * 128], srcT[:, f, t0:t0 + rows], C.ident[:, :]),
                reads=[src_key_fn(f), "ident"], writes=[pk], inc=(i == 3))
        outs.append((C.ps[b][0:rows, :], pk))
    return outs


def emit_postnorm_residual(P, C, outs, rows, x_ap, x_key, g_ap, g_key, tmp_ap, tmp_key):
    c = C.statcol(6)
    ss = C.stat[0:rows, c:c + 4]
    for n, (pap, pk) in enumerate(outs):
        P.op("act", lambda e, pap=pap, n=n: e.activation(
            out=C.junk[0:rows, 0:512], in_=pap, func=AF.Square, accum_out=C.stat[0:rows, c + n:c + n + 1]),
            reads=[pk], writes=["junk", "stat%d" % (c + n)])
    tot = C.stat[0:rows, c + 4:c + 5]
    P.op("dve", lambda e: e.reduce_sum(out=tot, in_=ss, axis=AX.X),
         reads=["stat%d" % (c + n) for n in range(4)], writes=["stat%d" % (c + 4)])
    rstd = C.stat[0:rows, c + 5:c + 6]
    emit_rstd(P, C, tot, "stat%d" % (c + 4), rows, rstd, "stat%d" % (c + 5), D)
    for n, (pap, pk) in enumerate(outs):
        sl = slice(n * 512, (n + 1) * 512)
        P.op("dve", lambda e, pap=pap, sl=sl: e.scalar_tensor_tensor(
            out=tmp_ap[0:rows, sl], in0=pap, scalar=rstd, in1=g_ap[0:rows, sl], op0=ALU.mult, op1=ALU.mult),
            reads=[pk, "stat%d" % (c + 5), g_key], writes=[tmp_key])
        P.op("pool", lambda e, sl=sl: e.tensor_tensor(
            out=x_ap[0:rows, sl], in0=x_ap[0:rows, sl], in1=tmp_ap[0:rows, sl], op=ALU.add),
            reads=[tmp_key, x_key], writes=[x_key])


def emit_prenorm_T(P, C, x_ap, x_key, rows, xs_ap, xs_key, a_col, sh_col, mod_key, dst_fn, banks=(4, 4)):
    c = C.statcol(2)
    ss = C.stat[0:rows, c:c + 1]
    P.op("act", lambda e: e.activation(out=C.junk[0:rows, :], in_=x_ap[0:rows, :], func=AF.Square, accum_out=ss),
         reads=[x_key], writes=["junk", "stat%d" % c])
    rstd = C.stat[0:rows, c + 1:c + 2]
    emit_rstd(P, C, ss, "stat%d" % c, rows, rstd, "stat%d" % (c + 1), D)
    P.op("act", lambda e: e.activation(out=xs_ap[0:rows, :], in_=x_ap[0:rows, :], func=AF.Copy, scale=rstd),
         reads=[x_key, "stat%d" % (c + 1)], writes=[xs_key])
    for n in range(4):
        b = C.bank(banks, 1)
        pk = "ps%d" % b
        for i in range(4):
            f = n * 4 + i
            P.op("pe", lambda e, b=b, i=i, f=f: e.transpose(
                C.ps[b][:, i * 128:i * 128 + rows], xs_ap[0:rows, f * 128:(f + 1) * 128], C.ident[0:rows, 0:rows]),
                reads=[xs_key, "ident"], writes=[pk], inc=(i == 3))
        for i in range(4):
            f = n * 4 + i
            d_ap, d_key = dst_fn(f)
            eng = "dve" if (n % 2 == 0) else "act"
            if eng == "dve":
                P.op("dve", lambda e, b=b, i=i, f=f, d_ap=d_ap: e.tensor_scalar(
                    out=d_ap, in0=C.ps[b][:, i * 128:i * 128 + rows], scalar1=a_col[:, f:f + 1],
                    scalar2=sh_col[:, f:f + 1], op0=ALU.mult, op1=ALU.add),
                    reads=[pk, mod_key], writes=[d_key])
            else:
                P.op("act", lambda e, b=b, i=i, f=f, d_ap=d_ap: e.activation(
                    out=d_ap, in_=C.ps[b][:, i * 128:i * 128 + rows], func=AF.Identity,
                    scale=a_col[:, f:f + 1], bias=sh_col[:, f:f + 1]),
                    reads=[pk, mod_key], writes=[d_key])


def tile_groups(n_ctx, n_lat):
    tiles = []
    if n_ctx:
        tiles.append((0, n_ctx, True))
    for i in range(n_lat // 128):
        tiles.append((n_ctx + i * 128, 128, False))
    groups = []
    cur = []
    cur_n = 0
    for t in tiles:
        if cur and cur_n + t[1] > 576:
            groups.append(cur)
            cur, cur_n = [], 0
        cur.append(t)
        cur_n += t[1]
    if cur:
        groups.append(cur)
    out = []
    for g in groups:
        g0 = g[0][0]
        gn = sum(t[1] for t in g)
        out.append((g0, gn, g))
    return out


def blocks_of(gn):
    bl = []
    c = 0
    while c < gn:
        n = min(512, gn - c)
        bl.append((c, n))
        c += n
    return bl


def build_dense(n_ctx, n_lat, do_c, do_a, a_ctx=True):
    P = Prog()
    nc = P.nc
    ntok = n_ctx + n_lat
    x_in = P.dram("x", [ntok, D], F32, "ExternalInput")
    ident_d = P.dram("ident", [128, 128], F32, "ExternalInput")
    modcols = P.dram("modcols", [128, 12, NCH], F32, "ExternalInput")
    normcols = P.dram("normcols", [128, 4, NCH], F32, "ExternalInput")
    if do_c:
        modrows = P.dram("modrows", [12, D], F32, "ExternalInput")
        normrows = P.dram("normrows", [4, D], F32, "ExternalInput")
        yT_in = P.dram("yT", [D, ntok], BF16, "ExternalInput")
        wo_r = P.dram("wo_r", [NCH, 128, NCH * 128], F32, "ExternalInput")
        wg_r = P.dram("wg_r", [NJ, 128, NCH * 128], F32, "ExternalInput")
        wu_r = P.dram("wu_r", [NJ, 128, NCH * 128], F32, "ExternalInput")
        wd_r = P.dram("wd_r", [NCH, 128, NJ * 128], F32, "ExternalInput")
        x_out = P.dram("x_out", [ntok, D], F32, "ExternalOutput")
    if do_a:
        hT_out = P.dram("hT_out", [D, ntok], BF16, "ExternalOutput")

    C = Ctx(P, ident_d)
    GN = 576
    actT = P.sbuf("actT", [128, NCH, GN], BF16)
    x1 = P.sbuf("x1", [128, 5, D], F32)
    xs = P.sbuf("xs", [128, D], F32)
    mc = P.sbuf("mc", [128, 12, NCH], F32)
    ncol = P.sbuf("ncol", [128, 4, NCH], F32)
    acol = P.sbuf("acol", [128, 8, NCH], F32)
    P.dma("sp", mc[:, :, :], modcols[:, :, :], writes=["mc"])
    P.dma("sp", ncol[:, :, :], normcols[:, :, :], writes=["ncol"])
    for idx, (nrm, mrow) in enumerate(((0, 1), (0, 7), (2, 4), (2, 10))):
        P.op("dve", lambda e, idx=idx, nrm=nrm, mrow=mrow: e.scalar_tensor_tensor(
            out=acol[:, idx, :], in0=mc[:, mrow, :], scalar=1.0, in1=ncol[:, nrm, :], op0=ALU.add, op1=ALU.mult),
            reads=["mc", "ncol"], writes=["acol"])
    if do_c:
        oT = P.sbuf("oT", [128, NCH, GN], F32)
        aT = P.sbuf("aT", [128, NJ, GN], BF16)
        gbuf = P.sbuf("gbuf", [128, 2, D], F32)
        NS = 6
        wst = P.sbuf("wst", [128, NS, NCH * 128], BF16)
        stage = [(wst[:, i, :], "wst%d" % i) for i in range(NS)]
        sg = P.sbuf("sg", [128, 2, 512], F32)

    def load_g(which):
        nrow = 1 if which == 2 else 3
        P.dma("sp", xs[:, :], normrows[nrow:nrow + 1, :].to_broadcast([128, D]), writes=["xs"])
        for v, mrow in enumerate((which, 6 + which)):
            P.dma("sp", gbuf[:, v, :], modrows[mrow:mrow + 1, :].to_broadcast([128, D]), writes=["gbuf%d" % v])
            P.op("pool", lambda e, v=v: e.tensor_tensor(out=gbuf[:, v, :], in0=gbuf[:, v, :], in1=xs[:, :], op=ALU.mult),
                 reads=["xs", "gbuf%d" % v], writes=["gbuf%d" % v])

    groups = tile_groups(n_ctx, n_lat)
    for (g0, gn, tiles) in groups:
        blocks = blocks_of(gn)
        for ti, (t0, rows, is_ctx) in enumerate(tiles):
            P.dma("sp", x1[0:rows, ti, :], x_in[t0:t0 + rows, :], writes=["x1_%d" % ti])
        if do_c:
            for f in range(NCH):
                P.dma("sp", actT[:, f, 0:gn], yT_in[f * 128:(f + 1) * 128, g0:g0 + gn], writes=["actT%d" % f])

            def ep_copy(j, bi, pap, pk, c0, n):
                P.op("act", lambda e: e.activation(out=oT[:, j, c0:c0 + n], in_=pap, func=AF.Copy),
                     reads=[pk], writes=["oT%d" % j])

            emit_linear_fm(P, C, lambda j: wo_r[j, :, :], NCH, NCH, stage,
                           lambda k, c0, n: (actT[:, k, c0:c0 + n], ["actT%d" % k]), blocks, ep_copy)
            load_g(2)
            for ti, (t0, rows, is_ctx) in enumerate(tiles):
                outs = emit_to_tokmajor(P, C, oT, lambda f: "oT%d" % f, t0 - g0, rows)
                v = 1 if is_ctx else 0
                emit_postnorm_residual(P, C, outs, rows, x1[:, ti, :], "x1_%d" % ti, gbuf[:, v, :], "gbuf%d" % v,
                                       xs, "xs")
            for ti, (t0, rows, is_ctx) in enumerate(tiles):
                a_i, s_row = (3, 9) if is_ctx else (2, 3)
                emit_prenorm_T(P, C, x1[:, ti, :], "x1_%d" % ti, rows, xs, "xs", acol[:, a_i, :], mc[:, s_row, :],
                               "acol", lambda f, t0=t0, rows=rows: (actT[:, f, t0 - g0:t0 - g0 + rows], "actT%d" % f))
            def w_gu(jj):
                return (wg_r if jj % 2 == 0 else wu_r)[jj // 2, :, :]

            def ep_gu(jj, bi, pap, pk, c0, n):
                j = jj // 2
                if jj % 2 == 0:
                    P.op("act", lambda e: e.activation(out=sg[:, bi, 0:n], in_=pap, func=AF.Silu),
                         reads=[pk], writes=["sg%d" % bi])
                else:
                    P.op("dve", lambda e: e.tensor_tensor(out=aT[:, j, c0:c0 + n], in0=sg[:, bi, 0:n], in1=pap, op=ALU.mult),
                         reads=[pk, "sg%d" % bi], writes=["aT%d" % j])

            emit_linear_fm(P, C, w_gu, NCH, 2 * NJ, stage,
                           lambda k, c0, n: (actT[:, k, c0:c0 + n], ["actT%d" % k]), blocks, ep_gu)
            subs = [(0, 16), (16, 16), (32, 12)]
            ns = len(stage)
            cnt = 0
            for f in range(NCH):
                sts = []
                for (k0, kk) in subs:
                    st, skey = stage[cnt % ns]
                    cnt += 1
                    P.dma("pool", st[:, 0:kk * 128], wd_r[f, :, k0 * 128:(k0 + kk) * 128], writes=[skey])
                    sts.append((st, skey, k0, kk))
                for bi, (c0, n) in enumerate(blocks):
                    b = C.bank((0, 4), 1)
                    pk = "ps%d" % b
                    pap = C.ps[b][:, 0:n]
                    for (st, skey, k0, kk) in sts:
                        for k in range(kk):
                            kg = k0 + k
                            P.op("pe", lambda e, pap=pap, st=st, k=k, kg=kg: e.matmul(
                                pap, lhsT=st[:, k * 128:(k + 1) * 128], rhs=aT[:, kg, c0:c0 + n],
                                start=(kg == 0), stop=(kg == NJ - 1)),
                                reads=[skey, "aT%d" % kg], writes=[pk], inc=(kg == NJ - 1))
                    P.op("act", lambda e, pap=pap, f=f, c0=c0, n=n: e.activation(out=oT[:, f, c0:c0 + n], in_=pap, func=AF.Copy),
                         reads=[pk], writes=["oT%d" % f])
            load_g(5)
            for ti, (t0, rows, is_ctx) in enumerate(tiles):
                outs = emit_to_tokmajor(P, C, oT, lambda f: "oT%d" % f, t0 - g0, rows)
                v = 1 if is_ctx else 0
                emit_postnorm_residual(P, C, outs, rows, x1[:, ti, :], "x1_%d" % ti, gbuf[:, v, :], "gbuf%d" % v,
                                       xs, "xs")
                P.dma("sp", x_out[t0:t0 + rows, :], x1[0:rows, ti, :], reads=["x1_%d" % ti])
        if do_a:
            for ti, (t0, rows, is_ctx) in enumerate(tiles):
                a_i, s_row = (1, 6) if is_ctx else (0, 0)
                emit_prenorm_T(P, C, x1[:, ti, :], "x1_%d" % ti, rows, xs, "xs", acol[:, a_i, :], mc[:, s_row, :],
                               "acol", lambda f, t0=t0, rows=rows: (actT[:, f, t0 - g0:t0 - g0 + rows], "actT%d" % f))
            for f in range(NCH):
                P.dma("sp", hT_out[f * 128:(f + 1) * 128, g0:g0 + gn], actT[:, f, 0:gn], reads=["actT%d" % f])
    return P.finish()


def token_blocks(N):
    bl = []
    c = 0
    while c < N:
        n = min(512, N - c)
        bl.append((c, n))
        c += n
    return bl


def chunk_order(nctx_c, ncn, d):
    if d == 0:
        return list(range(ncn))
    return list(range(nctx_c - 1, -1, -1)) + list(range(ncn - 1, nctx_c - 1, -1))


class HStream:
    def __init__(self, P, hT, N, name="hblk"):
        self.P, self.hT, self.N = P, hT, N
        self.buf = P.sbuf(name, [128, 2, NCH, 512], BF16)
        self.i = 0

    def load(self, c0, n):
        s = self.i % 2
        self.i += 1
        keys = ["hb%d_%d" % (s, k) for k in range(NCH)]
        for k in range(NCH):
            self.P.dma("sp", self.buf[:, s, k, 0:n], self.hT[k * 128:(k + 1) * 128, c0:c0 + n], writes=[keys[k]])
        return s, keys


def emit_proj_fm(P, C, H, s, keys, w_ap, wkey, M, n, bank):
    pk = "ps%d" % bank
    for k in range(NCH):
        P.op("pe", lambda e, k=k: e.matmul(C.ps[bank][0:M, 0:n], lhsT=w_ap[:, k, 0:M], rhs=H.buf[:, s, k, 0:n],
                                           start=(k == 0), stop=(k == NCH - 1)),
             reads=[wkey, keys[k]], writes=[pk], inc=(k == NCH - 1))
    return C.ps[bank][0:M, 0:n], pk


def emit_proj_tm(P, C, H, s, keys, w_ap, wkey, t0, rows, ncols, bank):
    pk = "ps%d" % bank
    for k in range(NCH):
        P.op("pe", lambda e, k=k: e.matmul(C.ps[bank][0:rows, 0:ncols], lhsT=H.buf[:, s, k, t0:t0 + rows],
                                           rhs=w_ap[:, k, 0:ncols], start=(k == 0), stop=(k == NCH - 1)),
             reads=[wkey, keys[k]], writes=[pk], inc=(k == NCH - 1))
    return C.ps[bank][0:rows, 0:ncols], pk


def emit_v64(P, C, H, s, keys, wv, wkey, c0, n, v64, vT32):
    b = C.bank((0, 4), 1)
    pap, pk = emit_proj_fm(P, C, H, s, keys, wv, wkey, 128, n, b)
    P.op("act", lambda e: e.activation(out=vT32[:, 0:n], in_=pap, func=AF.Copy), reads=[pk], writes=["vT32"])
    nch = n // 64
    t = 0
    while t < nch:
        g = min(4, nch - t)
        b2 = C.bank((4, 4), 1)
        pk2 = "ps%d" % b2
        for i in range(g):
            P.op("pe", lambda e, i=i, t=t: e.transpose(C.ps[b2][0:64, i * 128:(i + 1) * 128], vT32[:, (t + i) * 64:(t + i + 1) * 64], C.ident[:, :]),
                 reads=["vT32", "ident"], writes=[pk2], inc=(i == g - 1))
        ci = c0 // 64 + t
        P.op("dve", lambda e, ci=ci, g=g: e.tensor_copy(out=v64[:, ci:ci + g, :], in_=C.ps[b2][0:64, 0:g * 128].rearrange("p (c e) -> p c e", e=128)),
             reads=[pk2], writes=["v64_%d" % (ci + i) for i in range(g)])
        t += g


def build_mlstm(n_ctx, n_lat, debug=False, dirs=(0, 1)):
    P = Prog()
    N = n_ctx + n_lat
    NCN = N // 64
    NCC = n_ctx // 64
    hT = P.dram("hT", [D, N], BF16, "ExternalInput")
    ident_d = P.dram("ident", [128, 128], F32, "ExternalInput")
    w_fm = P.dram("w_fm", [3, 128, NCH * 128], F32, "ExternalInput")
    w_g = P.dram("w_g", [128, NCH * 4], F32, "ExternalInput")
    w_v = P.dram("w_v", [128, NCH * 128], F32, "ExternalInput")
    cw_d = P.dram("cw", [128, 8], F32, "ExternalInput")
    gb_d = P.dram("gb", [4, 1], F32, "ExternalInput")
    mn_d = P.dram("mn", [128, 1], F32, "ExternalInput")
    masks_d = P.dram("masks", [64, 2, 64], F32, "ExternalInput")
    tri_d = P.dram("tri", [NCN, 2, NCN], F32, "ExternalInput")
    gscr = P.dram("gscr", [4, N], F32, "Internal")
    yT = P.dram("yT", [128, N], BF16, "ExternalOutput")

    C = Ctx(P, ident_d)
    H = HStream(P, hT, N)
    wfm = P.sbuf("wfm", [128, 3, NCH, 128], BF16)
    wg = P.sbuf("wg", [128, NCH, 4], BF16)
    wv = P.sbuf("wv", [128, NCH, 128], BF16)
    for g in range(3):
        P.dma("pool", wfm[:, g, :, :], w_fm[g, :, :], writes=["wfm%d" % g])
    P.dma("pool", wg[:, :, :], w_g[:, :], writes=["wg"])
    P.dma("pool", wv[:, :, :], w_v[:, :], writes=["wv"])
    cw = P.sbuf("cw", [128, 8], F32)
    gb = P.sbuf("gb", [4, 1], F32)
    mn = P.sbuf("mn", [128, 1], F32)
    masks = P.sbuf("masks", [64, 2, 64], F32)
    tri = P.sbuf("tri", [NCN, 2, NCN], F32)
    for t, src, key in ((cw, cw_d, "cw"), (gb, gb_d, "gb"), (mn, mn_d, "mn")):
        P.dma("sp", t[:, :], src[:, :], writes=[key])
    P.dma("sp", masks[:, :, :], masks_d[:, :, :], writes=["masks"])
    P.dma("sp", tri[:, :, :], tri_d[:, :, :], writes=["tri"])

    big = P.sbuf("big", [128, 2 * N], F32)
    acc = P.sbuf("acc", [128, N], F32)
    mqT = P.sbuf("mqT", [128, N], BF16)
    mkT = P.sbuf("mkT", [128, N], BF16)
    moT = P.sbuf("moT", [128, N], BF16)
    v64 = P.sbuf("v64", [64, NCN, 128], BF16)
    vT32 = P.sbuf("vT32", [128, 512], F32)
    ktok = P.sbuf("ktok", [64, NCN, 128], BF16)
    gsb = P.sbuf("gsb", [4, 512], F32)
    zeros = P.sbuf("zeros", [128, 512], F32)
    ones = P.sbuf("ones", [128, 128], F32)
    P.op("pool", lambda e: e.memset(zeros[:, :], 0.0), writes=["zeros"])
    P.op("pool", lambda e: e.memset(ones[:, :], 1.0), writes=["ones"])

    for (c0, n) in token_blocks(N):
        s, keys = H.load(c0, n)
        for g, (dst, dkey) in enumerate(((big[:, 0:N], "mqraw"), (big[:, N:2 * N], "mkraw"))):
            b = C.bank((0, 4), 1)
            pap, pk = emit_proj_fm(P, C, H, s, keys, wfm[:, g, :, :], "wfm%d" % g, 128, n, b)
            P.op("act", lambda e, pap=pap, dst=dst: e.activation(out=dst[:, c0:c0 + n], in_=pap, func=AF.Copy),
                 reads=[pk], writes=[dkey])
        b = C.bank((0, 4), 1)
        pap, pk = emit_proj_fm(P, C, H, s, keys, wfm[:, 2, :, :], "wfm2", 128, n, b)
        P.op("act", lambda e, pap=pap: e.activation(out=moT[:, c0:c0 + n], in_=pap, func=AF.Sigmoid),
             reads=[pk], writes=["moT"])
        b = C.bank((0, 4), 1)
        pap, pk = emit_proj_fm(P, C, H, s, keys, wg, "wg", 4, n, b)
        P.op("dve", lambda e, pap=pap: e.tensor_scalar(out=gsb[:, 0:n], in0=pap, scalar1=gb[:, 0:1], scalar2=None, op0=ALU.add),
             reads=[pk, "gb"], writes=["gsb"])
        P.dma("sp", gscr[:, c0:c0 + n], gsb[:, 0:n], reads=["gsb"], writes=["gscr%d" % c0])
        emit_v64(P, C, H, s, keys, wv, "wv", c0, n, v64, vT32)

    segs = [(a, b_) for (a, b_) in ((0, n_ctx), (n_ctx, N)) if b_ > a]
    for qi, (raw, rkey, dstT, dkey) in enumerate(((big[:, 0:N], "mqraw", mqT, "mqT"), (big[:, N:2 * N], "mkraw", mkT, "mkT"))):
        w0, w1, w2, bcol = cw[:, 3 * qi:3 * qi + 1], cw[:, 3 * qi + 1:3 * qi + 2], cw[:, 3 * qi + 2:3 * qi + 3], cw[:, 6 + qi:7 + qi]
        P.op("dve", lambda e, raw=raw, w1=w1, bcol=bcol: e.tensor_scalar(out=acc[:, :], in0=raw, scalar1=w1, scalar2=bcol, op0=ALU.mult, op1=ALU.add),
             reads=[rkey, "cw"], writes=["acc"])
        for (a, b_) in segs:
            P.op("dve", lambda e, raw=raw, w0=w0, a=a, b_=b_: e.scalar_tensor_tensor(
                out=acc[:, a + 1:b_], in0=raw[:, a:b_ - 1], scalar=w0, in1=acc[:, a + 1:b_], op0=ALU.mult, op1=ALU.add),
                reads=[rkey, "cw", "acc"], writes=["acc"])
            P.op("dve", lambda e, raw=raw, w2=w2, a=a, b_=b_: e.scalar_tensor_tensor(
                out=acc[:, a:b_ - 1], in0=raw[:, a + 1:b_], scalar=w2, in1=acc[:, a:b_ - 1], op0=ALU.mult, op1=ALU.add),
                reads=[rkey, "cw", "acc"], writes=["acc"])
        P.op("act", lambda e: e.activation(out=acc[:, :], in_=acc[:, :], func=AF.Silu), reads=["acc"], writes=["acc"])
        if qi == 0:
            P.op("dve", lambda e: e.tensor_scalar(out=mqT[:, :], in0=acc[:, :], scalar1=128.0 ** -0.5, scalar2=None, op0=ALU.mult),
                 reads=["acc"], writes=["mqT"])
        else:
            P.op("dve", lambda e: e.tensor_copy(out=mkT[:, :], in_=acc[:, :]), reads=["acc"], writes=["mkT"])
            for c in range(NCN):
                b = C.bank((4, 4), 1)
                pk = "ps%d" % b
                P.op("pe", lambda e, b=b, c=c: e.transpose(C.ps[b][0:64, 0:128], acc[:, c * 64:(c + 1) * 64], C.ident[:, :]),
                     reads=["acc", "ident"], writes=[pk])
                P.op("act", lambda e, b=b, c=c: e.activation(out=ktok[:, c, :], in_=C.ps[b][0:64, 0:128], func=AF.Copy),
                     reads=[pk], writes=["ktok%d" % c])

    st = P.sbuf("st", [NCN, 40, 64], F32)
    sc = P.sbuf("sc", [NCN, 32], F32)
    rowt = P.sbuf("rowt", [1, 4, NCN], F32)
    colW = P.sbuf("colW", [64, 2, 3, NCN], F32)
    bca = P.sbuf("bca", [128, 2, NCN], F32)
    diag = P.sbuf("diag", [NCN, NCN], F32)
    skey = lambda i: "st%d" % i
    ckey = lambda i: "sc%d" % i

    def rv(ap, d):
        return ap[:, ::-1] if d == 1 else ap

    gkeys = ["gscr%d" % c0 for (c0, n) in token_blocks(N)]
    for d in range(2):
        base = d * 20
        I_, F_, L_, Pl, Pt, A_, Al, Gc, W_, R_, E_, T1 = [st[:, base + i, :] for i in range(12)]
        kI, kF, kL, kPl, kPt, kA, kAl, kGc, kW, kR, kE, kT1 = [skey(base + i) for i in range(12)]
        cb = d * 16
        cPc, cMx, cGk, cGkp, cNGkp, cAl = [sc[:, cb + i:cb + i + 1] for i in range(6)]
        kcPc, kcMx, kcGk, kcGkp, kcNGkp, kcAl = [ckey(cb + i) for i in range(6)]
        P.dma("sp", I_, gscr[2 * d:2 * d + 1, :].rearrange("o (c l) -> (o c) l", l=64), reads=gkeys, writes=[kI])
        P.dma("sp", F_, gscr[2 * d + 1:2 * d + 2, :].rearrange("o (c l) -> (o c) l", l=64), reads=gkeys, writes=[kF])
        P.op("act", lambda e: e.activation(out=T1, in_=F_, func=AF.Exp, scale=-1.0), reads=[kF], writes=[kT1])
        P.op("act", lambda e: e.activation(out=L_, in_=T1, func=AF.Ln, bias=1.0), reads=[kT1], writes=[kL])
        P.op("dve", lambda e: e.tensor_tensor_scan(out=rv(Pl, d), data0=rv(L_, d), data1=zeros[0:NCN, 0:64], initial=0.0,
                                                   op0=ALU.add, op1=ALU.add), reads=[kL, "zeros"], writes=[kPl])
        last = (lambda ap: ap[:, 0:1]) if d == 1 else (lambda ap: ap[:, 63:64])
        b = C.bank((0, 4), 1)
        pk = "ps%d" % b
        P.op("pe", lambda e, b=b: e.matmul(C.ps[b][0:NCN, 0:1], lhsT=tri[:, d, :], rhs=last(Pl), start=True, stop=True),
             reads=["tri", kPl], writes=[pk])
        P.op("act", lambda e, b=b: e.activation(out=cPc, in_=C.ps[b][0:NCN, 0:1], func=AF.Copy), reads=[pk], writes=[kcPc])
        P.op("dve", lambda e: e.tensor_scalar(out=Pt, in0=Pl, scalar1=cPc, scalar2=None, op0=ALU.add), reads=[kPl, kcPc], writes=[kPt])
        P.op("dve", lambda e: e.tensor_tensor(out=A_, in0=I_, in1=Pt, op=ALU.add), reads=[kI, kPt], writes=[kA])
        P.op("dve", lambda e: e.tensor_tensor_scan(out=rv(Al, d), data0=rv(A_, d), data1=rv(A_, d), initial=-1e30,
                                                   op0=ALU.max, op1=ALU.max), reads=[kA], writes=[kAl])
        b = C.bank((0, 4), 1)
        pk = "ps%d" % b
        P.op("pe", lambda e, b=b: e.transpose(C.ps[b][0:1, 0:NCN], last(Al), C.ident[0:NCN, 0:NCN]),
             reads=[kAl, "ident"], writes=[pk])
        mxr, gpr, gkr = rowt[:, 0, :], rowt[:, 1, :], rowt[:, 2, :]
        P.op("act", lambda e, b=b: e.activation(out=mxr, in_=C.ps[b][0:1, 0:NCN], func=AF.Copy), reads=[pk], writes=["rowt0"])
        if d == 0:
            P.op("dve", lambda e: e.tensor_tensor_scan(out=gpr, data0=mxr, data1=mxr, initial=0.0, op0=ALU.max, op1=ALU.max),
                 reads=["rowt0"], writes=["rowt1"])
            P.op("dve", lambda e: e.memset(gkr[:, 0:1], 0.0), writes=["rowt2"])
            P.op("dve", lambda e: e.tensor_copy(out=gkr[:, 1:NCN], in_=gpr[:, 0:NCN - 1]), reads=["rowt1"], writes=["rowt2"])
        else:
            if NCC > 0:
                P.op("dve", lambda e: e.tensor_tensor_scan(out=gpr[:, 0:NCC][:, ::-1], data0=mxr[:, 0:NCC][:, ::-1],
                                                           data1=mxr[:, 0:NCC][:, ::-1], initial=0.0, op0=ALU.max, op1=ALU.max),
                     reads=["rowt0"], writes=["rowt1"])
                P.op("dve", lambda e: e.tensor_tensor_scan(out=gpr[:, NCC:NCN][:, ::-1], data0=mxr[:, NCC:NCN][:, ::-1],
                                                           data1=mxr[:, NCC:NCN][:, ::-1], initial=gpr[:, 0:1], op0=ALU.max, op1=ALU.max),
                     reads=["rowt0", "rowt1"], writes=["rowt1"])
                P.op("dve", lambda e: e.memset(gkr[:, NCC - 1:NCC], 0.0), writes=["rowt2"])
                if NCC > 1:
                    P.op("dve", lambda e: e.tensor_copy(out=gkr[:, 0:NCC - 1], in_=gpr[:, 1:NCC]), reads=["rowt1"], writes=["rowt2"])
                P.op("dve", lambda e: e.tensor_copy(out=gkr[:, NCN - 1:NCN], in_=gpr[:, 0:1]), reads=["rowt1"], writes=["rowt2"])
            else:
                P.op("dve", lambda e: e.tensor_tensor_scan(out=gpr[:, ::-1], data0=mxr[:, ::-1], data1=mxr[:, ::-1], initial=0.0,
                                                           op0=ALU.max, op1=ALU.max), reads=["rowt0"], writes=["rowt1"])
                P.op("dve", lambda e: e.memset(gkr[:, NCN - 1:NCN], 0.0), writes=["rowt2"])
            P.op("dve", lambda e: e.tensor_copy(out=gkr[:, NCC:NCN - 1], in_=gpr[:, NCC + 1:NCN]), reads=["rowt1"], writes=["rowt2"])
        for (row, rk, col, ck) in ((gkr, "rowt2", cGk, kcGk), (gpr, "rowt1", cGkp, kcGkp)):
            b = C.bank((0, 4), 1)
            pk = "ps%d" % b
            P.op("pe", lambda e, b=b, row=row: e.transpose(C.ps[b][0:NCN, 0:1], row, C.ident[0:1, 0:1]),
                 reads=[rk, "ident"], writes=[pk])
            P.op("act", lambda e, b=b, col=col: e.activation(out=col, in_=C.ps[b][0:NCN, 0:1], func=AF.Copy), reads=[pk], writes=[ck])
        P.op("dve", lambda e: e.tensor_scalar(out=cNGkp, in0=cGkp, scalar1=-1.0, scalar2=None, op0=ALU.mult), reads=[kcGkp], writes=[kcNGkp])
        P.op("dve", lambda e: e.tensor_scalar(out=Gc, in0=Al, scalar1=cGk, scalar2=None, op0=ALU.max), reads=[kAl, kcGk], writes=[kGc])
        P.op("act", lambda e: e.activation(out=W_, in_=A_, func=AF.Exp, bias=cNGkp), reads=[kA, kcNGkp], writes=[kW])
        P.op("act", lambda e: e.activation(out=R_, in_=Gc, func=AF.Exp, scale=-1.0, bias=cGkp), reads=[kGc, kcGkp], writes=[kR])
        P.op("dve", lambda e: e.tensor_tensor(out=T1, in0=Pt, in1=Gc, op=ALU.subtract), reads=[kPt, kGc], writes=[kT1])
        P.op("act", lambda e: e.activation(out=E_, in_=T1, func=AF.Exp), reads=[kT1], writes=[kE])
        P.op("dve", lambda e: e.tensor_tensor(out=cAl, in0=cGk, in1=cGkp, op=ALU.subtract), reads=[kcGk, kcGkp], writes=[kcAl])
        P.op("act", lambda e: e.activation(out=cAl, in_=cAl, func=AF.Exp), reads=[kcAl], writes=[kcAl])
        for qi, (src, sk) in enumerate(((W_, kW), (R_, kR), (E_, kE))):
            b = C.bank((0, 4), 1)
            pk = "ps%d" % b
            P.op("pe", lambda e, b=b, src=src: e.transpose(C.ps[b][0:64, 0:NCN], src, C.ident[0:NCN, 0:NCN]),
                 reads=[sk, "ident"], writes=[pk])
            P.op("act", lambda e, b=b, qi=qi: e.activation(out=colW[:, d, qi, :], in_=C.ps[b][0:64, 0:NCN], func=AF.Copy),
                 reads=[pk], writes=["colW%d" % d])
        P.op("dve", lambda e: e.tensor_scalar(out=diag[:, :], in0=C.ident[0:NCN, 0:NCN], scalar1=cAl, scalar2=None, op0=ALU.mult),
             reads=["ident", kcAl], writes=["diag"])
        b = C.bank((0, 4), 1)
        pk = "ps%d" % b
        P.op("pe", lambda e, b=b: e.matmul(C.ps[b][:, 0:NCN], lhsT=ones[0:NCN, :], rhs=diag[:, :], start=True, stop=True),
             reads=["ones", "diag"], writes=[pk])
        P.op("act", lambda e, b=b: e.activation(out=bca[:, d, :], in_=C.ps[b][:, 0:NCN], func=AF.Copy), reads=[pk], writes=["bca%d" % d])

    hacc = big[0:64, :].rearrange("p (c e) -> p c e", e=128)
    Cst = P.sbuf("Cst", [128, 2, 132], F32)
    Cbf = P.sbuf("Cbf", [128, 2, 132], BF16)
    ctmp = P.sbuf("ctmp", [128, 2, 132], F32)
    vh = P.sbuf("vh", [64, 4, 132], BF16)
    PT = P.sbuf("PT", [64, 4, 64], BF16)
    maskb = P.sbuf("maskb", [64, 2, 64], F32)
    fcol = P.sbuf("fcol", [64, 8, 4], F32)
    hkey_guard = ["mqraw", "mkraw"]
    orders = [chunk_order(NCC, NCN, d) for d in range(2)]
    for d in range(2):
        P.op("pool", lambda e, d=d: e.memset(Cst[:, d, :], 0.0), writes=["Cst%d" % d])
        P.op("pool", lambda e, d=d: e.memset(Cbf[:, d, :], 0.0), writes=["Cbf%d" % d])
    it = 0
    hwritten = set()
    for step in range(NCN):
        for d in dirs:
            c = orders[d][step]
            cn = orders[d][step + 1] if step + 1 < NCN else None
            sl = slice(c * 64, (c + 1) * 64)
            j = it % 4
            it += 1
            wcol = colW[:, d, 0, c:c + 1]
            rcol = colW[:, d, 1, c:c + 1]
            ecol = colW[:, d, 2, c:c + 1]
            ck = "colW%d" % d
            P.op("act", lambda e, j=j, c=c, wcol=wcol: e.activation(out=vh[:, j, 0:128], in_=v64[:, c, :], func=AF.Copy, scale=wcol),
                 reads=["v64_%d" % c, ck], writes=["vh%d" % j])
            P.op("pool", lambda e, j=j, wcol=wcol: e.tensor_copy(out=vh[:, j, 128:129], in_=wcol), reads=[ck], writes=["vh%d" % j])
            b1 = C.bank((0, 3), 1)
            P.op("pe", lambda e, b1=b1, sl=sl: e.matmul(C.ps[b1][0:64, 0:64], lhsT=mkT[:, sl], rhs=mqT[:, sl], start=True, stop=True),
                 reads=["mkT", "mqT"], writes=["ps%d" % b1])
            P.op("dve", lambda e, b1=b1, j=j, d=d: e.tensor_tensor(out=PT[:, j, :], in0=C.ps[b1][0:64, 0:64], in1=masks[:, d, :], op=ALU.mult),
                 reads=["ps%d" % b1, "masks"], writes=["PT%d" % j])
            b2 = C.bank((3, 3), 1)
            pk2 = "ps%d" % b2
            P.op("pe", lambda e, b2=b2, sl=sl, d=d: e.matmul(C.ps[b2][0:64, 0:129], lhsT=mqT[:, sl], rhs=Cbf[:, d, 0:129], start=True, stop=False),
                 reads=["mqT", "Cbf%d" % d], writes=[pk2], inc=False)
            P.op("pe", lambda e, b2=b2, j=j: e.matmul(C.ps[b2][0:64, 0:129], lhsT=PT[:, j, :], rhs=vh[:, j, 0:129], start=False, stop=True),
                 reads=["PT%d" % j, "vh%d" % j], writes=[pk2])
            fj = it % 8
            f0, f1, f2 = fcol[:, fj, 0:1], fcol[:, fj, 1:2], fcol[:, fj, 2:3]
            fk = "fcol%d" % fj
            P.op("act", lambda e, b2=b2, f0=f0, rcol=rcol: e.activation(out=f0, in_=C.ps[b2][0:64, 128:129], func=AF.Abs, scale=rcol),
                 reads=[pk2, ck], writes=[fk])
            P.op("dve", lambda e, f0=f0, f1=f1, ecol=ecol: e.tensor_tensor(out=f1, in0=f0, in1=ecol, op=ALU.max), reads=[fk, ck], writes=[fk])
            P.op("dve", lambda e, f1=f1: e.reciprocal(out=f1, in_=f1), reads=[fk], writes=[fk])
            P.op("dve", lambda e, f1=f1, f2=f2, rcol=rcol: e.tensor_tensor(out=f2, in0=f1, in1=rcol, op=ALU.mult), reads=[fk, ck], writes=[fk])
            hk = "hacc%d" % c
            if c not in hwritten:
                hwritten.add(c)
                P.op("act", lambda e, b2=b2, c=c, f2=f2: e.activation(out=hacc[:, c, :], in_=C.ps[b2][0:64, 0:128], func=AF.Copy, scale=f2),
                     reads=[pk2, fk], writes=[hk] + hkey_guard)
            else:
                P.op("dve", lambda e, b2=b2, c=c, f2=f2: e.scalar_tensor_tensor(out=hacc[:, c, :], in0=C.ps[b2][0:64, 0:128], scalar=f2,
                                                                               in1=hacc[:, c, :], op0=ALU.mult, op1=ALU.add),
                     reads=[pk2, fk, hk], writes=[hk])
            if cn is not None:
                b3 = C.bank((6, 2), 1)
                pk3 = "ps%d" % b3
                P.op("pe", lambda e, b3=b3, c=c, j=j: e.matmul(C.ps[b3][:, 0:129], lhsT=ktok[:, c, :], rhs=vh[:, j, 0:129], start=True, stop=True),
                     reads=["ktok%d" % c, "vh%d" % j], writes=[pk3])
                acol = bca[:, d, cn:cn + 1]
                P.op("act", lambda e, b3=b3, d=d, acol=acol: e.activation(out=ctmp[:, d, 0:129], in_=C.ps[b3][:, 0:129], func=AF.Copy, scale=acol),
                     reads=[pk3, "bca%d" % d], writes=["ctmp%d" % d])
                P.op("dve", lambda e, d=d, acol=acol: e.scalar_tensor_tensor(out=Cst[:, d, 0:129], in0=Cst[:, d, 0:129], scalar=acol,
                                                                            in1=ctmp[:, d, 0:129], op0=ALU.mult, op1=ALU.add),
                     reads=["Cst%d" % d, "ctmp%d" % d, "bca%d" % d], writes=["Cst%d" % d])
                P.op("pool", lambda e, d=d: e.tensor_copy(out=Cbf[:, d, 0:129], in_=Cst[:, d, 0:129]), reads=["Cst%d" % d], writes=["Cbf%d" % d])

    if debug:
        d1 = P.dram("dbg_mq", [128, N], BF16, "ExternalOutput")
        d2 = P.dram("dbg_mk", [128, N], BF16, "ExternalOutput")
        d3 = P.dram("dbg_colW", [64, 2 * 3 * NCN], F32, "ExternalOutput")
        d4 = P.dram("dbg_bca", [128, 2 * NCN], F32, "ExternalOutput")
        d5 = P.dram("dbg_h", [64, NCN * 128], F32, "ExternalOutput")
        d6 = P.dram("dbg_v", [64, NCN * 128], BF16, "ExternalOutput")
        d7 = P.dram("dbg_kt", [64, NCN * 128], BF16, "ExternalOutput")
        P.dma("sp", d1[:, :], mqT[:, :], reads=["mqT"])
        P.dma("sp", d2[:, :], mkT[:, :], reads=["mkT"])
        P.dma("sp", d3[:, :], colW[:, :, :, :].rearrange("p a b c -> p (a b c)"), reads=["colW0", "colW1"])
        P.dma("sp", d4[:, :], bca[:, :, :].rearrange("p a c -> p (a c)"), reads=["bca0", "bca1"])
        P.dma("sp", d5[:, :], big[0:64, :], reads=["hacc%d" % c for c in range(NCN)])
        P.dma("sp", d6[:, :], v64[:, :, :].rearrange("p a c -> p (a c)"), reads=["v64_%d" % c for c in range(NCN)])
        P.dma("sp", d7[:, :], ktok[:, :, :].rearrange("p a c -> p (a c)"), reads=["ktok%d" % c for c in range(NCN)])
    ssq = P.sbuf("ssq", [64, NCN], F32)
    for c in range(NCN):
        P.op("act", lambda e, c=c: e.activation(out=C.junk[0:64, 0:128], in_=hacc[:, c, :], func=AF.Square, accum_out=ssq[:, c:c + 1]),
             reads=["hacc%d" % c], writes=["junk", "ssq"])
    emit_rstd(P, C, ssq[:, :], "ssq", 64, ssq[:, :], "ssq", 128)
    yst = acc
    for c in range(NCN):
        sl = slice(c * 64, (c + 1) * 64)
        P.op("act", lambda e, c=c: e.activation(out=hacc[:, c, :], in_=hacc[:, c, :], func=AF.Copy, scale=ssq[:, c:c + 1]),
             reads=["hacc%d" % c, "ssq"], writes=["hacc%d" % c])
        b = C.bank((0, 4), 1)
        pk = "ps%d" % b
        P.op("pe", lambda e, b=b, c=c: e.transpose(C.ps[b][:, 0:64], hacc[:, c, :], C.ident[0:64, 0:64]),
             reads=["hacc%d" % c, "ident"], writes=[pk])
        P.op("dve", lambda e, b=b, sl=sl: e.scalar_tensor_tensor(out=mkT[:, sl], in0=C.ps[b][:, 0:64], scalar=mn[:, 0:1], in1=moT[:, sl],
                                                               op0=ALU.mult, op1=ALU.mult),
             reads=[pk, "mn", "moT"], writes=["mkT"])
    P.dma("sp", yT[:, :], mkT[:, :], reads=["mkT"])
    return P.finish()


def mlstm_masks():
    s = np.arange(64)[:, None]
    t = np.arange(64)[None, :]
    m = np.zeros((64, 2, 64), np.float32)
    m[:, 0, :] = (t >= s)
    m[:, 1, :] = (t <= s)
    return m


def mlstm_tri(ncc, ncn):
    tri = np.zeros((ncn, 2, ncn), np.float32)
    for cp in range(ncn):
        for c in range(ncn):
            tri[cp, 0, c] = 1.0 if cp < c else 0.0
            cp_ctx, c_ctx = cp < ncc, c < ncc
            if cp_ctx == c_ctx:
                before = cp > c
            else:
                before = cp_ctx and not c_ctx
            tri[cp, 1, c] = 1.0 if before else 0.0
    return tri


def emit_head_finish(P, C, hacc, NCN, nw_col, nw_key, gateT, gate_key, outT, out_key, name, extra_w=()):
    ssq = P.sbuf("ssq_" + name, [64, NCN], F32)
    for c in range(NCN):
        P.op("act", lambda e, c=c: e.activation(out=C.junk[0:64, 0:128], in_=hacc[:, c, :], func=AF.Square, accum_out=ssq[:, c:c + 1]),
             reads=["hacc%d" % c], writes=["junk", "ssq"])
    emit_rstd(P, C, ssq[:, :], "ssq", 64, ssq[:, :], "ssq", 128)
    for c in range(NCN):
        sl = slice(c * 64, (c + 1) * 64)
        P.op("act", lambda e, c=c: e.activation(out=hacc[:, c, :], in_=hacc[:, c, :], func=AF.Copy, scale=ssq[:, c:c + 1]),
             reads=["hacc%d" % c, "ssq"], writes=["hacc%d" % c])
        b = C.bank((0, 4), 1)
        pk = "ps%d" % b
        P.op("pe", lambda e, b=b, c=c: e.transpose(C.ps[b][:, 0:64], hacc[:, c, :], C.ident[0:64, 0:64]),
             reads=["hacc%d" % c, "ident"], writes=[pk])
        P.op("dve", lambda e, b=b, sl=sl: e.scalar_tensor_tensor(out=outT[:, sl], in0=C.ps[b][:, 0:64], scalar=nw_col, in1=gateT[:, sl],
                                                               op0=ALU.mult, op1=ALU.mult),
             reads=[pk, nw_key, gate_key], writes=[out_key] + (list(extra_w) if c == 0 else []))


def gla_rmask():
    t = np.arange(512)
    m = np.zeros((64, 2, 512), np.float32)
    m[:, 0, :] = (t % 64 != 0)[None, :]
    m[:, 1, :] = (t % 64 != 63)[None, :]
    return m


def build_gla(n_ctx, n_lat):
    P = Prog()
    N = n_ctx + n_lat
    NCN = N // 64
    NCC = n_ctx // 64
    hT = P.dram("hT", [D, N], BF16, "ExternalInput")
    ident_d = P.dram("ident", [128, 128], F32, "ExternalInput")
    w_qk = P.dram("w_qk", [128, NCH * 128], F32, "ExternalInput")
    w_go = P.dram("w_go", [128, NCH * 128], F32, "ExternalInput")
    w_lr = P.dram("w_lr", [128, NCH * 32], F32, "ExternalInput")
    w_v = P.dram("w_v", [128, NCH * 128], F32, "ExternalInput")
    w2_d = P.dram("w2", [16, 2 * 64], F32, "ExternalInput")
    nb_d = P.dram("nb", [64, 2], F32, "ExternalInput")
    gn_d = P.dram("gn", [128, 1], F32, "ExternalInput")
    masks_d = P.dram("masks", [64, 2, 64], F32, "ExternalInput")
    rmask_d = P.dram("rmask", [64, 2, 512], F32, "ExternalInput")
    yT = P.dram("yT", [128, N], BF16, "ExternalOutput")

    C = Ctx(P, ident_d)
    H = HStream(P, hT, N)
    wqk = P.sbuf("wqk", [128, NCH, 128], BF16)
    wgo = P.sbuf("wgo", [128, NCH, 128], BF16)
    wlr = P.sbuf("wlr", [128, NCH, 32], BF16)
    wv = P.sbuf("wv", [128, NCH, 128], BF16)
    w2 = P.sbuf("w2", [16, 128], BF16)
    P.dma("pool", wqk[:, :, :], w_qk[:, :], writes=["wqk"])
    P.dma("pool", wgo[:, :, :], w_go[:, :], writes=["wgo"])
    P.dma("pool", wlr[:, :, :], w_lr[:, :], writes=["wlr"])
    P.dma("pool", wv[:, :, :], w_v[:, :], writes=["wv"])
    P.dma("pool", w2[:, :], w2_d[:, :], writes=["w2"])
    nb = P.sbuf("nb", [64, 2], F32)
    gn = P.sbuf("gn", [128, 1], F32)
    masks = P.sbuf("masks", [64, 2, 64], F32)
    rmask = P.sbuf("rmask", [64, 2, 512], F32)
    P.dma("sp", nb[:, :], nb_d[:, :], writes=["nb"])
    P.dma("sp", gn[:, :], gn_d[:, :], writes=["gn"])
    P.dma("sp", masks[:, :, :], masks_d[:, :, :], writes=["masks"])
    P.dma("sp", rmask[:, :, :], rmask_d[:, :, :], writes=["rmask"])

    qt = P.sbuf("qt", [64, 2, N], BF16)
    kt = P.sbuf("kt", [64, 2, N], BF16)
    ktok = P.sbuf("ktok", [64, 2, NCN, 64], BF16)
    dec = P.sbuf("dec", [64, 2, NCN], F32)
    goT = P.sbuf("goT", [128, N], BF16)
    v64 = P.sbuf("v64", [64, NCN, 128], BF16)
    vT32 = P.sbuf("vT32", [128, 512], F32)
    hacc_t = P.sbuf("hacc", [64, NCN * 128], F32)
    hacc = hacc_t[:, :].rearrange("p (c e) -> p c e", e=128)
    youT = H.buf[:, :, :, :].rearrange("p a k n -> p (a k n)")[:, 0:N]
    hbkeys = ["hb%d_%d" % (s_, k_) for s_ in range(2) for k_ in range(NCH)]
    qf = P.sbuf("qf", [64, 2, 512], F32)
    kf = P.sbuf("kf", [64, 2, 512], F32)
    lrb = P.sbuf("lrb", [16, 2, 2, 512], BF16)
    T1 = P.sbuf("T1", [64, 2, 512], F32)
    Lg = T1
    Gp = P.sbuf("Gp", [64, 2, 512], F32)
    E1 = P.sbuf("E1", [64, 2, 512], F32)
    E2 = P.sbuf("E2", [64, 2, 512], F32)
    ktmp = P.sbuf("ktmp", [64, 2, 512], F32)
    khf = Gp

    def stage_a(bi, c0, n):
        s, keys = H.load(c0, n)
        nch = n // 64
        cb0 = c0 // 64
        sl2 = bi % 2
        for g, (dst, dkey) in enumerate(((qf[:, sl2, :], "qf%d" % sl2), (kf[:, sl2, :], "kf%d" % sl2))):
            b = C.bank((0, 4), 1)
            pap, pk = emit_proj_fm(P, C, H, s, keys, wqk[:, :, g * 64:(g + 1) * 64], "wqk", 64, n, b)
            sc_ = 0.125 if g == 0 else 1.0
            P.op("act", lambda e, pap=pap, dst=dst, sc_=sc_: e.activation(out=dst[:, 0:n], in_=pap, func=AF.Copy, scale=sc_),
                 reads=[pk], writes=[dkey])
        b = C.bank((0, 4), 1)
        pap, pk = emit_proj_fm(P, C, H, s, keys, wgo, "wgo", 128, n, b)
        P.op("act", lambda e, pap=pap: e.activation(out=goT[:, c0:c0 + n], in_=pap, func=AF.Silu), reads=[pk], writes=["goT"])
        for d in range(2):
            b = C.bank((0, 4), 1)
            pap, pk = emit_proj_fm(P, C, H, s, keys, wlr[:, :, d * 16:(d + 1) * 16], "wlr", 16, n, b)
            P.op("dve", lambda e, pap=pap, d=d: e.tensor_copy(out=lrb[:, sl2, d, 0:n], in_=pap), reads=[pk], writes=["lrb%d_%d" % (sl2, d)])
        emit_v64(P, C, H, s, keys, wv, "wv", c0, n, v64, vT32)

    def stage_b(bi, c0, n):
        nch = n // 64
        cb0 = c0 // 64
        sl2 = bi % 2
        for d in range(2):
            dk = str(d)
            b = C.bank((0, 4), 1)
            pk = "ps%d" % b
            P.op("pe", lambda e, b=b, d=d: e.matmul(C.ps[b][0:64, 0:n], lhsT=w2[:, d * 64:(d + 1) * 64], rhs=lrb[:, sl2, d, 0:n], start=True, stop=True),
                 reads=["w2", "lrb%d_%d" % (sl2, d)], writes=[pk])
            P.op("act", lambda e, b=b, d=d: e.activation(out=T1[:, d, 0:n], in_=C.ps[b][0:64, 0:n], func=AF.Exp, scale=-1.0, bias=nb[:, d:d + 1]),
                 reads=[pk, "nb"], writes=["T1" + dk])
            P.op("act", lambda e, d=d: e.activation(out=Lg[:, d, 0:n], in_=T1[:, d, 0:n], func=AF.Ln, bias=1.0), reads=["T1" + dk], writes=["T1" + dk])
            rvv = (lambda ap: ap[:, ::-1]) if d == 1 else (lambda ap: ap)
            P.op("dve", lambda e, d=d, rvv=rvv: e.tensor_tensor_scan(out=rvv(Gp[:, d, 0:n]), data0=rvv(rmask[:, d, 0:n]), data1=rvv(Lg[:, d, 0:n]),
                                                                   initial=0.0, op0=ALU.mult, op1=ALU.add),
                 reads=["T1" + dk, "rmask"], writes=["Gp" + dk])
            P.op("act", lambda e, d=d: e.activation(out=E1[:, d, 0:n], in_=Gp[:, d, 0:n], func=AF.Exp, scale=-1.0 / 16.0), reads=["Gp" + dk], writes=["E1" + dk])
            P.op("act", lambda e, d=d: e.activation(out=E2[:, d, 0:n], in_=Gp[:, d, 0:n], func=AF.Exp, scale=1.0 / 16.0), reads=["Gp" + dk], writes=["E2" + dk])
            P.op("dve", lambda e, d=d: e.tensor_tensor(out=qt[:, d, c0:c0 + n], in0=qf[:, sl2, 0:n], in1=E1[:, d, 0:n], op=ALU.mult),
                 reads=["qf%d" % sl2, "E1" + dk], writes=["qt" + dk])
            P.op("pool", lambda e, d=d: e.tensor_tensor(out=ktmp[:, d, 0:n], in0=kf[:, sl2, 0:n], in1=E2[:, d, 0:n], op=ALU.mult),
                 reads=["kf%d" % sl2, "E2" + dk], writes=["ktmp" + dk])
            P.op("pool", lambda e, d=d: e.tensor_copy(out=kt[:, d, c0:c0 + n], in_=ktmp[:, d, 0:n]), reads=["ktmp" + dk], writes=["kt" + dk])
            endc = 0 if d == 1 else 63
            e3 = E1[:, d, 0:n].rearrange("p (c l) -> p c l", l=64)[:, :, endc:endc + 1]
            P.op("dve", lambda e, d=d, e3=e3: e.tensor_copy(out=dec[:, d, cb0:cb0 + nch].unsqueeze(2), in_=e3), reads=["E1" + dk], writes=["dec" + dk])
            P.op("dve", lambda e, d=d, e3=e3: e.tensor_tensor(out=khf[:, d, 0:n].rearrange("p (c l) -> p c l", l=64),
                                                             in0=ktmp[:, d, 0:n].rearrange("p (c l) -> p c l", l=64),
                                                             in1=e3.to_broadcast([64, nch, 64]), op=ALU.mult),
                 reads=["ktmp" + dk, "E1" + dk], writes=["Gp" + dk])

    def stage_c(bi, c0, n):
        nch = n // 64
        cb0 = c0 // 64
        for d in range(2):
            dk = str(d)
            for t in range(nch):
                b = C.bank((4, 4), 1)
                pk = "ps%d" % b
                ci = cb0 + t
                P.op("pe", lambda e, b=b, d=d, t=t: e.transpose(C.ps[b][0:64, 0:64], khf[:, d, t * 64:(t + 1) * 64], C.ident[0:64, 0:64]),
                     reads=["Gp" + dk, "ident"], writes=[pk])
                P.op("act", lambda e, b=b, d=d, ci=ci: e.activation(out=ktok[:, d, ci, :], in_=C.ps[b][0:64, 0:64], func=AF.Copy),
                     reads=[pk], writes=["ktok%d_%d" % (d, ci)])

    blks = token_blocks(N)
    nb_ = len(blks)
    for bi in range(nb_ + 2):
        if bi < nb_:
            stage_a(bi, blks[bi][0], blks[bi][1])
        if 2 <= bi:
            stage_c(bi - 2, blks[bi - 2][0], blks[bi - 2][1])
        if 1 <= bi <= nb_:
            stage_b(bi - 1, blks[bi - 1][0], blks[bi - 1][1])

    Sst = P.sbuf("Sst", [64, 2, 128], F32)
    Sbf = P.sbuf("Sbf", [64, 2, 128], BF16)
    PT = P.sbuf("PT", [64, 4, 64], BF16)
    orders = [chunk_order(NCC, NCN, d) for d in range(2)]
    for d in range(2):
        P.op("pool", lambda e, d=d: e.memset(Sst[:, d, :], 0.0), writes=["Sst%d" % d])
        P.op("pool", lambda e, d=d: e.memset(Sbf[:, d, :], 0.0), writes=["Sbf%d" % d])
    it = 0
    hwritten = set()
    for step in range(NCN):
        for d in range(2):
            dk = str(d)
            c = orders[d][step]
            last = step + 1 >= NCN
            sl = slice(c * 64, (c + 1) * 64)
            j = it % 4
            it += 1
            b1 = C.bank((0, 3), 1)
            P.op("pe", lambda e, b1=b1, sl=sl, d=d: e.matmul(C.ps[b1][0:64, 0:64], lhsT=kt[:, d, sl], rhs=qt[:, d, sl], start=True, stop=True),
                 reads=["kt" + dk, "qt" + dk], writes=["ps%d" % b1])
            P.op("dve", lambda e, b1=b1, j=j, d=d: e.tensor_tensor(out=PT[:, j, :], in0=C.ps[b1][0:64, 0:64], in1=masks[:, d, :], op=ALU.mult),
                 reads=["ps%d" % b1, "masks"], writes=["PT%d" % j])
            b2 = C.bank((3, 3), 1)
            pk2 = "ps%d" % b2
            P.op("pe", lambda e, b2=b2, sl=sl, d=d: e.matmul(C.ps[b2][0:64, 0:128], lhsT=qt[:, d, sl], rhs=Sbf[:, d, :], start=True, stop=False),
                 reads=["qt" + dk, "Sbf" + dk], writes=[pk2], inc=False)
            P.op("pe", lambda e, b2=b2, j=j, c=c: e.matmul(C.ps[b2][0:64, 0:128], lhsT=PT[:, j, :], rhs=v64[:, c, :], start=False, stop=True),
                 reads=["PT%d" % j, "v64_%d" % c], writes=[pk2])
            hk = "hacc%d" % c
            if c not in hwritten:
                hwritten.add(c)
                P.op("act", lambda e, b2=b2, c=c: e.activation(out=hacc[:, c, :], in_=C.ps[b2][0:64, 0:128], func=AF.Copy), reads=[pk2], writes=[hk])
            else:
                P.op("dve", lambda e, b2=b2, c=c: e.tensor_tensor(out=hacc[:, c, :], in0=hacc[:, c, :], in1=C.ps[b2][0:64, 0:128], op=ALU.add),
                     reads=[pk2, hk], writes=[hk])
            if not last:
                b3 = C.bank((6, 2), 1)
                pk3 = "ps%d" % b3
                P.op("pe", lambda e, b3=b3, c=c, d=d: e.matmul(C.ps[b3][0:64, 0:128], lhsT=ktok[:, d, c, :], rhs=v64[:, c, :], start=True, stop=True),
                     reads=["ktok%d_%d" % (d, c), "v64_%d" % c], writes=[pk3])
                P.op("dve", lambda e, b3=b3, d=d, c=c: e.scalar_tensor_tensor(out=Sst[:, d, :], in0=Sst[:, d, :], scalar=dec[:, d, c:c + 1],
                                                                             in1=C.ps[b3][0:64, 0:128], op0=ALU.mult, op1=ALU.add),
                     reads=["Sst" + dk, "dec" + dk, pk3], writes=["Sst" + dk])
                P.op("pool", lambda e, d=d: e.tensor_copy(out=Sbf[:, d, :], in_=Sst[:, d, :]), reads=["Sst" + dk], writes=["Sbf" + dk])

    emit_head_finish(P, C, hacc, NCN, gn[:, 0:1], "gn", goT, "goT", youT, "youT", "g", extra_w=hbkeys)
    P.dma("sp", yT[:, :], youT, reads=["youT"])
    return P.finish()


def rope_tables(n_lat):
    rows = n_lat // 64
    row = np.repeat(np.arange(rows, dtype=np.float32), 64)
    col = np.tile(np.arange(64, dtype=np.float32), rows)
    half = 8
    inv_freq = (10000.0 ** (-np.arange(half, dtype=np.float32) / half)).astype(np.float32)
    ang_r = row[:, None] * inv_freq
    ang_c = col[:, None] * inv_freq
    ang = np.concatenate([ang_r, ang_r, ang_c, ang_c], axis=-1)
    return ang


def rope_consts(n_lat):
    rows = n_lat // 64
    row = np.repeat(np.arange(rows, dtype=np.float32), 64)
    col = np.tile(np.arange(64, dtype=np.float32), rows)
    half = 16 // 1 // 2 * 1
    half = 16
    inv_freq = (np.float32(10000.0) ** (-np.arange(half, dtype=np.float32) / np.float32(half))).astype(np.float32)
    ang_r = (row[:, None] * inv_freq).astype(np.float32)
    ang_c = (col[:, None] * inv_freq).astype(np.float32)
    ang = np.concatenate([ang_r, ang_r, ang_c, ang_c], axis=-1)
    cos = np.cos(ang).astype(np.float32)
    sin = np.sin(ang).astype(np.float32)
    sgn = np.ones(64, np.float32)
    perm = np.zeros(64, np.int64)
    for d in range(64):
        blk, i = d // 32, d % 32
        if i < 16:
            perm[d] = blk * 32 + i + 16
            sgn[d] = -1.0
        else:
            perm[d] = blk * 32 + i - 16
    cosT = np.concatenate([cos.T, cos.T], 0)
    sinT = np.concatenate([(sin * sgn[None]).T, (sin * sgn[None]).T], 0)
    pm = np.zeros((128, 128), np.float32)
    for m in range(2):
        for d in range(64):
            pm[m * 64 + perm[d], m * 64 + d] = 1.0
    return np.ascontiguousarray(cosT), np.ascontiguousarray(sinT), pm


def build_attn(n_ctx, n_lat, lam_init, need_ctx):
    P = Prog()
    N = n_ctx + n_lat
    NT = N // 128
    hT = P.dram("hT", [D, N], BF16, "ExternalInput")
    ident_d = P.dram("ident", [128, 128], F32, "ExternalInput")
    w_qk = P.dram("w_qk", [4, 128, NCH * 128], F32, "ExternalInput")
    w_v = P.dram("w_v", [128, NCH * 256], F32, "ExternalInput")
    cos_d = P.dram("cosT", [128, n_lat], F32, "ExternalInput")
    sin_d = P.dram("sinT", [128, n_lat], F32, "ExternalInput")
    pm_d = P.dram("pm", [128, 128], F32, "ExternalInput")
    dl_d = P.dram("dlam", [1, 256], F32, "ExternalInput")
    sub_d = P.dram("subln", [128, 1], F32, "ExternalInput")
    yT = P.dram("yT", [256, N], BF16, "ExternalOutput")

    C = Ctx(P, ident_d)
    H = HStream(P, hT, N)
    wqk = P.sbuf("wqk", [128, 4, NCH, 128], BF16)
    wv = P.sbuf("wv", [128, NCH, 256], BF16)
    for g in range(4):
        P.dma("pool", wqk[:, g, :, :], w_qk[g, :, :], writes=["wqk%d" % g])
    P.dma("pool", wv[:, :, :], w_v[:, :], writes=["wv"])
    cosT = P.sbuf("cosT", [128, n_lat], F32)
    sinT = P.sbuf("sinT", [128, n_lat], F32)
    pm = P.sbuf("pm", [128, 128], F32)
    dl = P.sbuf("dl", [128, 256], F32)
    sub = P.sbuf("sub", [128, 1], F32)
    P.dma("sp", cosT[:, :], cos_d[:, :], writes=["cosT"])
    P.dma("sp", sinT[:, :], sin_d[:, :], writes=["sinT"])
    P.dma("sp", pm[:, :], pm_d[:, :], writes=["pm"])
    P.dma("sp", dl[:, :], dl_d[0:1, :].to_broadcast([128, 256]), writes=["dl"])
    P.dma("sp", sub[:, :], sub_d[:, :], writes=["sub"])
    lt = P.sbuf("lt", [128, 8], F32)
    ltmp = P.sbuf("ltmp", [128, 128], F32)
    for i in range(2):
        P.op("dve", lambda e, i=i: e.tensor_tensor(out=ltmp[:, i * 64:(i + 1) * 64], in0=dl[:, 128 * i:128 * i + 64], in1=dl[:, 128 * i + 64:128 * i + 128], op=ALU.mult),
             reads=["dl"], writes=["ltmp"])
        P.op("dve", lambda e, i=i: e.reduce_sum(out=lt[:, i:i + 1], in_=ltmp[:, i * 64:(i + 1) * 64], axis=AX.X), reads=["ltmp"], writes=["lt"])
    P.op("act", lambda e: e.activation(out=lt[:, 2:4], in_=lt[:, 0:2], func=AF.Exp), reads=["lt"], writes=["lt"])
    P.op("dve", lambda e: e.tensor_tensor(out=lt[:, 4:5], in0=lt[:, 3:4], in1=lt[:, 2:3], op=ALU.subtract), reads=["lt"], writes=["lt"])
    P.op("dve", lambda e: e.tensor_scalar(out=lt[:, 5:6], in0=lt[:, 4:5], scalar1=-float(lam_init), scalar2=None, op0=ALU.add), reads=["lt"], writes=["lt"])
    neglam = lt[:, 5:6]
    P.op("dve", lambda e: e.tensor_scalar(out=lt[:, 6:7], in0=sub[:, 0:1], scalar1=1.0 - float(lam_init), scalar2=None, op0=ALU.mult),
         reads=["sub"], writes=["lt"])
    subs = lt[:, 6:7]

    qkT = P.sbuf("qkT", [128, 4, N], BF16)
    vd = P.sbuf("vd", [128, NT, 256], BF16)
    yst = P.sbuf("yst", [128, 2, N], BF16)
    xf = P.sbuf("xf", [128, 512], F32)
    t1 = P.sbuf("t1", [128, 512], F32)
    t2 = P.sbuf("t2", [128, 512], F32)
    onesb = P.sbuf("onesb", [128, 1], BF16)
    P.op("pool", lambda e: e.memset(onesb[:, :], 1.0), writes=["onesb"])

    for (c0, n) in token_blocks(N):
        s, keys = H.load(c0, n)
        for g in range(4):
            b = C.bank((0, 4), 1)
            pap, pk = emit_proj_fm(P, C, H, s, keys, wqk[:, g, :, :], "wqk%d" % g, 128, n, b)
            sc_ = 0.125 if g < 2 else 1.0
            P.op("act", lambda e, pap=pap, sc_=sc_: e.activation(out=xf[:, 0:n], in_=pap, func=AF.Copy, scale=sc_), reads=[pk], writes=["xf"])
            nc_ = max(0, min(n, n_ctx - c0))
            if nc_ > 0:
                P.op("dve", lambda e, g=g, nc_=nc_: e.tensor_copy(out=qkT[:, g, c0:c0 + nc_], in_=xf[:, 0:nc_]), reads=["xf"], writes=["qkT%d" % g])
            if nc_ < n:
                l0 = c0 + nc_ - n_ctx
                nl = n - nc_
                b2 = C.bank((4, 4), 1)
                pk2 = "ps%d" % b2
                P.op("pe", lambda e, b2=b2, nc_=nc_, nl=nl: e.matmul(C.ps[b2][:, 0:nl], lhsT=pm[:, :], rhs=xf[:, nc_:nc_ + nl], start=True, stop=True),
                     reads=["pm", "xf"], writes=[pk2])
                P.op("dve", lambda e, nc_=nc_, nl=nl, l0=l0: e.tensor_tensor(out=t1[:, 0:nl], in0=xf[:, nc_:nc_ + nl], in1=cosT[:, l0:l0 + nl], op=ALU.mult),
                     reads=["xf", "cosT"], writes=["t1"])
                P.op("dve", lambda e, b2=b2, nl=nl, l0=l0: e.tensor_tensor(out=t2[:, 0:nl], in0=C.ps[b2][:, 0:nl], in1=sinT[:, l0:l0 + nl], op=ALU.mult),
                     reads=[pk2, "sinT"], writes=["t2"])
                P.op("pool", lambda e, g=g, nc_=nc_, nl=nl: e.tensor_tensor(out=qkT[:, g, c0 + nc_:c0 + n], in0=t1[:, 0:nl], in1=t2[:, 0:nl], op=ALU.add),
                     reads=["t1", "t2"], writes=["qkT%d" % g])
        for t in range(n // 128):
            b = C.bank((4, 4), 1)
            pap, pk = emit_proj_tm(P, C, H, s, keys, wv, "wv", t * 128, 128, 256, b)
            ti = c0 // 128 + t
            P.op("dve", lambda e, pap=pap, ti=ti: e.tensor_copy(out=vd[:, ti, :], in_=pap), reads=[pk], writes=["vd%d" % ti])

    Eb = P.sbuf("Eb", [128, 8, 512], BF16)
    Eacc = P.sbuf("Eacc", [128, 4, 512], F32)
    ones32 = P.sbuf("ones32", [128, 1], F32)
    P.op("pool", lambda e: e.memset(ones32[:, :], 1.0), writes=["ones32"])
    nsb = P.sbuf("nsb", [128, 2, 512], F32)
    drow = P.sbuf("drow", [1, 2, 512], F32)
    rc = P.sbuf("rc", [128, 4, 4], F32)
    hd = P.sbuf("hd", [128, 2, 128], F32)
    qblocks = []
    if need_ctx and n_ctx > 0:
        qblocks.append((0, n_ctx, 0, n_ctx // 128))
    for (c0, n) in token_blocks(n_lat):
        qblocks.append((n_ctx + c0, n, 0, NT))
    LOOK = 2
    SB = (0, 5)
    ACC = 6
    DEN0 = 5
    state = dict(ei=0, ri=0)
    pending = []

    def flush(keep):
        while len(pending) > keep:
            pending.pop(0)()

    def emit_scores(hh, q0, nq, ki, m):
        bs = C.bank(SB, 1)
        pks = "ps%d" % bs
        P.op("pe", lambda e: e.matmul(
            C.ps[bs][:, 0:nq], lhsT=qkT[64 * m:64 * m + 64, 2 + hh, ki * 128:(ki + 1) * 128],
            rhs=qkT[64 * m:64 * m + 64, hh, q0:q0 + nq], start=True, stop=True),
            reads=["qkT%d" % (2 + hh), "qkT%d" % hh], writes=[pks])
        ej = state["ei"] % 8
        state["ei"] += 1
        return bs, pks, ej

    def emit_exp(bs, pks, ej, nq):
        P.op("act", lambda e: e.activation(out=Eb[:, ej, 0:nq], in_=C.ps[bs][:, 0:nq], func=AF.Exp),
             reads=[pks], writes=["Eb%d" % ej])

    def emit_pv(hh, nq, ki, m, ej, first, lastk, kt0):
        P.op("pe", lambda e: e.matmul(
            C.ps[ACC + m][:, 0:nq], lhsT=vd[:, ki, hh * 128:(hh + 1) * 128], rhs=Eb[:, ej, 0:nq], start=first, stop=lastk),
            reads=["vd%d" % ki, "Eb%d" % ej], writes=["ps%d" % (ACC + m)])
        if m == 0:
            P.op("pe", lambda e: e.matmul(C.ps[DEN0][0:1, 0:nq], lhsT=onesb[:, 0:1], rhs=Eb[:, ej, 0:nq], start=first, stop=lastk),
                 reads=["onesb", "Eb%d" % ej], writes=["ps%d" % DEN0])
        else:
            a = ki % 2
            eng = "dve" if a == 0 else "pool"
            if ki - kt0 < 2:
                P.op(eng, lambda e: e.tensor_copy(out=Eacc[:, a, 0:nq], in_=Eb[:, ej, 0:nq]), reads=["Eb%d" % ej], writes=["Eacc%d" % a])
            else:
                P.op(eng, lambda e: e.tensor_tensor(out=Eacc[:, a, 0:nq], in0=Eacc[:, a, 0:nq], in1=Eb[:, ej, 0:nq], op=ALU.add),
                     reads=["Eb%d" % ej, "Eacc%d" % a], writes=["Eacc%d" % a])

    def emit_finish(hh, q0, nq, na):
        for m in range(2):
            P.op("act", lambda e, m=m: e.activation(out=nsb[:, m, 0:nq], in_=C.ps[ACC + m][:, 0:nq], func=AF.Copy),
                 reads=["ps%d" % (ACC + m)], writes=["nsb%d" % m])
        P.op("dve", lambda e: e.tensor_copy(out=drow[:, 0, 0:nq], in_=C.ps[DEN0][0:1, 0:nq]), reads=["ps%d" % DEN0], writes=["drow0"])
        bd = C.bank(SB, 1)
        for a in range(na):
            P.op("pe", lambda e, a=a, bd=bd: e.matmul(C.ps[bd][0:1, 0:nq], lhsT=ones32[:, 0:1], rhs=Eacc[:, a, 0:nq], start=(a == 0), stop=(a == na - 1)),
                 reads=["ones32", "Eacc%d" % a], writes=["ps%d" % bd], inc=(a == na - 1))
        P.op("dve", lambda e, bd=bd: e.tensor_copy(out=drow[:, 1, 0:nq], in_=C.ps[bd][0:1, 0:nq]), reads=["ps%d" % bd], writes=["drow1"])
        b = C.bank(SB, 1)
        pk = "ps%d" % b
        for qs in range(nq // 128):
            qsl = slice(qs * 128, (qs + 1) * 128)
            for m in range(2):
                P.op("pe", lambda e, m=m, qsl=qsl: e.transpose(C.ps[b][:, m * 128:(m + 1) * 128], nsb[:, m, qsl], C.ident[:, :]),
                     reads=["nsb%d" % m, "ident"], writes=[pk], inc=False)
            for m in range(2):
                P.op("pe", lambda e, m=m, qsl=qsl: e.transpose(C.ps[b][:, 256 + m:257 + m], drow[:, m, qsl], C.ident[0:1, 0:1]),
                     reads=["drow%d" % m, "ident"], writes=[pk], inc=(m == 1))
            rj = state["ri"] % 4
            state["ri"] += 1
            rk = "rc%d" % rj
            P.op("dve", lambda e, rj=rj: e.reciprocal(out=rc[:, rj, 0:2], in_=C.ps[b][:, 256:258]), reads=[pk], writes=[rk])
            P.op("dve", lambda e, rj=rj: e.tensor_scalar(out=rc[:, rj, 2:3], in0=rc[:, rj, 1:2], scalar1=neglam, scalar2=None, op0=ALU.mult),
                 reads=[rk, "lt"], writes=[rk])
            hj = rj % 2
            hk_ = "hd%d" % hj
            P.op("act", lambda e, rj=rj, hj=hj: e.activation(out=hd[:, hj, :], in_=C.ps[b][:, 0:128], func=AF.Copy, scale=rc[:, rj, 0:1]),
                 reads=[pk, rk], writes=[hk_])
            P.op("dve", lambda e, rj=rj, hj=hj: e.scalar_tensor_tensor(out=hd[:, hj, :], in0=C.ps[b][:, 128:256], scalar=rc[:, rj, 2:3],
                                                                       in1=hd[:, hj, :], op0=ALU.mult, op1=ALU.add),
                 reads=[pk, rk, hk_], writes=[hk_])
            c = C.statcol(2)
            P.op("act", lambda e, hj=hj, c=c: e.activation(out=C.junk[:, 0:128], in_=hd[:, hj, :], func=AF.Square, accum_out=C.stat[:, c:c + 1]),
                 reads=[hk_], writes=["junk", "stat%d" % c])
            emit_rstd(P, C, C.stat[:, c:c + 1], "stat%d" % c, 128, C.stat[:, c + 1:c + 2], "stat%d" % (c + 1), 128)
            P.op("act", lambda e, hj=hj, c=c: e.activation(out=hd[:, hj, :], in_=hd[:, hj, :], func=AF.Copy, scale=C.stat[:, c + 1:c + 2]),
                 reads=[hk_, "stat%d" % (c + 1)], writes=[hk_])
            P.op("pe", lambda e, hj=hj: e.transpose(C.ps[b][:, 0:128], hd[:, hj, :], C.ident[:, :]), reads=[hk_, "ident"], writes=[pk])
            P.op("dve", lambda e, qs=qs: e.tensor_scalar(out=yst[:, hh, q0 + qs * 128:q0 + (qs + 1) * 128], in0=C.ps[b][:, 0:128],
                                                         scalar1=subs, scalar2=None, op0=ALU.mult),
                 reads=[pk, "lt"], writes=["yst%d" % hh])

    for hh in range(2):
        for (q0, nq, kt0, kt1) in qblocks:
            for ki in range(kt0, kt1):
                sc = [emit_scores(hh, q0, nq, ki, m) for m in range(2)]
                for m in range(2):
                    emit_exp(sc[m][0], sc[m][1], sc[m][2], nq)

                def unit(hh=hh, nq=nq, ki=ki, sc=sc, kt0=kt0, kt1=kt1):
                    for m in range(2):
                        emit_pv(hh, nq, ki, m, sc[m][2], ki == kt0, ki == kt1 - 1, kt0)
                pending.append(unit)
                flush(LOOK)
            pending.append(lambda hh=hh, q0=q0, nq=nq, na=min(2, kt1 - kt0): emit_finish(hh, q0, nq, na))
    flush(0)
    for hh in range(2):
        if not (need_ctx and n_ctx > 0) and n_ctx > 0:
            P.op("pool", lambda e, hh=hh: e.memset(yst[:, hh, 0:n_ctx], 0.0), writes=["yst%d" % hh])
        P.dma("sp", yT[hh * 128:(hh + 1) * 128, :], yst[:, hh, :], reads=["yst%d" % hh])
    return P.finish()


MODC = 6 * D // NCORES


def build_mod():
    P = Prog()
    cT_d = P.dram("cT", [128, NCH * 3], F32, "ExternalInput")
    wm = P.dram("wm", [DEPTH, 128, NCH * MODC], F32, "ExternalInput")
    bm = P.dram("bm", [DEPTH, MODC], F32, "ExternalInput")
    out = P.dram("mod", [DEPTH, 3, MODC], F32, "ExternalOutput")
    cT = P.sbuf("cT", [128, NCH, 3], F32)
    P.dma("sp", cT[:, :, :], cT_d[:, :], writes=["cT"])
    P.op("act", lambda e: e.activation(out=cT[:, :, :], in_=cT[:, :, :], func=AF.Silu), reads=["cT"], writes=["cT"])
    ps = [P.psum("ps%d" % i, [128, 512], F32) for i in range(2)]
    wt = P.sbuf("wt", [128, 2, NCH, 512], F32)
    bt = P.sbuf("bt", [3, 2, 512], F32)
    ot = P.sbuf("ot", [3, 2, 512], F32)
    i = 0
    for l in range(DEPTH):
        wl = wm[l, :, :].rearrange("p (k m) -> p k m", m=MODC)
        for nb in range(MODC // 512):
            s = i % 2
            i += 1
            for k in range(NCH):
                P.dma("sp", wt[:, s, k, :], wl[:, k, nb * 512:(nb + 1) * 512], writes=["wt%d_%d" % (s, k)])
            P.dma("sp", bt[:, s, :], bm[l:l + 1, nb * 512:(nb + 1) * 512].to_broadcast([3, 512]), writes=["bt%d" % s])
            for k in range(NCH):
                P.op("pe", lambda e, s=s, k=k: e.matmul(ps[s][0:3, :], lhsT=cT[:, k, :], rhs=wt[:, s, k, :], start=(k == 0), stop=(k == NCH - 1)),
                     reads=["cT", "wt%d_%d" % (s, k)], writes=["ps%d" % s], inc=(k == NCH - 1))
            P.op("dve", lambda e, s=s: e.tensor_tensor(out=ot[:, s, :], in0=ps[s][0:3, :], in1=bt[:, s, :], op=ALU.add),
                 reads=["ps%d" % s, "bt%d" % s], writes=["ot%d" % s])
            P.dma("sp", out[l, :, nb * 512:(nb + 1) * 512], ot[:, s, :], reads=["ot%d" % s])
    return P.finish()


OFF = dict(m_q=0, m_k=512, m_v=1024, m_o=1536, m_g=2048, g_q=2064, g_k=2320, g_v=2576, g_out=3088, g_lr=3600,
           d_q=3632, d_k=4656, d_v=5680)


def _relay(w):
    K_, M = w.shape[0] // 128, w.shape[1]
    return np.ascontiguousarray(w.reshape(K_, 128, M).transpose(1, 0, 2).reshape(128, K_ * M))


def _relay_chunks(w):
    K_, J = w.shape[0] // 128, w.shape[1] // 128
    return np.ascontiguousarray(w.reshape(K_, 128, J, 128).transpose(2, 1, 0, 3).reshape(J, 128, K_ * 128))


def _cols16(v):
    v = np.asarray(v, np.float32).reshape(-1, NCH, 128)
    return np.ascontiguousarray(v.transpose(2, 0, 1))


_PROGS = {}
_DEBUG = None


def _prog(key, fn):
    if key not in _PROGS:
        _PROGS[key] = fn()
    return _PROGS[key]


def _run(nc, in_maps):
    res = run_bass_kernel_spmd(nc, in_maps, core_ids=list(range(NCORES)))
    return res.results


def kernel(x, c, ctx, c_ctx, w_mod, b_mod, norm_mix_pre, norm_mix_post, norm_ffn_pre, norm_ffn_post, w_in,
           mlstm_conv_w, mlstm_conv_b, mlstm_gate_b, mlstm_norm, gla_gate_w2, gla_gate_b, gla_norm,
           diff_lambda, diff_subln, w_out, w_ffn_gate, w_ffn_up, w_ffn_down):
    f32 = np.float32
    x = np.asarray(x, f32)
    ctx = np.asarray(ctx, f32)
    B = x.shape[0]
    ident = np.eye(128, dtype=f32)
    QT = SEQ // 4
    QC = CTX // 4

    cvec = np.concatenate([np.asarray(c, f32), np.asarray(c_ctx, f32)[None]], 0)
    cT = np.ascontiguousarray(cvec.reshape(3, NCH, 128).transpose(2, 1, 0).reshape(128, NCH * 3))
    w_mod = np.asarray(w_mod, f32)
    b_mod = np.asarray(b_mod, f32)
    maps = []
    for core in range(NCORES):
        cs = slice(core * MODC, (core + 1) * MODC)
        maps.append(dict(cT=cT, wm=np.stack([_relay(w_mod[l][:, cs]) for l in range(DEPTH)]),
                         bm=np.ascontiguousarray(b_mod[:, cs])))
    res = _run(_prog("mod", build_mod), maps)
    mod = np.concatenate([r["mod"] for r in res], axis=2)
    mod = mod.reshape(DEPTH, 3, 6, D)
    if _DEBUG is not None:
        _DEBUG["mod"] = mod

    def dense_maps(layer, xs_core, yT_core, do_c):
        la = min(layer + 1, DEPTH - 1) if do_c else layer
        norms = np.stack([np.asarray(a, f32)[layer] for a in (norm_mix_pre, norm_mix_post, norm_ffn_pre, norm_ffn_post)])
        ncols_src = norms.copy()
        ncols_src[0] = np.asarray(norm_mix_pre, f32)[la]
        ncols = _cols16(ncols_src)
        shared = {}
        if do_c:
            shared = dict(wo_r=_relay_chunks(np.asarray(w_out, f32)[layer]), wg_r=_relay_chunks(np.asarray(w_ffn_gate, f32)[layer]),
                          wu_r=_relay_chunks(np.asarray(w_ffn_up, f32)[layer]), wd_r=_relay_chunks(np.asarray(w_ffn_down, f32)[layer]),
                          normrows=norms)
        maps = []
        for core in range(NCORES):
            b = core // 4
            mrows = np.concatenate([mod[layer, b], mod[layer, 2]], 0)
            mcols_src = mrows.copy()
            mcols_src[0:2] = mod[la, b, 0:2]
            mcols_src[6:8] = mod[la, 2, 0:2]
            m = dict(x=xs_core[core], ident=ident, modcols=_cols16(mcols_src), normcols=ncols)
            if do_c:
                m.update(shared)
                m["modrows"] = np.ascontiguousarray(mrows)
                m["yT"] = yT_core[core]
            maps.append(m)
        return maps

    def gather_hT(res_list, n_ctx_core):
        out = []
        for b in range(B):
            hall = np.zeros((D, NTOK), dtype=ml_dtypes.bfloat16)
            for qq in range(4):
                h = res_list[b * 4 + qq]["hT_out"]
                hall[:, qq * QC:(qq + 1) * QC] = h[:, 0:QC]
                hall[:, CTX + qq * QT:CTX + (qq + 1) * QT] = h[:, QC:QC + QT]
            out.append(hall)
        return out

    xs_core = []
    for core in range(NCORES):
        b, qq = core // 4, core % 4
        xs_core.append(np.ascontiguousarray(np.concatenate([ctx[b, qq * QC:(qq + 1) * QC], x[b, qq * QT:(qq + 1) * QT]], 0)))
    res = _run(_prog("A", lambda: build_dense(QC, QT, False, True)), dense_maps(0, xs_core, None, False))
    hT_all = gather_hT(res, QC)
    if _DEBUG is not None:
        _DEBUG["hT0"] = hT_all

    w_in = np.asarray(w_in, f32)
    masks = mlstm_masks()
    tri = mlstm_tri(CTX // 64, NTOK // 64)
    rmask = gla_rmask()
    cosT, sinT, pm = rope_consts(SEQ)
    x_out = None
    for layer in range(DEPTH):
        last = layer == DEPTH - 1
        w = w_in[layer]
        lam_init = 0.8 - 0.6 * math.exp(-0.3 * layer)
        cw_all = np.asarray(mlstm_conv_w, f32)[layer]
        cb_all = np.asarray(mlstm_conv_b, f32)[layer]
        gb_all = np.asarray(mlstm_gate_b, f32)[layer]
        m_maps, g_maps, a_maps = [], [], []
        for core in range(NCORES):
            b, q = core // 4, core % 4
            cols = lambda name, a, n: w[:, OFF[name] + a:OFF[name] + a + n]
            cw = np.zeros((128, 8), f32)
            cw[:, 0:3] = cw_all[:, 128 * q:128 * q + 128].T
            cw[:, 3:6] = cw_all[:, 512 + 128 * q:512 + 128 * q + 128].T
            cw[:, 6] = cb_all[128 * q:128 * q + 128]
            cw[:, 7] = cb_all[512 + 128 * q:512 + 128 * q + 128]
            gidx = [OFF["m_g"] + i for i in (q, 4 + q, 8 + q, 12 + q)]
            m_maps.append(dict(
                hT=hT_all[b], ident=ident,
                w_fm=np.stack([_relay(cols("m_q", 128 * q, 128)), _relay(cols("m_k", 128 * q, 128)), _relay(cols("m_o", 128 * q, 128))]),
                w_g=_relay(w[:, gidx]), w_v=_relay(cols("m_v", 128 * q, 128)), cw=cw,
                gb=np.ascontiguousarray(gb_all[[q, 4 + q, 8 + q, 12 + q]].reshape(4, 1)),
                mn=np.ascontiguousarray(np.asarray(mlstm_norm, f32)[layer][128 * q:128 * q + 128].reshape(128, 1)),
                masks=masks, tri=tri))
            gw2 = np.asarray(gla_gate_w2, f32)[layer]
            gbb = np.asarray(gla_gate_b, f32)[layer]
            g_maps.append(dict(
                hT=hT_all[b], ident=ident,
                w_qk=_relay(np.concatenate([cols("g_q", 64 * q, 64), cols("g_k", 64 * q, 64)], 1)),
                w_go=_relay(cols("g_out", 128 * q, 128)), w_lr=_relay(cols("g_lr", 0, 32)), w_v=_relay(cols("g_v", 128 * q, 128)),
                w2=np.ascontiguousarray(np.concatenate([gw2[0][:, 64 * q:64 * q + 64], gw2[1][:, 64 * q:64 * q + 64]], 1)),
                nb=np.ascontiguousarray((gbb[:, 64 * q:64 * q + 64] * f32(-1.0)).T) if False else np.ascontiguousarray(np.negative(gbb[:, 64 * q:64 * q + 64]).T),
                gn=np.ascontiguousarray(np.asarray(gla_norm, f32)[layer][128 * q:128 * q + 128].reshape(128, 1)),
                masks=masks, rmask=rmask))
            a_maps.append(dict(
                hT=hT_all[b], ident=ident,
                w_qk=np.stack([_relay(cols("d_q", 256 * q, 128)), _relay(cols("d_q", 256 * q + 128, 128)),
                               _relay(cols("d_k", 256 * q, 128)), _relay(cols("d_k", 256 * q + 128, 128))]),
                w_v=_relay(cols("d_v", 256 * q, 256)), cosT=cosT, sinT=sinT, pm=pm,
                dlam=np.ascontiguousarray(np.asarray(diff_lambda, f32)[layer].reshape(1, 256)),
                subln=np.ascontiguousarray(np.asarray(diff_subln, f32)[layer].reshape(128, 1))))
        res_m = _run(_prog("mlstm", lambda: build_mlstm(CTX, SEQ)), m_maps)
        res_g = _run(_prog("gla", lambda: build_gla(CTX, SEQ)), g_maps)
        res_a = _run(_prog("attn%d" % layer, lambda: build_attn(CTX, SEQ, lam_init, not last)), a_maps)
        ymix = []
        for b in range(B):
            ym = np.zeros((D, NTOK), dtype=ml_dtypes.bfloat16)
            for q in range(4):
                core = b * 4 + q
                ym[128 * q:128 * q + 128] = res_m[core]["yT"]
                ym[512 + 128 * q:512 + 128 * q + 128] = res_g[core]["yT"]
                ym[1024 + 256 * q:1024 + 256 * q + 256] = res_a[core]["yT"]
            ymix.append(ym)
        if _DEBUG is not None:
            _DEBUG["ymix%d" % layer] = ymix
            if _DEBUG.get("stop_after_mix") == layer:
                return None
        n_ctx_core = 0 if last else QC
        yT_core = []
        for core in range(NCORES):
            b, qq = core // 4, core % 4
            parts = []
            if n_ctx_core:
                parts.append(ymix[b][:, qq * QC:(qq + 1) * QC])
            parts.append(ymix[b][:, CTX + qq * QT:CTX + (qq + 1) * QT])
            yT_core.append(np.ascontiguousarray(np.concatenate(parts, 1)))
        if last:
            xs_core = [np.ascontiguousarray(xc[xc.shape[0] - QT:]) for xc in xs_core]
        key = "C%d_%d" % (n_ctx_core, int(not last))
        res = _run(_prog(key, lambda: build_dense(n_ctx_core, QT, True, not last)), dense_maps(layer, xs_core, yT_core, True))
        xs_core = [r["x_out"] for r in res]
        if not last:
            hT_all = gather_hT(res, QC)
        if _DEBUG is not None:
            _DEBUG["xs%d" % layer] = xs_core
            _DEBUG["hT%d" % (layer + 1)] = hT_all
    out = np.zeros((B, SEQ, D), f32)
    for core in range(NCORES):
        b, qq = core // 4, core % 4
        out[b, qq * QT:(qq + 1) * QT] = xs_core[core][-QT:]
    return out
```

```python
import contextlib
import math

import ml_dtypes
import numpy as np

import concourse.bass as bass
import concourse.mybir as mybir
from concourse.bass_utils import run_bass_kernel_spmd

F32 = mybir.dt.float32
BF16 = mybir.dt.bfloat16
ALU = mybir.AluOpType
AF = mybir.ActivationFunctionType
AX = mybir.AxisListType

D = 2048
NCH = 16
FFN = 5632
NJ = 44
DEPTH = 2
CTX = 256
SEQ = 4096
NTOK = CTX + SEQ
EPS = 1e-6
IN_COLS = 6704
NCORES = 8


class Prog:
    K_RING = 12

    def __init__(self):
        self.nc = bass.Bass("TRN2", target_bir_lowering=False)
        self.es = contextlib.ExitStack()
        nc = self.nc
        self.eng = {}
        for name, h in (("pe", nc.tensor), ("act", nc.scalar), ("dve", nc.vector),
                        ("pool", nc.gpsimd), ("sp", nc.sync)):
            sem = self.es.enter_context(nc.semaphore("s_" + name))
            self.eng[name] = dict(h=h, sem=sem, sn="s_" + name, cnt=0, waited={})
        self.ring = {}
        self.rpos = {}
        for q in ("sp", "pool", "act"):
            self.ring[q] = []
            for i in range(self.K_RING):
                sem = self.es.enter_context(nc.semaphore("d_%s%d" % (q, i)))
                self.ring[q].append(dict(sem=sem, sn="d_%s%d" % (q, i), val=0))
            self.rpos[q] = 0
        self.lastw = {}
        self.readers = {}
        self.n_ops = 0
        self.psum_banks = []

    def sbuf(self, name, shape, dtype):
        return self.es.enter_context(self.nc.sbuf_tensor("sb_" + name, list(shape), dtype))

    def psum(self, name, shape, dtype):
        return self.es.enter_context(self.nc.psum_tensor("pp_" + name, list(shape), dtype))

    def dram(self, name, shape, dtype, kind):
        return self.nc.dram_tensor(name, list(shape), dtype, kind=kind).ap()

    def _collect(self, engname, reads, writes):
        own = self.eng[engname]["sn"]
        need = {}

        def add(ev, is_war):
            sn, sh, v = ev
            if sn == own:
                if engname == "pe":
                    return
            if sn not in need or need[sn][1] < v:
                need[sn] = (sh, v)

        for k in reads:
            e = self.lastw.get(k)
            if e is not None:
                add(e, False)
            if k.startswith("ps"):
                for e in self.readers.get(k, ()):
                    add(e, True)
        for k in writes:
            e = self.lastw.get(k)
            if e is not None:
                add(e, False)
            for e in self.readers.get(k, ()):
                add(e, True)
        return need

    def _emit_waits(self, engname, need):
        E = self.eng[engname]
        for sn, (sh, v) in need.items():
            if E["waited"].get(sn, 0) >= v:
                continue
            E["h"].wait_ge(sh, v)
            E["waited"][sn] = v

    def _record(self, ev, reads, writes):
        for k in writes:
            self.lastw[k] = ev
            self.readers[k] = []
        for k in reads:
            self.readers.setdefault(k, []).append(ev)

    def op(self, engname, fn, reads=(), writes=(), inc=True):
        E = self.eng[engname]
        need = self._collect(engname, reads, writes)
        self._emit_waits(engname, need)
        ins = fn(E["h"])
        if inc:
            E["cnt"] += 1
            ins.then_inc(E["sem"], 1)
            ev = (E["sn"], E["sem"], E["cnt"])
        else:
            ev = (E["sn"], E["sem"], E["cnt"] + 1)
        self._record(ev, reads, writes)
        self.n_ops += 1
        return ins

    def dma(self, q, out, in_, reads=(), writes=(), **kw):
        E = self.eng[q]
        slot = self.ring[q][self.rpos[q]]
        self.rpos[q] = (self.rpos[q] + 1) % self.K_RING
        need = self._collect(q, reads, writes)
        if slot["val"] > 0:
            if slot["sn"] not in need or need[slot["sn"]][1] < slot["val"]:
                need[slot["sn"]] = (slot["sem"], slot["val"])
        self._emit_waits(q, need)
        slot["val"] += 16
        E["h"].dma_start(out=out, in_=in_, **kw).then_inc(slot["sem"], 16)
        ev = (slot["sn"], slot["sem"], slot["val"])
        self._record(ev, reads, writes)
        self.n_ops += 1

    def coll(self, kind, in_ap, out_ap, reads=(), writes=(), groups=None):
        q = "pool"
        E = self.eng[q]
        slot = self.ring[q][self.rpos[q]]
        self.rpos[q] = (self.rpos[q] + 1) % self.K_RING
        need = self._collect(q, reads, writes)
        if slot["val"] > 0:
            if slot["sn"] not in need or need[slot["sn"]][1] < slot["val"]:
                need[slot["sn"]] = (slot["sem"], slot["val"])
        self._emit_waits(q, need)
        slot["val"] += 16
        groups = groups or [[0, 1, 2, 3], [4, 5, 6, 7]]
        E["h"].collective_compute(kind, ALU.bypass, groups, ins=[in_ap], outs=[out_ap]).then_inc(slot["sem"], 16)
        ev = (slot["sn"], slot["sem"], slot["val"])
        self._record(ev, reads, writes)
        self.n_ops += 1

    def barrier(self):
        evs = {}
        for name in ("pe", "act", "dve", "pool"):
            X = self.eng[name]
            if X["cnt"] > 0:
                evs[X["sn"]] = (X["sem"], X["cnt"])
        for q in ("sp", "pool", "act"):
            for slot in self.ring[q]:
                if slot["val"] > 0:
                    evs[slot["sn"]] = (slot["sem"], slot["val"])
        for name in ("pe", "act", "dve", "pool", "sp"):
            own = self.eng[name]["sn"]
            self._emit_waits(name, {k: v for k, v in evs.items() if k != own})
        self.lastw = {}
        self.readers = {}

    def finish(self):
        E = self.eng["sp"]
        for q in ("sp", "pool", "act"):
            for slot in self.ring[q]:
                if slot["val"] > 0 and E["waited"].get(slot["sn"], 0) < slot["val"]:
                    E["h"].wait_ge(slot["sem"], slot["val"])
                    E["waited"][slot["sn"]] = slot["val"]
        for name in ("pe", "act", "dve", "pool"):
            X = self.eng[name]
            if X["cnt"] > 0 and E["waited"].get(X["sn"], 0) < X["cnt"]:
                E["h"].wait_ge(X["sem"], X["cnt"])
        self.es.close()
        return self.nc


class Ctx:
    def __init__(self, P, ident_dram):
        self.P = P
        self.ps = [P.psum("ps%d" % i, [128, 512], F32) for i in range(8)]
        self.ident = P.sbuf("ident", [128, 128], F32)
        P.dma("sp", self.ident[:, :], ident_dram[:, :], writes=["ident"])
        self.identb = P.sbuf("identb", [128, 128], BF16)
        P.op("dve", lambda e: e.tensor_copy(out=self.identb[:, :], in_=self.ident[:, :]),
             reads=["ident"], writes=["identb"])
        self.junk = P.sbuf("junk", [128, 2048], BF16)
        self.stat = P.sbuf("stat", [128, 64], F32)
        self.stat_i = 0
        self.rot = {}

    def bank(self, group, n):
        lo, cnt = group
        i = self.rot.get(group, 0)
        self.rot[group] = (i + 1) % cnt
        return lo + i

    def statcol(self, n=1):
        i = self.stat_i
        if i + n > 64:
            i = 0
        self.stat_i = i + n
        return i


def emit_rstd(P, C, ss_ap, ss_key, rows, out_ap, out_key, n_feat):
    P.op("dve", lambda e: e.tensor_scalar(out=out_ap, in0=ss_ap, scalar1=1.0 / n_feat, scalar2=EPS,
                                          op0=ALU.mult, op1=ALU.add),
         reads=[ss_key], writes=[out_key])
    P.op("act", lambda e: e.activation(out=out_ap, in_=out_ap, func=AF.Sqrt),
         reads=[out_key], writes=[out_key])
    P.op("dve", lambda e: e.reciprocal(out=out_ap, in_=out_ap),
         reads=[out_key], writes=[out_key])


def emit_linear_fm(P, C, w_src, K, nout, stage, rhs, blocks, epilogue, banks=(0, 4), wq="pool",
                   name="lin"):
    ns = len(stage)
    for j in range(nout):
        st, skey = stage[j % ns]
        P.dma(wq, st, w_src(j), writes=[skey])
        for bi, (c0, n) in enumerate(blocks):
            b = C.bank(banks, 1)
            pk = "ps%d" % b
            pap = C.ps[b][:, 0:n]
            for k in range(K):
                r_ap, r_keys = rhs(k, c0, n)
                P.op("pe", lambda e, pap=pap, k=k, r_ap=r_ap: e.matmul(
                    pap, lhsT=st[:, k * 128:(k + 1) * 128], rhs=r_ap, start=(k == 0), stop=(k == K - 1)),
                    reads=[skey] + list(r_keys), writes=[pk], inc=(k == K - 1))
            epilogue(j, bi, pap, pk, c0, n)


def emit_to_tokmajor(P, C, srcT, src_key_fn, t0, rows, banks=(4, 4)):
    outs = []
    for n in range(4):
        b = C.bank(banks, 1)
        pk = "ps%d" % b
        for i in range(4):
            f = n * 4 + i
            P.op("pe", lambda e, b=b, i=i, f=f: e.transpose(
                C.ps[b][0:rows, i * 128:(i + 1) * 128], srcT[:, f, t0:t0 + rows], C.ident[:, :]),
                reads=[src_key_fn(f), "ident"], writes=[pk], inc=(i == 3))
        outs.append((C.ps[b][0:rows, :], pk))
    return outs


def emit_postnorm_residual(P, C, outs, rows, x_ap, x_key, g_ap, g_key, tmp_ap, tmp_key):
    c = C.statcol(6)
    ss = C.stat[0:rows, c:c + 4]
    for n, (pap, pk) in enumerate(outs):
        P.op("act", lambda e, pap=pap, n=n: e.activation(
            out=C.junk[0:rows, 0:512], in_=pap, func=AF.Square, accum_out=C.stat[0:rows, c + n:c + n + 1]),
            reads=[pk], writes=["junk", "stat%d" % (c + n)])
    tot = C.stat[0:rows, c + 4:c + 5]
    P.op("dve", lambda e: e.reduce_sum(out=tot, in_=ss, axis=AX.X),
         reads=["stat%d" % (c + n) for n in range(4)], writes=["stat%d" % (c + 4)])
    rstd = C.stat[0:rows, c + 5:c + 6]
    emit_rstd(P, C, tot, "stat%d" % (c + 4), rows, rstd, "stat%d" % (c + 5), D)
    for n, (pap, pk) in enumerate(outs):
        sl = slice(n * 512, (n + 1) * 512)
        P.op("dve", lambda e, pap=pap, sl=sl: e.scalar_tensor_tensor(
            out=tmp_ap[0:rows, sl], in0=pap, scalar=rstd, in1=g_ap[0:rows, sl], op0=ALU.mult, op1=ALU.mult),
            reads=[pk, "stat%d" % (c + 5), g_key], writes=[tmp_key])
        P.op("pool", lambda e, sl=sl: e.tensor_tensor(
            out=x_ap[0:rows, sl], in0=x_ap[0:rows, sl], in1=tmp_ap[0:rows, sl], op=ALU.add),
            reads=[tmp_key, x_key], writes=[x_key])


def emit_prenorm_T(P, C, x_ap, x_key, rows, xs_ap, xs_key, a_col, sh_col, mod_key, dst_fn, banks=(4, 4)):
    c = C.statcol(2)
    ss = C.stat[0:rows, c:c + 1]
    P.op("act", lambda e: e.activation(out=C.junk[0:rows, :], in_=x_ap[0:rows, :], func=AF.Square, accum_out=ss),
         reads=[x_key], writes=["junk", "stat%d" % c])
    rstd = C.stat[0:rows, c + 1:c + 2]
    emit_rstd(P, C, ss, "stat%d" % c, rows, rstd, "stat%d" % (c + 1), D)
    P.op("act", lambda e: e.activation(out=xs_ap[0:rows, :], in_=x_ap[0:rows, :], func=AF.Copy, scale=rstd),
         reads=[x_key, "stat%d" % (c + 1)], writes=[xs_key])
    for n in range(4):
        b = C.bank(banks, 1)
        pk = "ps%d" % b
        for i in range(4):
            f = n * 4 + i
            P.op("pe", lambda e, b=b, i=i, f=f: e.transpose(
                C.ps[b][:, i * 128:i * 128 + rows], xs_ap[0:rows, f * 128:(f + 1) * 128], C.ident[0:rows, 0:rows]),
                reads=[xs_key, "ident"], writes=[pk], inc=(i == 3))
        for i in range(4):
            f = n * 4 + i
            d_ap, d_key = dst_fn(f)
            eng = "dve" if (n % 2 == 0) else "act"
            if eng == "dve":
                P.op("dve", lambda e, b=b, i=i, f=f, d_ap=d_ap: e.tensor_scalar(
                    out=d_ap, in0=C.ps[b][:, i * 128:i * 128 + rows], scalar1=a_col[:, f:f + 1],
                    scalar2=sh_col[:, f:f + 1], op0=ALU.mult, op1=ALU.add),
                    reads=[pk, mod_key], writes=[d_key])
            else:
                P.op("act", lambda e, b=b, i=i, f=f, d_ap=d_ap: e.activation(
                    out=d_ap, in_=C.ps[b][:, i * 128:i * 128 + rows], func=AF.Identity,
                    scale=a_col[:, f:f + 1], bias=sh_col[:, f:f + 1]),
                    reads=[pk, mod_key], writes=[d_key])


def tile_groups(n_ctx, n_lat):
    tiles = []
    if n_ctx:
        tiles.append((0, n_ctx, True))
    for i in range(n_lat // 128):
        tiles.append((n_ctx + i * 128, 128, False))
    groups = []
    cur = []
    cur_n = 0
    for t in tiles:
        if cur and cur_n + t[1] > 576:
            groups.append(cur)
            cur, cur_n = [], 0
        cur.append(t)
        cur_n += t[1]
    if cur:
        groups.append(cur)
    out = []
    for g in groups:
        g0 = g[0][0]
        gn = sum(t[1] for t in g)
        out.append((g0, gn, g))
    return out


def blocks_of(gn):
    bl = []
    c = 0
    while c < gn:
        n = min(512, gn - c)
        bl.append((c, n))
        c += n
    return bl


def build_dense(n_ctx, n_lat, do_c, do_a, a_ctx=True):
    P = Prog()
    nc = P.nc
    ntok = n_ctx + n_lat
    x_in = P.dram("x", [ntok, D], F32, "ExternalInput")
    ident_d = P.dram("ident", [128, 128], F32, "ExternalInput")
    modcols = P.dram("modcols", [128, 12, NCH], F32, "ExternalInput")
    normcols = P.dram("normcols", [128, 4, NCH], F32, "ExternalInput")
    if do_c:
        modrows = P.dram("modrows", [12, D], F32, "ExternalInput")
        normrows = P.dram("normrows", [4, D], F32, "ExternalInput")
        yT_in = P.dram("yT", [D, ntok], BF16, "ExternalInput")
        wo_r = P.dram("wo_r", [NCH, 128, NCH * 128], F32, "ExternalInput")
        wg_r = P.dram("wg_r", [NJ, 128, NCH * 128], F32, "ExternalInput")
        wu_r = P.dram("wu_r", [NJ, 128, NCH * 128], F32, "ExternalInput")
        wd_r = P.dram("wd_r", [NCH, 128, NJ * 128], F32, "ExternalInput")
        x_out = P.dram("x_out", [ntok, D], F32, "ExternalOutput")
    if do_a:
        hT_out = P.dram("hT_out", [D, ntok], BF16, "ExternalOutput")

    C = Ctx(P, ident_d)
    GN = 576
    actT = P.sbuf("actT", [128, NCH, GN], BF16)
    x1 = P.sbuf("x1", [128, 5, D], F32)
    xs = P.sbuf("xs", [128, D], F32)
    mc = P.sbuf("mc", [128, 12, NCH], F32)
    ncol = P.sbuf("ncol", [128, 4, NCH], F32)
    acol = P.sbuf("acol", [128, 8, NCH], F32)
    P.dma("sp", mc[:, :, :], modcols[:, :, :], writes=["mc"])
    P.dma("sp", ncol[:, :, :], normcols[:, :, :], writes=["ncol"])
    for idx, (nrm, mrow) in enumerate(((0, 1), (0, 7), (2, 4), (2, 10))):
        P.op("dve", lambda e, idx=idx, nrm=nrm, mrow=mrow: e.scalar_tensor_tensor(
            out=acol[:, idx, :], in0=mc[:, mrow, :], scalar=1.0, in1=ncol[:, nrm, :], op0=ALU.add, op1=ALU.mult),
            reads=["mc", "ncol"], writes=["acol"])
    if do_c:
        oT = P.sbuf("oT", [128, NCH, GN], F32)
        aT = P.sbuf("aT", [128, NJ, GN], BF16)
        gbuf = P.sbuf("gbuf", [128, 2, D], F32)
        NS = 6
        wst = P.sbuf("wst", [128, NS, NCH * 128], BF16)
        stage = [(wst[:, i, :], "wst%d" % i) for i in range(NS)]
        sg = P.sbuf("sg", [128, 2, 512], F32)

    def load_g(which):
        nrow = 1 if which == 2 else 3
        P.dma("sp", xs[:, :], normrows[nrow:nrow + 1, :].to_broadcast([128, D]), writes=["xs"])
        for v, mrow in enumerate((which, 6 + which)):
            P.dma("sp", gbuf[:, v, :], modrows[mrow:mrow + 1, :].to_broadcast([128, D]), writes=["gbuf%d" % v])
            P.op("pool", lambda e, v=v: e.tensor_tensor(out=gbuf[:, v, :], in0=gbuf[:, v, :], in1=xs[:, :], op=ALU.mult),
                 reads=["xs", "gbuf%d" % v], writes=["gbuf%d" % v])

    groups = tile_groups(n_ctx, n_lat)
    for (g0, gn, tiles) in groups:
        blocks = blocks_of(gn)
        for ti, (t0, rows, is_ctx) in enumerate(tiles):
            P.dma("sp", x1[0:rows, ti, :], x_in[t0:t0 + rows, :], writes=["x1_%d" % ti])
        if do_c:
            for f in range(NCH):
                P.dma("sp", actT[:, f, 0:gn], yT_in[f * 128:(f + 1) * 128, g0:g0 + gn], writes=["actT%d" % f])

            def ep_copy(j, bi, pap, pk, c0, n):
                P.op("act", lambda e: e.activation(out=oT[:, j, c0:c0 + n], in_=pap, func=AF.Copy),
                     reads=[pk], writes=["oT%d" % j])

            emit_linear_fm(P, C, lambda j: wo_r[j, :, :], NCH, NCH, stage,
                           lambda k, c0, n: (actT[:, k, c0:c0 + n], ["actT%d" % k]), blocks, ep_copy)
            load_g(2)
            for ti, (t0, rows, is_ctx) in enumerate(tiles):
                outs = emit_to_tokmajor(P, C, oT, lambda f: "oT%d" % f, t0 - g0, rows)
                v = 1 if is_ctx else 0
                emit_postnorm_residual(P, C, outs, rows, x1[:, ti, :], "x1_%d" % ti, gbuf[:, v, :], "gbuf%d" % v,
                                       xs, "xs")
            for ti, (t0, rows, is_ctx) in enumerate(tiles):
                a_i, s_row = (3, 9) if is_ctx else (2, 3)
                emit_prenorm_T(P, C, x1[:, ti, :], "x1_%d" % ti, rows, xs, "xs", acol[:, a_i, :], mc[:, s_row, :],
                               "acol", lambda f, t0=t0, rows=rows: (actT[:, f, t0 - g0:t0 - g0 + rows], "actT%d" % f))
            def w_gu(jj):
                return (wg_r if jj % 2 == 0 else wu_r)[jj // 2, :, :]

            def ep_gu(jj, bi, pap, pk, c0, n):
                j = jj // 2
                if jj % 2 == 0:
                    P.op("act", lambda e: e.activation(out=sg[:, bi, 0:n], in_=pap, func=AF.Silu),
                         reads=[pk], writes=["sg%d" % bi])
                else:
                    P.op("dve", lambda e: e.tensor_tensor(out=aT[:, j, c0:c0 + n], in0=sg[:, bi, 0:n], in1=pap, op=ALU.mult),
                         reads=[pk, "sg%d" % bi], writes=["aT%d" % j])

            emit_linear_fm(P, C, w_gu, NCH, 2 * NJ, stage,
                           lambda k, c0, n: (actT[:, k, c0:c0 + n], ["actT%d" % k]), blocks, ep_gu)
            subs = [(0, 16), (16, 16), (32, 12)]
            ns = len(stage)
            cnt = 0
            for f in range(NCH):
                sts = []
                for (k0, kk) in subs:
                    st, skey = stage[cnt % ns]
                    cnt += 1
                    P.dma("pool", st[:, 0:kk * 128], wd_r[f, :, k0 * 128:(k0 + kk) * 128], writes=[skey])
                    sts.append((st, skey, k0, kk))
                for bi, (c0, n) in enumerate(blocks):
                    b = C.bank((0, 4), 1)
                    pk = "ps%d" % b
                    pap = C.ps[b][:, 0:n]
                    for (st, skey, k0, kk) in sts:
                        for k in range(kk):
                            kg = k0 + k
                            P.op("pe", lambda e, pap=pap, st=st, k=k, kg=kg: e.matmul(
                                pap, lhsT=st[:, k * 128:(k + 1) * 128], rhs=aT[:, kg, c0:c0 + n],
                                start=(kg == 0), stop=(kg == NJ - 1)),
                                reads=[skey, "aT%d" % kg], writes=[pk], inc=(kg == NJ - 1))
                    P.op("act", lambda e, pap=pap, f=f, c0=c0, n=n: e.activation(out=oT[:, f, c0:c0 + n], in_=pap, func=AF.Copy),
                         reads=[pk], writes=["oT%d" % f])
            load_g(5)
            for ti, (t0, rows, is_ctx) in enumerate(tiles):
                outs = emit_to_tokmajor(P, C, oT, lambda f: "oT%d" % f, t0 - g0, rows)
                v = 1 if is_ctx else 0
                emit_postnorm_residual(P, C, outs, rows, x1[:, ti, :], "x1_%d" % ti, gbuf[:, v, :], "gbuf%d" % v,
                                       xs, "xs")
                P.dma("sp", x_out[t0:t0 + rows, :], x1[0:rows, ti, :], reads=["x1_%d" % ti])
        if do_a:
            for ti, (t0, rows, is_ctx) in enumerate(tiles):
                a_i, s_row = (1, 6) if is_ctx else (0, 0)
                emit_prenorm_T(P, C, x1[:, ti, :], "x1_%d" % ti, rows, xs, "xs", acol[:, a_i, :], mc[:, s_row, :],
                               "acol", lambda f, t0=t0, rows=rows: (actT[:, f, t0 - g0:t0 - g0 + rows], "actT%d" % f))
            for f in range(NCH):
                P.dma("sp", hT_out[f * 128:(f + 1) * 128, g0:g0 + gn], actT[:, f, 0:gn], reads=["actT%d" % f])
    return P.finish()


def token_blocks(N):
    bl = []
    c = 0
    while c < N:
        n = min(512, N - c)
        bl.append((c, n))
        c += n
    return bl


def chunk_order(nctx_c, ncn, d):
    if d == 0:
        return list(range(ncn))
    return list(range(nctx_c - 1, -1, -1)) + list(range(ncn - 1, nctx_c - 1, -1))


class HStream:
    def __init__(self, P, hT, N, name="hblk"):
        self.P, self.hT, self.N = P, hT, N
        self.buf = P.sbuf(name, [128, 2, NCH, 512], BF16)
        self.i = 0

    def load(self, c0, n):
        s = self.i % 2
        self.i += 1
        keys = ["hb%d_%d" % (s, k) for k in range(NCH)]
        for k in range(NCH):
            self.P.dma("sp", self.buf[:, s, k, 0:n], self.hT[k * 128:(k + 1) * 128, c0:c0 + n], writes=[keys[k]])
        return s, keys


def emit_proj_fm(P, C, H, s, keys, w_ap, wkey, M, n, bank):
    pk = "ps%d" % bank
    for k in range(NCH):
        P.op("pe", lambda e, k=k: e.matmul(C.ps[bank][0:M, 0:n], lhsT=w_ap[:, k, 0:M], rhs=H.buf[:, s, k, 0:n],
                                           start=(k == 0), stop=(k == NCH - 1)),
             reads=[wkey, keys[k]], writes=[pk], inc=(k == NCH - 1))
    return C.ps[bank][0:M, 0:n], pk


def emit_proj_tm(P, C, H, s, keys, w_ap, wkey, t0, rows, ncols, bank):
    pk = "ps%d" % bank
    for k in range(NCH):
        P.op("pe", lambda e, k=k: e.matmul(C.ps[bank][0:rows, 0:ncols], lhsT=H.buf[:, s, k, t0:t0 + rows],
                                           rhs=w_ap[:, k, 0:ncols], start=(k == 0), stop=(k == NCH - 1)),
             reads=[wkey, keys[k]], writes=[pk], inc=(k == NCH - 1))
    return C.ps[bank][0:rows, 0:ncols], pk


def emit_v64(P, C, H, s, keys, wv, wkey, c0, n, v64, vT32):
    b = C.bank((0, 4), 1)
    pap, pk = emit_proj_fm(P, C, H, s, keys, wv, wkey, 128, n, b)
    P.op("act", lambda e: e.activation(out=vT32[:, 0:n], in_=pap, func=AF.Copy), reads=[pk], writes=["vT32"])
    nch = n // 64
    t = 0
    while t < nch:
        g = min(4, nch - t)
        b2 = C.bank((4, 4), 1)
        pk2 = "ps%d" % b2
        for i in range(g):
            P.op("pe", lambda e, i=i, t=t: e.transpose(C.ps[b2][0:64, i * 128:(i + 1) * 128], vT32[:, (t + i) * 64:(t + i + 1) * 64], C.ident[:, :]),
                 reads=["vT32", "ident"], writes=[pk2], inc=(i == g - 1))
        ci = c0 // 64 + t
        P.op("dve", lambda e, ci=ci, g=g: e.tensor_copy(out=v64[:, ci:ci + g, :], in_=C.ps[b2][0:64, 0:g * 128].rearrange("p (c e) -> p c e", e=128)),
             reads=[pk2], writes=["v64_%d" % (ci + i) for i in range(g)])
        t += g


def build_mlstm(n_ctx, n_lat, debug=False, dirs=(0, 1)):
    P = Prog()
    N = n_ctx + n_lat
    NCN = N // 64
    NCC = n_ctx // 64
    hT = P.dram("hT", [D, N], BF16, "ExternalInput")
    ident_d = P.dram("ident", [128, 128], F32, "ExternalInput")
    w_fm = P.dram("w_fm", [3, 128, NCH * 128], F32, "ExternalInput")
    w_g = P.dram("w_g", [128, NCH * 4], F32, "ExternalInput")
    w_v = P.dram("w_v", [128, NCH * 128], F32, "ExternalInput")
    cw_d = P.dram("cw", [128, 8], F32, "ExternalInput")
    gb_d = P.dram("gb", [4, 1], F32, "ExternalInput")
    mn_d = P.dram("mn", [128, 1], F32, "ExternalInput")
    masks_d = P.dram("masks", [64, 2, 64], F32, "ExternalInput")
    tri_d = P.dram("tri", [NCN, 2, NCN], F32, "ExternalInput")
    gscr = P.dram("gscr", [4, N], F32, "Internal")
    yT = P.dram("yT", [128, N], BF16, "ExternalOutput")

    C = Ctx(P, ident_d)
    H = HStream(P, hT, N)
    wfm = P.sbuf("wfm", [128, 3, NCH, 128], BF16)
    wg = P.sbuf("wg", [128, NCH, 4], BF16)
    wv = P.sbuf("wv", [128, NCH, 128], BF16)
    for g in range(3):
        P.dma("pool", wfm[:, g, :, :], w_fm[g, :, :], writes=["wfm%d" % g])
    P.dma("pool", wg[:, :, :], w_g[:, :], writes=["wg"])
    P.dma("pool", wv[:, :, :], w_v[:, :], writes=["wv"])
    cw = P.sbuf("cw", [128, 8], F32)
    gb = P.sbuf("gb", [4, 1], F32)
    mn = P.sbuf("mn", [128, 1], F32)
    masks = P.sbuf("masks", [64, 2, 64], F32)
    tri = P.sbuf("tri", [NCN, 2, NCN], F32)
    for t, src, key in ((cw, cw_d, "cw"), (gb, gb_d, "gb"), (mn, mn_d, "mn")):
        P.dma("sp", t[:, :], src[:, :], writes=[key])
    P.dma("sp", masks[:, :, :], masks_d[:, :, :], writes=["masks"])
    P.dma("sp", tri[:, :, :], tri_d[:, :, :], writes=["tri"])

    big = P.sbuf("big", [128, 2 * N], F32)
    acc = P.sbuf("acc", [128, N], F32)
    mqT = P.sbuf("mqT", [128, N], BF16)
    mkT = P.sbuf("mkT", [128, N], BF16)
    moT = P.sbuf("moT", [128, N], BF16)
    v64 = P.sbuf("v64", [64, NCN, 128], BF16)
    vT32 = P.sbuf("vT32", [128, 512], F32)
    ktok = P.sbuf("ktok", [64, NCN, 128], BF16)
    gsb = P.sbuf("gsb", [4, 512], F32)
    zeros = P.sbuf("zeros", [128, 512], F32)
    ones = P.sbuf("ones", [128, 128], F32)
    P.op("pool", lambda e: e.memset(zeros[:, :], 0.0), writes=["zeros"])
    P.op("pool", lambda e: e.memset(ones[:, :], 1.0), writes=["ones"])

    for (c0, n) in token_blocks(N):
        s, keys = H.load(c0, n)
        for g, (dst, dkey) in enumerate(((big[:, 0:N], "mqraw"), (big[:, N:2 * N], "mkraw"))):
            b = C.bank((0, 4), 1)
            pap, pk = emit_proj_fm(P, C, H, s, keys, wfm[:, g, :, :], "wfm%d" % g, 128, n, b)
            P.op("act", lambda e, pap=pap, dst=dst: e.activation(out=dst[:, c0:c0 + n], in_=pap, func=AF.Copy),
                 reads=[pk], writes=[dkey])
        b = C.bank((0, 4), 1)
        pap, pk = emit_proj_fm(P, C, H, s, keys, wfm[:, 2, :, :], "wfm2", 128, n, b)
        P.op("act", lambda e, pap=pap: e.activation(out=moT[:, c0:c0 + n], in_=pap, func=AF.Sigmoid),
             reads=[pk], writes=["moT"])
        b = C.bank((0, 4), 1)
        pap, pk = emit_proj_fm(P, C, H, s, keys, wg, "wg", 4, n, b)
        P.op("dve", lambda e, pap=pap: e.tensor_scalar(out=gsb[:, 0:n], in0=pap, scalar1=gb[:, 0:1], scalar2=None, op0=ALU.add),
             reads=[pk, "gb"], writes=["gsb"])
        P.dma("sp", gscr[:, c0:c0 + n], gsb[:, 0:n], reads=["gsb"], writes=["gscr%d" % c0])
        emit_v64(P, C, H, s, keys, wv, "wv", c0, n, v64, vT32)

    segs = [(a, b_) for (a, b_) in ((0, n_ctx), (n_ctx, N)) if b_ > a]
    for qi, (raw, rkey, dstT, dkey) in enumerate(((big[:, 0:N], "mqraw", mqT, "mqT"), (big[:, N:2 * N], "mkraw", mkT, "mkT"))):
        w0, w1, w2, bcol = cw[:, 3 * qi:3 * qi + 1], cw[:, 3 * qi + 1:3 * qi + 2], cw[:, 3 * qi + 2:3 * qi + 3], cw[:, 6 + qi:7 + qi]
        P.op("dve", lambda e, raw=raw, w1=w1, bcol=bcol: e.tensor_scalar(out=acc[:, :], in0=raw, scalar1=w1, scalar2=bcol, op0=ALU.mult, op1=ALU.add),
             reads=[rkey, "cw"], writes=["acc"])
        for (a, b_) in segs:
            P.op("dve", lambda e, raw=raw, w0=w0, a=a, b_=b_: e.scalar_tensor_tensor(
                out=acc[:, a + 1:b_], in0=raw[:, a:b_ - 1], scalar=w0, in1=acc[:, a + 1:b_], op0=ALU.mult, op1=ALU.add),
                reads=[rkey, "cw", "acc"], writes=["acc"])
            P.op("dve", lambda e, raw=raw, w2=w2, a=a, b_=b_: e.scalar_tensor_tensor(
                out=acc[:, a:b_ - 1], in0=raw[:, a + 1:b_], scalar=w2, in1=acc[:, a:b_ - 1], op0=ALU.mult, op1=ALU.add),
                reads=[rkey, "cw", "acc"], writes=["acc"])
        P.op("act", lambda e: e.activation(out=acc[:, :], in_=acc[:, :], func=AF.Silu), reads=["acc"], writes=["acc"])
        if qi == 0:
            P.op("dve", lambda e: e.tensor_scalar(out=mqT[:, :], in0=acc[:, :], scalar1=128.0 ** -0.5, scalar2=None, op0=ALU.mult),
                 reads=["acc"], writes=["mqT"])
        else:
            P.op("dve", lambda e: e.tensor_copy(out=mkT[:, :], in_=acc[:, :]), reads=["acc"], writes=["mkT"])
            for c in range(NCN):
                b = C.bank((4, 4), 1)
                pk = "ps%d" % b
                P.op("pe", lambda e, b=b, c=c: e.transpose(C.ps[b][0:64, 0:128], acc[:, c * 64:(c + 1) * 64], C.ident[:, :]),
                     reads=["acc", "ident"], writes=[pk])
                P.op("act", lambda e, b=b, c=c: e.activation(out=ktok[:, c, :], in_=C.ps[b][0:64, 0:128], func=AF.Copy),
                     reads=[pk], writes=["ktok%d" % c])

    st = P.sbuf("st", [NCN, 40, 64], F32)
    sc = P.sbuf("sc", [NCN, 32], F32)
    rowt = P.sbuf("rowt", [1, 4, NCN], F32)
    colW = P.sbuf("colW", [64, 2, 3, NCN], F32)
    bca = P.sbuf("bca", [128, 2, NCN], F32)
    diag = P.sbuf("diag", [NCN, NCN], F32)
    skey = lambda i: "st%d" % i
    ckey = lambda i: "sc%d" % i

    def rv(ap, d):
        return ap[:, ::-1] if d == 1 else ap

    gkeys = ["gscr%d" % c0 for (c0, n) in token_blocks(N)]
    for d in range(2):
        base = d * 20
        I_, F_, L_, Pl, Pt, A_, Al, Gc, W_, R_, E_, T1 = [st[:, base + i, :] for i in range(12)]
        kI, kF, kL, kPl, kPt, kA, kAl, kGc, kW, kR, kE, kT1 = [skey(base + i) for i in range(12)]
        cb = d * 16
        cPc, cMx, cGk, cGkp, cNGkp, cAl = [sc[:, cb + i:cb + i + 1] for i in range(6)]
        kcPc, kcMx, kcGk, kcGkp, kcNGkp, kcAl = [ckey(cb + i) for i in range(6)]
        P.dma("sp", I_, gscr[2 * d:2 * d + 1, :].rearrange("o (c l) -> (o c) l", l=64), reads=gkeys, writes=[kI])
        P.dma("sp", F_, gscr[2 * d + 1:2 * d + 2, :].rearrange("o (c l) -> (o c) l", l=64), reads=gkeys, writes=[kF])
        P.op("act", lambda e: e.activation(out=T1, in_=F_, func=AF.Exp, scale=-1.0), reads=[kF], writes=[kT1])
        P.op("act", lambda e: e.activation(out=L_, in_=T1, func=AF.Ln, bias=1.0), reads=[kT1], writes=[kL])
        P.op("dve", lambda e: e.tensor_tensor_scan(out=rv(Pl, d), data0=rv(L_, d), data1=zeros[0:NCN, 0:64], initial=0.0,
                                                   op0=ALU.add, op1=ALU.add), reads=[kL, "zeros"], writes=[kPl])
        last = (lambda ap: ap[:, 0:1]) if d == 1 else (lambda ap: ap[:, 63:64])
        b = C.bank((0, 4), 1)
        pk = "ps%d" % b
        P.op("pe", lambda e, b=b: e.matmul(C.ps[b][0:NCN, 0:1], lhsT=tri[:, d, :], rhs=last(Pl), start=True, stop=True),
             reads=["tri", kPl], writes=[pk])
        P.op("act", lambda e, b=b: e.activation(out=cPc, in_=C.ps[b][0:NCN, 0:1], func=AF.Copy), reads=[pk], writes=[kcPc])
        P.op("dve", lambda e: e.tensor_scalar(out=Pt, in0=Pl, scalar1=cPc, scalar2=None, op0=ALU.add), reads=[kPl, kcPc], writes=[kPt])
        P.op("dve", lambda e: e.tensor_tensor(out=A_, in0=I_, in1=Pt, op=ALU.add), reads=[kI, kPt], writes=[kA])
        P.op("dve", lambda e: e.tensor_tensor_scan(out=rv(Al, d), data0=rv(A_, d), data1=rv(A_, d), initial=-1e30,
                                                   op0=ALU.max, op1=ALU.max), reads=[kA], writes=[kAl])
        b = C.bank((0, 4), 1)
        pk = "ps%d" % b
        P.op("pe", lambda e, b=b: e.transpose(C.ps[b][0:1, 0:NCN], last(Al), C.ident[0:NCN, 0:NCN]),
             reads=[kAl, "ident"], writes=[pk])
        mxr, gpr, gkr = rowt[:, 0, :], rowt[:, 1, :], rowt[:, 2, :]
        P.op("act", lambda e, b=b: e.activation(out=mxr, in_=C.ps[b][0:1, 0:NCN], func=AF.Copy), reads=[pk], writes=["rowt0"])
        if d == 0:
            P.op("dve", lambda e: e.tensor_tensor_scan(out=gpr, data0=mxr, data1=mxr, initial=0.0, op0=ALU.max, op1=ALU.max),
                 reads=["rowt0"], writes=["rowt1"])
            P.op("dve", lambda e: e.memset(gkr[:, 0:1], 0.0), writes=["rowt2"])
            P.op("dve", lambda e: e.tensor_copy(out=gkr[:, 1:NCN], in_=gpr[:, 0:NCN - 1]), reads=["rowt1"], writes=["rowt2"])
        else:
            if NCC > 0:
                P.op("dve", lambda e: e.tensor_tensor_scan(out=gpr[:, 0:NCC][:, ::-1], data0=mxr[:, 0:NCC][:, ::-1],
                                                           data1=mxr[:, 0:NCC][:, ::-1], initial=0.0, op0=ALU.max, op1=ALU.max),
                     reads=["rowt0"], writes=["rowt1"])
                P.op("dve", lambda e: e.tensor_tensor_scan(out=gpr[:, NCC:NCN][:, ::-1], data0=mxr[:, NCC:NCN][:, ::-1],
                                                           data1=mxr[:, NCC:NCN][:, ::-1], initial=gpr[:, 0:1], op0=ALU.max, op1=ALU.max),
                     reads=["rowt0", "rowt1"], writes=["rowt1"])
                P.op("dve", lambda e: e.memset(gkr[:, NCC - 1:NCC], 0.0), writes=["rowt2"])
                if NCC > 1:
                    P.op("dve", lambda e: e.tensor_copy(out=gkr[:, 0:NCC - 1], in_=gpr[:, 1:NCC]), reads=["rowt1"], writes=["rowt2"])
                P.op("dve", lambda e: e.tensor_copy(out=gkr[:, NCN - 1:NCN], in_=gpr[:, 0:1]), reads=["rowt1"], writes=["rowt2"])
            else:
                P.op("dve", lambda e: e.tensor_tensor_scan(out=gpr[:, ::-1], data0=mxr[:, ::-1], data1=mxr[:, ::-1], initial=0.0,
                                                           op0=ALU.max, op1=ALU.max), reads=["rowt0"], writes=["rowt1"])
                P.op("dve", lambda e: e.memset(gkr[:, NCN - 1:NCN], 0.0), writes=["rowt2"])
            P.op("dve", lambda e: e.tensor_copy(out=gkr[:, NCC:NCN - 1], in_=gpr[:, NCC + 1:NCN]), reads=["rowt1"], writes=["rowt2"])
        for (row, rk, col, ck) in ((gkr, "rowt2", cGk, kcGk), (gpr, "rowt1", cGkp, kcGkp)):
            b = C.bank((0, 4), 1)
            pk = "ps%d" % b
            P.op("pe", lambda e, b=b, row=row: e.transpose(C.ps[b][0:NCN, 0:1], row, C.ident[0:1, 0:1]),
                 reads=[rk, "ident"], writes=[pk])
            P.op("act", lambda e, b=b, col=col: e.activation(out=col, in_=C.ps[b][0:NCN, 0:1], func=AF.Copy), reads=[pk], writes=[ck])
        P.op("dve", lambda e: e.tensor_scalar(out=cNGkp, in0=cGkp, scalar1=-1.0, scalar2=None, op0=ALU.mult), reads=[kcGkp], writes=[kcNGkp])
        P.op("dve", lambda e: e.tensor_scalar(out=Gc, in0=Al, scalar1=cGk, scalar2=None, op0=ALU.max), reads=[kAl, kcGk], writes=[kGc])
        P.op("act", lambda e: e.activation(out=W_, in_=A_, func=AF.Exp, bias=cNGkp), reads=[kA, kcNGkp], writes=[kW])
        P.op("act", lambda e: e.activation(out=R_, in_=Gc, func=AF.Exp, scale=-1.0, bias=cGkp), reads=[kGc, kcGkp], writes=[kR])
        P.op("dve", lambda e: e.tensor_tensor(out=T1, in0=Pt, in1=Gc, op=ALU.subtract), reads=[kPt, kGc], writes=[kT1])
        P.op("act", lambda e: e.activation(out=E_, in_=T1, func=AF.Exp), reads=[kT1], writes=[kE])
        P.op("dve", lambda e: e.tensor_tensor(out=cAl, in0=cGk, in1=cGkp, op=ALU.subtract), reads=[kcGk, kcGkp], writes=[kcAl])
        P.op("act", lambda e: e.activation(out=cAl, in_=cAl, func=AF.Exp), reads=[kcAl], writes=[kcAl])
        for qi, (src, sk) in enumerate(((W_, kW), (R_, kR), (E_, kE))):
            b = C.bank((0, 4), 1)
            pk = "ps%d" % b
            P.op("pe", lambda e, b=b, src=src: e.transpose(C.ps[b][0:64, 0:NCN], src, C.ident[0:NCN, 0:NCN]),
                 reads=[sk, "ident"], writes=[pk])
            P.op("act", lambda e, b=b, qi=qi: e.activation(out=colW[:, d, qi, :], in_=C.ps[b][0:64, 0:NCN], func=AF.Copy),
                 reads=[pk], writes=["colW%d" % d])
        P.op("dve", lambda e: e.tensor_scalar(out=diag[:, :], in0=C.ident[0:NCN, 0:NCN], scalar1=cAl, scalar2=None, op0=ALU.mult),
             reads=["ident", kcAl], writes=["diag"])
        b = C.bank((0, 4), 1)
        pk = "ps%d" % b
        P.op("pe", lambda e, b=b: e.matmul(C.ps[b][:, 0:NCN], lhsT=ones[0:NCN, :], rhs=diag[:, :], start=True, stop=True),
             reads=["ones", "diag"], writes=[pk])
        P.op("act", lambda e, b=b: e.activation(out=bca[:, d, :], in_=C.ps[b][:, 0:NCN], func=AF.Copy), reads=[pk], writes=["bca%d" % d])

    hacc = big[0:64, :].rearrange("p (c e) -> p c e", e=128)
    Cst = P.sbuf("Cst", [128, 2, 132], F32)
    Cbf = P.sbuf("Cbf", [128, 2, 132], BF16)
    ctmp = P.sbuf("ctmp", [128, 2, 132], F32)
    vh = P.sbuf("vh", [64, 4, 132], BF16)
    PT = P.sbuf("PT", [64, 4, 64], BF16)
    maskb = P.sbuf("maskb", [64, 2, 64], F32)
    fcol = P.sbuf("fcol", [64, 8, 4], F32)
    hkey_guard = ["mqraw", "mkraw"]
    orders = [chunk_order(NCC, NCN, d) for d in range(2)]
    for d in range(2):
        P.op("pool", lambda e, d=d: e.memset(Cst[:, d, :], 0.0), writes=["Cst%d" % d])
        P.op("pool", lambda e, d=d: e.memset(Cbf[:, d, :], 0.0), writes=["Cbf%d" % d])
    it = 0
    hwritten = set()
    for step in range(NCN):
        for d in dirs:
            c = orders[d][step]
            cn = orders[d][step + 1] if step + 1 < NCN else None
            sl = slice(c * 64, (c + 1) * 64)
            j = it % 4
            it += 1
            wcol = colW[:, d, 0, c:c + 1]
            rcol = colW[:, d, 1, c:c + 1]
            ecol = colW[:, d, 2, c:c + 1]
            ck = "colW%d" % d
            P.op("act", lambda e, j=j, c=c, wcol=wcol: e.activation(out=vh[:, j, 0:128], in_=v64[:, c, :], func=AF.Copy, scale=wcol),
                 reads=["v64_%d" % c, ck], writes=["vh%d" % j])
            P.op("pool", lambda e, j=j, wcol=wcol: e.tensor_copy(out=vh[:, j, 128:129], in_=wcol), reads=[ck], writes=["vh%d" % j])
            b1 = C.bank((0, 3), 1)
            P.op("pe", lambda e, b1=b1, sl=sl: e.matmul(C.ps[b1][0:64, 0:64], lhsT=mkT[:, sl], rhs=mqT[:, sl], start=True, stop=True),
                 reads=["mkT", "mqT"], writes=["ps%d" % b1])
            P.op("dve", lambda e, b1=b1, j=j, d=d: e.tensor_tensor(out=PT[:, j, :], in0=C.ps[b1][0:64, 0:64], in1=masks[:, d, :], op=ALU.mult),
                 reads=["ps%d" % b1, "masks"], writes=["PT%d" % j])
            b2 = C.bank((3, 3), 1)
            pk2 = "ps%d" % b2
            P.op("pe", lambda e, b2=b2, sl=sl, d=d: e.matmul(C.ps[b2][0:64, 0:129], lhsT=mqT[:, sl], rhs=Cbf[:, d, 0:129], start=True, stop=False),
                 reads=["mqT", "Cbf%d" % d], writes=[pk2], inc=False)
            P.op("pe", lambda e, b2=b2, j=j: e.matmul(C.ps[b2][0:64, 0:129], lhsT=PT[:, j, :], rhs=vh[:, j, 0:129], start=False, stop=True),
                 reads=["PT%d" % j, "vh%d" % j], writes=[pk2])
            fj = it % 8
            f0, f1, f2 = fcol[:, fj, 0:1], fcol[:, fj, 1:2], fcol[:, fj, 2:3]
            fk = "fcol%d" % fj
            P.op("act", lambda e, b2=b2, f0=f0, rcol=rcol: e.activation(out=f0, in_=C.ps[b2][0:64, 128:129], func=AF.Abs, scale=rcol),
                 reads=[pk2, ck], writes=[fk])
            P.op("dve", lambda e, f0=f0, f1=f1, ecol=ecol: e.tensor_tensor(out=f1, in0=f0, in1=ecol, op=ALU.max), reads=[fk, ck], writes=[fk])
            P.op("dve", lambda e, f1=f1: e.reciprocal(out=f1, in_=f1), reads=[fk], writes=[fk])
            P.op("dve", lambda e, f1=f1, f2=f2, rcol=rcol: e.tensor_tensor(out=f2, in0=f1, in1=rcol, op=ALU.mult), reads=[fk, ck], writes=[fk])
            hk = "hacc%d" % c
            if c not in hwritten:
                hwritten.add(c)
                P.op("act", lambda e, b2=b2, c=c, f2=f2: e.activation(out=hacc[:, c, :], in_=C.ps[b2][0:64, 0:128], func=AF.Copy, scale=f2),
                     reads=[pk2, fk], writes=[hk] + hkey_guard)
            else:
                P.op("dve", lambda e, b2=b2, c=c, f2=f2: e.scalar_tensor_tensor(out=hacc[:, c, :], in0=C.ps[b2][0:64, 0:128], scalar=f2,
                                                                               in1=hacc[:, c, :], op0=ALU.mult, op1=ALU.add),
                     reads=[pk2, fk, hk], writes=[hk])
            if cn is not None:
                b3 = C.bank((6, 2), 1)
                pk3 = "ps%d" % b3
                P.op("pe", lambda e, b3=b3, c=c, j=j: e.matmul(C.ps[b3][:, 0:129], lhsT=ktok[:, c, :], rhs=vh[:, j, 0:129], start=True, stop=True),
                     reads=["ktok%d" % c, "vh%d" % j], writes=[pk3])
                acol = bca[:, d, cn:cn + 1]
                P.op("act", lambda e, b3=b3, d=d, acol=acol: e.activation(out=ctmp[:, d, 0:129], in_=C.ps[b3][:, 0:129], func=AF.Copy, scale=acol),
                     reads=[pk3, "bca%d" % d], writes=["ctmp%d" % d])
                P.op("dve", lambda e, d=d, acol=acol: e.scalar_tensor_tensor(out=Cst[:, d, 0:129], in0=Cst[:, d, 0:129], scalar=acol,
                                                                            in1=ctmp[:, d, 0:129], op0=ALU.mult, op1=ALU.add),
                     reads=["Cst%d" % d, "ctmp%d" % d, "bca%d" % d], writes=["Cst%d" % d])
                P.op("pool", lambda e, d=d: e.tensor_copy(out=Cbf[:, d, 0:129], in_=Cst[:, d, 0:129]), reads=["Cst%d" % d], writes=["Cbf%d" % d])

    if debug:
        d1 = P.dram("dbg_mq", [128, N], BF16, "ExternalOutput")
        d2 = P.dram("dbg_mk", [128, N], BF16, "ExternalOutput")
        d3 = P.dram("dbg_colW", [64, 2 * 3 * NCN], F32, "ExternalOutput")
        d4 = P.dram("dbg_bca", [128, 2 * NCN], F32, "ExternalOutput")
        d5 = P.dram("dbg_h", [64, NCN * 128], F32, "ExternalOutput")
        d6 = P.dram("dbg_v", [64, NCN * 128], BF16, "ExternalOutput")
        d7 = P.dram("dbg_kt", [64, NCN * 128], BF16, "ExternalOutput")
        P.dma("sp", d1[:, :], mqT[:, :], reads=["mqT"])
        P.dma("sp", d2[:, :], mkT[:, :], reads=["mkT"])
        P.dma("sp", d3[:, :], colW[:, :, :, :].rearrange("p a b c -> p (a b c)"), reads=["colW0", "colW1"])
        P.dma("sp", d4[:, :], bca[:, :, :].rearrange("p a c -> p (a c)"), reads=["bca0", "bca1"])
        P.dma("sp", d5[:, :], big[0:64, :], reads=["hacc%d" % c for c in range(NCN)])
        P.dma("sp", d6[:, :], v64[:, :, :].rearrange("p a c -> p (a c)"), reads=["v64_%d" % c for c in range(NCN)])
        P.dma("sp", d7[:, :], ktok[:, :, :].rearrange("p a c -> p (a c)"), reads=["ktok%d" % c for c in range(NCN)])
    ssq = P.sbuf("ssq", [64, NCN], F32)
    for c in range(NCN):
        P.op("act", lambda e, c=c: e.activation(out=C.junk[0:64, 0:128], in_=hacc[:, c, :], func=AF.Square, accum_out=ssq[:, c:c + 1]),
             reads=["hacc%d" % c], writes=["junk", "ssq"])
    emit_rstd(P, C, ssq[:, :], "ssq", 64, ssq[:, :], "ssq", 128)
    yst = acc
    for c in range(NCN):
        sl = slice(c * 64, (c + 1) * 64)
        P.op("act", lambda e, c=c: e.activation(out=hacc[:, c, :], in_=hacc[:, c, :], func=AF.Copy, scale=ssq[:, c:c + 1]),
             reads=["hacc%d" % c, "ssq"], writes=["hacc%d" % c])
        b = C.bank((0, 4), 1)
        pk = "ps%d" % b
        P.op("pe", lambda e, b=b, c=c: e.transpose(C.ps[b][:, 0:64], hacc[:, c, :], C.ident[0:64, 0:64]),
             reads=["hacc%d" % c, "ident"], writes=[pk])
        P.op("dve", lambda e, b=b, sl=sl: e.scalar_tensor_tensor(out=mkT[:, sl], in0=C.ps[b][:, 0:64], scalar=mn[:, 0:1], in1=moT[:, sl],
                                                               op0=ALU.mult, op1=ALU.mult),
             reads=[pk, "mn", "moT"], writes=["mkT"])
    P.dma("sp", yT[:, :], mkT[:, :], reads=["mkT"])
    return P.finish()


def mlstm_masks():
    s = np.arange(64)[:, None]
    t = np.arange(64)[None, :]
    m = np.zeros((64, 2, 64), np.float32)
    m[:, 0, :] = (t >= s)
    m[:, 1, :] = (t <= s)
    return m


def mlstm_tri(ncc, ncn):
    tri = np.zeros((ncn, 2, ncn), np.float32)
    for cp in range(ncn):
        for c in range(ncn):
            tri[cp, 0, c] = 1.0 if cp < c else 0.0
            cp_ctx, c_ctx = cp < ncc, c < ncc
            if cp_ctx == c_ctx:
                before = cp > c
            else:
                before = cp_ctx and not c_ctx
            tri[cp, 1, c] = 1.0 if before else 0.0
    return tri


def emit_head_finish(P, C, hacc, NCN, nw_col, nw_key, gateT, gate_key, outT, out_key, name, extra_w=()):
    ssq = P.sbuf("ssq_" + name, [64, NCN], F32)
    for c in range(NCN):
        P.op("act", lambda e, c=c: e.activation(out=C.junk[0:64, 0:128], in_=hacc[:, c, :], func=AF.Square, accum_out=ssq[:, c:c + 1]),
             reads=["hacc%d" % c], writes=["junk", "ssq"])
    emit_rstd(P, C, ssq[:, :], "ssq", 64, ssq[:, :], "ssq", 128)
    for c in range(NCN):
        sl = slice(c * 64, (c + 1) * 64)
        P.op("act", lambda e, c=c: e.activation(out=hacc[:, c, :], in_=hacc[:, c, :], func=AF.Copy, scale=ssq[:, c:c + 1]),
             reads=["hacc%d" % c, "ssq"], writes=["hacc%d" % c])
        b = C.bank((0, 4), 1)
        pk = "ps%d" % b
        P.op("pe", lambda e, b=b, c=c: e.transpose(C.ps[b][:, 0:64], hacc[:, c, :], C.ident[0:64, 0:64]),
             reads=["hacc%d" % c, "ident"], writes=[pk])
        P.op("dve", lambda e, b=b, sl=sl: e.scalar_tensor_tensor(out=outT[:, sl], in0=C.ps[b][:, 0:64], scalar=nw_col, in1=gateT[:, sl],
                                                               op0=ALU.mult, op1=ALU.mult),
             reads=[pk, nw_key, gate_key], writes=[out_key] + (list(extra_w) if c == 0 else []))


def gla_rmask():
    t = np.arange(512)
    m = np.zeros((64, 2, 512), np.float32)
    m[:, 0, :] = (t % 64 != 0)[None, :]
    m[:, 1, :] = (t % 64 != 63)[None, :]
    return m


def build_gla(n_ctx, n_lat):
    P = Prog()
    N = n_ctx + n_lat
    NCN = N // 64
    NCC = n_ctx // 64
    hT = P.dram("hT", [D, N], BF16, "ExternalInput")
    ident_d = P.dram("ident", [128, 128], F32, "ExternalInput")
    w_qk = P.dram("w_qk", [128, NCH * 128], F32, "ExternalInput")
    w_go = P.dram("w_go", [128, NCH * 128], F32, "ExternalInput")
    w_lr = P.dram("w_lr", [128, NCH * 32], F32, "ExternalInput")
    w_v = P.dram("w_v", [128, NCH * 128], F32, "ExternalInput")
    w2_d = P.dram("w2", [16, 2 * 64], F32, "ExternalInput")
    nb_d = P.dram("nb", [64, 2], F32, "ExternalInput")
    gn_d = P.dram("gn", [128, 1], F32, "ExternalInput")
    masks_d = P.dram("masks", [64, 2, 64], F32, "ExternalInput")
    rmask_d = P.dram("rmask", [64, 2, 512], F32, "ExternalInput")
    yT = P.dram("yT", [128, N], BF16, "ExternalOutput")

    C = Ctx(P, ident_d)
    H = HStream(P, hT, N)
    wqk = P.sbuf("wqk", [128, NCH, 128], BF16)
    wgo = P.sbuf("wgo", [128, NCH, 128], BF16)
    wlr = P.sbuf("wlr", [128, NCH, 32], BF16)
    wv = P.sbuf("wv", [128, NCH, 128], BF16)
    w2 = P.sbuf("w2", [16, 128], BF16)
    P.dma("pool", wqk[:, :, :], w_qk[:, :], writes=["wqk"])
    P.dma("pool", wgo[:, :, :], w_go[:, :], writes=["wgo"])
    P.dma("pool", wlr[:, :, :], w_lr[:, :], writes=["wlr"])
    P.dma("pool", wv[:, :, :], w_v[:, :], writes=["wv"])
    P.dma("pool", w2[:, :], w2_d[:, :], writes=["w2"])
    nb = P.sbuf("nb", [64, 2], F32)
    gn = P.sbuf("gn", [128, 1], F32)
    masks = P.sbuf("masks", [64, 2, 64], F32)
    rmask = P.sbuf("rmask", [64, 2, 512], F32)
    P.dma("sp", nb[:, :], nb_d[:, :], writes=["nb"])
    P.dma("sp", gn[:, :], gn_d[:, :], writes=["gn"])
    P.dma("sp", masks[:, :, :], masks_d[:, :, :], writes=["masks"])
    P.dma("sp", rmask[:, :, :], rmask_d[:, :, :], writes=["rmask"])

    qt = P.sbuf("qt", [64, 2, N], BF16)
    kt = P.sbuf("kt", [64, 2, N], BF16)
    ktok = P.sbuf("ktok", [64, 2, NCN, 64], BF16)
    dec = P.sbuf("dec", [64, 2, NCN], F32)
    goT = P.sbuf("goT", [128, N], BF16)
    v64 = P.sbuf("v64", [64, NCN, 128], BF16)
    vT32 = P.sbuf("vT32", [128, 512], F32)
    hacc_t = P.sbuf("hacc", [64, NCN * 128], F32)
    hacc = hacc_t[:, :].rearrange("p (c e) -> p c e", e=128)
    youT = H.buf[:, :, :, :].rearrange("p a k n -> p (a k n)")[:, 0:N]
    hbkeys = ["hb%d_%d" % (s_, k_) for s_ in range(2) for k_ in range(NCH)]
    qf = P.sbuf("qf", [64, 2, 512], F32)
    kf = P.sbuf("kf", [64, 2, 512], F32)
    lrb = P.sbuf("lrb", [16, 2, 2, 512], BF16)
    T1 = P.sbuf("T1", [64, 2, 512], F32)
    Lg = T1
    Gp = P.sbuf("Gp", [64, 2, 512], F32)
    E1 = P.sbuf("E1", [64, 2, 512], F32)
    E2 = P.sbuf("E2", [64, 2, 512], F32)
    ktmp = P.sbuf("ktmp", [64, 2, 512], F32)
    khf = Gp

    def stage_a(bi, c0, n):
        s, keys = H.load(c0, n)
        nch = n // 64
        cb0 = c0 // 64
        sl2 = bi % 2
        for g, (dst, dkey) in enumerate(((qf[:, sl2, :], "qf%d" % sl2), (kf[:, sl2, :], "kf%d" % sl2))):
            b = C.bank((0, 4), 1)
            pap, pk = emit_proj_fm(P, C, H, s, keys, wqk[:, :, g * 64:(g + 1) * 64], "wqk", 64, n, b)
            sc_ = 0.125 if g == 0 else 1.0
            P.op("act", lambda e, pap=pap, dst=dst, sc_=sc_: e.activation(out=dst[:, 0:n], in_=pap, func=AF.Copy, scale=sc_),
                 reads=[pk], writes=[dkey])
        b = C.bank((0, 4), 1)
        pap, pk = emit_proj_fm(P, C, H, s, keys, wgo, "wgo", 128, n, b)
        P.op("act", lambda e, pap=pap: e.activation(out=goT[:, c0:c0 + n], in_=pap, func=AF.Silu), reads=[pk], writes=["goT"])
        for d in range(2):
            b = C.bank((0, 4), 1)
            pap, pk = emit_proj_fm(P, C, H, s, keys, wlr[:, :, d * 16:(d + 1) * 16], "wlr", 16, n, b)
            P.op("dve", lambda e, pap=pap, d=d: e.tensor_copy(out=lrb[:, sl2, d, 0:n], in_=pap), reads=[pk], writes=["lrb%d_%d" % (sl2, d)])
        emit_v64(P, C, H, s, keys, wv, "wv", c0, n, v64, vT32)

    def stage_b(bi, c0, n):
        nch = n // 64
        cb0 = c0 // 64
        sl2 = bi % 2
        for d in range(2):
            dk = str(d)
            b = C.bank((0, 4), 1)
            pk = "ps%d" % b
            P.op("pe", lambda e, b=b, d=d: e.matmul(C.ps[b][0:64, 0:n], lhsT=w2[:, d * 64:(d + 1) * 64], rhs=lrb[:, sl2, d, 0:n], start=True, stop=True),
                 reads=["w2", "lrb%d_%d" % (sl2, d)], writes=[pk])
            P.op("act", lambda e, b=b, d=d: e.activation(out=T1[:, d, 0:n], in_=C.ps[b][0:64, 0:n], func=AF.Exp, scale=-1.0, bias=nb[:, d:d + 1]),
                 reads=[pk, "nb"], writes=["T1" + dk])
            P.op("act", lambda e, d=d: e.activation(out=Lg[:, d, 0:n], in_=T1[:, d, 0:n], func=AF.Ln, bias=1.0), reads=["T1" + dk], writes=["T1" + dk])
            rvv = (lambda ap: ap[:, ::-1]) if d == 1 else (lambda ap: ap)
            P.op("dve", lambda e, d=d, rvv=rvv: e.tensor_tensor_scan(out=rvv(Gp[:, d, 0:n]), data0=rvv(rmask[:, d, 0:n]), data1=rvv(Lg[:, d, 0:n]),
                                                                   initial=0.0, op0=ALU.mult, op1=ALU.add),
                 reads=["T1" + dk, "rmask"], writes=["Gp" + dk])
            P.op("act", lambda e, d=d: e.activation(out=E1[:, d, 0:n], in_=Gp[:, d, 0:n], func=AF.Exp, scale=-1.0 / 16.0), reads=["Gp" + dk], writes=["E1" + dk])
            P.op("act", lambda e, d=d: e.activation(out=E2[:, d, 0:n], in_=Gp[:, d, 0:n], func=AF.Exp, scale=1.0 / 16.0), reads=["Gp" + dk], writes=["E2" + dk])
            P.op("dve", lambda e, d=d: e.tensor_tensor(out=qt[:, d, c0:c0 + n], in0=qf[:, sl2, 0:n], in1=E1[:, d, 0:n], op=ALU.mult),
                 reads=["qf%d" % sl2, "E1" + dk], writes=["qt" + dk])
            P.op("pool", lambda e, d=d: e.tensor_tensor(out=ktmp[:, d, 0:n], in0=kf[:, sl2, 0:n], in1=E2[:, d, 0:n], op=ALU.mult),
                 reads=["kf%d" % sl2, "E2" + dk], writes=["ktmp" + dk])
            P.op("pool", lambda e, d=d: e.tensor_copy(out=kt[:, d, c0:c0 + n], in_=ktmp[:, d, 0:n]), reads=["ktmp" + dk], writes=["kt" + dk])
            endc = 0 if d == 1 else 63
            e3 = E1[:, d, 0:n].rearrange("p (c l) -> p c l", l=64)[:, :, endc:endc + 1]
            P.op("dve", lambda e, d=d, e3=e3: e.tensor_copy(out=dec[:, d, cb0:cb0 + nch].unsqueeze(2), in_=e3), reads=["E1" + dk], writes=["dec" + dk])
            P.op("dve", lambda e, d=d, e3=e3: e.tensor_tensor(out=khf[:, d, 0:n].rearrange("p (c l) -> p c l", l=64),
                                                             in0=ktmp[:, d, 0:n].rearrange("p (c l) -> p c l", l=64),
                                                             in1=e3.to_broadcast([64, nch, 64]), op=ALU.mult),
                 reads=["ktmp" + dk, "E1" + dk], writes=["Gp" + dk])

    def stage_c(bi, c0, n):
        nch = n // 64
        cb0 = c0 // 64
        for d in range(2):
            dk = str(d)
            for t in range(nch):
                b = C.bank((4, 4), 1)
                pk = "ps%d" % b
                ci = cb0 + t
                P.op("pe", lambda e, b=b, d=d, t=t: e.transpose(C.ps[b][0:64, 0:64], khf[:, d, t * 64:(t + 1) * 64], C.ident[0:64, 0:64]),
                     reads=["Gp" + dk, "ident"], writes=[pk])
                P.op("act", lambda e, b=b, d=d, ci=ci: e.activation(out=ktok[:, d, ci, :], in_=C.ps[b][0:64, 0:64], func=AF.Copy),
                     reads=[pk], writes=["ktok%d_%d" % (d, ci)])

    blks = token_blocks(N)
    nb_ = len(blks)
    for bi in range(nb_ + 2):
        if bi < nb_:
            stage_a(bi, blks[bi][0], blks[bi][1])
        if 2 <= bi:
            stage_c(bi - 2, blks[bi - 2][0], blks[bi - 2][1])
        if 1 <= bi <= nb_:
            stage_b(bi - 1, blks[bi - 1][0], blks[bi - 1][1])

    Sst = P.sbuf("Sst", [64, 2, 128], F32)
    Sbf = P.sbuf("Sbf", [64, 2, 128], BF16)
    PT = P.sbuf("PT", [64, 4, 64], BF16)
    orders = [chunk_order(NCC, NCN, d) for d in range(2)]
    for d in range(2):
        P.op("pool", lambda e, d=d: e.memset(Sst[:, d, :], 0.0), writes=["Sst%d" % d])
        P.op("pool", lambda e, d=d: e.memset(Sbf[:, d, :], 0.0), writes=["Sbf%d" % d])
    it = 0
    hwritten = set()
    for step in range(NCN):
        for d in range(2):
            dk = str(d)
            c = orders[d][step]
            last = step + 1 >= NCN
            sl = slice(c * 64, (c + 1) * 64)
            j = it % 4
            it += 1
            b1 = C.bank((0, 3), 1)
            P.op("pe", lambda e, b1=b1, sl=sl, d=d: e.matmul(C.ps[b1][0:64, 0:64], lhsT=kt[:, d, sl], rhs=qt[:, d, sl], start=True, stop=True),
                 reads=["kt" + dk, "qt" + dk], writes=["ps%d" % b1])
            P.op("dve", lambda e, b1=b1, j=j, d=d: e.tensor_tensor(out=PT[:, j, :], in0=C.ps[b1][0:64, 0:64], in1=masks[:, d, :], op=ALU.mult),
                 reads=["ps%d" % b1, "masks"], writes=["PT%d" % j])
            b2 = C.bank((3, 3), 1)
            pk2 = "ps%d" % b2
            P.op("pe", lambda e, b2=b2, sl=sl, d=d: e.matmul(C.ps[b2][0:64, 0:128], lhsT=qt[:, d, sl], rhs=Sbf[:, d, :], start=True, stop=False),
                 reads=["qt" + dk, "Sbf" + dk], writes=[pk2], inc=False)
            P.op("pe", lambda e, b2=b2, j=j, c=c: e.matmul(C.ps[b2][0:64, 0:128], lhsT=PT[:, j, :], rhs=v64[:, c, :], start=False, stop=True),
                 reads=["PT%d" % j, "v64_%d" % c], writes=[pk2])
            hk = "hacc%d" % c
            if c not in hwritten:
                hwritten.add(c)
                P.op("act", lambda e, b2=b2, c=c: e.activation(out=hacc[:, c, :], in_=C.ps[b2][0:64, 0:128], func=AF.Copy), reads=[pk2], writes=[hk])
            else:
                P.op("dve", lambda e, b2=b2, c=c: e.tensor_tensor(out=hacc[:, c, :], in0=hacc[:, c, :], in1=C.ps[b2][0:64, 0:128], op=ALU.add),
                     reads=[pk2, hk], writes=[hk])
            if not last:
                b3 = C.bank((6, 2), 1)
                pk3 = "ps%d" % b3
                P.op("pe", lambda e, b3=b3, c=c, d=d: e.matmul(C.ps[b3][0:64, 0:128], lhsT=ktok[:, d, c, :], rhs=v64[:, c, :], start=True, stop=True),
                     reads=["ktok%d_%d" % (d, c), "v64_%d" % c], writes=[pk3])
                P.op("dve", lambda e, b3=b3, d=d, c=c: e.scalar_tensor_tensor(out=Sst[:, d, :], in0=Sst[:, d, :], scalar=dec[:, d, c:c + 1],
                                                                             in1=C.ps[b3][0:64, 0:128], op0=ALU.mult, op1=ALU.add),
                     reads=["Sst" + dk, "dec" + dk, pk3], writes=["Sst" + dk])
                P.op("pool", lambda e, d=d: e.tensor_copy(out=Sbf[:, d, :], in_=Sst[:, d, :]), reads=["Sst" + dk], writes=["Sbf" + dk])

    emit_head_finish(P, C, hacc, NCN, gn[:, 0:1], "gn", goT, "goT", youT, "youT", "g", extra_w=hbkeys)
    P.dma("sp", yT[:, :], youT, reads=["youT"])
    return P.finish()


def rope_tables(n_lat):
    rows = n_lat // 64
    row = np.repeat(np.arange(rows, dtype=np.float32), 64)
    col = np.tile(np.arange(64, dtype=np.float32), rows)
    half = 8
    inv_freq = (10000.0 ** (-np.arange(half, dtype=np.float32) / half)).astype(np.float32)
    ang_r = row[:, None] * inv_freq
    ang_c = col[:, None] * inv_freq
    ang = np.concatenate([ang_r, ang_r, ang_c, ang_c], axis=-1)
    return ang


def rope_consts(n_lat):
    rows = n_lat // 64
    row = np.repeat(np.arange(rows, dtype=np.float32), 64)
    col = np.tile(np.arange(64, dtype=np.float32), rows)
    half = 16 // 1 // 2 * 1
    half = 16
    inv_freq = (np.float32(10000.0) ** (-np.arange(half, dtype=np.float32) / np.float32(half))).astype(np.float32)
    ang_r = (row[:, None] * inv_freq).astype(np.float32)
    ang_c = (col[:, None] * inv_freq).astype(np.float32)
    ang = np.concatenate([ang_r, ang_r, ang_c, ang_c], axis=-1)
    cos = np.cos(ang).astype(np.float32)
    sin = np.sin(ang).astype(np.float32)
    sgn = np.ones(64, np.float32)
    perm = np.zeros(64, np.int64)
    for d in range(64):
        blk, i = d // 32, d % 32
        if i < 16:
            perm[d] = blk * 32 + i + 16
            sgn[d] = -1.0
        else:
            perm[d] = blk * 32 + i - 16
    cosT = np.concatenate([cos.T, cos.T], 0)
    sinT = np.concatenate([(sin * sgn[None]).T, (sin * sgn[None]).T], 0)
    pm = np.zeros((128, 128), np.float32)
    for m in range(2):
        for d in range(64):
            pm[m * 64 + perm[d], m * 64 + d] = 1.0
    return np.ascontiguousarray(cosT), np.ascontiguousarray(sinT), pm


def build_attn(n_ctx, n_lat, lam_init, need_ctx):
    P = Prog()
    N = n_ctx + n_lat
    NT = N // 128
    hT = P.dram("hT", [D, N], BF16, "ExternalInput")
    ident_d = P.dram("ident", [128, 128], F32, "ExternalInput")
    w_qk = P.dram("w_qk", [4, 128, NCH * 128], F32, "ExternalInput")
    w_v = P.dram("w_v", [128, NCH * 256], F32, "ExternalInput")
    cos_d = P.dram("cosT", [128, n_lat], F32, "ExternalInput")
    sin_d = P.dram("sinT", [128, n_lat], F32, "ExternalInput")
    pm_d = P.dram("pm", [128, 128], F32, "ExternalInput")
    dl_d = P.dram("dlam", [1, 256], F32, "ExternalInput")
    sub_d = P.dram("subln", [128, 1], F32, "ExternalInput")
    yT = P.dram("yT", [256, N], BF16, "ExternalOutput")

    C = Ctx(P, ident_d)
    H = HStream(P, hT, N)
    wqk = P.sbuf("wqk", [128, 4, NCH, 128], BF16)
    wv = P.sbuf("wv", [128, NCH, 256], BF16)
    for g in range(4):
        P.dma("pool", wqk[:, g, :, :], w_qk[g, :, :], writes=["wqk%d" % g])
    P.dma("pool", wv[:, :, :], w_v[:, :], writes=["wv"])
    cosT = P.sbuf("cosT", [128, n_lat], F32)
    sinT = P.sbuf("sinT", [128, n_lat], F32)
    pm = P.sbuf("pm", [128, 128], F32)
    dl = P.sbuf("dl", [128, 256], F32)
    sub = P.sbuf("sub", [128, 1], F32)
    P.dma("sp", cosT[:, :], cos_d[:, :], writes=["cosT"])
    P.dma("sp", sinT[:, :], sin_d[:, :], writes=["sinT"])
    P.dma("sp", pm[:, :], pm_d[:, :], writes=["pm"])
    P.dma("sp", dl[:, :], dl_d[0:1, :].to_broadcast([128, 256]), writes=["dl"])
    P.dma("sp", sub[:, :], sub_d[:, :], writes=["sub"])
    lt = P.sbuf("lt", [128, 8], F32)
    ltmp = P.sbuf("ltmp", [128, 128], F32)
    for i in range(2):
        P.op("dve", lambda e, i=i: e.tensor_tensor(out=ltmp[:, i * 64:(i + 1) * 64], in0=dl[:, 128 * i:128 * i + 64], in1=dl[:, 128 * i + 64:128 * i + 128], op=ALU.mult),
             reads=["dl"], writes=["ltmp"])
        P.op("dve", lambda e, i=i: e.reduce_sum(out=lt[:, i:i + 1], in_=ltmp[:, i * 64:(i + 1) * 64], axis=AX.X), reads=["ltmp"], writes=["lt"])
    P.op("act", lambda e: e.activation(out=lt[:, 2:4], in_=lt[:, 0:2], func=AF.Exp), reads=["lt"], writes=["lt"])
    P.op("dve", lambda e: e.tensor_tensor(out=lt[:, 4:5], in0=lt[:, 3:4], in1=lt[:, 2:3], op=ALU.subtract), reads=["lt"], writes=["lt"])
    P.op("dve", lambda e: e.tensor_scalar(out=lt[:, 5:6], in0=lt[:, 4:5], scalar1=-float(lam_init), scalar2=None, op0=ALU.add), reads=["lt"], writes=["lt"])
    neglam = lt[:, 5:6]
    P.op("dve", lambda e: e.tensor_scalar(out=lt[:, 6:7], in0=sub[:, 0:1], scalar1=1.0 - float(lam_init), scalar2=None, op0=ALU.mult),
         reads=["sub"], writes=["lt"])
    subs = lt[:, 6:7]

    qkT = P.sbuf("qkT", [128, 4, N], BF16)
    vd = P.sbuf("vd", [128, NT, 256], BF16)
    yst = P.sbuf("yst", [128, 2, N], BF16)
    t1 = P.sbuf("t1", [128, 512], F32)
    t2 = P.sbuf("t2", [128, 512], F32)
    onesb = P.sbuf("onesb", [128, 1], BF16)
    P.op("pool", lambda e: e.memset(onesb[:, :], 1.0), writes=["onesb"])

    xf4 = P.sbuf("xf4", [128, 2, 4, 512], F32)

    def stage_a(bi, c0, n):
        s, keys = H.load(c0, n)
        sl2 = bi % 2
        for g in range(4):
            b = C.bank((0, 4), 1)
            pap, pk = emit_proj_fm(P, C, H, s, keys, wqk[:, g, :, :], "wqk%d" % g, 128, n, b)
            sc_ = 0.125 if g < 2 else 1.0
            P.op("act", lambda e, pap=pap, sc_=sc_, g=g: e.activation(out=xf4[:, sl2, g, 0:n], in_=pap, func=AF.Copy, scale=sc_),
                 reads=[pk], writes=["xf%d_%d" % (sl2, g)])
        for t in range(n // 128):
            b = C.bank((4, 4), 1)
            pap, pk = emit_proj_tm(P, C, H, s, keys, wv, "wv", t * 128, 128, 256, b)
            ti = c0 // 128 + t
            P.op("dve", lambda e, pap=pap, ti=ti: e.tensor_copy(out=vd[:, ti, :], in_=pap), reads=[pk], writes=["vd%d" % ti])

    def stage_b(bi, c0, n):
        sl2 = bi % 2
        for g in range(4):
            xg = xf4[:, sl2, g, :]
            xk = "xf%d_%d" % (sl2, g)
            nc_ = max(0, min(n, n_ctx - c0))
            if nc_ > 0:
                P.op("dve", lambda e, g=g, nc_=nc_, xg=xg: e.tensor_copy(out=qkT[:, g, c0:c0 + nc_], in_=xg[:, 0:nc_]), reads=[xk], writes=["qkT%d" % g])
            if nc_ < n:
                l0 = c0 + nc_ - n_ctx
                nl = n - nc_
                b2 = C.bank((4, 4), 1)
                pk2 = "ps%d" % b2
                P.op("pe", lambda e, b2=b2, nc_=nc_, nl=nl, xg=xg: e.matmul(C.ps[b2][:, 0:nl], lhsT=pm[:, :], rhs=xg[:, nc_:nc_ + nl], start=True, stop=True),
                     reads=["pm", xk], writes=[pk2])
                P.op("dve", lambda e, nc_=nc_, nl=nl, l0=l0, xg=xg: e.tensor_tensor(out=t1[:, 0:nl], in0=xg[:, nc_:nc_ + nl], in1=cosT[:, l0:l0 + nl], op=ALU.mult),
                     reads=[xk, "cosT"], writes=["t1"])
                P.op("dve", lambda e, b2=b2, nl=nl, l0=l0: e.tensor_tensor(out=t2[:, 0:nl], in0=C.ps[b2][:, 0:nl], in1=sinT[:, l0:l0 + nl], op=ALU.mult),
                     reads=[pk2, "sinT"], writes=["t2"])
                P.op("pool", lambda e, g=g, nc_=nc_, nl=nl: e.tensor_tensor(out=qkT[:, g, c0 + nc_:c0 + n], in0=t1[:, 0:nl], in1=t2[:, 0:nl], op=ALU.add),
                     reads=["t1", "t2"], writes=["qkT%d" % g])

    blks = token_blocks(N)
    for bi in range(len(blks) + 1):
        if bi < len(blks):
            stage_a(bi, blks[bi][0], blks[bi][1])
        if bi >= 1:
            stage_b(bi - 1, blks[bi - 1][0], blks[bi - 1][1])

    Eb = P.sbuf("Eb", [128, 6, 512], BF16)
    Eacc = P.sbuf("Eacc", [128, 4, 512], F32)
    ones32 = P.sbuf("ones32", [128, 1], F32)
    P.op("pool", lambda e: e.memset(ones32[:, :], 1.0), writes=["ones32"])
    nsb = P.sbuf("nsb", [128, 2, 512], F32)
    drow = P.sbuf("drow", [1, 2, 512], F32)
    rc = P.sbuf("rc", [128, 4, 4], F32)
    hd = P.sbuf("hd", [128, 2, 128], F32)
    qblocks = []
    if need_ctx and n_ctx > 0:
        qblocks.append((0, n_ctx, 0, n_ctx // 128))
    for (c0, n) in token_blocks(n_lat):
        qblocks.append((n_ctx + c0, n, 0, NT))
    LOOK = 2
    SB = (0, 5)
    ACC = 6
    DEN0 = 5
    state = dict(ei=0, ri=0)
    pending = []

    def flush(keep):
        while len(pending) > keep:
            pending.pop(0)()

    def emit_scores(hh, q0, nq, ki, m):
        bs = C.bank(SB, 1)
        pks = "ps%d" % bs
        P.op("pe", lambda e: e.matmul(
            C.ps[bs][:, 0:nq], lhsT=qkT[64 * m:64 * m + 64, 2 + hh, ki * 128:(ki + 1) * 128],
            rhs=qkT[64 * m:64 * m + 64, hh, q0:q0 + nq], start=True, stop=True),
            reads=["qkT%d" % (2 + hh), "qkT%d" % hh], writes=[pks])
        ej = state["ei"] % 6
        state["ei"] += 1
        return bs, pks, ej

    def emit_exp(bs, pks, ej, nq):
        P.op("act", lambda e: e.activation(out=Eb[:, ej, 0:nq], in_=C.ps[bs][:, 0:nq], func=AF.Exp),
             reads=[pks], writes=["Eb%d" % ej])

    def emit_pv(hh, nq, ki, m, ej, first, lastk, kt0):
        P.op("pe", lambda e: e.matmul(
            C.ps[ACC + m][:, 0:nq], lhsT=vd[:, ki, hh * 128:(hh + 1) * 128], rhs=Eb[:, ej, 0:nq], start=first, stop=lastk),
            reads=["vd%d" % ki, "Eb%d" % ej], writes=["ps%d" % (ACC + m)])
        if m == 0:
            P.op("pe", lambda e: e.matmul(C.ps[DEN0][0:1, 0:nq], lhsT=onesb[:, 0:1], rhs=Eb[:, ej, 0:nq], start=first, stop=lastk),
                 reads=["onesb", "Eb%d" % ej], writes=["ps%d" % DEN0])
        else:
            a = ki % 2
            eng = "dve" if a == 0 else "pool"
            if ki - kt0 < 2:
                P.op(eng, lambda e: e.tensor_copy(out=Eacc[:, a, 0:nq], in_=Eb[:, ej, 0:nq]), reads=["Eb%d" % ej], writes=["Eacc%d" % a])
            else:
                P.op(eng, lambda e: e.tensor_tensor(out=Eacc[:, a, 0:nq], in0=Eacc[:, a, 0:nq], in1=Eb[:, ej, 0:nq], op=ALU.add),
                     reads=["Eb%d" % ej, "Eacc%d" % a], writes=["Eacc%d" % a])

    def emit_finish(hh, q0, nq, na):
        for m in range(2):
            P.op("act", lambda e, m=m: e.activation(out=nsb[:, m, 0:nq], in_=C.ps[ACC + m][:, 0:nq], func=AF.Copy),
                 reads=["ps%d" % (ACC + m)], writes=["nsb%d" % m])
        P.op("dve", lambda e: e.tensor_copy(out=drow[:, 0, 0:nq], in_=C.ps[DEN0][0:1, 0:nq]), reads=["ps%d" % DEN0], writes=["drow0"])
        bd = C.bank(SB, 1)
        for a in range(na):
            P.op("pe", lambda e, a=a, bd=bd: e.matmul(C.ps[bd][0:1, 0:nq], lhsT=ones32[:, 0:1], rhs=Eacc[:, a, 0:nq], start=(a == 0), stop=(a == na - 1)),
                 reads=["ones32", "Eacc%d" % a], writes=["ps%d" % bd], inc=(a == na - 1))
        P.op("dve", lambda e, bd=bd: e.tensor_copy(out=drow[:, 1, 0:nq], in_=C.ps[bd][0:1, 0:nq]), reads=["ps%d" % bd], writes=["drow1"])
        b = C.bank(SB, 1)
        pk = "ps%d" % b
        for qs in range(nq // 128):
            qsl = slice(qs * 128, (qs + 1) * 128)
            for m in range(2):
                P.op("pe", lambda e, m=m, qsl=qsl: e.transpose(C.ps[b][:, m * 128:(m + 1) * 128], nsb[:, m, qsl], C.ident[:, :]),
                     reads=["nsb%d" % m, "ident"], writes=[pk], inc=False)
            for m in range(2):
                P.op("pe", lambda e, m=m, qsl=qsl: e.transpose(C.ps[b][:, 256 + m:257 + m], drow[:, m, qsl], C.ident[0:1, 0:1]),
                     reads=["drow%d" % m, "ident"], writes=[pk], inc=(m == 1))
            rj = state["ri"] % 4
            state["ri"] += 1
            rk = "rc%d" % rj
            P.op("dve", lambda e, rj=rj: e.reciprocal(out=rc[:, rj, 0:2], in_=C.ps[b][:, 256:258]), reads=[pk], writes=[rk])
            P.op("dve", lambda e, rj=rj: e.tensor_scalar(out=rc[:, rj, 2:3], in0=rc[:, rj, 1:2], scalar1=neglam, scalar2=None, op0=ALU.mult),
                 reads=[rk, "lt"], writes=[rk])
            hj = rj % 2
            hk_ = "hd%d" % hj
            P.op("act", lambda e, rj=rj, hj=hj: e.activation(out=hd[:, hj, :], in_=C.ps[b][:, 0:128], func=AF.Copy, scale=rc[:, rj, 0:1]),
                 reads=[pk, rk], writes=[hk_])
            P.op("dve", lambda e, rj=rj, hj=hj: e.scalar_tensor_tensor(out=hd[:, hj, :], in0=C.ps[b][:, 128:256], scalar=rc[:, rj, 2:3],
                                                                       in1=hd[:, hj, :], op0=ALU.mult, op1=ALU.add),
                 reads=[pk, rk, hk_], writes=[hk_])
            c = C.statcol(2)
            P.op("act", lambda e, hj=hj, c=c: e.activation(out=C.junk[:, 0:128], in_=hd[:, hj, :], func=AF.Square, accum_out=C.stat[:, c:c + 1]),
                 reads=[hk_], writes=["junk", "stat%d" % c])
            emit_rstd(P, C, C.stat[:, c:c + 1], "stat%d" % c, 128, C.stat[:, c + 1:c + 2], "stat%d" % (c + 1), 128)
            P.op("act", lambda e, hj=hj, c=c: e.activation(out=hd[:, hj, :], in_=hd[:, hj, :], func=AF.Copy, scale=C.stat[:, c + 1:c + 2]),
                 reads=[hk_, "stat%d" % (c + 1)], writes=[hk_])
            P.op("pe", lambda e, hj=hj: e.transpose(C.ps[b][:, 0:128], hd[:, hj, :], C.ident[:, :]), reads=[hk_, "ident"], writes=[pk])
            P.op("dve", lambda e, qs=qs: e.tensor_scalar(out=yst[:, hh, q0 + qs * 128:q0 + (qs + 1) * 128], in0=C.ps[b][:, 0:128],
                                                         scalar1=subs, scalar2=None, op0=ALU.mult),
                 reads=[pk, "lt"], writes=["yst%d" % hh])

    for hh in range(2):
        for (q0, nq, kt0, kt1) in qblocks:
            for ki in range(kt0, kt1):
                sc = [emit_scores(hh, q0, nq, ki, m) for m in range(2)]
                for m in range(2):
                    emit_exp(sc[m][0], sc[m][1], sc[m][2], nq)

                def unit(hh=hh, nq=nq, ki=ki, sc=sc, kt0=kt0, kt1=kt1):
                    for m in range(2):
                        emit_pv(hh, nq, ki, m, sc[m][2], ki == kt0, ki == kt1 - 1, kt0)
                pending.append(unit)
                flush(LOOK)
            pending.append(lambda hh=hh, q0=q0, nq=nq, na=min(2, kt1 - kt0): emit_finish(hh, q0, nq, na))
    flush(0)
    for hh in range(2):
        if not (need_ctx and n_ctx > 0) and n_ctx > 0:
            P.op("pool", lambda e, hh=hh: e.memset(yst[:, hh, 0:n_ctx], 0.0), writes=["yst%d" % hh])
        P.dma("sp", yT[hh * 128:(hh + 1) * 128, :], yst[:, hh, :], reads=["yst%d" % hh])
    return P.finish()


MODC = 6 * D // NCORES


def build_mod():
    P = Prog()
    cT_d = P.dram("cT", [128, NCH * 3], F32, "ExternalInput")
    wm = P.dram("wm", [DEPTH, 128, NCH * MODC], F32, "ExternalInput")
    bm = P.dram("bm", [DEPTH, MODC], F32, "ExternalInput")
    out = P.dram("mod", [DEPTH, 3, MODC], F32, "ExternalOutput")
    cT = P.sbuf("cT", [128, NCH, 3], F32)
    P.dma("sp", cT[:, :, :], cT_d[:, :], writes=["cT"])
    P.op("act", lambda e: e.activation(out=cT[:, :, :], in_=cT[:, :, :], func=AF.Silu), reads=["cT"], writes=["cT"])
    ps = [P.psum("ps%d" % i, [128, 512], F32) for i in range(2)]
    wt = P.sbuf("wt", [128, 2, NCH, 512], F32)
    bt = P.sbuf("bt", [3, 2, 512], F32)
    ot = P.sbuf("ot", [3, 2, 512], F32)
    i = 0
    for l in range(DEPTH):
        wl = wm[l, :, :].rearrange("p (k m) -> p k m", m=MODC)
        for nb in range(MODC // 512):
            s = i % 2
            i += 1
            for k in range(NCH):
                P.dma("sp", wt[:, s, k, :], wl[:, k, nb * 512:(nb + 1) * 512], writes=["wt%d_%d" % (s, k)])
            P.dma("sp", bt[:, s, :], bm[l:l + 1, nb * 512:(nb + 1) * 512].to_broadcast([3, 512]), writes=["bt%d" % s])
            for k in range(NCH):
                P.op("pe", lambda e, s=s, k=k: e.matmul(ps[s][0:3, :], lhsT=cT[:, k, :], rhs=wt[:, s, k, :], start=(k == 0), stop=(k == NCH - 1)),
                     reads=["cT", "wt%d_%d" % (s, k)], writes=["ps%d" % s], inc=(k == NCH - 1))
            P.op("dve", lambda e, s=s: e.tensor_tensor(out=ot[:, s, :], in0=ps[s][0:3, :], in1=bt[:, s, :], op=ALU.add),
                 reads=["ps%d" % s, "bt%d" % s], writes=["ot%d" % s])
            P.dma("sp", out[l, :, nb * 512:(nb + 1) * 512], ot[:, s, :], reads=["ot%d" % s])
    return P.finish()


OFF = dict(m_q=0, m_k=512, m_v=1024, m_o=1536, m_g=2048, g_q=2064, g_k=2320, g_v=2576, g_out=3088, g_lr=3600,
           d_q=3632, d_k=4656, d_v=5680)


def _relay(w):
    K_, M = w.shape[0] // 128, w.shape[1]
    return np.ascontiguousarray(w.reshape(K_, 128, M).transpose(1, 0, 2).reshape(128, K_ * M))


def _relay_chunks(w):
    K_, J = w.shape[0] // 128, w.shape[1] // 128
    return np.ascontiguousarray(w.reshape(K_, 128, J, 128).transpose(2, 1, 0, 3).reshape(J, 128, K_ * 128))


def _cols16(v):
    v = np.asarray(v, np.float32).reshape(-1, NCH, 128)
    return np.ascontiguousarray(v.transpose(2, 0, 1))


_PROGS = {}
_DEBUG = None


def _prog(key, fn):
    if key not in _PROGS:
        _PROGS[key] = fn()
    return _PROGS[key]


def _run(nc, in_maps):
    res = run_bass_kernel_spmd(nc, in_maps, core_ids=list(range(NCORES)))
    return res.results


def kernel(x, c, ctx, c_ctx, w_mod, b_mod, norm_mix_pre, norm_mix_post, norm_ffn_pre, norm_ffn_post, w_in,
           mlstm_conv_w, mlstm_conv_b, mlstm_gate_b, mlstm_norm, gla_gate_w2, gla_gate_b, gla_norm,
           diff_lambda, diff_subln, w_out, w_ffn_gate, w_ffn_up, w_ffn_down):
    f32 = np.float32
    x = np.asarray(x, f32)
    ctx = np.asarray(ctx, f32)
    B = x.shape[0]
    ident = np.eye(128, dtype=f32)
    QT = SEQ // 4
    QC = CTX // 4

    cvec = np.concatenate([np.asarray(c, f32), np.asarray(c_ctx, f32)[None]], 0)
    cT = np.ascontiguousarray(cvec.reshape(3, NCH, 128).transpose(2, 1, 0).reshape(128, NCH * 3))
    w_mod = np.asarray(w_mod, f32)
    b_mod = np.asarray(b_mod, f32)
    maps = []
    for core in range(NCORES):
        cs = slice(core * MODC, (core + 1) * MODC)
        maps.append(dict(cT=cT, wm=np.stack([_relay(w_mod[l][:, cs]) for l in range(DEPTH)]),
                         bm=np.ascontiguousarray(b_mod[:, cs])))
    res = _run(_prog("mod", build_mod), maps)
    mod = np.concatenate([r["mod"] for r in res], axis=2)
    mod = mod.reshape(DEPTH, 3, 6, D)
    if _DEBUG is not None:
        _DEBUG["mod"] = mod

    def dense_maps(layer, xs_core, yT_core, do_c):
        la = min(layer + 1, DEPTH - 1) if do_c else layer
        norms = np.stack([np.asarray(a, f32)[layer] for a in (norm_mix_pre, norm_mix_post, norm_ffn_pre, norm_ffn_post)])
        ncols_src = norms.copy()
        ncols_src[0] = np.asarray(norm_mix_pre, f32)[la]
        ncols = _cols16(ncols_src)
        shared = {}
        if do_c:
            shared = dict(wo_r=_relay_chunks(np.asarray(w_out, f32)[layer]), wg_r=_relay_chunks(np.asarray(w_ffn_gate, f32)[layer]),
                          wu_r=_relay_chunks(np.asarray(w_ffn_up, f32)[layer]), wd_r=_relay_chunks(np.asarray(w_ffn_down, f32)[layer]),
                          normrows=norms)
        maps = []
        for core in range(NCORES):
            b = core // 4
            mrows = np.concatenate([mod[layer, b], mod[layer, 2]], 0)
            mcols_src = mrows.copy()
            mcols_src[0:2] = mod[la, b, 0:2]
            mcols_src[6:8] = mod[la, 2, 0:2]
            m = dict(x=xs_core[core], ident=ident, modcols=_cols16(mcols_src), normcols=ncols)
            if do_c:
                m.update(shared)
                m["modrows"] = np.ascontiguousarray(mrows)
                m["yT"] = yT_core[core]
            maps.append(m)
        return maps

    def gather_hT(res_list, n_ctx_core):
        out = []
        for b in range(B):
            hall = np.zeros((D, NTOK), dtype=ml_dtypes.bfloat16)
            for qq in range(4):
                h = res_list[b * 4 + qq]["hT_out"]
                hall[:, qq * QC:(qq + 1) * QC] = h[:, 0:QC]
                hall[:, CTX + qq * QT:CTX + (qq + 1) * QT] = h[:, QC:QC + QT]
            out.append(hall)
        return out

    xs_core = []
    for core in range(NCORES):
        b, qq = core // 4, core % 4
        xs_core.append(np.ascontiguousarray(np.concatenate([ctx[b, qq * QC:(qq + 1) * QC], x[b, qq * QT:(qq + 1) * QT]], 0)))
    res = _run(_prog("A", lambda: build_dense(QC, QT, False, True)), dense_maps(0, xs_core, None, False))
    hT_all = gather_hT(res, QC)
    if _DEBUG is not None:
        _DEBUG["hT0"] = hT_all

    w_in = np.asarray(w_in, f32)
    masks = mlstm_masks()
    tri = mlstm_tri(CTX // 64, NTOK // 64)
    rmask = gla_rmask()
    cosT, sinT, pm = rope_consts(SEQ)
    x_out = None
    for layer in range(DEPTH):
        last = layer == DEPTH - 1
        w = w_in[layer]
        lam_init = 0.8 - 0.6 * math.exp(-0.3 * layer)
        cw_all = np.asarray(mlstm_conv_w, f32)[layer]
        cb_all = np.asarray(mlstm_conv_b, f32)[layer]
        gb_all = np.asarray(mlstm_gate_b, f32)[layer]
        m_maps, g_maps, a_maps = [], [], []
        for core in range(NCORES):
            b, q = core // 4, core % 4
            cols = lambda name, a, n: w[:, OFF[name] + a:OFF[name] + a + n]
            cw = np.zeros((128, 8), f32)
            cw[:, 0:3] = cw_all[:, 128 * q:128 * q + 128].T
            cw[:, 3:6] = cw_all[:, 512 + 128 * q:512 + 128 * q + 128].T
            cw[:, 6] = cb_all[128 * q:128 * q + 128]
            cw[:, 7] = cb_all[512 + 128 * q:512 + 128 * q + 128]
            gidx = [OFF["m_g"] + i for i in (q, 4 + q, 8 + q, 12 + q)]
            m_maps.append(dict(
                hT=hT_all[b], ident=ident,
                w_fm=np.stack([_relay(cols("m_q", 128 * q, 128)), _relay(cols("m_k", 128 * q, 128)), _relay(cols("m_o", 128 * q, 128))]),
                w_g=_relay(w[:, gidx]), w_v=_relay(cols("m_v", 128 * q, 128)), cw=cw,
                gb=np.ascontiguousarray(gb_all[[q, 4 + q, 8 + q, 12 + q]].reshape(4, 1)),
                mn=np.ascontiguousarray(np.asarray(mlstm_norm, f32)[layer][128 * q:128 * q + 128].reshape(128, 1)),
                masks=masks, tri=tri))
            gw2 = np.asarray(gla_gate_w2, f32)[layer]
            gbb = np.asarray(gla_gate_b, f32)[layer]
            g_maps.append(dict(
                hT=hT_all[b], ident=ident,
                w_qk=_relay(np.concatenate([cols("g_q", 64 * q, 64), cols("g_k", 64 * q, 64)], 1)),
                w_go=_relay(cols("g_out", 128 * q, 128)), w_lr=_relay(cols("g_lr", 0, 32)), w_v=_relay(cols("g_v", 128 * q, 128)),
                w2=np.ascontiguousarray(np.concatenate([gw2[0][:, 64 * q:64 * q + 64], gw2[1][:, 64 * q:64 * q + 64]], 1)),
                nb=np.ascontiguousarray((gbb[:, 64 * q:64 * q + 64] * f32(-1.0)).T) if False else np.ascontiguousarray(np.negative(gbb[:, 64 * q:64 * q + 64]).T),
                gn=np.ascontiguousarray(np.asarray(gla_norm, f32)[layer][128 * q:128 * q + 128].reshape(128, 1)),
                masks=masks, rmask=rmask))
            a_maps.append(dict(
                hT=hT_all[b], ident=ident,
                w_qk=np.stack([_relay(cols("d_q", 256 * q, 128)), _relay(cols("d_q", 256 * q + 128, 128)),
                               _relay(cols("d_k", 256 * q, 128)), _relay(cols("d_k", 256 * q + 128, 128))]),
                w_v=_relay(cols("d_v", 256 * q, 256)), cosT=cosT, sinT=sinT, pm=pm,
                dlam=np.ascontiguousarray(np.asarray(diff_lambda, f32)[layer].reshape(1, 256)),
                subln=np.ascontiguousarray(np.asarray(diff_subln, f32)[layer].reshape(128, 1))))
        res_m = _run(_prog("mlstm", lambda: build_mlstm(CTX, SEQ)), m_maps)
        res_g = _run(_prog("gla", lambda: build_gla(CTX, SEQ)), g_maps)
        res_a = _run(_prog("attn%d" % layer, lambda: build_attn(CTX, SEQ, lam_init, not last)), a_maps)
        ymix = []
        for b in range(B):
            ym = np.zeros((D, NTOK), dtype=ml_dtypes.bfloat16)
            for q in range(4):
                core = b * 4 + q
                ym[128 * q:128 * q + 128] = res_m[core]["yT"]
                ym[512 + 128 * q:512 + 128 * q + 128] = res_g[core]["yT"]
                ym[1024 + 256 * q:1024 + 256 * q + 256] = res_a[core]["yT"]
            ymix.append(ym)
        if _DEBUG is not None:
            _DEBUG["ymix%d" % layer] = ymix
            if _DEBUG.get("stop_after_mix") == layer:
                return None
        n_ctx_core = 0 if last else QC
        yT_core = []
        for core in range(NCORES):
            b, qq = core // 4, core % 4
            parts = []
            if n_ctx_core:
                parts.append(ymix[b][:, qq * QC:(qq + 1) * QC])
            parts.append(ymix[b][:, CTX + qq * QT:CTX + (qq + 1) * QT])
            yT_core.append(np.ascontiguousarray(np.concatenate(parts, 1)))
        if last:
            xs_core = [np.ascontiguousarray(xc[xc.shape[0] - QT:]) for xc in xs_core]
        key = "C%d_%d" % (n_ctx_core, int(not last))
        res = _run(_prog(key, lambda: build_dense(n_ctx_core, QT, True, not last)), dense_maps(layer, xs_core, yT_core, True))
        xs_core = [r["x_out"] for r in res]
        if not last:
            hT_all = gather_hT(res, QC)
        if _DEBUG is not None:
            _DEBUG["xs%d" % layer] = xs_core
            _DEBUG["hT%d" % (layer + 1)] = hT_all
    out = np.zeros((B, SEQ, D), f32)
    for core in range(NCORES):
        b, qq = core // 4, core % 4
        out[b, qq * QT:(qq + 1) * QT] = xs_core[core][-QT:]
    return out
```
